# Optimizing a Trainium2 kernel written in Bass

```python
import math
import jax, jax.numpy as jnp
from jax import lax
import numpy as np

D_MODEL = 1024
BATCH = 32
SEQ = 2048
DEPTH = 4

N_MEM = 256
N_MIXERS = 2
N_A = (DEPTH + 1) // 2
N_B = DEPTH // 2
MIX_WIDTH = D_MODEL
MEM_HEADS = 4
MEM_HEAD_DIM = MIX_WIDTH // 4 // MEM_HEADS
MEM_WIDTH = MEM_HEADS * MEM_HEAD_DIM
MIXER_WIDTH = MIX_WIDTH - MEM_WIDTH

GLA_HEADS = 4
GLA_DV = MIXER_WIDTH // GLA_HEADS
GLA_DK = GLA_DV // 2
GLA_HK = GLA_HEADS * GLA_DK
GLA_GATE_RANK = 16
GLA_GATE_TAU = 16.0
GLA_CHUNK = 64
GLA_IN = 2 * GLA_HK + MIXER_WIDTH + GLA_GATE_RANK + MEM_WIDTH + MIX_WIDTH

DIL_GROUPS = ((128, 1), (512, 4), (2048, 16))
DIL_HEADS = 6
DIL_HEAD_DIM = MIXER_WIDTH // DIL_HEADS
DIL_IN = len(DIL_GROUPS) * 3 * MIXER_WIDTH + MEM_WIDTH + MIX_WIDTH

ROPE_THETA = 500000.0
ROPE_DIM = DIL_HEAD_DIM // 4
NORM_EPS = 1e-6

kernel_name = "hybrid_gla_dilated_memxattn"


def rmsnorm(x, w):
    xf = x.astype(jnp.float32)
    y = xf * lax.rsqrt(jnp.mean(xf * xf, axis=-1, keepdims=True) + NORM_EPS)
    return (y * w.astype(jnp.float32)).astype(x.dtype)


def partial_rope(x, pos):
    half = ROPE_DIM // 2
    inv = ROPE_THETA ** (-jnp.arange(half, dtype=jnp.float32) / half)
    ang = pos.astype(jnp.float32)[:, None] * inv[None, :]
    cos = jnp.cos(ang)[None, :, None, :]
    sin = jnp.sin(ang)[None, :, None, :]
    xr = x[..., :ROPE_DIM].astype(jnp.float32)
    x1, x2 = xr[..., :half], xr[..., half:]
    rot = jnp.concatenate([x1 * cos - x2 * sin, x1 * sin + x2 * cos], axis=-1).astype(x.dtype)
    return jnp.concatenate([rot, x[..., ROPE_DIM:]], axis=-1)


def memory_attention(q, mem_n, w_memkv):
    B, T, _ = q.shape
    kv = mem_n @ w_memkv
    k, v = jnp.split(kv, 2, axis=-1)
    q = q.reshape(B, T, MEM_HEADS, MEM_HEAD_DIM)
    k = k.reshape(B, N_MEM, MEM_HEADS, MEM_HEAD_DIM)
    v = v.reshape(B, N_MEM, MEM_HEADS, MEM_HEAD_DIM)
    s = jnp.einsum('bthd,bnhd->bhtn', q, k).astype(jnp.float32) * (MEM_HEAD_DIM ** -0.5)
    p = jax.nn.softmax(s, axis=-1).astype(v.dtype)
    o = jnp.einsum('bhtn,bnhd->bthd', p, v)
    return o.reshape(B, T, MEM_WIDTH)


def gla_chunked(q, k, v, log_a):
    B, T, H, Dk = q.shape
    Dv = v.shape[-1]
    C = GLA_CHUNK
    N = T // C
    f32 = jnp.float32
    q = (q.astype(f32) * (Dk ** -0.5)).reshape(B, N, C, H, Dk)
    k = k.astype(f32).reshape(B, N, C, H, Dk)
    v = v.astype(f32).reshape(B, N, C, H, Dv)
    b = jnp.cumsum(log_a.astype(f32).reshape(B, N, C, H, Dk), axis=2)
    b_last = b[:, :, -1:]
    q_in = q * jnp.exp(b)
    k_in = k * jnp.exp(-b)
    k_out = k * jnp.exp(b_last - b)
    A = jnp.einsum('bnihd,bnjhd->bnhij', q_in, k_in)
    causal = jnp.tril(jnp.ones((C, C), dtype=bool))
    A = jnp.where(causal, A, 0.0)
    o_intra = jnp.einsum('bnhij,bnjhe->bnihe', A, v)
    dS = jnp.einsum('bnjhd,bnjhe->nbhde', k_out, v)
    decay = jnp.exp(b_last[:, :, 0]).transpose(1, 0, 2, 3)

    def step(S, inp):
        dS_c, dec_c = inp
        return dec_c[..., None] * S + dS_c, S

    S0 = jnp.zeros((B, H, Dk, Dv), f32)
    _, S_prev = lax.scan(step, S0, (dS, decay))
    o_inter = jnp.einsum('bnihd,nbhde->bnihe', q_in, S_prev)
    return (o_intra + o_inter).reshape(B, T, H, Dv)


def gla_mixer(h, w_in, w_gate_up, b_gate, gla_norm_w):
    B, T, _ = h.shape
    proj = h @ w_in
    cuts = np.cumsum([GLA_HK, GLA_HK, MIXER_WIDTH, GLA_GATE_RANK, MEM_WIDTH]).tolist()
    q, k, v, g_low, q_mem, gate = jnp.split(proj, cuts, axis=-1)
    log_a = jax.nn.log_sigmoid((g_low @ w_gate_up + b_gate).astype(jnp.float32)) / GLA_GATE_TAU
    o = gla_chunked(q.reshape(B, T, GLA_HEADS, GLA_DK),
                    k.reshape(B, T, GLA_HEADS, GLA_DK),
                    v.reshape(B, T, GLA_HEADS, GLA_DV),
                    log_a.reshape(B, T, GLA_HEADS, GLA_DK))
    o = rmsnorm(o, gla_norm_w).astype(h.dtype).reshape(B, T, MIXER_WIDTH)
    return o, q_mem, gate


def dilated_group(q, k, v, window, dilation):
    B, T, H, Dh = q.shape
    r = dilation
    n = window // dilation
    L = T // r
    nb = -(-L // n)
    Lp = nb * n

    def phase_blocks(x):
        x = x.reshape(B, L, r, H, Dh).transpose(0, 2, 3, 1, 4)
        x = jnp.pad(x, ((0, 0), (0, 0), (0, 0), (0, Lp - L), (0, 0)))
        return x.reshape(B, r, H, nb, n, Dh)

    def with_prev(x):
        prev = jnp.pad(x, ((0, 0), (0, 0), (0, 0), (1, 0), (0, 0), (0, 0)))[:, :, :, :-1]
        return jnp.concatenate([prev, x], axis=4)

    qb = phase_blocks(q)
    kw = with_prev(phase_blocks(k))
    vw = with_prev(phase_blocks(v))
    s = jnp.einsum('bphjqd,bphjkd->bphjqk', qb, kw).astype(jnp.float32) * (Dh ** -0.5)
    qi = jnp.arange(n)[:, None] + n
    ki = jnp.arange(2 * n)[None, :]
    dist = qi - ki
    key_abs = jnp.arange(nb)[:, None, None] * n + ki[None] - n
    mask = (dist >= 0)[None] & (dist <= n)[None] & (key_abs >= 0)
    s = jnp.where(mask, s, -jnp.inf)
    m = jnp.max(s, axis=-1, keepdims=True)
    p = jnp.exp(s - m)
    den = jnp.sum(p, axis=-1, keepdims=True)
    o = jnp.einsum('bphjqk,bphjkd->bphjqd', (p / den).astype(v.dtype), vw)
    lse = (m + jnp.log(den))[..., 0]
    o = o.reshape(B, r, H, Lp, Dh)[:, :, :, :L].transpose(0, 3, 1, 2, 4).reshape(B, T, H, Dh)
    lse = lse.reshape(B, r, H, Lp)[..., :L].transpose(0, 3, 1, 2).reshape(B, T, H)
    return o, lse


def dilated_mixer(h, w_in):
    B, T, _ = h.shape
    pos = jnp.arange(T)
    outs, lses = [], []
    for g, (window, dil) in enumerate(DIL_GROUPS):
        base = g * 3 * MIXER_WIDTH
        qkv = (h @ w_in[:, base:base + 3 * MIXER_WIDTH]).reshape(B, T, 3, DIL_HEADS, DIL_HEAD_DIM)
        q = partial_rope(qkv[:, :, 0], pos)
        k = partial_rope(qkv[:, :, 1], pos)
        o, lse = dilated_group(q, k, qkv[:, :, 2], window, dil)
        outs.append(o)
        lses.append(lse)
    wts = jax.nn.softmax(jnp.stack(lses, axis=0), axis=0)
    o = jnp.einsum('gbth,gbthd->bthd', wts, jnp.stack(outs, axis=0).astype(jnp.float32))
    o = o.astype(h.dtype).reshape(B, T, MIXER_WIDTH)
    rest = h @ w_in[:, len(DIL_GROUPS) * 3 * MIXER_WIDTH:]
    q_mem, gate = jnp.split(rest, [MEM_WIDTH], axis=-1)
    return o, q_mem, gate


def setup_inputs(seed: int = 0) -> dict:
    key = jax.random.key(seed)
    ks = jax.random.split(key, 16)
    f32 = jnp.float32
    nrm = lambda k, shape, scale: jax.random.normal(k, shape, f32) * scale
    return {
        "x": nrm(ks[0], (BATCH, SEQ, D_MODEL), 1.0),
        "mem": nrm(ks[1], (BATCH, N_MEM, D_MODEL), 1.0),
        "mem_norm_w": 1.0 + nrm(ks[2], (D_MODEL,), 0.02),
        "norm_w": 1.0 + nrm(ks[3], (DEPTH, D_MODEL), 0.02),
        "w_memkv": nrm(ks[4], (DEPTH, D_MODEL, 2 * MEM_WIDTH), D_MODEL ** -0.5),
        "w_out": nrm(ks[5], (DEPTH, MIX_WIDTH, D_MODEL), MIX_WIDTH ** -0.5),
        "w_in_a": nrm(ks[6], (N_A, D_MODEL, GLA_IN), D_MODEL ** -0.5),
        "w_gate_up": nrm(ks[7], (N_A, GLA_GATE_RANK, GLA_HK), GLA_GATE_RANK ** -0.5),
        "b_gate": nrm(ks[8], (N_A, GLA_HK), 0.1),
        "gla_norm_w": 1.0 + nrm(ks[9], (N_A, GLA_DV), 0.02),
        "w_in_b": nrm(ks[10], (N_B, D_MODEL, DIL_IN), D_MODEL ** -0.5),
        "final_norm_w": 1.0 + nrm(ks[11], (D_MODEL,), 0.02),
    }


def reference(x, mem, mem_norm_w, norm_w, w_memkv, w_out, w_in_a, w_gate_up, b_gate,
              gla_norm_w, w_in_b, final_norm_w):
    mem_n = rmsnorm(mem, mem_norm_w)
    for i in range(DEPTH):
        h = rmsnorm(x, norm_w[i])
        j = i // N_MIXERS
        if i % N_MIXERS == 0:
            mix, q_mem, gate = gla_mixer(h, w_in_a[j], w_gate_up[j], b_gate[j], gla_norm_w[j])
        else:
            mix, q_mem, gate = dilated_mixer(h, w_in_b[j])
        mem_o = memory_attention(q_mem, mem_n, w_memkv[i])
        branch = jnp.concatenate([mix, mem_o], axis=-1) * jax.nn.silu(gate)
        x = x + branch @ w_out[i]
    return rmsnorm(x, final_norm_w)
```

```python
import os
import numpy as np
from contextlib import ExitStack
import concourse.bass as bass
import concourse.mybir as mybir
from concourse.bass_utils import run_bass_kernel_spmd

F32 = mybir.dt.float32
BF16 = mybir.dt.bfloat16
AF = mybir.ActivationFunctionType
ALU = mybir.AluOpType

NCORES = 8
SEQ = 2048
D = 1024
NMEM = 256
DIL = [1, 4, 16]
EPOCH = 12000
NSLOT = 3
SLOTW = 384
EPS = 1e-6
KSTAGE = int(os.environ.get('KSTAGE', '9'))
KSUB = int(os.environ.get('KSUB', '9'))
T0S = int(os.environ.get('T0S', '9'))


class Trk:
    __slots__ = ("w", "r", "rd", "excl")

    def __init__(self, excl=False):
        self.excl = excl
        self.w = None
        self.r = {}
        self.rd = []


class Eng:
    def __init__(self, name, h):
        self.name = name
        self.h = h
        self.count = 0
        self.pending = False
        self.sems = []
        self.waited = {}


class Queue:
    def __init__(self, name, h, nslots):
        self.name = name
        self.h = h
        self.sems = []
        self.uses = [0] * nslots
        self.k = 0
        self.waited = {}


class Kern:
    def __init__(self, nseq, nlayers, dbg=None):
        self.nseq = nseq
        self.nlayers = nlayers
        self.dry = True
        self.jobs = []
        self.dbg = dbg

    def setup_engines(self, es):
        nc = self.nc
        self.pe = Eng("pe", nc.tensor)
        self.act = Eng("act", nc.scalar)
        self.dve = Eng("dve", nc.vector)
        self.engs = {"pe": self.pe, "act": self.act, "dve": self.dve}
        self.qsp = Queue("sp", nc.sync, 8)
        self.qpl = Queue("pool", nc.gpsimd, 8)
        if not self.dry:
            for e in self.engs.values():
                nep = self.est_counts[e.name] // EPOCH + 1
                for i in range(nep):
                    e.sems.append(es.enter_context(nc.semaphore(f"s_{e.name}{i}")))
            for q in (self.qsp, self.qpl):
                for i in range(len(q.uses)):
                    q.sems.append(es.enter_context(nc.semaphore(f"q_{q.name}{i}")))

    def _tok_sem(self, tok):
        if tok[0] == "d":
            return tok[1], tok[2]
        e = self.engs[tok[0]]
        c = tok[1]
        ep = (c - 1) // EPOCH
        return e.sems[ep], c - ep * EPOCH

    def _need(self, waiter, tok, out):
        if tok is None:
            return
        if tok[0] == "d":
            key = ("d", tok[3])
            if waiter.waited.get(key, 0) >= tok[2]:
                return
            if out.get(key, (None, 0))[1] < tok[2]:
                out[key] = (tok, tok[2])
        else:
            key = tok[0]
            if waiter.waited.get(key, 0) >= tok[1]:
                return
            if out.get(key, (None, 0))[1] < tok[1]:
                out[key] = (tok, tok[1])

    def _collect(self, waiter, reads, writes, acc, own):
        out = {}
        for t in reads:
            if t.w is not None and not (t.w[0] == own and own == "pe"):
                self._need(waiter, t.w, out)
            if t.excl:
                for en, c in t.r.items():
                    if en != own:
                        self._need(waiter, (en, c), out)
        if not acc:
            for t in writes:
                if t.w is not None and not (t.w[0] == own and own == "pe"):
                    self._need(waiter, t.w, out)
                for en, c in t.r.items():
                    if not (en == own and own == "pe"):
                        self._need(waiter, (en, c), out)
                for dt_ in t.rd:
                    self._need(waiter, dt_, out)
        return out

    def _emit_waits(self, waiter, h, need):
        for key, (tok, v) in need.items():
            waiter.waited[key] = v
            if not self.dry:
                sem, val = self._tok_sem(tok)
                h.wait_ge(sem, val)

    def op(self, eng, fn, reads=(), writes=(), inc=True, acc=False):
        need = self._collect(eng, reads, writes, acc, eng.name)
        self._emit_waits(eng, eng.h, need)
        if inc:
            eng.count += 1
            eng.pending = False
            c = eng.count
        else:
            eng.pending = True
            c = eng.count + 1
        if not self.dry:
            ins = fn()
            if inc:
                ep = (c - 1) // EPOCH
                ins.then_inc(eng.sems[ep], 1)
        tok = (eng.name, c)
        for t in reads:
            if t.r.get(eng.name, 0) < c:
                t.r[eng.name] = c
        for t in writes:
            if acc:
                t.w = tok
            else:
                t.w = tok
                t.r = {}
                t.rd = []
        return tok

    def dma(self, q, out_ap, in_ap=None, reads=(), writes=()):
        pairs = out_ap if in_ap is None else [(out_ap, in_ap)]
        need = self._collect(q, reads, writes, False, q.name)
        self._emit_waits(q, q.h, need)
        slot = q.k % len(q.uses)
        q.k += 1
        prev = q.uses[slot] * 16
        q.uses[slot] += len(pairs)
        val = q.uses[slot] * 16
        if not self.dry:
            sem = q.sems[slot]
            if prev > 0:
                q.h.wait_ge(sem, prev)
            for (o, i) in pairs:
                q.h.dma_start(out=o, in_=i).then_inc(sem, 16)
            tok = ("d", sem, val, (q.name, slot))
        else:
            tok = ("d", None, val, (q.name, slot))
        for t in reads:
            t.rd.append(tok)
        for t in writes:
            t.w = tok
            t.r = {}
            t.rd = []
        return tok

    def barrier(self):
        for e in self.engs.values():
            assert not e.pending, e.name
        waiters = list(self.engs.values()) + [self.qsp, self.qpl]
        for w in waiters:
            need = {}
            for o in self.engs.values():
                if o is w or o.count == 0:
                    continue
                self._need(w, (o.name, o.count), need)
            self._emit_waits(w, w.h, need)

    def un(self, name):
        self.uid += 1
        return f"{name}_{self.uid}"

    def ps_alloc(self):
        return self.ps_free.pop(0)

    def ps_release(self, i):
        self.ps_free.append(i)

    def wnext(self, pieces):
        if self.dry:
            self.jobs.append(pieces)
            i = len(self.jobs) - 1
            return i % NSLOT, self.wtrk[i % NSLOT]
        i = self.wjob
        self.wjob += 1
        while self.wissued < min(i + NSLOT, len(self.jobs)):
            j = self.wissued
            s = j % NSLOT
            pairs = []
            for (name, idx, sc0, n, dc0) in self.jobs[j]:
                src = getattr(self, name + "_d")[idx, :, sc0:sc0 + n]
                pairs.append((self.wslot[s][:, :, dc0:dc0 + n], src.rearrange("(kc p) c -> p kc c", p=128)))
            self.dma(self.qpl, pairs, writes=[self.wtrk[s]])
            self.wissued += 1
        return i % NSLOT, self.wtrk[i % NSLOT]

    def drip(self, n=1):
        while n > 0 and self.t0q:
            self.t0q.pop(0)()
            n -= 1

    def flush_t0(self):
        while self.t0q:
            self.t0q.pop(0)()

    def mm(self, out, lhsT, rhs, start, stop, reads, writes, inc):
        nc = self.nc
        self.mmc += 1
        if self.mmc % 3 == 0:
            self.drip()
        return self.op(self.pe,
                       lambda: nc.tensor.matmul(out, lhsT, rhs, start=start, stop=stop,
                                                skip_group_check=True),
                       reads=reads, writes=writes, inc=inc, acc=not start)

    def proj_fm(self, bank, M, slot, strk, c0, T, ncols=512, pbase=0):
        out = self.ps[bank][pbase:pbase + M, 0:ncols]
        for kc in range(8):
            self.mm(out, self.wslot[slot][:, kc, c0:c0 + M],
                    self.hT[:, kc, T * 512:T * 512 + ncols],
                    start=(kc == 0), stop=(kc == 7),
                    reads=[strk, self.hT_t[T]], writes=[self.ps_t[bank]], inc=(kc == 7))

    def build(self):
        self.dry = True
        self._build_once()
        self.est_counts = {n: e.count for n, e in self.engs.items()}
        jobs = self.jobs
        self.dry = False
        self.jobs = jobs
        self._build_once()
        return self.nc

    def _build_once(self):
        nseq = self.nseq
        nc = bass.Bass("TRN2", target_bir_lowering=False)
        self.nc = nc
        self.wjob = 0
        self.wissued = 0
        dt = lambda name, shape, kind="ExternalInput", dtype=F32: nc.dram_tensor(name, shape, dtype, kind=kind).ap()
        self.x_d = dt("x", [nseq, SEQ, D])
        self.mem_d = dt("mem", [nseq, NMEM, D])
        self.memnw_d = dt("memnw_fm", [128, 8])
        self.normw_d = dt("normw_fm", [128, 4 * 8])
        self.wmemkv_d = dt("w_memkv", [4, D, 512])
        self.wout_d = dt("w_out", [4, D, D])
        self.wina_d = dt("w_in_a", [2, D, 2832])
        self.wgu_d = dt("w_gate_up", [2, 16, 384])
        self.bg_d = dt("bg_fm", [96, 2 * 4])
        self.gnw_d = dt("gnw_fm", [128, 2 * 6])
        self.winb_d = dt("w_in_b", [2, D, 8192])
        self.fnw_d = dt("fnw_bc", [128, D])
        self.c_ident_d = dt("c_ident", [128, 128])
        self.c_ones_d = dt("c_ones", [128, 128])
        self.c_mask2_d = dt("c_mask2", [128, 256])
        self.c_gmask_d = dt("c_gmask", [128, 128])
        self.c_rmask_d = dt("c_rmask", [128, 512])
        self.c_cos_d = dt("c_cos", [128, 3 * 16 * 16])
        self.c_sin_d = dt("c_sin", [128, 3 * 16 * 16])
        self.out_d = dt("out", [nseq, SEQ, D], kind="ExternalOutput")
        if self.dbg:
            self.dbg_d = {k: dt("dbg_" + k, shp, kind="ExternalOutput") for k, shp in self.dbg.items()}

        with ExitStack() as es:
            self.setup_engines(es)
            sb = lambda name, shape, dtype: es.enter_context(nc.sbuf_tensor(name, shape, dtype))
            self.x = sb("x_sb", [128, 16, D], F32)
            self.x_t = [Trk() for _ in range(16)]
            self.hT = sb("hT", [128, 8, SEQ], BF16)
            self.hT_t = [Trk() for _ in range(4)]
            self.preT = sb("preT", [128, 8, SEQ], BF16)
            self.preT_t = [[Trk() for _ in range(4)] for _ in range(8)]
            self.memT = sb("memT", [128, 8, NMEM], BF16)
            self.memT_t = Trk()
            self.kmT = sb("kmT", [128, 2, NMEM], BF16)
            self.kmT_t = Trk()
            self.vm = sb("vm", [128, 2, 256], BF16)
            self.vm_t = Trk()
            self.wslot = [sb(f"wslot{i}", [128, 8, SLOTW], BF16) for i in range(NSLOT)]
            self.wtrk = [Trk() for _ in range(NSLOT)]
            self.ps = [es.enter_context(nc.psum_tensor(f"ps{i}", [128, 512], F32)) for i in range(8)]
            self.ps_t = [Trk(excl=True) for _ in range(8)]
            self.ps_free = list(range(7))
            self.ident = sb("ident", [128, 128], BF16)
            self.mask2 = sb("mask2", [128, 256], BF16)
            self.gmask = sb("gmask", [128, 128], BF16)
            self.ones = sb("ones", [128, 128], BF16)
            self.cos = sb("cos", [128, 3, 16, 16], F32)
            self.sin = sb("sin", [128, 3, 16, 16], F32)
            self.memnw = sb("memnw", [128, 8], F32)
            self.normw = sb("normw", [128, 4, 8], F32)
            self.bg = sb("bg", [96, 8], F32)
            self.nbg = sb("nbg", [96, 8], F32)
            self.gnw = sb("gnw", [128, 12], F32)
            self.const_t = Trk()
            self.ident32 = sb("ident32", [128, 128], F32)
            self.ones32 = sb("ones32", [128, 128], F32)
            self.w32 = [sb(f"w32_{i}", [128, 8, 128], F32) for i in range(2)]
            self.w32_t = [Trk() for _ in range(2)]
            self.w32_n = 0
            self.t0_prev_pos = None
            self.kmT32 = sb("kmT32", [128, 2, NMEM], F32)
            self.vm32 = sb("vm32", [128, 2, 256], F32)
            self.km32_t = Trk()
            self.vm32_t = Trk()
            self.x0T = sb("x0T", [128, 8], F32)
            self.h0T = sb("h0T", [128, 8], F32)
            self.m0T = sb("m0T", [128, 8], F32)
            self.sg0 = sb("sg0", [128, 8], F32)
            self.t0s = sb("t0s", [128, 32], F32)
            self.e8 = sb("e8", [128, 8], F32)
            self.x0_t = Trk()
            self.h0_t = Trk()
            self.m0_t = Trk()
            self.t0s_t = Trk()
            self.tb = 7
            self.tbc = 0
            self.t0q = []
            self.mmc = 0
            self.junk_t = Trk()
            self.ss16 = sb("ss16", [128, 16], F32)
            self.rstd16 = sb("rstd16", [128, 16], F32)
            self.st_t = Trk()
            self.ss_t = [Trk() for _ in range(16)]
            self.uid = 0

            self.load_consts()
            for si in range(nseq):
                self.do_seq(si)
            if not self.dry:
                for tok in self.out_toks:
                    key = ("d", tok[3])
                    if self.qsp.waited.get(key, 0) < tok[2]:
                        self.qsp.waited[key] = tok[2]
                        nc.sync.wait_ge(tok[1], tok[2])

    def load_consts(self):
        nc = self.nc
        toks = []
        D_ = lambda q, o, i: toks.append(self.dma(q, o, i))
        D_(self.qpl, self.ident[:], self.c_ident_d)
        D_(self.qsp, self.ident32[:], self.c_ident_d)
        D_(self.qsp, self.ones32[:], self.c_ones_d)
        D_(self.qpl, self.mask2[:], self.c_mask2_d)
        D_(self.qpl, self.gmask[:], self.c_gmask_d)
        D_(self.qsp, self.cos[:].rearrange("p a b c -> p (a b c)"), self.c_cos_d)
        D_(self.qsp, self.sin[:].rearrange("p a b c -> p (a b c)"), self.c_sin_d)
        D_(self.qsp, self.memnw[:], self.memnw_d)
        D_(self.qsp, self.normw[:].rearrange("p a b -> p (a b)"), self.normw_d)
        D_(self.qsp, self.bg[:], self.bg_d)
        D_(self.qsp, self.gnw[:], self.gnw_d)
        for e in self.engs.values():
            need = {}
            for tok in toks:
                self._need(e, tok, need)
            self._emit_waits(e, e.h, need)
        self.op(self.dve, lambda: nc.vector.memset(self.ones[:], 1.0), writes=[self.const_t])
        self.op(self.dve, lambda: nc.vector.tensor_scalar(self.nbg[:], self.bg[:], -1.0, None, ALU.mult),
                writes=[self.const_t])
        self.out_toks = []

    def norm_to_fm(self, src_aps, src_trks, w_ap, dst, dst_trk_of, ntiles):
        nc = self.nc
        with ExitStack() as es:
            xs0 = es.enter_context(nc.sbuf_tensor(self.un("xs"), [128, D], BF16))
            self.xs = [xs0, xs0]
            t_ = Trk()
            self.xs_t = [t_, t_]
            self.junk = es.enter_context(nc.sbuf_tensor(self.un("junk"), [128, D], BF16))
            self.junk_t = Trk()
            self._norm_to_fm(src_aps, src_trks, w_ap, dst, dst_trk_of, ntiles)
            self.barrier()

    def _norm_to_fm(self, src_aps, src_trks, w_ap, dst, dst_trk_of, ntiles):
        nc = self.nc
        for i in range(ntiles):
            self.op(self.act, lambda i=i: nc.scalar.activation(
                out=self.junk[:], in_=src_aps[i], func=AF.Square, accum_out=self.ss16[:, i:i + 1]),
                reads=[src_trks[i]], writes=[self.ss_t[i], self.junk_t])
        self.op(self.act, lambda: nc.scalar.activation(
            out=self.rstd16[:, 0:ntiles], in_=self.ss16[:, 0:ntiles], func=AF.Ln, scale=1.0 / D, bias=EPS),
            reads=self.ss_t[0:ntiles], writes=[self.st_t])
        self.op(self.act, lambda: nc.scalar.activation(
            out=self.rstd16[:, 0:ntiles], in_=self.rstd16[:, 0:ntiles], func=AF.Exp, scale=-0.5),
            reads=[self.st_t], writes=[self.st_t])
        for i in range(ntiles):
            b = i % 2
            self.op(self.dve, lambda i=i, b=b: nc.vector.tensor_scalar(
                self.xs[b][:], src_aps[i], self.rstd16[:, i:i + 1], None, ALU.mult),
                reads=[src_trks[i], self.st_t], writes=[self.xs_t[b]])
            bank = self.ps_alloc()
            pb = self.ps[bank][:].bitcast(BF16)
            for kc in range(8):
                self.op(self.pe, lambda kc=kc, b=b, pb=pb: nc.tensor.transpose(
                    pb[:, kc * 128:(kc + 1) * 128], self.xs[b][:, kc * 128:(kc + 1) * 128], self.ident[:]),
                    reads=[self.xs_t[b], self.const_t], writes=[self.ps_t[bank]], inc=(kc == 7), acc=(kc > 0))
            dtrk = dst_trk_of(i)
            self.op(self.dve, lambda i=i, pb=pb: nc.vector.tensor_tensor(
                dst[:, :, i * 128:(i + 1) * 128],
                pb.rearrange("p (k t) -> p k t", k=8),
                w_ap.unsqueeze(2).to_broadcast([128, 8, 128]), ALU.mult),
                reads=[self.ps_t[bank], self.const_t], writes=[dtrk])
            self.ps_release(bank)

    def do_seq(self, si):
        nc = self.nc
        for tt in range(16):
            self.dma(self.qsp, self.x[:, tt, :], self.x_d[si, tt * 128:(tt + 1) * 128, :], writes=[self.x_t[tt]])
        with ExitStack() as es:
            memraw = es.enter_context(nc.sbuf_tensor(self.un("memraw"), [128, 2, D], F32))
            mr_t = [Trk(), Trk()]
            for i in range(2):
                self.dma(self.qsp, memraw[:, i, :], self.mem_d[si, i * 128:(i + 1) * 128, :], writes=[mr_t[i]])
            self.norm_to_fm([memraw[:, i, :] for i in range(2)], mr_t, self.memnw[:], self.memT,
                            lambda i: self.memT_t, 2)
            self.barrier()
        self.t0_init()
        for li in range(self.nlayers):
            self.do_layer(si, li)
        self.final_norm(si)

    def do_layer(self, si, li):
        nc = self.nc
        j = li // 2
        self.norm_to_fm([self.x[:, tt, :] for tt in range(16)], self.x_t, self.normw[:, li, :], self.hT,
                        lambda i: self.hT_t[i // 4], 16)
        self.mem_kv(li)
        self.t0_layer(li)
        mode = self.t0_mode(li)
        if li % 2 == 0:
            self.mixer_a(j)
            wname, qm0, g0 = "wina", 1552, 1808
        else:
            self.mixer_b(j)
            wname, qm0, g0 = "winb", 6912, 7168
        if mode == "heads" and T0S >= 3:
            self.flush_t0()
            self.op(self.dve, lambda: nc.vector.tensor_copy(self.preT[:, 0:6, 0:1], self.m0T[:, 0:6].unsqueeze(2)),
                    reads=[self.m0_t], writes=[self.preT_t[c][0] for c in range(6)])
        self.tail(li, j, wname, qm0, g0)
        if mode == "full":
            self.flush_t0()
            self.t0_inject()

    def mem_kv(self, li):
        nc = self.nc
        s, st = self.wnext([("wmemkv", li, 0, 256, 0)])
        for c2 in range(2):
            bank = self.ps_alloc()
            out = self.ps[bank][:, 0:256]
            for kc in range(8):
                self.mm(out, self.wslot[s][:, kc, c2 * 128:(c2 + 1) * 128], self.memT[:, kc, :],
                        start=(kc == 0), stop=(kc == 7), reads=[st, self.memT_t], writes=[self.ps_t[bank]],
                        inc=(kc == 7))
            self.op(self.act, lambda c2=c2, out=out: nc.scalar.copy(self.kmT[:, c2, :], out),
                    reads=[self.ps_t[bank]], writes=[self.kmT_t])
            self.op(self.act, lambda c2=c2, out=out: nc.scalar.copy(self.kmT32[:, c2, :], out),
                    reads=[self.ps_t[bank]], writes=[self.km32_t])
            self.ps_release(bank)
        s, st = self.wnext([("wmemkv", li, 256, 256, 0)])
        for kt in range(2):
            bank = self.ps_alloc()
            out = self.ps[bank][:, 0:256]
            for kc in range(8):
                self.mm(out, self.memT[:, kc, kt * 128:(kt + 1) * 128], self.wslot[s][:, kc, 0:256],
                        start=(kc == 0), stop=(kc == 7), reads=[st, self.memT_t], writes=[self.ps_t[bank]],
                        inc=(kc == 7))
            self.op(self.act, lambda kt=kt, out=out: nc.scalar.copy(self.vm[:, kt, :], out),
                    reads=[self.ps_t[bank]], writes=[self.vm_t])
            self.op(self.act, lambda kt=kt, out=out: nc.scalar.copy(self.vm32[:, kt, :], out),
                    reads=[self.ps_t[bank]], writes=[self.vm32_t])
            self.ps_release(bank)

    def tail(self, li, j, wname, qm0, g0):
        nc = self.nc
        with ExitStack() as es:
            sbt = lambda name, shape, dtype: es.enter_context(nc.sbuf_tensor(self.un(name), shape, dtype))
            qm = sbt("qm", [128, 2, 512], BF16)
            qm_t = Trk()
            pT = [sbt(f"pTm{i}", [128, 512], BF16) for i in range(3)]
            pT_t = [Trk() for _ in range(3)]
            rden = sbt("rdenm", [128, 512], F32)
            rden_t = Trk()
            sg = [sbt(f"sg{i}", [128, 512], BF16) for i in range(2)]
            sg_t = [Trk() for _ in range(2)]
            pk = 0
            for T in range(4):
                cols = slice(T * 512, (T + 1) * 512)
                s, st = self.wnext([(wname, j, qm0, 256, 0)])
                for c2 in range(2):
                    bank = self.ps_alloc()
                    self.proj_fm(bank, 128, s, st, c2 * 128, T)
                    self.op(self.act, lambda c2=c2, bank=bank: nc.scalar.copy(qm[:, c2, :], self.ps[bank][:, :]),
                            reads=[self.ps_t[bank]], writes=[qm_t])
                    self.ps_release(bank)
                gate_jobs = [(0, 384), (384, 384), (768, 256)]
                gstate = {"ji": 0, "cc": 0, "s": None, "st": None}

                def emit_gate_chunk():
                    if gstate["ji"] >= len(gate_jobs):
                        return False
                    c0, n = gate_jobs[gstate["ji"]]
                    if gstate["cc"] == 0:
                        gstate["s"], gstate["st"] = self.wnext([(wname, j, g0 + c0, n, 0)])
                    s_, st_ = gstate["s"], gstate["st"]
                    cc = gstate["cc"]
                    c = c0 // 128 + cc
                    bank = self.ps_alloc()
                    self.proj_fm(bank, 128, s_, st_, cc * 128, T)
                    b_ = c % 2
                    self.op(self.act, lambda b_=b_, bank=bank: nc.scalar.activation(
                        out=sg[b_][:], in_=self.ps[bank][:, :], func=AF.Silu),
                        reads=[self.ps_t[bank]], writes=[sg_t[b_]])
                    self.ps_release(bank)
                    self.gate_pending.append((c, b_))
                    gstate["cc"] += 1
                    if gstate["cc"] >= n // 128:
                        gstate["cc"] = 0
                        gstate["ji"] += 1
                    return True

                def emit_gate_mults(upto=None):
                    keep = []
                    for (c, b_) in self.gate_pending:
                        if c < 6 or self.mem_done:
                            self.op(self.dve, lambda b_=b_, c=c: nc.vector.tensor_tensor(
                                self.preT[:, c, cols], self.preT[:, c, cols], sg[b_][:], ALU.mult),
                                reads=[sg_t[b_], self.preT_t[c][T]], writes=[self.preT_t[c][T]])
                        else:
                            keep.append((c, b_))
                    self.gate_pending = keep

                self.gate_pending = []
                self.mem_done = False
                pend = []
                banks = {}
                step = 0
                for c2 in range(2):
                    for half in range(2):
                        hm = 2 * c2 + half
                        pr = slice(64 * half, 64 * half + 64)
                        for kt in range(2):
                            if c2 not in banks:
                                banks[c2] = (self.ps_alloc(), self.ps_alloc())
                            bs = self.ps_alloc()
                            self.mm(self.ps[bs][:, :], self.kmT[pr, c2, kt * 128:(kt + 1) * 128], qm[pr, c2, :],
                                    start=True, stop=True, reads=[self.kmT_t, qm_t], writes=[self.ps_t[bs]], inc=True)
                            p = pk % 3
                            pk += 1
                            self.op(self.act, lambda p=p, bs=bs: nc.scalar.activation(
                                out=pT[p][:], in_=self.ps[bs][:, :], func=AF.Exp, scale=0.125),
                                reads=[self.ps_t[bs]], writes=[pT_t[p]])
                            self.ps_release(bs)

                            def emit_pv(c2=c2, half=half, hm=hm, pr=pr, kt=kt, p=p):
                                bn, bd = banks[c2]
                                self.mm(self.ps[bn][pr, :], self.vm[:, kt, hm * 64:(hm + 1) * 64], pT[p][:],
                                        start=(kt == 0), stop=(kt == 1), reads=[self.vm_t, pT_t[p]],
                                        writes=[self.ps_t[bn]], inc=False)
                                self.mm(self.ps[bd][pr, :], self.ones[:, 0:64], pT[p][:],
                                        start=(kt == 0), stop=(kt == 1), reads=[pT_t[p]],
                                        writes=[self.ps_t[bd]], inc=True)
                                if half == 1 and kt == 1:
                                    self.op(self.dve, lambda bd=bd: nc.vector.reciprocal(rden[:], self.ps[bd][:, :]),
                                            reads=[self.ps_t[bd]], writes=[rden_t])
                                    self.op(self.dve, lambda bn=bn, c2=c2: nc.vector.tensor_tensor(
                                        self.preT[:, 6 + c2, cols], self.ps[bn][:, :], rden[:], ALU.mult),
                                        reads=[self.ps_t[bn], rden_t], writes=[self.preT_t[6 + c2][T]])
                                    self.ps_release(bn)
                                    self.ps_release(bd)
                            pend.append(emit_pv)
                            if len(pend) > 1:
                                pend.pop(0)()
                            if step < 6:
                                emit_gate_chunk()
                                emit_gate_mults()
                            step += 1
                while pend:
                    pend.pop(0)()
                self.mem_done = True
                while emit_gate_chunk():
                    emit_gate_mults()
                emit_gate_mults()
                for (c0, n) in [(0, 384), (384, 384), (768, 256)]:
                    s, st = self.wnext([("wout", li, c0, n, 0)])
                    for sub in range(4):
                        tt = 4 * T + sub
                        bank = self.ps_alloc()
                        out = self.ps[bank][:, 0:n]
                        for kc in range(8):
                            self.mm(out, self.preT[:, kc, tt * 128:(tt + 1) * 128], self.wslot[s][:, kc, 0:n],
                                    start=(kc == 0), stop=(kc == 7), reads=[st, self.preT_t[kc][T]],
                                    writes=[self.ps_t[bank]], inc=(kc == 7))
                        self.op(self.dve, lambda tt=tt, out=out, c0=c0, n=n: nc.vector.tensor_tensor(
                            self.x[:, tt, c0:c0 + n], out, self.x[:, tt, c0:c0 + n], ALU.add),
                            reads=[self.ps_t[bank], self.x_t[tt]], writes=[self.x_t[tt]])
                        self.ps_release(bank)
            self.barrier()

    def final_norm(self, si):
        nc = self.nc
        with ExitStack() as es:
            fnw = es.enter_context(nc.sbuf_tensor(self.un("fnw"), [128, D], F32))
            self.junk = es.enter_context(nc.sbuf_tensor(self.un("junk"), [128, D], BF16))
            self.junk_t = Trk()
            fnw_t = Trk()
            self.dma(self.qsp, fnw[:], self.fnw_d, writes=[fnw_t])
            self._final_norm(si, fnw, fnw_t)
            self.barrier()

    def _final_norm(self, si, fnw, fnw_t):
        nc = self.nc
        for i in range(16):
            self.op(self.act, lambda i=i: nc.scalar.activation(
                out=self.junk[:], in_=self.x[:, i, :], func=AF.Square, accum_out=self.ss16[:, i:i + 1]),
                reads=[self.x_t[i]], writes=[self.ss_t[i], self.junk_t])
        self.op(self.act, lambda: nc.scalar.activation(
            out=self.rstd16[:], in_=self.ss16[:], func=AF.Ln, scale=1.0 / D, bias=EPS),
            reads=self.ss_t, writes=[self.st_t])
        self.op(self.act, lambda: nc.scalar.activation(
            out=self.rstd16[:], in_=self.rstd16[:], func=AF.Exp, scale=-0.5),
            reads=[self.st_t], writes=[self.st_t])
        for i in range(16):
            self.op(self.dve, lambda i=i: nc.vector.scalar_tensor_tensor(
                self.x[:, i, :], self.x[:, i, :], self.rstd16[:, i:i + 1], fnw[:], ALU.mult, ALU.mult),
                reads=[self.x_t[i], self.st_t, fnw_t], writes=[self.x_t[i]])
            tok = self.dma(self.qsp, self.out_d[si, i * 128:(i + 1) * 128, :], self.x[:, i, :], reads=[self.x_t[i]])
            self.out_toks.append(tok)


    def tcol(self, n=1):
        if self.tbc + n > 512:
            self.tbc = 0
        c = self.tbc
        self.tbc += n
        return c

    def T(self, eng, fn, reads, writes, **kw):
        self.t0q.append(lambda: self.op(eng, fn, reads=reads, writes=writes, **kw))

    def t0_init(self):
        if T0S < 1:
            return
        nc = self.nc
        tb = self.ps[self.tb]
        tbt = self.ps_t[self.tb]
        c = self.tcol(8)
        for kc in range(8):
            self.op(self.pe, lambda kc=kc: nc.tensor.matmul(
                tb[:, c + kc:c + kc + 1], self.x[0:1, 0, kc * 128:(kc + 1) * 128], self.ones32[0:1, 0:1],
                start=True, stop=True, skip_group_check=True),
                reads=[self.x_t[0], self.const_t], writes=[tbt], inc=(kc == 7), acc=(kc > 0))
        self.op(self.dve, lambda: nc.vector.tensor_copy(self.x0T[:], tb[:, c:c + 8]),
                reads=[tbt], writes=[self.x0_t])

    def t0_inject(self):
        if T0S < 1:
            return
        nc = self.nc
        tb = self.ps[self.tb]
        tbt = self.ps_t[self.tb]
        for hf in range(2):
            for k4 in range(4):
                kc = 4 * hf + k4
                self.op(self.pe, lambda kc=kc, k4=k4: nc.tensor.matmul(
                    tb[0:1, k4 * 128:(k4 + 1) * 128], self.x0T[:, kc:kc + 1], self.ident32[:],
                    start=True, stop=True, skip_group_check=True),
                    reads=[self.x0_t, self.const_t], writes=[tbt], inc=(k4 == 3), acc=(k4 > 0))
            self.op(self.dve, lambda hf=hf: nc.vector.tensor_copy(self.x[0:1, 0, hf * 512:(hf + 1) * 512], tb[0:1, :]),
                    reads=[tbt], writes=[self.x_t[0]])
        self.tbc = 0

    def t0_cproj(self, vec, vec_t, wname, idx, c0, M, col, pbase=0):
        nc = self.nc
        tb = self.ps[self.tb]
        tbt = self.ps_t[self.tb]
        k = self.w32_n % 2
        self.w32_n += 1
        w32, w32_t = self.w32[k], self.w32_t[k]

        def load():
            srcap = getattr(self, wname + "_d")[idx, :, c0:c0 + M].rearrange("(kc p) c -> p kc c", p=128)
            self.dma(self.qsp, w32[:, :, 0:M], srcap, writes=[w32_t])
        if self.t0_prev_pos is not None:
            self.t0q.insert(self.t0_prev_pos, load)
        else:
            self.t0q.append(load)
        self.t0_prev_pos = len(self.t0q)
        for kc in range(8):
            self.T(self.pe, lambda kc=kc: nc.tensor.matmul(
                tb[pbase:pbase + M, col:col + 1], w32[:, kc, 0:M], vec[:, kc:kc + 1],
                start=(kc == 0), stop=(kc == 7), skip_group_check=True),
                [w32_t, vec_t], [tbt], inc=(kc == 7), acc=(kc > 0))

    def t0_mode(self, li):
        last_gla = max([l for l in range(self.nlayers) if l % 2 == 0], default=-1)
        if li < last_gla:
            return "full"
        if li == last_gla:
            return "heads"
        return "none"

    def t0_layer(self, li):
        nc = self.nc
        j = li // 2
        self.t0_prev_pos = None
        assert not self.t0q
        mode = self.t0_mode(li)
        if mode == "none":
            return
        tb = self.ps[self.tb]
        tbt = self.ps_t[self.tb]
        s = self.t0s
        st = self.t0s_t
        T = self.T
        X = mybir.AxisListType.X
        if T0S < 2:
            return
        c = self.tcol()
        T(self.dve, lambda: nc.vector.tensor_tensor(s[:, 8:16], self.x0T[:], self.x0T[:], ALU.mult), [self.x0_t], [st])
        T(self.dve, lambda: nc.vector.tensor_reduce(out=s[:, 0:1], in_=s[:, 8:16], axis=X, op=ALU.add), [st], [st])
        T(self.pe, lambda: nc.tensor.matmul(tb[:, c:c + 1], self.ones32[:], s[:, 0:1], start=True, stop=True,
                                            skip_group_check=True), [st, self.const_t], [tbt])
        T(self.act, lambda: nc.scalar.activation(out=s[:, 1:2], in_=tb[:, c:c + 1], func=AF.Ln, scale=1.0 / D, bias=EPS),
          [tbt], [st])
        T(self.act, lambda: nc.scalar.activation(out=s[:, 1:2], in_=s[:, 1:2], func=AF.Exp, scale=-0.5), [st], [st])
        T(self.dve, lambda: nc.vector.scalar_tensor_tensor(
            self.h0T[:], self.x0T[:], s[:, 1:2], self.normw[:, li, :], ALU.mult, ALU.mult),
          [self.x0_t, st, self.const_t], [self.h0_t])
        hv, ht = self.h0T, self.h0_t
        if T0S < 3:
            return
        if li % 2 == 0:
            wname, qm0, g0 = "wina", 1552, 1808
            for h in range(4):
                c = self.tcol(8)
                self.t0_cproj(hv, ht, wname, j, 96 * h, 96, c)
                self.t0_cproj(hv, ht, wname, j, 384 + 96 * h, 96, c + 1)
                T(self.act, lambda c=c: nc.scalar.copy(s[0:96, 2:3], tb[0:96, c:c + 1]), [tbt], [st])
                T(self.dve, lambda c=c: nc.vector.tensor_tensor(s[0:96, 3:4], s[0:96, 2:3], tb[0:96, c + 1:c + 2], ALU.mult),
                  [tbt, st], [st])
                T(self.pe, lambda c=c: nc.tensor.matmul(tb[:, c + 2:c + 3], self.ones32[0:96, :], s[0:96, 3:4],
                                                        start=True, stop=True, skip_group_check=True),
                  [st, self.const_t], [tbt])
                f0 = 192 * h
                if h % 2 == 0:
                    cfull, chalf, phalf, fs0, hs0 = f0 // 128, f0 // 128 + 1, 0, 0, 128
                else:
                    chalf, cfull, phalf, hs0, fs0 = f0 // 128, f0 // 128 + 1, 64, 0, 64
                pr = slice(phalf, phalf + 64)
                self.t0_cproj(hv, ht, wname, j, 768 + f0 + fs0, 128, c + 3)
                self.t0_cproj(hv, ht, wname, j, 768 + f0 + hs0, 64, c + 4, pbase=phalf)
                T(self.dve, lambda: nc.vector.memset(s[:, 4:6], 0.0), [st], [st])
                T(self.act, lambda c=c: nc.scalar.copy(s[:, 4:5], tb[:, c + 3:c + 4]), [tbt], [st])
                T(self.act, lambda c=c, pr=pr: nc.scalar.copy(s[pr, 5:6], tb[pr, c + 4:c + 5]), [tbt], [st])
                T(self.dve, lambda: nc.vector.tensor_tensor(s[:, 16:18], s[:, 4:6], s[:, 4:6], ALU.mult), [st], [st])
                T(self.dve, lambda: nc.vector.tensor_reduce(out=s[:, 6:7], in_=s[:, 16:18], axis=X, op=ALU.add), [st], [st])
                T(self.pe, lambda c=c: nc.tensor.matmul(tb[:, c + 5:c + 6], self.ones32[:], s[:, 6:7],
                                                        start=True, stop=True, skip_group_check=True),
                  [st, self.const_t], [tbt])
                T(self.act, lambda c=c: nc.scalar.activation(out=s[:, 7:8], in_=tb[:, c + 2:c + 3], func=AF.Copy,
                                                             scale=96.0 ** -0.5), [tbt], [st])
                T(self.dve, lambda: nc.vector.tensor_tensor(s[:, 18:19], s[:, 7:8], s[:, 7:8], ALU.mult), [st], [st])
                T(self.dve, lambda c=c: nc.vector.tensor_tensor(s[:, 18:19], s[:, 18:19], tb[:, c + 5:c + 6], ALU.mult),
                  [st, tbt], [st])
                T(self.act, lambda: nc.scalar.activation(out=s[:, 18:19], in_=s[:, 18:19], func=AF.Ln,
                                                         scale=1.0 / 192, bias=EPS), [st], [st])
                T(self.act, lambda: nc.scalar.activation(out=s[:, 18:19], in_=s[:, 18:19], func=AF.Exp, scale=-0.5),
                  [st], [st])
                T(self.dve, lambda: nc.vector.tensor_tensor(s[:, 19:20], s[:, 7:8], s[:, 18:19], ALU.mult), [st], [st])
                T(self.dve, lambda cfull=cfull: nc.vector.scalar_tensor_tensor(
                    self.m0T[:, cfull:cfull + 1], s[:, 4:5], s[:, 19:20], self.gnw[:, 6 * j + cfull:6 * j + cfull + 1],
                    ALU.mult, ALU.mult), [st, self.const_t], [self.m0_t])
                T(self.dve, lambda chalf=chalf, pr=pr: nc.vector.scalar_tensor_tensor(
                    self.m0T[pr, chalf:chalf + 1], s[pr, 5:6], s[pr, 19:20], self.gnw[pr, 6 * j + chalf:6 * j + chalf + 1],
                    ALU.mult, ALU.mult), [st, self.const_t], [self.m0_t])
        else:
            wname, qm0, g0 = "winb", 6912, 7168
            for h in range(6):
                for g in range(3):
                    base = g * 2304
                    c = self.tcol(4)
                    self.t0_cproj(hv, ht, wname, j, base + 128 * h, 128, c)
                    self.t0_cproj(hv, ht, wname, j, base + 768 + 128 * h, 128, c + 1)
                    self.t0_cproj(hv, ht, wname, j, base + 1536 + 128 * h, 128, c + 2)
                    T(self.act, lambda c=c: nc.scalar.copy(s[:, 2:3], tb[:, c:c + 1]), [tbt], [st])
                    T(self.dve, lambda c=c: nc.vector.tensor_tensor(s[:, 3:4], s[:, 2:3], tb[:, c + 1:c + 2], ALU.mult),
                      [tbt, st], [st])
                    T(self.pe, lambda c=c: nc.tensor.matmul(tb[:, c + 3:c + 4], self.ones32[:], s[:, 3:4],
                                                            start=True, stop=True, skip_group_check=True),
                      [st, self.const_t], [tbt])
                    T(self.act, lambda c=c, g=g: nc.scalar.activation(out=s[:, 8 + g:9 + g], in_=tb[:, c + 3:c + 4],
                                                                      func=AF.Exp, scale=128.0 ** -0.5), [tbt], [st])
                    T(self.act, lambda c=c, g=g: nc.scalar.copy(s[:, 12 + g:13 + g], tb[:, c + 2:c + 3]), [tbt], [st])
                T(self.dve, lambda: nc.vector.tensor_reduce(out=s[:, 16:17], in_=s[:, 8:11], axis=X, op=ALU.add), [st], [st])
                T(self.dve, lambda: nc.vector.reciprocal(s[:, 16:17], s[:, 16:17]), [st], [st])
                T(self.dve, lambda: nc.vector.tensor_tensor(s[:, 20:23], s[:, 8:11], s[:, 12:15], ALU.mult), [st], [st])
                T(self.dve, lambda: nc.vector.tensor_reduce(out=s[:, 17:18], in_=s[:, 20:23], axis=X, op=ALU.add), [st], [st])
                T(self.dve, lambda h=h: nc.vector.tensor_tensor(self.m0T[:, h:h + 1], s[:, 17:18], s[:, 16:17], ALU.mult),
                  [st], [self.m0_t])
        if T0S < 4 or mode == "heads":
            return
        c = self.tcol(2)
        for c2 in range(2):
            self.t0_cproj(hv, ht, wname, j, qm0 + 128 * c2, 128, c + c2)
        T(self.act, lambda c=c: nc.scalar.copy(s[:, 24:26], tb[:, c:c + 2]), [tbt], [st])
        c = self.tcol(8)
        for hm in range(4):
            c2, half = hm // 2, hm % 2
            pr = slice(64 * half, 64 * half + 64)
            for kt in range(2):
                T(self.pe, lambda c=c, hm=hm, kt=kt, c2=c2, pr=pr: nc.tensor.matmul(
                    tb[:, c + 2 * hm + kt:c + 2 * hm + kt + 1], self.kmT32[pr, c2, kt * 128:(kt + 1) * 128],
                    s[pr, 24 + c2:25 + c2], start=True, stop=True, skip_group_check=True),
                  [self.km32_t, st], [tbt], inc=(hm == 3 and kt == 1), acc=not (hm == 0 and kt == 0))
        T(self.act, lambda c=c: nc.scalar.activation(out=self.e8[:], in_=tb[:, c:c + 8], func=AF.Exp, scale=0.125),
          [tbt], [st])
        c = self.tcol(4)
        first = True
        for hm in range(4):
            c2, half = hm // 2, hm % 2
            pr = slice(64 * half, 64 * half + 64)
            for kt in range(2):
                T(self.pe, lambda c=c, hm=hm, kt=kt, c2=c2, pr=pr: nc.tensor.matmul(
                    tb[pr, c + c2:c + c2 + 1], self.vm32[:, kt, 64 * hm:64 * hm + 64], self.e8[:, 2 * hm + kt:2 * hm + kt + 1],
                    start=(kt == 0), stop=(kt == 1), skip_group_check=True),
                  [self.vm32_t, st], [tbt], inc=False, acc=not first)
                first = False
            for kt in range(2):
                T(self.pe, lambda c=c, hm=hm, kt=kt, c2=c2, pr=pr: nc.tensor.matmul(
                    tb[pr, c + 2 + c2:c + 3 + c2], self.ones32[:, 0:64], self.e8[:, 2 * hm + kt:2 * hm + kt + 1],
                    start=(kt == 0), stop=(kt == 1), skip_group_check=True),
                  [self.const_t, st], [tbt], inc=(hm == 3 and kt == 1), acc=True)
        T(self.dve, lambda c=c: nc.vector.reciprocal(s[:, 26:28], tb[:, c + 2:c + 4]), [tbt], [st])
        T(self.dve, lambda c=c: nc.vector.tensor_tensor(self.m0T[:, 6:8], tb[:, c:c + 2], s[:, 26:28], ALU.mult),
          [tbt, st], [self.m0_t])
        if T0S < 5:
            return
        c = self.tcol(8)
        for cc in range(8):
            self.t0_cproj(hv, ht, wname, j, g0 + 128 * cc, 128, c + cc)
        T(self.act, lambda c=c: nc.scalar.activation(out=self.sg0[:], in_=tb[:, c:c + 8], func=AF.Silu), [tbt], [st])
        T(self.dve, lambda: nc.vector.tensor_tensor(self.m0T[:], self.m0T[:], self.sg0[:], ALU.mult),
          [self.m0_t, st], [self.m0_t])
        c = self.tcol(8)
        for cc in range(8):
            self.t0_cproj(self.m0T, self.m0_t, "wout", li, 128 * cc, 128, c + cc)
        T(self.dve, lambda c=c: nc.vector.tensor_tensor(self.x0T[:], self.x0T[:], tb[:, c:c + 8], ALU.add),
          [tbt, self.x0_t], [self.x0_t])

    def mixer_a(self, j):
        nc = self.nc
        win = self.wina_d[j]
        with ExitStack() as es:
            sbt = lambda name, shape, dtype: es.enter_context(nc.sbuf_tensor(self.un(name), shape, dtype))
            wgl = sbt("wgl", [128, 8, 16], BF16)
            wgl_t = Trk()
            glT = sbt("glT", [16, 512], F32)
            glT_t = Trk()
            sp = sbt("sp", [96, 512], F32)
            cc_ = sbt("cc", [96, 512], F32)
            eb = sbt("eb", [96, 512], F32)
            enb = sbt("enb", [96, 512], F32)
            g_t = Trk()
            qinT = sbt("qinT", [96, 512], BF16)
            kinT = sbt("kinT", [96, 512], BF16)
            qk_t = Trk()
            kin_tok = sbt("kin_tok", [128, 4, 96], BF16)
            kin_tok_t = Trk()
            V = sbt("Vh", [128, 4, 192], BF16)
            V_t = Trk()
            R = [sbt(f"R{h}", [96, 192], F32) for h in range(4)]
            R_t = [Trk() for _ in range(4)]
            Sbf = [sbt(f"Sbf{i}", [96, 192], BF16) for i in range(4)]
            Sbf_t = [Trk() for _ in range(4)]
            ATm = [sbt(f"ATm{i}", [128, 128], BF16) for i in range(2)]
            ATm_t = [Trk() for _ in range(2)]
            ssh = sbt("ssh", [128, 4], F32)
            ssh_t = [Trk() for _ in range(4)]
            mixt = [sbt(f"mixt{i}", [128, 192], BF16) for i in range(2)]
            mixt_t = [Trk() for _ in range(2)]
            dec = sbt("dec", [96, 4], F32)
            dec_t = [Trk() for _ in range(4)]
            self.junk = sbt("junkA", [128, 192], BF16)
            self.junk_t = Trk()
            wgu = sbt("wgu", [16, 384], F32)
            rmask = sbt("rmask", [128, 512], BF16)
            cl_t = Trk()
            self.dma(self.qsp, wgu[:], self.wgu_d[j], writes=[cl_t])
            self.dma(self.qpl, rmask[:], self.c_rmask_d, writes=[cl_t])
            self.dma(self.qpl, wgl[:], win[:, 1536:1552].rearrange("(kc p) c -> p kc c", p=128), writes=[wgl_t])
            sk = 0
            ak = 0
            mk = 0
            for T in range(4):
                cols = slice(T * 512, (T + 1) * 512)
                bank = self.ps_alloc()
                for kc in range(8):
                    self.mm(self.ps[bank][0:16, :], wgl[:, kc, :], self.hT[:, kc, cols], start=(kc == 0),
                            stop=(kc == 7), reads=[wgl_t, self.hT_t[T]], writes=[self.ps_t[bank]], inc=(kc == 7))
                self.op(self.act, lambda bank=bank: nc.scalar.copy(glT[:], self.ps[bank][0:16, :]),
                        reads=[self.ps_t[bank]], writes=[glT_t])
                self.ps_release(bank)
                for h in range(4):
                    s, st = self.wnext([("wina", j, 96 * h, 96, 0), ("wina", j, 384 + 96 * h, 96, 96),
                                        ("wina", j, 768 + 192 * h, 192, 192)])
                    bank = self.ps_alloc()
                    self.mm(self.ps[bank][0:96, :], wgu[:, 96 * h:96 * h + 96], glT[:], start=True, stop=True,
                            reads=[cl_t, glT_t], writes=[self.ps_t[bank]], inc=True)
                    self.op(self.act, lambda bank=bank, h=h: nc.scalar.activation(
                        out=sp[:], in_=self.ps[bank][0:96, :], func=AF.Exp, scale=-1.0,
                        bias=self.nbg[:, 4 * j + h:4 * j + h + 1]),
                        reads=[self.ps_t[bank], self.const_t], writes=[g_t])
                    self.ps_release(bank)
                    self.op(self.act, lambda: nc.scalar.activation(out=sp[:], in_=sp[:], func=AF.Ln, bias=1.0),
                            reads=[g_t], writes=[g_t])
                    self.op(self.dve, lambda: nc.vector.tensor_tensor_scan(
                        cc_[:], rmask[0:96, :], sp[:], 0.0, ALU.mult, ALU.add),
                        reads=[g_t, cl_t], writes=[g_t])
                    self.op(self.act, lambda: nc.scalar.activation(out=eb[:], in_=cc_[:], func=AF.Exp, scale=-1.0 / 16),
                            reads=[g_t], writes=[g_t])
                    self.op(self.act, lambda: nc.scalar.activation(out=enb[:], in_=cc_[:], func=AF.Exp, scale=1.0 / 16),
                            reads=[g_t], writes=[g_t])
                    bank = self.ps_alloc()
                    self.proj_fm(bank, 96, s, st, 0, T)
                    self.op(self.dve, lambda bank=bank: nc.vector.scalar_tensor_tensor(
                        qinT[:], self.ps[bank][0:96, :], 96.0 ** -0.5, eb[:], ALU.mult, ALU.mult),
                        reads=[self.ps_t[bank], g_t], writes=[qk_t])
                    self.ps_release(bank)
                    bank = self.ps_alloc()
                    self.proj_fm(bank, 96, s, st, 96, T)
                    self.op(self.dve, lambda bank=bank: nc.vector.tensor_tensor(
                        kinT[:], self.ps[bank][0:96, :], enb[:], ALU.mult),
                        reads=[self.ps_t[bank], g_t], writes=[qk_t])
                    self.ps_release(bank)
                    for sub in range(4):
                        tt = 4 * T + sub
                        bank = self.ps_alloc()
                        out = self.ps[bank][:, 0:192]
                        for kc in range(8):
                            self.mm(out, self.hT[:, kc, tt * 128:(tt + 1) * 128], self.wslot[s][:, kc, 192:384],
                                    start=(kc == 0), stop=(kc == 7), reads=[st, self.hT_t[T]],
                                    writes=[self.ps_t[bank]], inc=(kc == 7))
                        self.op(self.act, lambda sub=sub, out=out: nc.scalar.copy(V[:, sub, :], out),
                                reads=[self.ps_t[bank]], writes=[V_t])
                        self.ps_release(bank)
                    bank = self.ps_alloc()
                    pb = self.ps[bank][:].bitcast(BF16)
                    for sub in range(4):
                        self.op(self.pe, lambda sub=sub, pb=pb: nc.tensor.transpose(
                            pb[:, sub * 96:(sub + 1) * 96], kinT[:, sub * 128:(sub + 1) * 128], self.ident[0:96, 0:96]),
                            reads=[qk_t, self.const_t], writes=[self.ps_t[bank]], inc=(sub == 3), acc=(sub > 0))
                    self.op(self.act, lambda pb=pb: nc.scalar.copy(
                        kin_tok[:].rearrange("p a b -> p (a b)"), pb[:, 0:384]),
                        reads=[self.ps_t[bank]], writes=[kin_tok_t])
                    self.ps_release(bank)
                    for sub in range(4):
                        tt = 4 * T + sub
                        tc_ = slice(sub * 128, (sub + 1) * 128)
                        bA = self.ps_alloc()
                        self.mm(self.ps[bA][:, 0:128], kinT[:, tc_], qinT[:, tc_], start=True, stop=True,
                                reads=[qk_t], writes=[self.ps_t[bA]], inc=True)
                        a = ak % 2
                        ak += 1
                        self.op(self.dve, lambda a=a, bA=bA: nc.vector.tensor_tensor(
                            ATm[a][:], self.ps[bA][:, 0:128], self.gmask[:], ALU.mult),
                            reads=[self.ps_t[bA], self.const_t], writes=[ATm_t[a]])
                        self.ps_release(bA)
                        bO = self.ps_alloc()
                        self.mm(self.ps[bO][:, 0:192], ATm[a][:], V[:, sub, :], start=True, stop=False,
                                reads=[ATm_t[a], V_t], writes=[self.ps_t[bO]], inc=False)
                        for ch in range(2):
                            c = 2 * tt + ch
                            pr = slice(64 * ch, 64 * ch + 64)
                            lc = 64 * (2 * sub + ch)
                            if c > 0:
                                dk = eb[:, lc - 1:lc] if lc > 0 else dec[:, h:h + 1]
                                dk_t = g_t if lc > 0 else dec_t[h]
                                sb_ = sk % 4
                                sk += 1
                                self.op(self.act, lambda sb_=sb_, h=h, dk=dk: nc.scalar.activation(
                                    out=Sbf[sb_][:], in_=R[h][:], func=AF.Copy, scale=dk),
                                    reads=[R_t[h], dk_t], writes=[Sbf_t[sb_]])
                                self.mm(self.ps[bO][pr, 0:192], qinT[:, lc:lc + 64], Sbf[sb_][:], start=False,
                                        stop=(ch == 1), reads=[qk_t, Sbf_t[sb_]], writes=[self.ps_t[bO]],
                                        inc=(ch == 1))
                            bU = self.ps_alloc()
                            self.mm(self.ps[bU][0:96, 0:192], kin_tok[pr, sub, :], V[pr, sub, :], start=True, stop=True,
                                    reads=[kin_tok_t, V_t], writes=[self.ps_t[bU]], inc=True)
                            if c == 0:
                                self.op(self.dve, lambda bU=bU, h=h: nc.vector.tensor_copy(
                                    R[h][:], self.ps[bU][0:96, 0:192]),
                                    reads=[self.ps_t[bU]], writes=[R_t[h]])
                            else:
                                self.op(self.dve, lambda bU=bU, h=h, dk=dk: nc.vector.scalar_tensor_tensor(
                                    R[h][:], R[h][:], dk, self.ps[bU][0:96, 0:192], ALU.mult, ALU.add),
                                    reads=[self.ps_t[bU], R_t[h], dk_t], writes=[R_t[h]])
                            self.ps_release(bU)
                        m = mk % 2
                        mk += 1
                        self.op(self.act, lambda bO=bO, h=h: nc.scalar.activation(
                            out=self.junk[:, 0:192], in_=self.ps[bO][:, 0:192], func=AF.Square,
                            accum_out=ssh[:, h:h + 1]),
                            reads=[self.ps_t[bO]], writes=[ssh_t[h], self.junk_t])
                        self.op(self.act, lambda h=h: nc.scalar.activation(
                            out=ssh[:, h:h + 1], in_=ssh[:, h:h + 1], func=AF.Ln, scale=1.0 / 192, bias=EPS),
                            reads=[ssh_t[h]], writes=[ssh_t[h]])
                        self.op(self.act, lambda h=h: nc.scalar.activation(
                            out=ssh[:, h:h + 1], in_=ssh[:, h:h + 1], func=AF.Exp, scale=-0.5),
                            reads=[ssh_t[h]], writes=[ssh_t[h]])
                        self.op(self.act, lambda bO=bO, h=h, m=m: nc.scalar.activation(
                            out=mixt[m][:], in_=self.ps[bO][:, 0:192], func=AF.Copy, scale=ssh[:, h:h + 1]),
                            reads=[self.ps_t[bO], ssh_t[h]], writes=[mixt_t[m]])
                        self.ps_release(bO)
                        f0 = 192 * h
                        if h % 2 == 0:
                            cfull, chalf, phalf = f0 // 128, f0 // 128 + 1, 0
                            full_src, half_src = slice(0, 128), slice(128, 192)
                        else:
                            chalf, cfull, phalf = f0 // 128, f0 // 128 + 1, 64
                            half_src, full_src = slice(0, 64), slice(64, 192)
                        bT = self.ps_alloc()
                        pb = self.ps[bT][:].bitcast(BF16)
                        self.op(self.pe, lambda m=m, pb=pb, full_src=full_src: nc.tensor.transpose(
                            pb[:, 0:128], mixt[m][:, full_src], self.ident[:]),
                            reads=[mixt_t[m], self.const_t], writes=[self.ps_t[bT]], inc=False)
                        self.op(self.pe, lambda m=m, pb=pb, half_src=half_src, phalf=phalf: nc.tensor.transpose(
                            pb[phalf:phalf + 64, 128:256], mixt[m][:, half_src], self.ident[:]),
                            reads=[mixt_t[m], self.const_t], writes=[self.ps_t[bT]], inc=True, acc=True)
                        tcol = slice(tt * 128, (tt + 1) * 128)
                        self.op(self.dve, lambda pb=pb, cfull=cfull, tcol=tcol: nc.vector.tensor_scalar(
                            self.preT[:, cfull, tcol], pb[:, 0:128], self.gnw[:, 6 * j + cfull:6 * j + cfull + 1],
                            None, ALU.mult),
                            reads=[self.ps_t[bT], self.const_t], writes=[self.preT_t[cfull][T]])
                        self.op(self.dve, lambda pb=pb, chalf=chalf, phalf=phalf, tcol=tcol: nc.vector.tensor_scalar(
                            self.preT[phalf:phalf + 64, chalf, tcol], pb[phalf:phalf + 64, 128:256],
                            self.gnw[phalf:phalf + 64, 6 * j + chalf:6 * j + chalf + 1], None, ALU.mult),
                            reads=[self.ps_t[bT], self.const_t], writes=[self.preT_t[chalf][T]])
                        self.ps_release(bT)
                    self.op(self.dve, lambda h=h: nc.vector.tensor_copy(dec[:, h:h + 1], eb[:, 511:512]),
                            reads=[g_t], writes=[dec_t[h]])
            self.barrier()

    def mixer_b(self, j):
        nc = self.nc
        win = self.winb_d[j]
        with ExitStack() as es:
            sbt = lambda name, shape, dtype: es.enter_context(nc.sbuf_tensor(self.un(name), shape, dtype))
            qkT = sbt("qkT", [128, 2, SEQ], BF16)
            qkT_t = [Trk() for _ in range(4)]
            Vb = sbt("Vb", [128, 16, 128], BF16)
            Vb_t = [Trk() for _ in range(4)]
            qkt = [sbt(f"qkt{i}", [128, 2, 128], BF16) for i in range(3)]
            qkt_t = [Trk() for _ in range(3)]
            ta = sbt("ropeA", [128, 2, 2, 16], F32)
            tb = sbt("ropeB", [128, 2, 2, 16], F32)
            rope_t = Trk()
            pT = [sbt(f"pT{i}", [128, 256], BF16) for i in range(4)]
            pT_t = [Trk() for _ in range(4)]
            accN = sbt("accN", [128, SEQ], F32)
            accD = sbt("accD", [128, SEQ], F32)
            acc_t = [Trk() for _ in range(4)]
            pk = 0
            qk_i = 0
            for h in range(6):
                for g in range(3):
                    r = DIL[g]
                    nbp = 16 // r
                    base = g * 2304
                    s, st = self.wnext([("winb", j, base + 128 * h, 128, 0), ("winb", j, base + 768 + 128 * h, 128, 128),
                                        ("winb", j, base + 1536 + 128 * h, 128, 256)])
                    pend_tr = []
                    for b in range(16):
                        phase, jb = b // nbp, b % nbp
                        t0 = r * 128 * jb + phase
                        tsl = slice(t0, t0 + 127 * r + 1, r) if r > 1 else slice(t0, t0 + 128)
                        if r == 1:
                            hts = [self.hT_t[jb // 4]]
                        elif r == 4:
                            hts = [self.hT_t[jb]]
                        else:
                            hts = self.hT_t
                        bank = self.ps_alloc()
                        out = self.ps[bank][:, 0:384]
                        for kc in range(8):
                            self.mm(out, self.hT[:, kc, tsl], self.wslot[s][:, kc, :], start=(kc == 0), stop=(kc == 7),
                                    reads=[st] + list(hts), writes=[self.ps_t[bank]], inc=(kc == 7))
                        ps3 = out.rearrange("p (a d) -> p a d", a=3)
                        qi = qk_i % 3
                        qk_i += 1
                        rt = [self.ps_t[bank], self.const_t]
                        X = ps3[:, 0:2, 0:32].rearrange("p a (u d) -> p a u d", u=2)
                        Cb = self.cos[:, g, b, :].unsqueeze(1).unsqueeze(1).to_broadcast([128, 2, 2, 16])
                        Sb = self.sin[:, g, b, :].unsqueeze(1).unsqueeze(1).to_broadcast([128, 2, 2, 16])
                        self.op(self.dve, lambda X=X, Cb=Cb: nc.vector.tensor_tensor(ta[:], X, Cb, ALU.mult),
                                reads=rt, writes=[rope_t])
                        self.op(self.dve, lambda X=X, Sb=Sb: nc.vector.tensor_tensor(tb[:], X, Sb, ALU.mult),
                                reads=rt, writes=[rope_t])
                        self.op(self.dve, lambda qi=qi: nc.vector.tensor_tensor(
                            qkt[qi][:, :, 0:16], ta[:, :, 0, :], tb[:, :, 1, :], ALU.subtract),
                            reads=[rope_t], writes=[qkt_t[qi]])
                        self.op(self.dve, lambda qi=qi: nc.vector.tensor_tensor(
                            qkt[qi][:, :, 16:32], ta[:, :, 1, :], tb[:, :, 0, :], ALU.add),
                            reads=[rope_t], writes=[qkt_t[qi]])
                        self.op(self.act, lambda b=b, ps3=ps3: nc.scalar.copy(Vb[:, b, :], ps3[:, 2, :]),
                                reads=[self.ps_t[bank]], writes=[Vb_t[b // 4]])
                        self.op(self.act, lambda qi=qi, ps3=ps3: nc.scalar.copy(
                            qkt[qi][:, :, 32:128], ps3[:, 0:2, 32:128]),
                            reads=[self.ps_t[bank]], writes=[qkt_t[qi]])
                        self.ps_release(bank)
                        def emit_tr(b=b, qi=qi):
                            if b % 4 == 0:
                                self.bankT = self.ps_alloc()
                            bankT = self.bankT
                            pb = self.ps[bankT][:].bitcast(BF16).rearrange("p (a t) -> p a t", a=2)
                            for a in range(2):
                                self.op(self.pe, lambda a=a, pb=pb, qi=qi, b=b: nc.tensor.transpose(
                                    pb[:, a, (b % 4) * 128:(b % 4 + 1) * 128], qkt[qi][:, a, :], self.ident[:]),
                                    reads=[qkt_t[qi], self.const_t], writes=[self.ps_t[bankT]],
                                    inc=(a == 1), acc=not (b % 4 == 0 and a == 0))
                            if b % 4 == 3:
                                b0 = b - 3
                                self.op(self.act, lambda pb=pb, b0=b0: nc.scalar.copy(
                                    qkT[:, :, b0 * 128:(b0 + 4) * 128], pb),
                                    reads=[self.ps_t[bankT]], writes=[qkT_t[b // 4]])
                                self.ps_release(bankT)
                        pend_tr.append(emit_tr)
                        if len(pend_tr) > 2:
                            pend_tr.pop(0)()
                    while pend_tr:
                        pend_tr.pop(0)()
                    if KSTAGE <= 2:
                        continue
                    bn = {}
                    bd = {}
                    started = set()
                    pend_pv = []
                    for kb in range(16):
                        phase, jb = kb // nbp, kb % nbp
                        has_next = jb < nbp - 1
                        N = 256 if has_next else 128
                        m = kb // 4
                        if m not in bn:
                            bn[m] = self.ps_alloc()
                            bd[m] = self.ps_alloc()
                        if has_next and (kb + 1) // 4 not in bn:
                            bn[m + 1] = self.ps_alloc()
                            bd[m + 1] = self.ps_alloc()
                        bs = self.ps_alloc()
                        qts = [qkT_t[kb // 4]] + ([qkT_t[(kb + 1) // 4]] if has_next else [])
                        self.mm(self.ps[bs][:, 0:N], qkT[:, 1, kb * 128:(kb + 1) * 128], qkT[:, 0, kb * 128:kb * 128 + N],
                                start=True, stop=True, reads=qts, writes=[self.ps_t[bs]], inc=True)
                        p = pk % 4
                        pk += 1
                        self.op(self.act, lambda p=p, bs=bs, N=N: nc.scalar.activation(
                            out=pT[p][:, 0:N], in_=self.ps[bs][:, 0:N], func=AF.Exp, scale=128.0 ** -0.5),
                            reads=[self.ps_t[bs]], writes=[pT_t[p]])
                        self.ps_release(bs)
                        self.op(self.dve, lambda p=p, N=N: nc.vector.tensor_tensor(
                            pT[p][:, 0:N], pT[p][:, 0:N], self.mask2[:, 0:N], ALU.mult),
                            reads=[pT_t[p], self.const_t], writes=[pT_t[p]])
                        def emit_pv(kb=kb, has_next=has_next, m=m, p=p):
                            if has_next and (kb + 1) // 4 == m:
                                segs = [(m, (kb % 4) * 128, 0, 256)]
                            elif has_next:
                                segs = [(m, (kb % 4) * 128, 0, 128), (m + 1, 0, 128, 128)]
                            else:
                                segs = [(m, (kb % 4) * 128, 0, 128)]
                            for si_, (mm_, oc, pc, n) in enumerate(segs):
                                first = mm_ not in started
                                started.add(mm_)
                                last = (si_ == len(segs) - 1)
                                self.mm(self.ps[bn[mm_]][:, oc:oc + n], Vb[:, kb, :], pT[p][:, pc:pc + n], start=first, stop=False,
                                        reads=[Vb_t[kb // 4], pT_t[p]], writes=[self.ps_t[bn[mm_]]], inc=False)
                                self.mm(self.ps[bd[mm_]][:, oc:oc + n], self.ones[:], pT[p][:, pc:pc + n], start=first, stop=False,
                                        reads=[pT_t[p]], writes=[self.ps_t[bd[mm_]]], inc=last)
                            if kb % 4 == 3 and KSTAGE <= 3:
                                self.ps_release(bn[m])
                                self.ps_release(bd[m])
                            elif kb % 4 == 3:
                                if r == 1:
                                    dN, dD = accN[:, m * 512:(m + 1) * 512], accD[:, m * 512:(m + 1) * 512]
                                    sN, sD = self.ps[bn[m]][:, :], self.ps[bd[m]][:, :]
                                    trks = [acc_t[m]]
                                elif r == 4:
                                    dN = accN[:, m:SEQ:4]
                                    dD = accD[:, m:SEQ:4]
                                    sN, sD = self.ps[bn[m]][:, :], self.ps[bd[m]][:, :]
                                    trks = acc_t
                                else:
                                    dN = accN[:].rearrange("d (s ph) -> d ph s", ph=16)[:, 4 * m:4 * m + 4, :]
                                    dD = accD[:].rearrange("d (s ph) -> d ph s", ph=16)[:, 4 * m:4 * m + 4, :]
                                    sN = self.ps[bn[m]][:, :].rearrange("d (ph s) -> d ph s", ph=4)
                                    sD = self.ps[bd[m]][:, :].rearrange("d (ph s) -> d ph s", ph=4)
                                    trks = acc_t
                                if g == 0:
                                    self.op(self.act, lambda dN=dN, sN=sN: nc.scalar.copy(dN, sN),
                                            reads=[self.ps_t[bn[m]]], writes=trks)
                                    self.op(self.act, lambda dD=dD, sD=sD: nc.scalar.copy(dD, sD),
                                            reads=[self.ps_t[bd[m]]], writes=trks)
                                else:
                                    self.op(self.dve, lambda dN=dN, sN=sN: nc.vector.tensor_tensor(dN, sN, dN, ALU.add),
                                            reads=[self.ps_t[bn[m]]] + trks, writes=trks)
                                    self.op(self.dve, lambda dD=dD, sD=sD: nc.vector.tensor_tensor(dD, sD, dD, ALU.add),
                                            reads=[self.ps_t[bd[m]]] + trks, writes=trks)
                                self.ps_release(bn[m])
                                self.ps_release(bd[m])
                        pend_pv.append(emit_pv)
                        if len(pend_pv) > 2:
                            pend_pv.pop(0)()
                    while pend_pv:
                        pend_pv.pop(0)()
                for T in range(4 if KSTAGE > 4 else 0):
                    cols = slice(T * 512, (T + 1) * 512)
                    self.op(self.dve, lambda cols=cols: nc.vector.reciprocal(accD[:, cols], accD[:, cols]),
                            reads=[acc_t[T]], writes=[acc_t[T]])
                    self.op(self.dve, lambda cols=cols, h=h: nc.vector.tensor_tensor(
                        self.preT[:, h, cols], accN[:, cols], accD[:, cols], ALU.mult),
                        reads=[acc_t[T]], writes=[self.preT_t[h][T]])
            self.barrier()


def _consts():
    p = np.arange(128)
    ident = np.eye(128, dtype=np.float32)
    mask2 = np.zeros((128, 256), np.float32)
    mask2[:, 0:128] = (p[:, None] <= p[None, :])
    mask2[:, 128:256] = (p[:, None] >= p[None, :])
    gmask = ((p[:, None] // 64 == p[None, :] // 64) & (p[:, None] <= p[None, :])).astype(np.float32)
    rmask = np.ones((128, 512), np.float32)
    rmask[:, 0::64] = 0.0
    half = 16
    inv = (np.float32(500000.0) ** (-(np.arange(half, dtype=np.float32) / np.float32(half)))).astype(np.float32)
    cos = np.zeros((128, 3, 16, 16), np.float32)
    sin = np.zeros((128, 3, 16, 16), np.float32)
    for g, r in enumerate(DIL):
        nbp = 16 // r
        for b in range(16):
            phase, jb = b // nbp, b % nbp
            t = (r * (128 * jb + p) + phase).astype(np.float32)
            ang = (t[:, None] * inv[None, :]).astype(np.float32)
            cos[:, g, b, :] = np.cos(ang)
            sin[:, g, b, :] = np.sin(ang)
    return dict(c_ident=ident, c_ones=np.ones((128, 128), np.float32), c_mask2=mask2, c_gmask=gmask, c_rmask=rmask,
                c_cos=cos.reshape(128, -1), c_sin=sin.reshape(128, -1))


_NC_CACHE = {}


def _get_nc(nseq, nlayers, dbg=None):
    key = (nseq, nlayers, None if dbg is None else tuple(sorted(dbg)))
    if key not in _NC_CACHE:
        k = Kern(nseq, nlayers, dbg)
        _NC_CACHE[key] = k.build()
    return _NC_CACHE[key]


def _layout_params(mem_norm_w, norm_w, w_memkv, w_out, w_in_a, w_gate_up, b_gate, gla_norm_w, w_in_b, final_norm_w):
    f = lambda a: np.ascontiguousarray(np.asarray(a, dtype=np.float32))
    fm8 = lambda v: f(np.asarray(v).reshape(8, 128).T)
    d = {}
    d["memnw_fm"] = fm8(mem_norm_w)
    d["normw_fm"] = f(np.concatenate([fm8(norm_w[i]) for i in range(4)], axis=1))
    d["w_memkv"] = f(w_memkv)
    d["w_out"] = f(w_out)
    d["w_in_a"] = f(w_in_a)
    d["w_gate_up"] = f(w_gate_up)
    bg = np.asarray(b_gate)
    d["bg_fm"] = f(np.concatenate([bg[j].reshape(4, 96).T for j in range(2)], axis=1))
    gn = np.asarray(gla_norm_w)
    idx = (np.arange(768) % 192).reshape(6, 128).T
    d["gnw_fm"] = f(np.concatenate([gn[j][idx] for j in range(2)], axis=1))
    d["w_in_b"] = f(w_in_b)
    d["fnw_bc"] = f(np.broadcast_to(np.asarray(final_norm_w)[None, :], (128, D)))
    d.update(_consts())
    return d


def kernel(x, mem, mem_norm_w, norm_w, w_memkv, w_out, w_in_a, w_gate_up, b_gate, gla_norm_w, w_in_b,
           final_norm_w, _nlayers=4, _nseq_launch=4):
    x = np.asarray(x, dtype=np.float32)
    mem = np.asarray(mem, dtype=np.float32)
    B = x.shape[0]
    per_core = B // NCORES
    params = _layout_params(mem_norm_w, norm_w, w_memkv, w_out, w_in_a, w_gate_up, b_gate, gla_norm_w,
                            w_in_b, final_norm_w)
    out = np.empty_like(x)
    nseq = _nseq_launch
    nc = _get_nc(nseq, _nlayers)
    for s0 in range(0, per_core, nseq):
        in_maps = []
        for c in range(NCORES):
            b0 = c * per_core + s0
            m = dict(params)
            m["x"] = np.ascontiguousarray(x[b0:b0 + nseq])
            m["mem"] = np.ascontiguousarray(mem[b0:b0 + nseq])
            in_maps.append(m)
        res = run_bass_kernel_spmd(nc, in_maps, core_ids=list(range(NCORES)))
        for c in range(NCORES):
            b0 = c * per_core + s0
            out[b0:b0 + nseq] = np.asarray(res.results[c]["out"]).reshape(nseq, SEQ, D)
    return out
```

```python
import os
import numpy as np
from contextlib import ExitStack
import concourse.bass as bass
import concourse.mybir as mybir
from concourse.bass_utils import run_bass_kernel_spmd

F32 = mybir.dt.float32
BF16 = mybir.dt.bfloat16
AF = mybir.ActivationFunctionType
ALU = mybir.AluOpType

NCORES = 8
SEQ = 2048
D = 1024
NMEM = 256
DIL = [1, 4, 16]
EPOCH = 12000
NSLOT = 3
SLOTW = 384
EPS = 1e-6
KSTAGE = int(os.environ.get('KSTAGE', '9'))
KSUB = int(os.environ.get('KSUB', '9'))
T0S = int(os.environ.get('T0S', '9'))


class Trk:
    __slots__ = ("w", "r", "rd", "excl")

    def __init__(self, excl=False):
        self.excl = excl
        self.w = None
        self.r = {}
        self.rd = []


class Eng:
    def __init__(self, name, h):
        self.name = name
        self.h = h
        self.count = 0
        self.pending = False
        self.sems = []
        self.waited = {}


class Queue:
    def __init__(self, name, h, nslots):
        self.name = name
        self.h = h
        self.sems = []
        self.uses = [0] * nslots
        self.k = 0
        self.waited = {}


class Kern:
    def __init__(self, nseq, nlayers, dbg=None):
        self.nseq = nseq
        self.nlayers = nlayers
        self.dry = True
        self.jobs = []
        self.dbg = dbg

    def setup_engines(self, es):
        nc = self.nc
        self.pe = Eng("pe", nc.tensor)
        self.act = Eng("act", nc.scalar)
        self.dve = Eng("dve", nc.vector)
        self.engs = {"pe": self.pe, "act": self.act, "dve": self.dve}
        self.qsp = Queue("sp", nc.sync, 8)
        self.qpl = Queue("pool", nc.gpsimd, 8)
        if not self.dry:
            for e in self.engs.values():
                nep = self.est_counts[e.name] // EPOCH + 1
                for i in range(nep):
                    e.sems.append(es.enter_context(nc.semaphore(f"s_{e.name}{i}")))
            for q in (self.qsp, self.qpl):
                for i in range(len(q.uses)):
                    q.sems.append(es.enter_context(nc.semaphore(f"q_{q.name}{i}")))

    def _tok_sem(self, tok):
        if tok[0] == "d":
            return tok[1], tok[2]
        e = self.engs[tok[0]]
        c = tok[1]
        ep = (c - 1) // EPOCH
        return e.sems[ep], c - ep * EPOCH

    def _need(self, waiter, tok, out):
        if tok is None:
            return
        if tok[0] == "d":
            key = ("d", tok[3])
            if waiter.waited.get(key, 0) >= tok[2]:
                return
            if out.get(key, (None, 0))[1] < tok[2]:
                out[key] = (tok, tok[2])
        else:
            key = tok[0]
            if waiter.waited.get(key, 0) >= tok[1]:
                return
            if out.get(key, (None, 0))[1] < tok[1]:
                out[key] = (tok, tok[1])

    def _collect(self, waiter, reads, writes, acc, own):
        out = {}
        for t in reads:
            if t.w is not None and not (t.w[0] == own and own == "pe"):
                self._need(waiter, t.w, out)
            if t.excl:
                for en, c in t.r.items():
                    if en != own:
                        self._need(waiter, (en, c), out)
        if not acc:
            for t in writes:
                if t.w is not None and not (t.w[0] == own and own == "pe"):
                    self._need(waiter, t.w, out)
                for en, c in t.r.items():
                    if not (en == own and own == "pe"):
                        self._need(waiter, (en, c), out)
                for dt_ in t.rd:
                    self._need(waiter, dt_, out)
        return out

    def _emit_waits(self, waiter, h, need):
        for key, (tok, v) in need.items():
            waiter.waited[key] = v
            if not self.dry:
                sem, val = self._tok_sem(tok)
                h.wait_ge(sem, val)

    def op(self, eng, fn, reads=(), writes=(), inc=True, acc=False):
        need = self._collect(eng, reads, writes, acc, eng.name)
        self._emit_waits(eng, eng.h, need)
        if inc:
            eng.count += 1
            eng.pending = False
            c = eng.count
        else:
            eng.pending = True
            c = eng.count + 1
        if not self.dry:
            ins = fn()
            if inc:
                ep = (c - 1) // EPOCH
                ins.then_inc(eng.sems[ep], 1)
        tok = (eng.name, c)
        for t in reads:
            if t.r.get(eng.name, 0) < c:
                t.r[eng.name] = c
        for t in writes:
            if acc:
                t.w = tok
            else:
                t.w = tok
                t.r = {}
                t.rd = []
        return tok

    def dma(self, q, out_ap, in_ap=None, reads=(), writes=()):
        pairs = out_ap if in_ap is None else [(out_ap, in_ap)]
        need = self._collect(q, reads, writes, False, q.name)
        self._emit_waits(q, q.h, need)
        slot = q.k % len(q.uses)
        q.k += 1
        prev = q.uses[slot] * 16
        q.uses[slot] += len(pairs)
        val = q.uses[slot] * 16
        if not self.dry:
            sem = q.sems[slot]
            if prev > 0:
                q.h.wait_ge(sem, prev)
            for (o, i) in pairs:
                q.h.dma_start(out=o, in_=i).then_inc(sem, 16)
            tok = ("d", sem, val, (q.name, slot))
        else:
            tok = ("d", None, val, (q.name, slot))
        for t in reads:
            t.rd.append(tok)
        for t in writes:
            t.w = tok
            t.r = {}
            t.rd = []
        return tok

    def barrier(self):
        for e in self.engs.values():
            assert not e.pending, e.name
        waiters = list(self.engs.values()) + [self.qsp, self.qpl]
        for w in waiters:
            need = {}
            for o in self.engs.values():
                if o is w or o.count == 0:
                    continue
                self._need(w, (o.name, o.count), need)
            self._emit_waits(w, w.h, need)

    def un(self, name):
        self.uid += 1
        return f"{name}_{self.uid}"

    def ps_alloc(self):
        return self.ps_free.pop(0)

    def ps_release(self, i):
        self.ps_free.append(i)

    def wnext(self, pieces):
        if self.dry:
            self.jobs.append(pieces)
            i = len(self.jobs) - 1
            return i % NSLOT, self.wtrk[i % NSLOT]
        i = self.wjob
        self.wjob += 1
        while self.wissued < min(i + NSLOT, len(self.jobs)):
            j = self.wissued
            s = j % NSLOT
            pairs = []
            for (name, idx, sc0, n, dc0) in self.jobs[j]:
                src = getattr(self, name + "_d")[idx, :, sc0:sc0 + n]
                pairs.append((self.wslot[s][:, :, dc0:dc0 + n], src.rearrange("(kc p) c -> p kc c", p=128)))
            self.dma(self.qpl, pairs, writes=[self.wtrk[s]])
            self.wissued += 1
        return i % NSLOT, self.wtrk[i % NSLOT]

    def drip(self, n=1):
        while n > 0 and self.t0q:
            self.t0q.pop(0)()
            n -= 1

    def flush_t0(self):
        while self.t0q:
            self.t0q.pop(0)()

    def mm(self, out, lhsT, rhs, start, stop, reads, writes, inc):
        nc = self.nc
        self.mmc += 1
        if self.mmc % 3 == 0:
            self.drip()
        return self.op(self.pe,
                       lambda: nc.tensor.matmul(out, lhsT, rhs, start=start, stop=stop,
                                                skip_group_check=True),
                       reads=reads, writes=writes, inc=inc, acc=not start)

    def proj_fm(self, bank, M, slot, strk, c0, T, ncols=512, pbase=0):
        out = self.ps[bank][pbase:pbase + M, 0:ncols]
        for kc in range(8):
            self.mm(out, self.wslot[slot][:, kc, c0:c0 + M],
                    self.hT[:, kc, T * 512:T * 512 + ncols],
                    start=(kc == 0), stop=(kc == 7),
                    reads=[strk, self.hT_t[T]], writes=[self.ps_t[bank]], inc=(kc == 7))

    def build(self):
        self.dry = True
        self._build_once()
        self.est_counts = {n: e.count for n, e in self.engs.items()}
        jobs = self.jobs
        self.dry = False
        self.jobs = jobs
        self._build_once()
        return self.nc

    def _build_once(self):
        nseq = self.nseq
        nc = bass.Bass("TRN2", target_bir_lowering=False)
        self.nc = nc
        self.wjob = 0
        self.wissued = 0
        dt = lambda name, shape, kind="ExternalInput", dtype=F32: nc.dram_tensor(name, shape, dtype, kind=kind).ap()
        self.x_d = dt("x", [nseq, SEQ, D])
        self.mem_d = dt("mem", [nseq, NMEM, D])
        self.memnw_d = dt("memnw_fm", [128, 8])
        self.normw_d = dt("normw_fm", [128, 4 * 8])
        self.wmemkv_d = dt("w_memkv", [4, D, 512])
        self.wout_d = dt("w_out", [4, D, D])
        self.wina_d = dt("w_in_a", [2, D, 2832])
        self.wgu_d = dt("w_gate_up", [2, 16, 384])
        self.bg_d = dt("bg_fm", [96, 2 * 4])
        self.gnw_d = dt("gnw_fm", [128, 2 * 6])
        self.winb_d = dt("w_in_b", [2, D, 8192])
        self.fnw_d = dt("fnw_bc", [128, D])
        self.c_ident_d = dt("c_ident", [128, 128])
        self.c_ones_d = dt("c_ones", [128, 128])
        self.c_mask2_d = dt("c_mask2", [128, 256])
        self.c_gmask_d = dt("c_gmask", [128, 128])
        self.c_rmask_d = dt("c_rmask", [128, 512])
        self.c_cos_d = dt("c_cos", [128, 3 * 16 * 16])
        self.c_sin_d = dt("c_sin", [128, 3 * 16 * 16])
        self.out_d = dt("out", [nseq, SEQ, D], kind="ExternalOutput")
        if self.dbg:
            self.dbg_d = {k: dt("dbg_" + k, shp, kind="ExternalOutput") for k, shp in self.dbg.items()}

        with ExitStack() as es:
            self.setup_engines(es)
            sb = lambda name, shape, dtype: es.enter_context(nc.sbuf_tensor(name, shape, dtype))
            self.x = sb("x_sb", [128, 16, D], F32)
            self.x_t = [Trk() for _ in range(16)]
            self.hT = sb("hT", [128, 8, SEQ], BF16)
            self.hT_t = [Trk() for _ in range(4)]
            self.preT = sb("preT", [128, 8, SEQ], BF16)
            self.preT_t = [[Trk() for _ in range(4)] for _ in range(8)]
            self.memT = sb("memT", [128, 8, NMEM], BF16)
            self.memT_t = Trk()
            self.kmT = sb("kmT", [128, 2, NMEM], BF16)
            self.kmT_t = Trk()
            self.vm = sb("vm", [128, 2, 256], BF16)
            self.vm_t = Trk()
            self.wslot = [sb(f"wslot{i}", [128, 8, SLOTW], BF16) for i in range(NSLOT)]
            self.wtrk = [Trk() for _ in range(NSLOT)]
            self.ps = [es.enter_context(nc.psum_tensor(f"ps{i}", [128, 512], F32)) for i in range(8)]
            self.ps_t = [Trk(excl=True) for _ in range(8)]
            self.ps_free = list(range(7))
            self.ident = sb("ident", [128, 128], BF16)
            self.mask2 = sb("mask2", [128, 256], BF16)
            self.gmask = sb("gmask", [128, 128], BF16)
            self.ones = sb("ones", [128, 128], BF16)
            self.cos = sb("cos", [128, 3, 16, 16], F32)
            self.sin = sb("sin", [128, 3, 16, 16], F32)
            self.memnw = sb("memnw", [128, 8], F32)
            self.normw = sb("normw", [128, 4, 8], F32)
            self.bg = sb("bg", [96, 8], F32)
            self.nbg = sb("nbg", [96, 8], F32)
            self.gnw = sb("gnw", [128, 12], F32)
            self.const_t = Trk()
            self.ident32 = sb("ident32", [128, 128], F32)
            self.ones32 = sb("ones32", [128, 128], F32)
            self.w32 = [sb(f"w32_{i}", [128, 8, 128], F32) for i in range(2)]
            self.w32_t = [Trk() for _ in range(2)]
            self.w32_n = 0
            self.t0_prev_pos = None
            self.kmT32 = sb("kmT32", [128, 2, NMEM], F32)
            self.vm32 = sb("vm32", [128, 2, 256], F32)
            self.km32_t = Trk()
            self.vm32_t = Trk()
            self.x0T = sb("x0T", [128, 8], F32)
            self.h0T = sb("h0T", [128, 8], F32)
            self.m0T = sb("m0T", [128, 8], F32)
            self.sg0 = sb("sg0", [128, 8], F32)
            self.t0s = sb("t0s", [128, 32], F32)
            self.e8 = sb("e8", [128, 8], F32)
            self.x0_t = Trk()
            self.h0_t = Trk()
            self.m0_t = Trk()
            self.t0s_t = Trk()
            self.tb = 7
            self.tbc = 0
            self.t0q = []
            self.mmc = 0
            self.junk_t = Trk()
            self.ss16 = sb("ss16", [128, 16], F32)
            self.rstd16 = sb("rstd16", [128, 16], F32)
            self.st_t = Trk()
            self.ss_t = [Trk() for _ in range(16)]
            self.uid = 0

            self.load_consts()
            for si in range(nseq):
                self.do_seq(si)
            if not self.dry:
                for tok in self.out_toks:
                    key = ("d", tok[3])
                    if self.qsp.waited.get(key, 0) < tok[2]:
                        self.qsp.waited[key] = tok[2]
                        nc.sync.wait_ge(tok[1], tok[2])

    def load_consts(self):
        nc = self.nc
        toks = []
        D_ = lambda q, o, i: toks.append(self.dma(q, o, i))
        D_(self.qpl, self.ident[:], self.c_ident_d)
        D_(self.qsp, self.ident32[:], self.c_ident_d)
        D_(self.qsp, self.ones32[:], self.c_ones_d)
        D_(self.qpl, self.mask2[:], self.c_mask2_d)
        D_(self.qpl, self.gmask[:], self.c_gmask_d)
        D_(self.qsp, self.cos[:].rearrange("p a b c -> p (a b c)"), self.c_cos_d)
        D_(self.qsp, self.sin[:].rearrange("p a b c -> p (a b c)"), self.c_sin_d)
        D_(self.qsp, self.memnw[:], self.memnw_d)
        D_(self.qsp, self.normw[:].rearrange("p a b -> p (a b)"), self.normw_d)
        D_(self.qsp, self.bg[:], self.bg_d)
        D_(self.qsp, self.gnw[:], self.gnw_d)
        for e in self.engs.values():
            need = {}
            for tok in toks:
                self._need(e, tok, need)
            self._emit_waits(e, e.h, need)
        self.op(self.dve, lambda: nc.vector.memset(self.ones[:], 1.0), writes=[self.const_t])
        self.op(self.dve, lambda: nc.vector.tensor_scalar(self.nbg[:], self.bg[:], -1.0, None, ALU.mult),
                writes=[self.const_t])
        self.out_toks = []

    def norm_to_fm(self, src_aps, src_trks, w_ap, dst, dst_trk_of, ntiles):
        nc = self.nc
        with ExitStack() as es:
            xs0 = es.enter_context(nc.sbuf_tensor(self.un("xs"), [128, D], BF16))
            self.xs = [xs0, xs0]
            t_ = Trk()
            self.xs_t = [t_, t_]
            self.junk = es.enter_context(nc.sbuf_tensor(self.un("junk"), [128, D], BF16))
            self.junk_t = Trk()
            self._norm_to_fm(src_aps, src_trks, w_ap, dst, dst_trk_of, ntiles)
            self.barrier()

    def _norm_to_fm(self, src_aps, src_trks, w_ap, dst, dst_trk_of, ntiles):
        nc = self.nc
        for i in range(ntiles):
            self.op(self.act, lambda i=i: nc.scalar.activation(
                out=self.junk[:], in_=src_aps[i], func=AF.Square, accum_out=self.ss16[:, i:i + 1]),
                reads=[src_trks[i]], writes=[self.ss_t[i], self.junk_t])
        self.op(self.act, lambda: nc.scalar.activation(
            out=self.rstd16[:, 0:ntiles], in_=self.ss16[:, 0:ntiles], func=AF.Ln, scale=1.0 / D, bias=EPS),
            reads=self.ss_t[0:ntiles], writes=[self.st_t])
        self.op(self.act, lambda: nc.scalar.activation(
            out=self.rstd16[:, 0:ntiles], in_=self.rstd16[:, 0:ntiles], func=AF.Exp, scale=-0.5),
            reads=[self.st_t], writes=[self.st_t])
        for i in range(ntiles):
            b = i % 2
            self.op(self.dve, lambda i=i, b=b: nc.vector.tensor_scalar(
                self.xs[b][:], src_aps[i], self.rstd16[:, i:i + 1], None, ALU.mult),
                reads=[src_trks[i], self.st_t], writes=[self.xs_t[b]])
            bank = self.ps_alloc()
            pb = self.ps[bank][:].bitcast(BF16)
            for kc in range(8):
                self.op(self.pe, lambda kc=kc, b=b, pb=pb: nc.tensor.transpose(
                    pb[:, kc * 128:(kc + 1) * 128], self.xs[b][:, kc * 128:(kc + 1) * 128], self.ident[:]),
                    reads=[self.xs_t[b], self.const_t], writes=[self.ps_t[bank]], inc=(kc == 7), acc=(kc > 0))
            dtrk = dst_trk_of(i)
            self.op(self.dve, lambda i=i, pb=pb: nc.vector.tensor_tensor(
                dst[:, :, i * 128:(i + 1) * 128],
                pb.rearrange("p (k t) -> p k t", k=8),
                w_ap.unsqueeze(2).to_broadcast([128, 8, 128]), ALU.mult),
                reads=[self.ps_t[bank], self.const_t], writes=[dtrk])
            self.ps_release(bank)

    def do_seq(self, si):
        nc = self.nc
        for tt in range(16):
            self.dma(self.qsp, self.x[:, tt, :], self.x_d[si, tt * 128:(tt + 1) * 128, :], writes=[self.x_t[tt]])
        with ExitStack() as es:
            memraw = es.enter_context(nc.sbuf_tensor(self.un("memraw"), [128, 2, D], F32))
            mr_t = [Trk(), Trk()]
            for i in range(2):
                self.dma(self.qsp, memraw[:, i, :], self.mem_d[si, i * 128:(i + 1) * 128, :], writes=[mr_t[i]])
            self.norm_to_fm([memraw[:, i, :] for i in range(2)], mr_t, self.memnw[:], self.memT,
                            lambda i: self.memT_t, 2)
            self.barrier()
        self.t0_init()
        for li in range(self.nlayers):
            self.do_layer(si, li)
        self.final_norm(si)

    def do_layer(self, si, li):
        nc = self.nc
        j = li // 2
        self.norm_to_fm([self.x[:, tt, :] for tt in range(16)], self.x_t, self.normw[:, li, :], self.hT,
                        lambda i: self.hT_t[i // 4], 16)
        self.mem_kv(li)
        self.t0_layer(li)
        mode = self.t0_mode(li)
        if li % 2 == 0:
            self.mixer_a(j)
            wname, qm0, g0 = "wina", 1552, 1808
        else:
            self.mixer_b(j)
            wname, qm0, g0 = "winb", 6912, 7168
        if mode == "heads" and T0S >= 3:
            self.flush_t0()
            self.op(self.dve, lambda: nc.vector.tensor_copy(self.preT[:, 0:6, 0:1], self.m0T[:, 0:6].unsqueeze(2)),
                    reads=[self.m0_t], writes=[self.preT_t[c][0] for c in range(6)])
        self.tail(li, j, wname, qm0, g0)
        if mode == "full":
            self.flush_t0()
            self.t0_inject()

    def mem_kv(self, li):
        nc = self.nc
        s, st = self.wnext([("wmemkv", li, 0, 256, 0)])
        for c2 in range(2):
            bank = self.ps_alloc()
            out = self.ps[bank][:, 0:256]
            for kc in range(8):
                self.mm(out, self.wslot[s][:, kc, c2 * 128:(c2 + 1) * 128], self.memT[:, kc, :],
                        start=(kc == 0), stop=(kc == 7), reads=[st, self.memT_t], writes=[self.ps_t[bank]],
                        inc=(kc == 7))
            self.op(self.act, lambda c2=c2, out=out: nc.scalar.copy(self.kmT[:, c2, :], out),
                    reads=[self.ps_t[bank]], writes=[self.kmT_t])
            self.op(self.act, lambda c2=c2, out=out: nc.scalar.copy(self.kmT32[:, c2, :], out),
                    reads=[self.ps_t[bank]], writes=[self.km32_t])
            self.ps_release(bank)
        s, st = self.wnext([("wmemkv", li, 256, 256, 0)])
        for kt in range(2):
            bank = self.ps_alloc()
            out = self.ps[bank][:, 0:256]
            for kc in range(8):
                self.mm(out, self.memT[:, kc, kt * 128:(kt + 1) * 128], self.wslot[s][:, kc, 0:256],
                        start=(kc == 0), stop=(kc == 7), reads=[st, self.memT_t], writes=[self.ps_t[bank]],
                        inc=(kc == 7))
            self.op(self.act, lambda kt=kt, out=out: nc.scalar.copy(self.vm[:, kt, :], out),
                    reads=[self.ps_t[bank]], writes=[self.vm_t])
            self.op(self.act, lambda kt=kt, out=out: nc.scalar.copy(self.vm32[:, kt, :], out),
                    reads=[self.ps_t[bank]], writes=[self.vm32_t])
            self.ps_release(bank)

    def tail(self, li, j, wname, qm0, g0):
        nc = self.nc
        with ExitStack() as es:
            sbt = lambda name, shape, dtype: es.enter_context(nc.sbuf_tensor(self.un(name), shape, dtype))
            qm = sbt("qm", [128, 2, 512], BF16)
            qm_t = Trk()
            pT = [sbt(f"pTm{i}", [128, 512], BF16) for i in range(3)]
            pT_t = [Trk() for _ in range(3)]
            rden = sbt("rdenm", [128, 512], F32)
            rden_t = Trk()
            sg = [sbt(f"sg{i}", [128, 512], BF16) for i in range(2)]
            sg_t = [Trk() for _ in range(2)]
            pk = 0
            for T in range(4):
                cols = slice(T * 512, (T + 1) * 512)
                s, st = self.wnext([(wname, j, qm0, 256, 0)])
                for c2 in range(2):
                    bank = self.ps_alloc()
                    self.proj_fm(bank, 128, s, st, c2 * 128, T)
                    self.op(self.act, lambda c2=c2, bank=bank: nc.scalar.copy(qm[:, c2, :], self.ps[bank][:, :]),
                            reads=[self.ps_t[bank]], writes=[qm_t])
                    self.ps_release(bank)
                gate_jobs = [(0, 384), (384, 384), (768, 256)]
                gstate = {"ji": 0, "cc": 0, "s": None, "st": None}

                def emit_gate_chunk():
                    if gstate["ji"] >= len(gate_jobs):
                        return False
                    c0, n = gate_jobs[gstate["ji"]]
                    if gstate["cc"] == 0:
                        gstate["s"], gstate["st"] = self.wnext([(wname, j, g0 + c0, n, 0)])
                    s_, st_ = gstate["s"], gstate["st"]
                    cc = gstate["cc"]
                    c = c0 // 128 + cc
                    bank = self.ps_alloc()
                    self.proj_fm(bank, 128, s_, st_, cc * 128, T)
                    b_ = c % 2
                    self.op(self.act, lambda b_=b_, bank=bank: nc.scalar.activation(
                        out=sg[b_][:], in_=self.ps[bank][:, :], func=AF.Silu),
                        reads=[self.ps_t[bank]], writes=[sg_t[b_]])
                    self.ps_release(bank)
                    self.gate_pending.append((c, b_))
                    gstate["cc"] += 1
                    if gstate["cc"] >= n // 128:
                        gstate["cc"] = 0
                        gstate["ji"] += 1
                    return True

                def emit_gate_mults(upto=None):
                    keep = []
                    for (c, b_) in self.gate_pending:
                        if c < 6 or self.mem_done:
                            self.op(self.dve, lambda b_=b_, c=c: nc.vector.tensor_tensor(
                                self.preT[:, c, cols], self.preT[:, c, cols], sg[b_][:], ALU.mult),
                                reads=[sg_t[b_], self.preT_t[c][T]], writes=[self.preT_t[c][T]])
                        else:
                            keep.append((c, b_))
                    self.gate_pending = keep

                self.gate_pending = []
                self.mem_done = False
                pend = []
                banks = {}
                step = 0
                for c2 in range(2):
                    for half in range(2):
                        hm = 2 * c2 + half
                        pr = slice(64 * half, 64 * half + 64)
                        for kt in range(2):
                            if c2 not in banks:
                                banks[c2] = (self.ps_alloc(), self.ps_alloc())
                            bs = self.ps_alloc()
                            self.mm(self.ps[bs][:, :], self.kmT[pr, c2, kt * 128:(kt + 1) * 128], qm[pr, c2, :],
                                    start=True, stop=True, reads=[self.kmT_t, qm_t], writes=[self.ps_t[bs]], inc=True)
                            p = pk % 3
                            pk += 1
                            self.op(self.act, lambda p=p, bs=bs: nc.scalar.activation(
                                out=pT[p][:], in_=self.ps[bs][:, :], func=AF.Exp, scale=0.125),
                                reads=[self.ps_t[bs]], writes=[pT_t[p]])
                            self.ps_release(bs)

                            def emit_pv(c2=c2, half=half, hm=hm, pr=pr, kt=kt, p=p):
                                bn, bd = banks[c2]
                                self.mm(self.ps[bn][pr, :], self.vm[:, kt, hm * 64:(hm + 1) * 64], pT[p][:],
                                        start=(kt == 0), stop=(kt == 1), reads=[self.vm_t, pT_t[p]],
                                        writes=[self.ps_t[bn]], inc=False)
                                self.mm(self.ps[bd][pr, :], self.ones[:, 0:64], pT[p][:],
                                        start=(kt == 0), stop=(kt == 1), reads=[pT_t[p]],
                                        writes=[self.ps_t[bd]], inc=True)
                                if half == 1 and kt == 1:
                                    self.op(self.dve, lambda bd=bd: nc.vector.reciprocal(rden[:], self.ps[bd][:, :]),
                                            reads=[self.ps_t[bd]], writes=[rden_t])
                                    self.op(self.dve, lambda bn=bn, c2=c2: nc.vector.tensor_tensor(
                                        self.preT[:, 6 + c2, cols], self.ps[bn][:, :], rden[:], ALU.mult),
                                        reads=[self.ps_t[bn], rden_t], writes=[self.preT_t[6 + c2][T]])
                                    self.ps_release(bn)
                                    self.ps_release(bd)
                            pend.append(emit_pv)
                            if len(pend) > 1:
                                pend.pop(0)()
                            if step < 6:
                                emit_gate_chunk()
                                emit_gate_mults()
                            step += 1
                while pend:
                    pend.pop(0)()
                self.mem_done = True
                while emit_gate_chunk():
                    emit_gate_mults()
                emit_gate_mults()
                for (c0, n) in [(0, 384), (384, 384), (768, 256)]:
                    s, st = self.wnext([("wout", li, c0, n, 0)])
                    for sub in range(4):
                        tt = 4 * T + sub
                        bank = self.ps_alloc()
                        out = self.ps[bank][:, 0:n]
                        for kc in range(8):
                            self.mm(out, self.preT[:, kc, tt * 128:(tt + 1) * 128], self.wslot[s][:, kc, 0:n],
                                    start=(kc == 0), stop=(kc == 7), reads=[st, self.preT_t[kc][T]],
                                    writes=[self.ps_t[bank]], inc=(kc == 7))
                        self.op(self.dve, lambda tt=tt, out=out, c0=c0, n=n: nc.vector.tensor_tensor(
                            self.x[:, tt, c0:c0 + n], out, self.x[:, tt, c0:c0 + n], ALU.add),
                            reads=[self.ps_t[bank], self.x_t[tt]], writes=[self.x_t[tt]])
                        self.ps_release(bank)
            self.barrier()

    def final_norm(self, si):
        nc = self.nc
        with ExitStack() as es:
            fnw = es.enter_context(nc.sbuf_tensor(self.un("fnw"), [128, D], F32))
            self.junk = es.enter_context(nc.sbuf_tensor(self.un("junk"), [128, D], BF16))
            self.junk_t = Trk()
            fnw_t = Trk()
            self.dma(self.qsp, fnw[:], self.fnw_d, writes=[fnw_t])
            self._final_norm(si, fnw, fnw_t)
            self.barrier()

    def _final_norm(self, si, fnw, fnw_t):
        nc = self.nc
        for i in range(16):
            self.op(self.act, lambda i=i: nc.scalar.activation(
                out=self.junk[:], in_=self.x[:, i, :], func=AF.Square, accum_out=self.ss16[:, i:i + 1]),
                reads=[self.x_t[i]], writes=[self.ss_t[i], self.junk_t])
        self.op(self.act, lambda: nc.scalar.activation(
            out=self.rstd16[:], in_=self.ss16[:], func=AF.Ln, scale=1.0 / D, bias=EPS),
            reads=self.ss_t, writes=[self.st_t])
        self.op(self.act, lambda: nc.scalar.activation(
            out=self.rstd16[:], in_=self.rstd16[:], func=AF.Exp, scale=-0.5),
            reads=[self.st_t], writes=[self.st_t])
        for i in range(16):
            self.op(self.dve, lambda i=i: nc.vector.scalar_tensor_tensor(
                self.x[:, i, :], self.x[:, i, :], self.rstd16[:, i:i + 1], fnw[:], ALU.mult, ALU.mult),
                reads=[self.x_t[i], self.st_t, fnw_t], writes=[self.x_t[i]])
            tok = self.dma(self.qsp, self.out_d[si, i * 128:(i + 1) * 128, :], self.x[:, i, :], reads=[self.x_t[i]])
            self.out_toks.append(tok)


    def tcol(self, n=1):
        if self.tbc + n > 512:
            self.tbc = 0
        c = self.tbc
        self.tbc += n
        return c

    def T(self, eng, fn, reads, writes, **kw):
        self.t0q.append(lambda: self.op(eng, fn, reads=reads, writes=writes, **kw))

    def t0_init(self):
        if T0S < 1:
            return
        nc = self.nc
        tb = self.ps[self.tb]
        tbt = self.ps_t[self.tb]
        c = self.tcol(8)
        for kc in range(8):
            self.op(self.pe, lambda kc=kc: nc.tensor.matmul(
                tb[:, c + kc:c + kc + 1], self.x[0:1, 0, kc * 128:(kc + 1) * 128], self.ones32[0:1, 0:1],
                start=True, stop=True, skip_group_check=True),
                reads=[self.x_t[0], self.const_t], writes=[tbt], inc=(kc == 7), acc=(kc > 0))
        self.op(self.dve, lambda: nc.vector.tensor_copy(self.x0T[:], tb[:, c:c + 8]),
                reads=[tbt], writes=[self.x0_t])

    def t0_inject(self):
        if T0S < 1:
            return
        nc = self.nc
        tb = self.ps[self.tb]
        tbt = self.ps_t[self.tb]
        for hf in range(2):
            for k4 in range(4):
                kc = 4 * hf + k4
                self.op(self.pe, lambda kc=kc, k4=k4: nc.tensor.matmul(
                    tb[0:1, k4 * 128:(k4 + 1) * 128], self.x0T[:, kc:kc + 1], self.ident32[:],
                    start=True, stop=True, skip_group_check=True),
                    reads=[self.x0_t, self.const_t], writes=[tbt], inc=(k4 == 3), acc=(k4 > 0))
            self.op(self.dve, lambda hf=hf: nc.vector.tensor_copy(self.x[0:1, 0, hf * 512:(hf + 1) * 512], tb[0:1, :]),
                    reads=[tbt], writes=[self.x_t[0]])
        self.tbc = 0

    def t0_cproj(self, vec, vec_t, wname, idx, c0, M, col, pbase=0):
        nc = self.nc
        tb = self.ps[self.tb]
        tbt = self.ps_t[self.tb]
        k = self.w32_n % 2
        self.w32_n += 1
        w32, w32_t = self.w32[k], self.w32_t[k]

        def load():
            srcap = getattr(self, wname + "_d")[idx, :, c0:c0 + M].rearrange("(kc p) c -> p kc c", p=128)
            self.dma(self.qsp, w32[:, :, 0:M], srcap, writes=[w32_t])
        if self.t0_prev_pos is not None:
            self.t0q.insert(self.t0_prev_pos, load)
        else:
            self.t0q.append(load)
        self.t0_prev_pos = len(self.t0q)
        for kc in range(8):
            self.T(self.pe, lambda kc=kc: nc.tensor.matmul(
                tb[pbase:pbase + M, col:col + 1], w32[:, kc, 0:M], vec[:, kc:kc + 1],
                start=(kc == 0), stop=(kc == 7), skip_group_check=True),
                [w32_t, vec_t], [tbt], inc=(kc == 7), acc=(kc > 0))

    def t0_mode(self, li):
        last_gla = max([l for l in range(self.nlayers) if l % 2 == 0], default=-1)
        if li < last_gla:
            return "full"
        if li == last_gla:
            return "heads"
        return "none"

    def t0_layer(self, li):
        nc = self.nc
        j = li // 2
        self.t0_prev_pos = None
        assert not self.t0q
        mode = self.t0_mode(li)
        if mode == "none":
            return
        tb = self.ps[self.tb]
        tbt = self.ps_t[self.tb]
        s = self.t0s
        st = self.t0s_t
        T = self.T
        X = mybir.AxisListType.X
        if T0S < 2:
            return
        c = self.tcol()
        T(self.dve, lambda: nc.vector.tensor_tensor(s[:, 8:16], self.x0T[:], self.x0T[:], ALU.mult), [self.x0_t], [st])
        T(self.dve, lambda: nc.vector.tensor_reduce(out=s[:, 0:1], in_=s[:, 8:16], axis=X, op=ALU.add), [st], [st])
        T(self.pe, lambda: nc.tensor.matmul(tb[:, c:c + 1], self.ones32[:], s[:, 0:1], start=True, stop=True,
                                            skip_group_check=True), [st, self.const_t], [tbt])
        T(self.act, lambda: nc.scalar.activation(out=s[:, 1:2], in_=tb[:, c:c + 1], func=AF.Ln, scale=1.0 / D, bias=EPS),
          [tbt], [st])
        T(self.act, lambda: nc.scalar.activation(out=s[:, 1:2], in_=s[:, 1:2], func=AF.Exp, scale=-0.5), [st], [st])
        T(self.dve, lambda: nc.vector.scalar_tensor_tensor(
            self.h0T[:], self.x0T[:], s[:, 1:2], self.normw[:, li, :], ALU.mult, ALU.mult),
          [self.x0_t, st, self.const_t], [self.h0_t])
        hv, ht = self.h0T, self.h0_t
        if T0S < 3:
            return
        if li % 2 == 0:
            wname, qm0, g0 = "wina", 1552, 1808
            for h in range(4):
                c = self.tcol(8)
                self.t0_cproj(hv, ht, wname, j, 96 * h, 96, c)
                self.t0_cproj(hv, ht, wname, j, 384 + 96 * h, 96, c + 1)
                T(self.act, lambda c=c: nc.scalar.copy(s[0:96, 2:3], tb[0:96, c:c + 1]), [tbt], [st])
                T(self.dve, lambda c=c: nc.vector.tensor_tensor(s[0:96, 3:4], s[0:96, 2:3], tb[0:96, c + 1:c + 2], ALU.mult),
                  [tbt, st], [st])
                T(self.pe, lambda c=c: nc.tensor.matmul(tb[:, c + 2:c + 3], self.ones32[0:96, :], s[0:96, 3:4],
                                                        start=True, stop=True, skip_group_check=True),
                  [st, self.const_t], [tbt])
                f0 = 192 * h
                if h % 2 == 0:
                    cfull, chalf, phalf, fs0, hs0 = f0 // 128, f0 // 128 + 1, 0, 0, 128
                else:
                    chalf, cfull, phalf, hs0, fs0 = f0 // 128, f0 // 128 + 1, 64, 0, 64
                pr = slice(phalf, phalf + 64)
                self.t0_cproj(hv, ht, wname, j, 768 + f0 + fs0, 128, c + 3)
                self.t0_cproj(hv, ht, wname, j, 768 + f0 + hs0, 64, c + 4, pbase=phalf)
                T(self.dve, lambda: nc.vector.memset(s[:, 4:6], 0.0), [st], [st])
                T(self.act, lambda c=c: nc.scalar.copy(s[:, 4:5], tb[:, c + 3:c + 4]), [tbt], [st])
                T(self.act, lambda c=c, pr=pr: nc.scalar.copy(s[pr, 5:6], tb[pr, c + 4:c + 5]), [tbt], [st])
                T(self.dve, lambda: nc.vector.tensor_tensor(s[:, 16:18], s[:, 4:6], s[:, 4:6], ALU.mult), [st], [st])
                T(self.dve, lambda: nc.vector.tensor_reduce(out=s[:, 6:7], in_=s[:, 16:18], axis=X, op=ALU.add), [st], [st])
                T(self.pe, lambda c=c: nc.tensor.matmul(tb[:, c + 5:c + 6], self.ones32[:], s[:, 6:7],
                                                        start=True, stop=True, skip_group_check=True),
                  [st, self.const_t], [tbt])
                T(self.act, lambda c=c: nc.scalar.activation(out=s[:, 7:8], in_=tb[:, c + 2:c + 3], func=AF.Copy,
                                                             scale=96.0 ** -0.5), [tbt], [st])
                T(self.dve, lambda: nc.vector.tensor_tensor(s[:, 18:19], s[:, 7:8], s[:, 7:8], ALU.mult), [st], [st])
                T(self.dve, lambda c=c: nc.vector.tensor_tensor(s[:, 18:19], s[:, 18:19], tb[:, c + 5:c + 6], ALU.mult),
                  [st, tbt], [st])
                T(self.act, lambda: nc.scalar.activation(out=s[:, 18:19], in_=s[:, 18:19], func=AF.Ln,
                                                         scale=1.0 / 192, bias=EPS), [st], [st])
                T(self.act, lambda: nc.scalar.activation(out=s[:, 18:19], in_=s[:, 18:19], func=AF.Exp, scale=-0.5),
                  [st], [st])
                T(self.dve, lambda: nc.vector.tensor_tensor(s[:, 19:20], s[:, 7:8], s[:, 18:19], ALU.mult), [st], [st])
                T(self.dve, lambda cfull=cfull: nc.vector.scalar_tensor_tensor(
                    self.m0T[:, cfull:cfull + 1], s[:, 4:5], s[:, 19:20], self.gnw[:, 6 * j + cfull:6 * j + cfull + 1],
                    ALU.mult, ALU.mult), [st, self.const_t], [self.m0_t])
                T(self.dve, lambda chalf=chalf, pr=pr: nc.vector.scalar_tensor_tensor(
                    self.m0T[pr, chalf:chalf + 1], s[pr, 5:6], s[pr, 19:20], self.gnw[pr, 6 * j + chalf:6 * j + chalf + 1],
                    ALU.mult, ALU.mult), [st, self.const_t], [self.m0_t])
        else:
            wname, qm0, g0 = "winb", 6912, 7168
            for h in range(6):
                for g in range(3):
                    base = g * 2304
                    c = self.tcol(4)
                    self.t0_cproj(hv, ht, wname, j, base + 128 * h, 128, c)
                    self.t0_cproj(hv, ht, wname, j, base + 768 + 128 * h, 128, c + 1)
                    self.t0_cproj(hv, ht, wname, j, base + 1536 + 128 * h, 128, c + 2)
                    T(self.act, lambda c=c: nc.scalar.copy(s[:, 2:3], tb[:, c:c + 1]), [tbt], [st])
                    T(self.dve, lambda c=c: nc.vector.tensor_tensor(s[:, 3:4], s[:, 2:3], tb[:, c + 1:c + 2], ALU.mult),
                      [tbt, st], [st])
                    T(self.pe, lambda c=c: nc.tensor.matmul(tb[:, c + 3:c + 4], self.ones32[:], s[:, 3:4],
                                                            start=True, stop=True, skip_group_check=True),
                      [st, self.const_t], [tbt])
                    T(self.act, lambda c=c, g=g: nc.scalar.activation(out=s[:, 8 + g:9 + g], in_=tb[:, c + 3:c + 4],
                                                                      func=AF.Exp, scale=128.0 ** -0.5), [tbt], [st])
                    T(self.act, lambda c=c, g=g: nc.scalar.copy(s[:, 12 + g:13 + g], tb[:, c + 2:c + 3]), [tbt], [st])
                T(self.dve, lambda: nc.vector.tensor_reduce(out=s[:, 16:17], in_=s[:, 8:11], axis=X, op=ALU.add), [st], [st])
                T(self.dve, lambda: nc.vector.reciprocal(s[:, 16:17], s[:, 16:17]), [st], [st])
                T(self.dve, lambda: nc.vector.tensor_tensor(s[:, 20:23], s[:, 8:11], s[:, 12:15], ALU.mult), [st], [st])
                T(self.dve, lambda: nc.vector.tensor_reduce(out=s[:, 17:18], in_=s[:, 20:23], axis=X, op=ALU.add), [st], [st])
                T(self.dve, lambda h=h: nc.vector.tensor_tensor(self.m0T[:, h:h + 1], s[:, 17:18], s[:, 16:17], ALU.mult),
                  [st], [self.m0_t])
        if T0S < 4 or mode == "heads":
            return
        c = self.tcol(2)
        for c2 in range(2):
            self.t0_cproj(hv, ht, wname, j, qm0 + 128 * c2, 128, c + c2)
        T(self.act, lambda c=c: nc.scalar.copy(s[:, 24:26], tb[:, c:c + 2]), [tbt], [st])
        c = self.tcol(8)
        for hm in range(4):
            c2, half = hm // 2, hm % 2
            pr = slice(64 * half, 64 * half + 64)
            for kt in range(2):
                T(self.pe, lambda c=c, hm=hm, kt=kt, c2=c2, pr=pr: nc.tensor.matmul(
                    tb[:, c + 2 * hm + kt:c + 2 * hm + kt + 1], self.kmT32[pr, c2, kt * 128:(kt + 1) * 128],
                    s[pr, 24 + c2:25 + c2], start=True, stop=True, skip_group_check=True),
                  [self.km32_t, st], [tbt], inc=(hm == 3 and kt == 1), acc=not (hm == 0 and kt == 0))
        T(self.act, lambda c=c: nc.scalar.activation(out=self.e8[:], in_=tb[:, c:c + 8], func=AF.Exp, scale=0.125),
          [tbt], [st])
        c = self.tcol(4)
        first = True
        for hm in range(4):
            c2, half = hm // 2, hm % 2
            pr = slice(64 * half, 64 * half + 64)
            for kt in range(2):
                T(self.pe, lambda c=c, hm=hm, kt=kt, c2=c2, pr=pr: nc.tensor.matmul(
                    tb[pr, c + c2:c + c2 + 1], self.vm32[:, kt, 64 * hm:64 * hm + 64], self.e8[:, 2 * hm + kt:2 * hm + kt + 1],
                    start=(kt == 0), stop=(kt == 1), skip_group_check=True),
                  [self.vm32_t, st], [tbt], inc=False, acc=not first)
                first = False
            for kt in range(2):
                T(self.pe, lambda c=c, hm=hm, kt=kt, c2=c2, pr=pr: nc.tensor.matmul(
                    tb[pr, c + 2 + c2:c + 3 + c2], self.ones32[:, 0:64], self.e8[:, 2 * hm + kt:2 * hm + kt + 1],
                    start=(kt == 0), stop=(kt == 1), skip_group_check=True),
                  [self.const_t, st], [tbt], inc=(hm == 3 and kt == 1), acc=True)
        T(self.dve, lambda c=c: nc.vector.reciprocal(s[:, 26:28], tb[:, c + 2:c + 4]), [tbt], [st])
        T(self.dve, lambda c=c: nc.vector.tensor_tensor(self.m0T[:, 6:8], tb[:, c:c + 2], s[:, 26:28], ALU.mult),
          [tbt, st], [self.m0_t])
        if T0S < 5:
            return
        c = self.tcol(8)
        for cc in range(8):
            self.t0_cproj(hv, ht, wname, j, g0 + 128 * cc, 128, c + cc)
        T(self.act, lambda c=c: nc.scalar.activation(out=self.sg0[:], in_=tb[:, c:c + 8], func=AF.Silu), [tbt], [st])
        T(self.dve, lambda: nc.vector.tensor_tensor(self.m0T[:], self.m0T[:], self.sg0[:], ALU.mult),
          [self.m0_t, st], [self.m0_t])
        c = self.tcol(8)
        for cc in range(8):
            self.t0_cproj(self.m0T, self.m0_t, "wout", li, 128 * cc, 128, c + cc)
        T(self.dve, lambda c=c: nc.vector.tensor_tensor(self.x0T[:], self.x0T[:], tb[:, c:c + 8], ALU.add),
          [tbt, self.x0_t], [self.x0_t])

    def mixer_a(self, j):
        nc = self.nc
        win = self.wina_d[j]
        with ExitStack() as es:
            sbt = lambda name, shape, dtype: es.enter_context(nc.sbuf_tensor(self.un(name), shape, dtype))
            wgl = sbt("wgl", [128, 8, 16], BF16)
            wgl_t = Trk()
            glT = sbt("glT", [16, 512], F32)
            glT_t = Trk()
            sp = sbt("sp", [96, 512], F32)
            cc_ = sbt("cc", [96, 512], F32)
            eb = sbt("eb", [96, 512], F32)
            enb = sbt("enb", [96, 512], F32)
            g_t = Trk()
            qinT = sbt("qinT", [96, 512], BF16)
            kinT = sbt("kinT", [96, 512], BF16)
            qk_t = Trk()
            kin_tok = sbt("kin_tok", [128, 4, 96], BF16)
            kin_tok_t = Trk()
            V = sbt("Vh", [128, 4, 192], BF16)
            V_t = Trk()
            R = [sbt(f"R{h}", [96, 192], F32) for h in range(4)]
            R_t = [Trk() for _ in range(4)]
            Sbf = [sbt(f"Sbf{i}", [96, 192], BF16) for i in range(4)]
            Sbf_t = [Trk() for _ in range(4)]
            ATm = [sbt(f"ATm{i}", [128, 128], BF16) for i in range(2)]
            ATm_t = [Trk() for _ in range(2)]
            ssh = sbt("ssh", [128, 4], F32)
            ssh_t = [Trk() for _ in range(4)]
            mixt = [sbt(f"mixt{i}", [128, 192], BF16) for i in range(2)]
            mixt_t = [Trk() for _ in range(2)]
            dec = sbt("dec", [96, 4], F32)
            dec_t = [Trk() for _ in range(4)]
            self.junk = sbt("junkA", [128, 192], BF16)
            self.junk_t = Trk()
            wgu = sbt("wgu", [16, 384], F32)
            rmask = sbt("rmask", [128, 512], BF16)
            cl_t = Trk()
            self.dma(self.qsp, wgu[:], self.wgu_d[j], writes=[cl_t])
            self.dma(self.qpl, rmask[:], self.c_rmask_d, writes=[cl_t])
            self.dma(self.qpl, wgl[:], win[:, 1536:1552].rearrange("(kc p) c -> p kc c", p=128), writes=[wgl_t])
            sk = 0
            ak = 0
            mk = 0
            for T in range(4):
                cols = slice(T * 512, (T + 1) * 512)
                bank = self.ps_alloc()
                for kc in range(8):
                    self.mm(self.ps[bank][0:16, :], wgl[:, kc, :], self.hT[:, kc, cols], start=(kc == 0),
                            stop=(kc == 7), reads=[wgl_t, self.hT_t[T]], writes=[self.ps_t[bank]], inc=(kc == 7))
                self.op(self.act, lambda bank=bank: nc.scalar.copy(glT[:], self.ps[bank][0:16, :]),
                        reads=[self.ps_t[bank]], writes=[glT_t])
                self.ps_release(bank)
                for h in range(4):
                    s, st = self.wnext([("wina", j, 96 * h, 96, 0), ("wina", j, 384 + 96 * h, 96, 96),
                                        ("wina", j, 768 + 192 * h, 192, 192)])
                    bank = self.ps_alloc()
                    self.mm(self.ps[bank][0:96, :], wgu[:, 96 * h:96 * h + 96], glT[:], start=True, stop=True,
                            reads=[cl_t, glT_t], writes=[self.ps_t[bank]], inc=True)
                    self.op(self.act, lambda bank=bank, h=h: nc.scalar.activation(
                        out=sp[:], in_=self.ps[bank][0:96, :], func=AF.Exp, scale=-1.0,
                        bias=self.nbg[:, 4 * j + h:4 * j + h + 1]),
                        reads=[self.ps_t[bank], self.const_t], writes=[g_t])
                    self.ps_release(bank)
                    self.op(self.act, lambda: nc.scalar.activation(out=sp[:], in_=sp[:], func=AF.Ln, bias=1.0),
                            reads=[g_t], writes=[g_t])
                    self.op(self.dve, lambda: nc.vector.tensor_tensor_scan(
                        cc_[:], rmask[0:96, :], sp[:], 0.0, ALU.mult, ALU.add),
                        reads=[g_t, cl_t], writes=[g_t])
                    self.op(self.act, lambda: nc.scalar.activation(out=eb[:], in_=cc_[:], func=AF.Exp, scale=-1.0 / 16),
                            reads=[g_t], writes=[g_t])
                    self.op(self.act, lambda: nc.scalar.activation(out=enb[:], in_=cc_[:], func=AF.Exp, scale=1.0 / 16),
                            reads=[g_t], writes=[g_t])
                    bank = self.ps_alloc()
                    self.proj_fm(bank, 96, s, st, 0, T)
                    self.op(self.dve, lambda bank=bank: nc.vector.scalar_tensor_tensor(
                        qinT[:], self.ps[bank][0:96, :], 96.0 ** -0.5, eb[:], ALU.mult, ALU.mult),
                        reads=[self.ps_t[bank], g_t], writes=[qk_t])
                    self.ps_release(bank)
                    bank = self.ps_alloc()
                    self.proj_fm(bank, 96, s, st, 96, T)
                    self.op(self.dve, lambda bank=bank: nc.vector.tensor_tensor(
                        kinT[:], self.ps[bank][0:96, :], enb[:], ALU.mult),
                        reads=[self.ps_t[bank], g_t], writes=[qk_t])
                    self.ps_release(bank)
                    for sub in range(4):
                        tt = 4 * T + sub
                        bank = self.ps_alloc()
                        out = self.ps[bank][:, 0:192]
                        for kc in range(8):
                            self.mm(out, self.hT[:, kc, tt * 128:(tt + 1) * 128], self.wslot[s][:, kc, 192:384],
                                    start=(kc == 0), stop=(kc == 7), reads=[st, self.hT_t[T]],
                                    writes=[self.ps_t[bank]], inc=(kc == 7))
                        self.op(self.act, lambda sub=sub, out=out: nc.scalar.copy(V[:, sub, :], out),
                                reads=[self.ps_t[bank]], writes=[V_t])
                        self.ps_release(bank)
                    bank = self.ps_alloc()
                    pb = self.ps[bank][:].bitcast(BF16)
                    for sub in range(4):
                        self.op(self.pe, lambda sub=sub, pb=pb: nc.tensor.transpose(
                            pb[:, sub * 96:(sub + 1) * 96], kinT[:, sub * 128:(sub + 1) * 128], self.ident[0:96, 0:96]),
                            reads=[qk_t, self.const_t], writes=[self.ps_t[bank]], inc=(sub == 3), acc=(sub > 0))
                    self.op(self.act, lambda pb=pb: nc.scalar.copy(
                        kin_tok[:].rearrange("p a b -> p (a b)"), pb[:, 0:384]),
                        reads=[self.ps_t[bank]], writes=[kin_tok_t])
                    self.ps_release(bank)
                    for sub in range(4):
                        tt = 4 * T + sub
                        tc_ = slice(sub * 128, (sub + 1) * 128)
                        bA = self.ps_alloc()
                        self.mm(self.ps[bA][:, 0:128], kinT[:, tc_], qinT[:, tc_], start=True, stop=True,
                                reads=[qk_t], writes=[self.ps_t[bA]], inc=True)
                        a = ak % 2
                        ak += 1
                        self.op(self.dve, lambda a=a, bA=bA: nc.vector.tensor_tensor(
                            ATm[a][:], self.ps[bA][:, 0:128], self.gmask[:], ALU.mult),
                            reads=[self.ps_t[bA], self.const_t], writes=[ATm_t[a]])
                        self.ps_release(bA)
                        bO = self.ps_alloc()
                        self.mm(self.ps[bO][:, 0:192], ATm[a][:], V[:, sub, :], start=True, stop=False,
                                reads=[ATm_t[a], V_t], writes=[self.ps_t[bO]], inc=False)
                        for ch in range(2):
                            c = 2 * tt + ch
                            pr = slice(64 * ch, 64 * ch + 64)
                            lc = 64 * (2 * sub + ch)
                            if c > 0:
                                dk = eb[:, lc - 1:lc] if lc > 0 else dec[:, h:h + 1]
                                dk_t = g_t if lc > 0 else dec_t[h]
                                sb_ = sk % 4
                                sk += 1
                                self.op(self.act, lambda sb_=sb_, h=h, dk=dk: nc.scalar.activation(
                                    out=Sbf[sb_][:], in_=R[h][:], func=AF.Copy, scale=dk),
                                    reads=[R_t[h], dk_t], writes=[Sbf_t[sb_]])
                                self.mm(self.ps[bO][pr, 0:192], qinT[:, lc:lc + 64], Sbf[sb_][:], start=False,
                                        stop=(ch == 1), reads=[qk_t, Sbf_t[sb_]], writes=[self.ps_t[bO]],
                                        inc=(ch == 1))
                            bU = self.ps_alloc()
                            self.mm(self.ps[bU][0:96, 0:192], kin_tok[pr, sub, :], V[pr, sub, :], start=True, stop=True,
                                    reads=[kin_tok_t, V_t], writes=[self.ps_t[bU]], inc=True)
                            if c == 0:
                                self.op(self.dve, lambda bU=bU, h=h: nc.vector.tensor_copy(
                                    R[h][:], self.ps[bU][0:96, 0:192]),
                                    reads=[self.ps_t[bU]], writes=[R_t[h]])
                            else:
                                self.op(self.dve, lambda bU=bU, h=h, dk=dk: nc.vector.scalar_tensor_tensor(
                                    R[h][:], R[h][:], dk, self.ps[bU][0:96, 0:192], ALU.mult, ALU.add),
                                    reads=[self.ps_t[bU], R_t[h], dk_t], writes=[R_t[h]])
                            self.ps_release(bU)
                        m = mk % 2
                        mk += 1
                        self.op(self.act, lambda bO=bO, h=h: nc.scalar.activation(
                            out=self.junk[:, 0:192], in_=self.ps[bO][:, 0:192], func=AF.Square,
                            accum_out=ssh[:, h:h + 1]),
                            reads=[self.ps_t[bO]], writes=[ssh_t[h], self.junk_t])
                        self.op(self.act, lambda h=h: nc.scalar.activation(
                            out=ssh[:, h:h + 1], in_=ssh[:, h:h + 1], func=AF.Ln, scale=1.0 / 192, bias=EPS),
                            reads=[ssh_t[h]], writes=[ssh_t[h]])
                        self.op(self.act, lambda h=h: nc.scalar.activation(
                            out=ssh[:, h:h + 1], in_=ssh[:, h:h + 1], func=AF.Exp, scale=-0.5),
                            reads=[ssh_t[h]], writes=[ssh_t[h]])
                        self.op(self.act, lambda bO=bO, h=h, m=m: nc.scalar.activation(
                            out=mixt[m][:], in_=self.ps[bO][:, 0:192], func=AF.Copy, scale=ssh[:, h:h + 1]),
                            reads=[self.ps_t[bO], ssh_t[h]], writes=[mixt_t[m]])
                        self.ps_release(bO)
                        f0 = 192 * h
                        if h % 2 == 0:
                            cfull, chalf, phalf = f0 // 128, f0 // 128 + 1, 0
                            full_src, half_src = slice(0, 128), slice(128, 192)
                        else:
                            chalf, cfull, phalf = f0 // 128, f0 // 128 + 1, 64
                            half_src, full_src = slice(0, 64), slice(64, 192)
                        bT = self.ps_alloc()
                        pb = self.ps[bT][:].bitcast(BF16)
                        self.op(self.pe, lambda m=m, pb=pb, full_src=full_src: nc.tensor.transpose(
                            pb[:, 0:128], mixt[m][:, full_src], self.ident[:]),
                            reads=[mixt_t[m], self.const_t], writes=[self.ps_t[bT]], inc=False)
                        self.op(self.pe, lambda m=m, pb=pb, half_src=half_src, phalf=phalf: nc.tensor.transpose(
                            pb[phalf:phalf + 64, 128:256], mixt[m][:, half_src], self.ident[:]),
                            reads=[mixt_t[m], self.const_t], writes=[self.ps_t[bT]], inc=True, acc=True)
                        tcol = slice(tt * 128, (tt + 1) * 128)
                        self.op(self.dve, lambda pb=pb, cfull=cfull, tcol=tcol: nc.vector.tensor_scalar(
                            self.preT[:, cfull, tcol], pb[:, 0:128], self.gnw[:, 6 * j + cfull:6 * j + cfull + 1],
                            None, ALU.mult),
                            reads=[self.ps_t[bT], self.const_t], writes=[self.preT_t[cfull][T]])
                        self.op(self.dve, lambda pb=pb, chalf=chalf, phalf=phalf, tcol=tcol: nc.vector.tensor_scalar(
                            self.preT[phalf:phalf + 64, chalf, tcol], pb[phalf:phalf + 64, 128:256],
                            self.gnw[phalf:phalf + 64, 6 * j + chalf:6 * j + chalf + 1], None, ALU.mult),
                            reads=[self.ps_t[bT], self.const_t], writes=[self.preT_t[chalf][T]])
                        self.ps_release(bT)
                    self.op(self.dve, lambda h=h: nc.vector.tensor_copy(dec[:, h:h + 1], eb[:, 511:512]),
                            reads=[g_t], writes=[dec_t[h]])
            self.barrier()

    def mixer_b(self, j):
        nc = self.nc
        win = self.winb_d[j]
        with ExitStack() as es:
            sbt = lambda name, shape, dtype: es.enter_context(nc.sbuf_tensor(self.un(name), shape, dtype))
            qkT = sbt("qkT", [128, 2, SEQ], BF16)
            qkT_t = [Trk() for _ in range(4)]
            Vb = sbt("Vb", [128, 16, 128], BF16)
            Vb_t = [Trk() for _ in range(4)]
            qkt = [sbt(f"qkt{i}", [128, 2, 128], BF16) for i in range(3)]
            qkt_t = [Trk() for _ in range(3)]
            ta = sbt("ropeA", [128, 2, 2, 16], F32)
            tb = sbt("ropeB", [128, 2, 2, 16], F32)
            rope_t = Trk()
            pT = [sbt(f"pT{i}", [128, 256], BF16) for i in range(6)]
            pT_t = [Trk() for _ in range(6)]
            accN = sbt("accN", [128, SEQ], F32)
            accD = sbt("accD", [128, SEQ], F32)
            acc_t = [Trk() for _ in range(4)]
            pk = 0
            qk_i = 0
            carry = []
            for h in range(6):
                for g in range(3):
                    r = DIL[g]
                    nbp = 16 // r
                    base = g * 2304
                    s, st = self.wnext([("winb", j, base + 128 * h, 128, 0), ("winb", j, base + 768 + 128 * h, 128, 128),
                                        ("winb", j, base + 1536 + 128 * h, 128, 256)])
                    pend_tr = []
                    for b in range(16):
                        phase, jb = b // nbp, b % nbp
                        t0 = r * 128 * jb + phase
                        tsl = slice(t0, t0 + 127 * r + 1, r) if r > 1 else slice(t0, t0 + 128)
                        if r == 1:
                            hts = [self.hT_t[jb // 4]]
                        elif r == 4:
                            hts = [self.hT_t[jb]]
                        else:
                            hts = self.hT_t
                        bank = self.ps_alloc()
                        out = self.ps[bank][:, 0:384]
                        for kc in range(8):
                            self.mm(out, self.hT[:, kc, tsl], self.wslot[s][:, kc, :], start=(kc == 0), stop=(kc == 7),
                                    reads=[st] + list(hts), writes=[self.ps_t[bank]], inc=(kc == 7))
                        ps3 = out.rearrange("p (a d) -> p a d", a=3)
                        qi = qk_i % 3
                        qk_i += 1
                        rt = [self.ps_t[bank], self.const_t]
                        X = ps3[:, 0:2, 0:32].rearrange("p a (u d) -> p a u d", u=2)
                        Cb = self.cos[:, g, b, :].unsqueeze(1).unsqueeze(1).to_broadcast([128, 2, 2, 16])
                        Sb = self.sin[:, g, b, :].unsqueeze(1).unsqueeze(1).to_broadcast([128, 2, 2, 16])
                        self.op(self.dve, lambda X=X, Cb=Cb: nc.vector.tensor_tensor(ta[:], X, Cb, ALU.mult),
                                reads=rt, writes=[rope_t])
                        self.op(self.dve, lambda X=X, Sb=Sb: nc.vector.tensor_tensor(tb[:], X, Sb, ALU.mult),
                                reads=rt, writes=[rope_t])
                        self.op(self.dve, lambda qi=qi: nc.vector.tensor_tensor(
                            qkt[qi][:, :, 0:16], ta[:, :, 0, :], tb[:, :, 1, :], ALU.subtract),
                            reads=[rope_t], writes=[qkt_t[qi]])
                        self.op(self.dve, lambda qi=qi: nc.vector.tensor_tensor(
                            qkt[qi][:, :, 16:32], ta[:, :, 1, :], tb[:, :, 0, :], ALU.add),
                            reads=[rope_t], writes=[qkt_t[qi]])
                        self.op(self.act, lambda b=b, ps3=ps3: nc.scalar.copy(Vb[:, b, :], ps3[:, 2, :]),
                                reads=[self.ps_t[bank]], writes=[Vb_t[b // 4]])
                        self.op(self.act, lambda qi=qi, ps3=ps3: nc.scalar.copy(
                            qkt[qi][:, :, 32:128], ps3[:, 0:2, 32:128]),
                            reads=[self.ps_t[bank]], writes=[qkt_t[qi]])
                        self.ps_release(bank)
                        def emit_tr(b=b, qi=qi):
                            if b % 4 == 0:
                                self.bankT = self.ps_alloc()
                            bankT = self.bankT
                            pb = self.ps[bankT][:].bitcast(BF16).rearrange("p (a t) -> p a t", a=2)
                            for a in range(2):
                                self.op(self.pe, lambda a=a, pb=pb, qi=qi, b=b: nc.tensor.transpose(
                                    pb[:, a, (b % 4) * 128:(b % 4 + 1) * 128], qkt[qi][:, a, :], self.ident[:]),
                                    reads=[qkt_t[qi], self.const_t], writes=[self.ps_t[bankT]],
                                    inc=(a == 1), acc=not (b % 4 == 0 and a == 0))
                            if b % 4 == 3:
                                b0 = b - 3
                                self.op(self.act, lambda pb=pb, b0=b0: nc.scalar.copy(
                                    qkT[:, :, b0 * 128:(b0 + 4) * 128], pb),
                                    reads=[self.ps_t[bankT]], writes=[qkT_t[b // 4]])
                                self.ps_release(bankT)
                        pend_tr.append(emit_tr)
                        if len(pend_tr) > 2:
                            pend_tr.pop(0)()
                        if carry and b % 3 == 2:
                            carry.pop(0)()
                    while pend_tr:
                        pend_tr.pop(0)()
                    while carry:
                        carry.pop(0)()
                    if KSTAGE <= 2:
                        continue
                    bn = {}
                    bd = {}
                    started = set()
                    pend_pv = []
                    for kb in range(16):
                        phase, jb = kb // nbp, kb % nbp
                        has_next = jb < nbp - 1
                        N = 256 if has_next else 128
                        m = kb // 4
                        if m not in bn:
                            bn[m] = self.ps_alloc()
                            bd[m] = self.ps_alloc()
                        if has_next and (kb + 1) // 4 not in bn:
                            bn[m + 1] = self.ps_alloc()
                            bd[m + 1] = self.ps_alloc()
                        bs = self.ps_alloc()
                        qts = [qkT_t[kb // 4]] + ([qkT_t[(kb + 1) // 4]] if has_next else [])
                        self.mm(self.ps[bs][:, 0:N], qkT[:, 1, kb * 128:(kb + 1) * 128], qkT[:, 0, kb * 128:kb * 128 + N],
                                start=True, stop=True, reads=qts, writes=[self.ps_t[bs]], inc=True)
                        p = pk % 6
                        pk += 1
                        self.op(self.act, lambda p=p, bs=bs, N=N: nc.scalar.activation(
                            out=pT[p][:, 0:N], in_=self.ps[bs][:, 0:N], func=AF.Exp, scale=128.0 ** -0.5),
                            reads=[self.ps_t[bs]], writes=[pT_t[p]])
                        self.ps_release(bs)
                        self.op(self.dve, lambda p=p, N=N: nc.vector.tensor_tensor(
                            pT[p][:, 0:N], pT[p][:, 0:N], self.mask2[:, 0:N], ALU.mult),
                            reads=[pT_t[p], self.const_t], writes=[pT_t[p]])
                        def emit_pv(kb=kb, has_next=has_next, m=m, p=p):
                            if has_next and (kb + 1) // 4 == m:
                                segs = [(m, (kb % 4) * 128, 0, 256)]
                            elif has_next:
                                segs = [(m, (kb % 4) * 128, 0, 128), (m + 1, 0, 128, 128)]
                            else:
                                segs = [(m, (kb % 4) * 128, 0, 128)]
                            for si_, (mm_, oc, pc, n) in enumerate(segs):
                                first = mm_ not in started
                                started.add(mm_)
                                last = (si_ == len(segs) - 1)
                                self.mm(self.ps[bn[mm_]][:, oc:oc + n], Vb[:, kb, :], pT[p][:, pc:pc + n], start=first, stop=False,
                                        reads=[Vb_t[kb // 4], pT_t[p]], writes=[self.ps_t[bn[mm_]]], inc=False)
                                self.mm(self.ps[bd[mm_]][:, oc:oc + n], self.ones[:], pT[p][:, pc:pc + n], start=first, stop=False,
                                        reads=[pT_t[p]], writes=[self.ps_t[bd[mm_]]], inc=last)
                            if kb % 4 == 3 and KSTAGE <= 3:
                                self.ps_release(bn[m])
                                self.ps_release(bd[m])
                            elif kb % 4 == 3:
                                if r == 1:
                                    dN, dD = accN[:, m * 512:(m + 1) * 512], accD[:, m * 512:(m + 1) * 512]
                                    sN, sD = self.ps[bn[m]][:, :], self.ps[bd[m]][:, :]
                                    trks = [acc_t[m]]
                                elif r == 4:
                                    dN = accN[:, m:SEQ:4]
                                    dD = accD[:, m:SEQ:4]
                                    sN, sD = self.ps[bn[m]][:, :], self.ps[bd[m]][:, :]
                                    trks = acc_t
                                else:
                                    dN = accN[:].rearrange("d (s ph) -> d ph s", ph=16)[:, 4 * m:4 * m + 4, :]
                                    dD = accD[:].rearrange("d (s ph) -> d ph s", ph=16)[:, 4 * m:4 * m + 4, :]
                                    sN = self.ps[bn[m]][:, :].rearrange("d (ph s) -> d ph s", ph=4)
                                    sD = self.ps[bd[m]][:, :].rearrange("d (ph s) -> d ph s", ph=4)
                                    trks = acc_t
                                def do_acc(dN=dN, dD=dD, sN=sN, sD=sD, trks=trks, bnm=bn[m], bdm=bd[m], g=g):
                                    if g == 0:
                                        self.op(self.act, lambda: nc.scalar.copy(dN, sN),
                                                reads=[self.ps_t[bnm]], writes=trks)
                                        self.op(self.act, lambda: nc.scalar.copy(dD, sD),
                                                reads=[self.ps_t[bdm]], writes=trks)
                                    else:
                                        self.op(self.dve, lambda: nc.vector.tensor_tensor(dN, sN, dN, ALU.add),
                                                reads=[self.ps_t[bnm]] + trks, writes=trks)
                                        self.op(self.dve, lambda: nc.vector.tensor_tensor(dD, sD, dD, ALU.add),
                                                reads=[self.ps_t[bdm]] + trks, writes=trks)
                                    self.ps_release(bnm)
                                    self.ps_release(bdm)
                                if m == 3:
                                    carry.append(do_acc)
                                else:
                                    do_acc()
                        pend_pv.append(emit_pv)
                        if len(pend_pv) > 4:
                            pend_pv.pop(0)()
                    while pend_pv:
                        pend_pv.pop(0)()
                for T in range(4 if KSTAGE > 4 else 0):
                    def do_fin(T=T, h=h):
                        cols = slice(T * 512, (T + 1) * 512)
                        self.op(self.dve, lambda: nc.vector.reciprocal(accD[:, cols], accD[:, cols]),
                                reads=[acc_t[T]], writes=[acc_t[T]])
                        self.op(self.dve, lambda: nc.vector.tensor_tensor(
                            self.preT[:, h, cols], accN[:, cols], accD[:, cols], ALU.mult),
                            reads=[acc_t[T]], writes=[self.preT_t[h][T]])
                    carry.append(do_fin)
            while carry:
                carry.pop(0)()
            self.barrier()


def _consts():
    p = np.arange(128)
    ident = np.eye(128, dtype=np.float32)
    mask2 = np.zeros((128, 256), np.float32)
    mask2[:, 0:128] = (p[:, None] <= p[None, :])
    mask2[:, 128:256] = (p[:, None] >= p[None, :])
    gmask = ((p[:, None] // 64 == p[None, :] // 64) & (p[:, None] <= p[None, :])).astype(np.float32)
    rmask = np.ones((128, 512), np.float32)
    rmask[:, 0::64] = 0.0
    half = 16
    inv = (np.float32(500000.0) ** (-(np.arange(half, dtype=np.float32) / np.float32(half)))).astype(np.float32)
    cos = np.zeros((128, 3, 16, 16), np.float32)
    sin = np.zeros((128, 3, 16, 16), np.float32)
    for g, r in enumerate(DIL):
        nbp = 16 // r
        for b in range(16):
            phase, jb = b // nbp, b % nbp
            t = (r * (128 * jb + p) + phase).astype(np.float32)
            ang = (t[:, None] * inv[None, :]).astype(np.float32)
            cos[:, g, b, :] = np.cos(ang)
            sin[:, g, b, :] = np.sin(ang)
    return dict(c_ident=ident, c_ones=np.ones((128, 128), np.float32), c_mask2=mask2, c_gmask=gmask, c_rmask=rmask,
                c_cos=cos.reshape(128, -1), c_sin=sin.reshape(128, -1))


_NC_CACHE = {}


def _get_nc(nseq, nlayers, dbg=None):
    key = (nseq, nlayers, None if dbg is None else tuple(sorted(dbg)))
    if key not in _NC_CACHE:
        k = Kern(nseq, nlayers, dbg)
        _NC_CACHE[key] = k.build()
    return _NC_CACHE[key]


def _layout_params(mem_norm_w, norm_w, w_memkv, w_out, w_in_a, w_gate_up, b_gate, gla_norm_w, w_in_b, final_norm_w):
    f = lambda a: np.ascontiguousarray(np.asarray(a, dtype=np.float32))
    fm8 = lambda v: f(np.asarray(v).reshape(8, 128).T)
    d = {}
    d["memnw_fm"] = fm8(mem_norm_w)
    d["normw_fm"] = f(np.concatenate([fm8(norm_w[i]) for i in range(4)], axis=1))
    d["w_memkv"] = f(w_memkv)
    d["w_out"] = f(w_out)
    d["w_in_a"] = f(w_in_a)
    d["w_gate_up"] = f(w_gate_up)
    bg = np.asarray(b_gate)
    d["bg_fm"] = f(np.concatenate([bg[j].reshape(4, 96).T for j in range(2)], axis=1))
    gn = np.asarray(gla_norm_w)
    idx = (np.arange(768) % 192).reshape(6, 128).T
    d["gnw_fm"] = f(np.concatenate([gn[j][idx] for j in range(2)], axis=1))
    d["w_in_b"] = f(w_in_b)
    d["fnw_bc"] = f(np.broadcast_to(np.asarray(final_norm_w)[None, :], (128, D)))
    d.update(_consts())
    return d


def kernel(x, mem, mem_norm_w, norm_w, w_memkv, w_out, w_in_a, w_gate_up, b_gate, gla_norm_w, w_in_b,
           final_norm_w, _nlayers=4, _nseq_launch=4):
    x = np.asarray(x, dtype=np.float32)
    mem = np.asarray(mem, dtype=np.float32)
    B = x.shape[0]
    per_core = B // NCORES
    params = _layout_params(mem_norm_w, norm_w, w_memkv, w_out, w_in_a, w_gate_up, b_gate, gla_norm_w,
                            w_in_b, final_norm_w)
    out = np.empty_like(x)
    nseq = _nseq_launch
    nc = _get_nc(nseq, _nlayers)
    for s0 in range(0, per_core, nseq):
        in_maps = []
        for c in range(NCORES):
            b0 = c * per_core + s0
            m = dict(params)
            m["x"] = np.ascontiguousarray(x[b0:b0 + nseq])
            m["mem"] = np.ascontiguousarray(mem[b0:b0 + nseq])
            in_maps.append(m)
        res = run_bass_kernel_spmd(nc, in_maps, core_ids=list(range(NCORES)))
        for c in range(NCORES):
            b0 = c * per_core + s0
            out[b0:b0 + nseq] = np.asarray(res.results[c]["out"]).reshape(nseq, SEQ, D)
    return out
```

```python
import os
import numpy as np
from contextlib import ExitStack
import concourse.bass as bass
import concourse.mybir as mybir
from concourse.bass_utils import run_bass_kernel_spmd

F32 = mybir.dt.float32
BF16 = mybir.dt.bfloat16
AF = mybir.ActivationFunctionType
ALU = mybir.AluOpType

NCORES = 8
SEQ = 2048
D = 1024
NMEM = 256
DIL = [1, 4, 16]
EPOCH = 12000
NSLOT = 3
SLOTW = 384
EPS = 1e-6
KSTAGE = int(os.environ.get('KSTAGE', '9'))
KSUB = int(os.environ.get('KSUB', '9'))
T0S = int(os.environ.get('T0S', '9'))


class Trk:
    __slots__ = ("w", "r", "rd", "excl")

    def __init__(self, excl=False):
        self.excl = excl
        self.w = None
        self.r = {}
        self.rd = []


class Eng:
    def __init__(self, name, h):
        self.name = name
        self.h = h
        self.count = 0
        self.pending = False
        self.sems = []
        self.waited = {}


class Queue:
    def __init__(self, name, h, nslots):
        self.name = name
        self.h = h
        self.sems = []
        self.uses = [0] * nslots
        self.k = 0
        self.waited = {}


class Kern:
    def __init__(self, nseq, nlayers, dbg=None):
        self.nseq = nseq
        self.nlayers = nlayers
        self.dry = True
        self.jobs = []
        self.dbg = dbg

    def setup_engines(self, es):
        nc = self.nc
        self.pe = Eng("pe", nc.tensor)
        self.act = Eng("act", nc.scalar)
        self.dve = Eng("dve", nc.vector)
        self.engs = {"pe": self.pe, "act": self.act, "dve": self.dve}
        self.qsp = Queue("sp", nc.sync, 8)
        self.qpl = Queue("pool", nc.gpsimd, 8)
        if not self.dry:
            for e in self.engs.values():
                nep = self.est_counts[e.name] // EPOCH + 1
                for i in range(nep):
                    e.sems.append(es.enter_context(nc.semaphore(f"s_{e.name}{i}")))
            for q in (self.qsp, self.qpl):
                for i in range(len(q.uses)):
                    q.sems.append(es.enter_context(nc.semaphore(f"q_{q.name}{i}")))

    def _tok_sem(self, tok):
        if tok[0] == "d":
            return tok[1], tok[2]
        e = self.engs[tok[0]]
        c = tok[1]
        ep = (c - 1) // EPOCH
        return e.sems[ep], c - ep * EPOCH

    def _need(self, waiter, tok, out):
        if tok is None:
            return
        if tok[0] == "d":
            key = ("d", tok[3])
            if waiter.waited.get(key, 0) >= tok[2]:
                return
            if out.get(key, (None, 0))[1] < tok[2]:
                out[key] = (tok, tok[2])
        else:
            key = tok[0]
            if waiter.waited.get(key, 0) >= tok[1]:
                return
            if out.get(key, (None, 0))[1] < tok[1]:
                out[key] = (tok, tok[1])

    def _collect(self, waiter, reads, writes, acc, own):
        out = {}
        for t in reads:
            if t.w is not None and not (t.w[0] == own and own == "pe"):
                self._need(waiter, t.w, out)
            if t.excl:
                for en, c in t.r.items():
                    if en != own:
                        self._need(waiter, (en, c), out)
        if not acc:
            for t in writes:
                if t.w is not None and not (t.w[0] == own and own == "pe"):
                    self._need(waiter, t.w, out)
                for en, c in t.r.items():
                    if not (en == own and own == "pe"):
                        self._need(waiter, (en, c), out)
                for dt_ in t.rd:
                    self._need(waiter, dt_, out)
        return out

    def _emit_waits(self, waiter, h, need):
        for key, (tok, v) in need.items():
            waiter.waited[key] = v
            if not self.dry:
                sem, val = self._tok_sem(tok)
                h.wait_ge(sem, val)

    def op(self, eng, fn, reads=(), writes=(), inc=True, acc=False):
        need = self._collect(eng, reads, writes, acc, eng.name)
        self._emit_waits(eng, eng.h, need)
        if inc:
            eng.count += 1
            eng.pending = False
            c = eng.count
        else:
            eng.pending = True
            c = eng.count + 1
        if not self.dry:
            ins = fn()
            if inc:
                ep = (c - 1) // EPOCH
                ins.then_inc(eng.sems[ep], 1)
        tok = (eng.name, c)
        for t in reads:
            if t.r.get(eng.name, 0) < c:
                t.r[eng.name] = c
        for t in writes:
            if acc:
                t.w = tok
            else:
                t.w = tok
                t.r = {}
                t.rd = []
        return tok

    def dma(self, q, out_ap, in_ap=None, reads=(), writes=()):
        pairs = out_ap if in_ap is None else [(out_ap, in_ap)]
        need = self._collect(q, reads, writes, False, q.name)
        self._emit_waits(q, q.h, need)
        slot = q.k % len(q.uses)
        q.k += 1
        prev = q.uses[slot] * 16
        q.uses[slot] += len(pairs)
        val = q.uses[slot] * 16
        if not self.dry:
            sem = q.sems[slot]
            if prev > 0:
                q.h.wait_ge(sem, prev)
            for (o, i) in pairs:
                q.h.dma_start(out=o, in_=i).then_inc(sem, 16)
            tok = ("d", sem, val, (q.name, slot))
        else:
            tok = ("d", None, val, (q.name, slot))
        for t in reads:
            t.rd.append(tok)
        for t in writes:
            t.w = tok
            t.r = {}
            t.rd = []
        return tok

    def barrier(self):
        for e in self.engs.values():
            assert not e.pending, e.name
        waiters = list(self.engs.values()) + [self.qsp, self.qpl]
        for w in waiters:
            need = {}
            for o in self.engs.values():
                if o is w or o.count == 0:
                    continue
                self._need(w, (o.name, o.count), need)
            self._emit_waits(w, w.h, need)

    def un(self, name):
        self.uid += 1
        return f"{name}_{self.uid}"

    def ps_alloc(self):
        return self.ps_free.pop(0)

    def ps_release(self, i):
        self.ps_free.append(i)

    def wnext(self, pieces):
        if self.dry:
            self.jobs.append(pieces)
            i = len(self.jobs) - 1
            return i % NSLOT, self.wtrk[i % NSLOT]
        i = self.wjob
        self.wjob += 1
        while self.wissued < min(i + NSLOT, len(self.jobs)):
            j = self.wissued
            s = j % NSLOT
            pairs = []
            for (name, idx, sc0, n, dc0) in self.jobs[j]:
                src = getattr(self, name + "_d")[idx, :, sc0:sc0 + n]
                pairs.append((self.wslot[s][:, :, dc0:dc0 + n], src.rearrange("(kc p) c -> p kc c", p=128)))
            self.dma(self.qpl, pairs, writes=[self.wtrk[s]])
            self.wissued += 1
        return i % NSLOT, self.wtrk[i % NSLOT]

    def drip(self, n=1):
        while n > 0 and self.t0q:
            self.t0q.pop(0)()
            n -= 1

    def flush_t0(self):
        while self.t0q:
            self.t0q.pop(0)()

    def mm(self, out, lhsT, rhs, start, stop, reads, writes, inc):
        nc = self.nc
        self.mmc += 1
        if self.mmc % 3 == 0:
            self.drip()
        return self.op(self.pe,
                       lambda: nc.tensor.matmul(out, lhsT, rhs, start=start, stop=stop,
                                                skip_group_check=True),
                       reads=reads, writes=writes, inc=inc, acc=not start)

    def proj_fm(self, bank, M, slot, strk, c0, T, ncols=512, pbase=0):
        out = self.ps[bank][pbase:pbase + M, 0:ncols]
        for kc in range(8):
            self.mm(out, self.wslot[slot][:, kc, c0:c0 + M],
                    self.hT[:, kc, T * 512:T * 512 + ncols],
                    start=(kc == 0), stop=(kc == 7),
                    reads=[strk, self.hT_t[T]], writes=[self.ps_t[bank]], inc=(kc == 7))

    def build(self):
        self.dry = True
        self._build_once()
        self.est_counts = {n: e.count for n, e in self.engs.items()}
        jobs = self.jobs
        self.dry = False
        self.jobs = jobs
        self._build_once()
        return self.nc

    def _build_once(self):
        nseq = self.nseq
        nc = bass.Bass("TRN2", target_bir_lowering=False)
        self.nc = nc
        self.wjob = 0
        self.wissued = 0
        dt = lambda name, shape, kind="ExternalInput", dtype=F32: nc.dram_tensor(name, shape, dtype, kind=kind).ap()
        self.x_d = dt("x", [nseq, SEQ, D])
        self.mem_d = dt("mem", [nseq, NMEM, D])
        self.memnw_d = dt("memnw_fm", [128, 8])
        self.normw_d = dt("normw_fm", [128, 4 * 8])
        self.wmemkv_d = dt("w_memkv", [4, D, 512])
        self.wout_d = dt("w_out", [4, D, D])
        self.wina_d = dt("w_in_a", [2, D, 2832])
        self.wgu_d = dt("w_gate_up", [2, 16, 384])
        self.bg_d = dt("bg_fm", [96, 2 * 4])
        self.gnw_d = dt("gnw_fm", [128, 2 * 6])
        self.winb_d = dt("w_in_b", [2, D, 8192])
        self.fnw_d = dt("fnw_bc", [128, D])
        self.c_ident_d = dt("c_ident", [128, 128])
        self.c_ones_d = dt("c_ones", [128, 128])
        self.c_mask2_d = dt("c_mask2", [128, 256])
        self.c_gmask_d = dt("c_gmask", [128, 128])
        self.c_rmask_d = dt("c_rmask", [128, 512])
        self.c_cos_d = dt("c_cos", [128, 3 * 16 * 16])
        self.c_sin_d = dt("c_sin", [128, 3 * 16 * 16])
        self.out_d = dt("out", [nseq, SEQ, D], kind="ExternalOutput")
        if self.dbg:
            self.dbg_d = {k: dt("dbg_" + k, shp, kind="ExternalOutput") for k, shp in self.dbg.items()}

        with ExitStack() as es:
            self.setup_engines(es)
            sb = lambda name, shape, dtype: es.enter_context(nc.sbuf_tensor(name, shape, dtype))
            self.x = sb("x_sb", [128, 16, D], F32)
            self.x_t = [Trk() for _ in range(16)]
            self.hT = sb("hT", [128, 8, SEQ], BF16)
            self.hT_t = [Trk() for _ in range(4)]
            self.preT = sb("preT", [128, 8, SEQ], BF16)
            self.preT_t = [[Trk() for _ in range(4)] for _ in range(8)]
            self.memT = sb("memT", [128, 8, NMEM], BF16)
            self.memT_t = Trk()
            self.kmT = sb("kmT", [128, 2, NMEM], BF16)
            self.kmT_t = Trk()
            self.vm = sb("vm", [128, 2, 256], BF16)
            self.vm_t = Trk()
            self.wslot = [sb(f"wslot{i}", [128, 8, SLOTW], BF16) for i in range(NSLOT)]
            self.wtrk = [Trk() for _ in range(NSLOT)]
            self.ps = [es.enter_context(nc.psum_tensor(f"ps{i}", [128, 512], F32)) for i in range(8)]
            self.ps_t = [Trk(excl=True) for _ in range(8)]
            self.ps_free = list(range(7))
            self.ident = sb("ident", [128, 128], BF16)
            self.mask2 = sb("mask2", [128, 256], BF16)
            self.gmask = sb("gmask", [128, 128], BF16)
            self.ones = sb("ones", [128, 128], BF16)
            self.cos = sb("cos", [128, 3, 16, 16], F32)
            self.sin = sb("sin", [128, 3, 16, 16], F32)
            self.memnw = sb("memnw", [128, 8], F32)
            self.normw = sb("normw", [128, 4, 8], F32)
            self.bg = sb("bg", [96, 8], F32)
            self.nbg = sb("nbg", [96, 8], F32)
            self.gnw = sb("gnw", [128, 12], F32)
            self.const_t = Trk()
            self.ident32 = sb("ident32", [128, 128], F32)
            self.ones32 = sb("ones32", [128, 128], F32)
            self.w32 = [sb(f"w32_{i}", [128, 8, 128], F32) for i in range(2)]
            self.w32_t = [Trk() for _ in range(2)]
            self.w32_n = 0
            self.t0_prev_pos = None
            self.kmT32 = sb("kmT32", [128, 2, NMEM], F32)
            self.vm32 = sb("vm32", [128, 2, 256], F32)
            self.km32_t = Trk()
            self.vm32_t = Trk()
            self.x0T = sb("x0T", [128, 8], F32)
            self.h0T = sb("h0T", [128, 8], F32)
            self.m0T = sb("m0T", [128, 8], F32)
            self.sg0 = sb("sg0", [128, 8], F32)
            self.t0s = sb("t0s", [128, 32], F32)
            self.e8 = sb("e8", [128, 8], F32)
            self.x0_t = Trk()
            self.h0_t = Trk()
            self.m0_t = Trk()
            self.t0s_t = Trk()
            self.tb = 7
            self.tbc = 0
            self.t0q = []
            self.mmc = 0
            self.junk_t = Trk()
            self.ss16 = sb("ss16", [128, 16], F32)
            self.rstd16 = sb("rstd16", [128, 16], F32)
            self.st_t = Trk()
            self.ss_t = [Trk() for _ in range(16)]
            self.uid = 0

            self.load_consts()
            for si in range(nseq):
                self.do_seq(si)
            if not self.dry:
                for tok in self.out_toks:
                    key = ("d", tok[3])
                    if self.qsp.waited.get(key, 0) < tok[2]:
                        self.qsp.waited[key] = tok[2]
                        nc.sync.wait_ge(tok[1], tok[2])

    def load_consts(self):
        nc = self.nc
        toks = []
        D_ = lambda q, o, i: toks.append(self.dma(q, o, i))
        D_(self.qpl, self.ident[:], self.c_ident_d)
        D_(self.qsp, self.ident32[:], self.c_ident_d)
        D_(self.qsp, self.ones32[:], self.c_ones_d)
        D_(self.qpl, self.mask2[:], self.c_mask2_d)
        D_(self.qpl, self.gmask[:], self.c_gmask_d)
        D_(self.qsp, self.cos[:].rearrange("p a b c -> p (a b c)"), self.c_cos_d)
        D_(self.qsp, self.sin[:].rearrange("p a b c -> p (a b c)"), self.c_sin_d)
        D_(self.qsp, self.memnw[:], self.memnw_d)
        D_(self.qsp, self.normw[:].rearrange("p a b -> p (a b)"), self.normw_d)
        D_(self.qsp, self.bg[:], self.bg_d)
        D_(self.qsp, self.gnw[:], self.gnw_d)
        for e in self.engs.values():
            need = {}
            for tok in toks:
                self._need(e, tok, need)
            self._emit_waits(e, e.h, need)
        self.op(self.dve, lambda: nc.vector.memset(self.ones[:], 1.0), writes=[self.const_t])
        self.op(self.dve, lambda: nc.vector.tensor_scalar(self.nbg[:], self.bg[:], -1.0, None, ALU.mult),
                writes=[self.const_t])
        self.out_toks = []

    def norm_to_fm(self, src_aps, src_trks, w_ap, dst, dst_trk_of, ntiles):
        nc = self.nc
        with ExitStack() as es:
            self.xs = [es.enter_context(nc.sbuf_tensor(self.un("xs"), [128, D], BF16)) for _ in range(2)]
            self.xs_t = [Trk(), Trk()]
            self.junk = es.enter_context(nc.sbuf_tensor(self.un("junk"), [128, D], BF16))
            self.junk_t = Trk()
            self._norm_to_fm(src_aps, src_trks, w_ap, dst, dst_trk_of, ntiles)
            self.barrier()

    def _norm_to_fm(self, src_aps, src_trks, w_ap, dst, dst_trk_of, ntiles):
        nc = self.nc
        for i in range(ntiles):
            self.op(self.act, lambda i=i: nc.scalar.activation(
                out=self.junk[:], in_=src_aps[i], func=AF.Square, accum_out=self.ss16[:, i:i + 1]),
                reads=[src_trks[i]], writes=[self.ss_t[i], self.junk_t])
        self.op(self.act, lambda: nc.scalar.activation(
            out=self.rstd16[:, 0:ntiles], in_=self.ss16[:, 0:ntiles], func=AF.Ln, scale=1.0 / D, bias=EPS),
            reads=self.ss_t[0:ntiles], writes=[self.st_t])
        self.op(self.act, lambda: nc.scalar.activation(
            out=self.rstd16[:, 0:ntiles], in_=self.rstd16[:, 0:ntiles], func=AF.Exp, scale=-0.5),
            reads=[self.st_t], writes=[self.st_t])
        for i in range(ntiles):
            b = i % 2
            self.op(self.act, lambda i=i, b=b: nc.scalar.activation(
                out=self.xs[b][:], in_=src_aps[i], func=AF.Copy, scale=self.rstd16[:, i:i + 1]),
                reads=[src_trks[i], self.st_t], writes=[self.xs_t[b]])
            bank = self.ps_alloc()
            pb = self.ps[bank][:].bitcast(BF16)
            for kc in range(8):
                self.op(self.pe, lambda kc=kc, b=b, pb=pb: nc.tensor.transpose(
                    pb[:, kc * 128:(kc + 1) * 128], self.xs[b][:, kc * 128:(kc + 1) * 128], self.ident[:]),
                    reads=[self.xs_t[b], self.const_t], writes=[self.ps_t[bank]], inc=(kc == 7), acc=(kc > 0))
            dtrk = dst_trk_of(i)
            self.op(self.dve, lambda i=i, pb=pb: nc.vector.tensor_tensor(
                dst[:, :, i * 128:(i + 1) * 128],
                pb.rearrange("p (k t) -> p k t", k=8),
                w_ap.unsqueeze(2).to_broadcast([128, 8, 128]), ALU.mult),
                reads=[self.ps_t[bank], self.const_t], writes=[dtrk])
            self.ps_release(bank)

    def do_seq(self, si):
        nc = self.nc
        for tt in range(16):
            self.dma(self.qsp, self.x[:, tt, :], self.x_d[si, tt * 128:(tt + 1) * 128, :], writes=[self.x_t[tt]])
        with ExitStack() as es:
            memraw = es.enter_context(nc.sbuf_tensor(self.un("memraw"), [128, 2, D], F32))
            mr_t = [Trk(), Trk()]
            for i in range(2):
                self.dma(self.qsp, memraw[:, i, :], self.mem_d[si, i * 128:(i + 1) * 128, :], writes=[mr_t[i]])
            self.norm_to_fm([memraw[:, i, :] for i in range(2)], mr_t, self.memnw[:], self.memT,
                            lambda i: self.memT_t, 2)
            self.barrier()
        self.t0_init()
        for li in range(self.nlayers):
            self.do_layer(si, li)
        self.final_norm(si)

    def do_layer(self, si, li):
        nc = self.nc
        j = li // 2
        self.norm_to_fm([self.x[:, tt, :] for tt in range(16)], self.x_t, self.normw[:, li, :], self.hT,
                        lambda i: self.hT_t[i // 4], 16)
        self.mem_kv(li)
        self.t0_layer(li)
        mode = self.t0_mode(li)
        if li % 2 == 0:
            self.mixer_a(j)
            wname, qm0, g0 = "wina", 1552, 1808
        else:
            self.mixer_b(j)
            wname, qm0, g0 = "winb", 6912, 7168
        if mode == "heads" and T0S >= 3:
            self.flush_t0()
            self.op(self.dve, lambda: nc.vector.tensor_copy(self.preT[:, 0:6, 0:1], self.m0T[:, 0:6].unsqueeze(2)),
                    reads=[self.m0_t], writes=[self.preT_t[c][0] for c in range(6)])
        self.tail(li, j, wname, qm0, g0)
        if mode == "full":
            self.flush_t0()
            self.t0_inject()

    def mem_kv(self, li):
        nc = self.nc
        s, st = self.wnext([("wmemkv", li, 0, 256, 0)])
        for c2 in range(2):
            bank = self.ps_alloc()
            out = self.ps[bank][:, 0:256]
            for kc in range(8):
                self.mm(out, self.wslot[s][:, kc, c2 * 128:(c2 + 1) * 128], self.memT[:, kc, :],
                        start=(kc == 0), stop=(kc == 7), reads=[st, self.memT_t], writes=[self.ps_t[bank]],
                        inc=(kc == 7))
            self.op(self.act, lambda c2=c2, out=out: nc.scalar.copy(self.kmT[:, c2, :], out),
                    reads=[self.ps_t[bank]], writes=[self.kmT_t])
            self.op(self.act, lambda c2=c2, out=out: nc.scalar.copy(self.kmT32[:, c2, :], out),
                    reads=[self.ps_t[bank]], writes=[self.km32_t])
            self.ps_release(bank)
        s, st = self.wnext([("wmemkv", li, 256, 256, 0)])
        for kt in range(2):
            bank = self.ps_alloc()
            out = self.ps[bank][:, 0:256]
            for kc in range(8):
                self.mm(out, self.memT[:, kc, kt * 128:(kt + 1) * 128], self.wslot[s][:, kc, 0:256],
                        start=(kc == 0), stop=(kc == 7), reads=[st, self.memT_t], writes=[self.ps_t[bank]],
                        inc=(kc == 7))
            self.op(self.act, lambda kt=kt, out=out: nc.scalar.copy(self.vm[:, kt, :], out),
                    reads=[self.ps_t[bank]], writes=[self.vm_t])
            self.op(self.act, lambda kt=kt, out=out: nc.scalar.copy(self.vm32[:, kt, :], out),
                    reads=[self.ps_t[bank]], writes=[self.vm32_t])
            self.ps_release(bank)

    def tail(self, li, j, wname, qm0, g0):
        nc = self.nc
        with ExitStack() as es:
            sbt = lambda name, shape, dtype: es.enter_context(nc.sbuf_tensor(self.un(name), shape, dtype))
            qm = sbt("qm", [128, 2, 512], BF16)
            qm_t = Trk()
            pT = [sbt(f"pTm{i}", [128, 512], BF16) for i in range(3)]
            pT_t = [Trk() for _ in range(3)]
            rden = sbt("rdenm", [128, 512], F32)
            rden_t = Trk()
            sg = [sbt(f"sg{i}", [128, 512], BF16) for i in range(2)]
            sg_t = [Trk() for _ in range(2)]
            pk = 0
            for T in range(4):
                cols = slice(T * 512, (T + 1) * 512)
                s, st = self.wnext([(wname, j, qm0, 256, 0)])
                for c2 in range(2):
                    bank = self.ps_alloc()
                    self.proj_fm(bank, 128, s, st, c2 * 128, T)
                    self.op(self.act, lambda c2=c2, bank=bank: nc.scalar.copy(qm[:, c2, :], self.ps[bank][:, :]),
                            reads=[self.ps_t[bank]], writes=[qm_t])
                    self.ps_release(bank)
                gate_jobs = [(0, 384), (384, 384), (768, 256)]
                gstate = {"ji": 0, "cc": 0, "s": None, "st": None}

                def emit_gate_chunk():
                    if gstate["ji"] >= len(gate_jobs):
                        return False
                    c0, n = gate_jobs[gstate["ji"]]
                    if gstate["cc"] == 0:
                        gstate["s"], gstate["st"] = self.wnext([(wname, j, g0 + c0, n, 0)])
                    s_, st_ = gstate["s"], gstate["st"]
                    cc = gstate["cc"]
                    c = c0 // 128 + cc
                    bank = self.ps_alloc()
                    self.proj_fm(bank, 128, s_, st_, cc * 128, T)
                    b_ = c % 2
                    self.op(self.act, lambda b_=b_, bank=bank: nc.scalar.activation(
                        out=sg[b_][:], in_=self.ps[bank][:, :], func=AF.Silu),
                        reads=[self.ps_t[bank]], writes=[sg_t[b_]])
                    self.ps_release(bank)
                    self.gate_pending.append((c, b_))
                    gstate["cc"] += 1
                    if gstate["cc"] >= n // 128:
                        gstate["cc"] = 0
                        gstate["ji"] += 1
                    return True

                def emit_gate_mults(upto=None):
                    keep = []
                    for (c, b_) in self.gate_pending:
                        if c < 6 or self.mem_done:
                            self.op(self.dve, lambda b_=b_, c=c: nc.vector.tensor_tensor(
                                self.preT[:, c, cols], self.preT[:, c, cols], sg[b_][:], ALU.mult),
                                reads=[sg_t[b_], self.preT_t[c][T]], writes=[self.preT_t[c][T]])
                        else:
                            keep.append((c, b_))
                    self.gate_pending = keep

                self.gate_pending = []
                self.mem_done = False
                pend = []
                banks = {}
                step = 0
                for c2 in range(2):
                    for half in range(2):
                        hm = 2 * c2 + half
                        pr = slice(64 * half, 64 * half + 64)
                        for kt in range(2):
                            if c2 not in banks:
                                banks[c2] = (self.ps_alloc(), self.ps_alloc())
                            bs = self.ps_alloc()
                            self.mm(self.ps[bs][:, :], self.kmT[pr, c2, kt * 128:(kt + 1) * 128], qm[pr, c2, :],
                                    start=True, stop=True, reads=[self.kmT_t, qm_t], writes=[self.ps_t[bs]], inc=True)
                            p = pk % 3
                            pk += 1
                            self.op(self.act, lambda p=p, bs=bs: nc.scalar.activation(
                                out=pT[p][:], in_=self.ps[bs][:, :], func=AF.Exp, scale=0.125),
                                reads=[self.ps_t[bs]], writes=[pT_t[p]])
                            self.ps_release(bs)

                            def emit_pv(c2=c2, half=half, hm=hm, pr=pr, kt=kt, p=p):
                                bn, bd = banks[c2]
                                self.mm(self.ps[bn][pr, :], self.vm[:, kt, hm * 64:(hm + 1) * 64], pT[p][:],
                                        start=(kt == 0), stop=(kt == 1), reads=[self.vm_t, pT_t[p]],
                                        writes=[self.ps_t[bn]], inc=False)
                                self.mm(self.ps[bd][pr, :], self.ones[:, 0:64], pT[p][:],
                                        start=(kt == 0), stop=(kt == 1), reads=[pT_t[p]],
                                        writes=[self.ps_t[bd]], inc=True)
                                if half == 1 and kt == 1:
                                    self.op(self.dve, lambda bd=bd: nc.vector.reciprocal(rden[:], self.ps[bd][:, :]),
                                            reads=[self.ps_t[bd]], writes=[rden_t])
                                    self.op(self.dve, lambda bn=bn, c2=c2: nc.vector.tensor_tensor(
                                        self.preT[:, 6 + c2, cols], self.ps[bn][:, :], rden[:], ALU.mult),
                                        reads=[self.ps_t[bn], rden_t], writes=[self.preT_t[6 + c2][T]])
                                    self.ps_release(bn)
                                    self.ps_release(bd)
                            pend.append(emit_pv)
                            if len(pend) > 1:
                                pend.pop(0)()
                            if step < 6:
                                emit_gate_chunk()
                                emit_gate_mults()
                            step += 1
                while pend:
                    pend.pop(0)()
                self.mem_done = True
                while emit_gate_chunk():
                    emit_gate_mults()
                emit_gate_mults()
                for (c0, n) in [(0, 384), (384, 384), (768, 256)]:
                    s, st = self.wnext([("wout", li, c0, n, 0)])
                    for sub in range(4):
                        tt = 4 * T + sub
                        bank = self.ps_alloc()
                        out = self.ps[bank][:, 0:n]
                        for kc in range(8):
                            self.mm(out, self.preT[:, kc, tt * 128:(tt + 1) * 128], self.wslot[s][:, kc, 0:n],
                                    start=(kc == 0), stop=(kc == 7), reads=[st, self.preT_t[kc][T]],
                                    writes=[self.ps_t[bank]], inc=(kc == 7))
                        self.op(self.dve, lambda tt=tt, out=out, c0=c0, n=n: nc.vector.tensor_tensor(
                            self.x[:, tt, c0:c0 + n], out, self.x[:, tt, c0:c0 + n], ALU.add),
                            reads=[self.ps_t[bank], self.x_t[tt]], writes=[self.x_t[tt]])
                        self.ps_release(bank)
            self.barrier()

    def final_norm(self, si):
        nc = self.nc
        with ExitStack() as es:
            fnw = es.enter_context(nc.sbuf_tensor(self.un("fnw"), [128, D], F32))
            self.junk = es.enter_context(nc.sbuf_tensor(self.un("junk"), [128, D], BF16))
            self.junk_t = Trk()
            fnw_t = Trk()
            self.dma(self.qsp, fnw[:], self.fnw_d, writes=[fnw_t])
            self._final_norm(si, fnw, fnw_t)
            self.barrier()

    def _final_norm(self, si, fnw, fnw_t):
        nc = self.nc
        for i in range(16):
            self.op(self.act, lambda i=i: nc.scalar.activation(
                out=self.junk[:], in_=self.x[:, i, :], func=AF.Square, accum_out=self.ss16[:, i:i + 1]),
                reads=[self.x_t[i]], writes=[self.ss_t[i], self.junk_t])
        self.op(self.act, lambda: nc.scalar.activation(
            out=self.rstd16[:], in_=self.ss16[:], func=AF.Ln, scale=1.0 / D, bias=EPS),
            reads=self.ss_t, writes=[self.st_t])
        self.op(self.act, lambda: nc.scalar.activation(
            out=self.rstd16[:], in_=self.rstd16[:], func=AF.Exp, scale=-0.5),
            reads=[self.st_t], writes=[self.st_t])
        for i in range(16):
            self.op(self.dve, lambda i=i: nc.vector.scalar_tensor_tensor(
                self.x[:, i, :], self.x[:, i, :], self.rstd16[:, i:i + 1], fnw[:], ALU.mult, ALU.mult),
                reads=[self.x_t[i], self.st_t, fnw_t], writes=[self.x_t[i]])
            tok = self.dma(self.qsp, self.out_d[si, i * 128:(i + 1) * 128, :], self.x[:, i, :], reads=[self.x_t[i]])
            self.out_toks.append(tok)


    def tcol(self, n=1):
        if self.tbc + n > 512:
            self.tbc = 0
        c = self.tbc
        self.tbc += n
        return c

    def T(self, eng, fn, reads, writes, **kw):
        self.t0q.append(lambda: self.op(eng, fn, reads=reads, writes=writes, **kw))

    def t0_init(self):
        if T0S < 1:
            return
        nc = self.nc
        tb = self.ps[self.tb]
        tbt = self.ps_t[self.tb]
        c = self.tcol(8)
        for kc in range(8):
            self.op(self.pe, lambda kc=kc: nc.tensor.matmul(
                tb[:, c + kc:c + kc + 1], self.x[0:1, 0, kc * 128:(kc + 1) * 128], self.ones32[0:1, 0:1],
                start=True, stop=True, skip_group_check=True),
                reads=[self.x_t[0], self.const_t], writes=[tbt], inc=(kc == 7), acc=(kc > 0))
        self.op(self.dve, lambda: nc.vector.tensor_copy(self.x0T[:], tb[:, c:c + 8]),
                reads=[tbt], writes=[self.x0_t])

    def t0_inject(self):
        if T0S < 1:
            return
        nc = self.nc
        tb = self.ps[self.tb]
        tbt = self.ps_t[self.tb]
        for hf in range(2):
            for k4 in range(4):
                kc = 4 * hf + k4
                self.op(self.pe, lambda kc=kc, k4=k4: nc.tensor.matmul(
                    tb[0:1, k4 * 128:(k4 + 1) * 128], self.x0T[:, kc:kc + 1], self.ident32[:],
                    start=True, stop=True, skip_group_check=True),
                    reads=[self.x0_t, self.const_t], writes=[tbt], inc=(k4 == 3), acc=(k4 > 0))
            self.op(self.dve, lambda hf=hf: nc.vector.tensor_copy(self.x[0:1, 0, hf * 512:(hf + 1) * 512], tb[0:1, :]),
                    reads=[tbt], writes=[self.x_t[0]])
        self.tbc = 0

    def t0_cproj(self, vec, vec_t, wname, idx, c0, M, col, pbase=0):
        nc = self.nc
        tb = self.ps[self.tb]
        tbt = self.ps_t[self.tb]
        k = self.w32_n % 2
        self.w32_n += 1
        w32, w32_t = self.w32[k], self.w32_t[k]

        def load():
            srcap = getattr(self, wname + "_d")[idx, :, c0:c0 + M].rearrange("(kc p) c -> p kc c", p=128)
            self.dma(self.qsp, w32[:, :, 0:M], srcap, writes=[w32_t])
        if self.t0_prev_pos is not None:
            self.t0q.insert(self.t0_prev_pos, load)
        else:
            self.t0q.append(load)
        self.t0_prev_pos = len(self.t0q)
        for kc in range(8):
            self.T(self.pe, lambda kc=kc: nc.tensor.matmul(
                tb[pbase:pbase + M, col:col + 1], w32[:, kc, 0:M], vec[:, kc:kc + 1],
                start=(kc == 0), stop=(kc == 7), skip_group_check=True),
                [w32_t, vec_t], [tbt], inc=(kc == 7), acc=(kc > 0))

    def t0_mode(self, li):
        last_gla = max([l for l in range(self.nlayers) if l % 2 == 0], default=-1)
        if li < last_gla:
            return "full"
        if li == last_gla:
            return "heads"
        return "none"

    def t0_layer(self, li):
        nc = self.nc
        j = li // 2
        self.t0_prev_pos = None
        assert not self.t0q
        mode = self.t0_mode(li)
        if mode == "none":
            return
        tb = self.ps[self.tb]
        tbt = self.ps_t[self.tb]
        s = self.t0s
        st = self.t0s_t
        T = self.T
        X = mybir.AxisListType.X
        if T0S < 2:
            return
        c = self.tcol()
        T(self.dve, lambda: nc.vector.tensor_tensor(s[:, 8:16], self.x0T[:], self.x0T[:], ALU.mult), [self.x0_t], [st])
        T(self.dve, lambda: nc.vector.tensor_reduce(out=s[:, 0:1], in_=s[:, 8:16], axis=X, op=ALU.add), [st], [st])
        T(self.pe, lambda: nc.tensor.matmul(tb[:, c:c + 1], self.ones32[:], s[:, 0:1], start=True, stop=True,
                                            skip_group_check=True), [st, self.const_t], [tbt])
        T(self.act, lambda: nc.scalar.activation(out=s[:, 1:2], in_=tb[:, c:c + 1], func=AF.Ln, scale=1.0 / D, bias=EPS),
          [tbt], [st])
        T(self.act, lambda: nc.scalar.activation(out=s[:, 1:2], in_=s[:, 1:2], func=AF.Exp, scale=-0.5), [st], [st])
        T(self.dve, lambda: nc.vector.scalar_tensor_tensor(
            self.h0T[:], self.x0T[:], s[:, 1:2], self.normw[:, li, :], ALU.mult, ALU.mult),
          [self.x0_t, st, self.const_t], [self.h0_t])
        hv, ht = self.h0T, self.h0_t
        if T0S < 3:
            return
        if li % 2 == 0:
            wname, qm0, g0 = "wina", 1552, 1808
            for h in range(4):
                c = self.tcol(8)
                self.t0_cproj(hv, ht, wname, j, 96 * h, 96, c)
                self.t0_cproj(hv, ht, wname, j, 384 + 96 * h, 96, c + 1)
                T(self.act, lambda c=c: nc.scalar.copy(s[0:96, 2:3], tb[0:96, c:c + 1]), [tbt], [st])
                T(self.dve, lambda c=c: nc.vector.tensor_tensor(s[0:96, 3:4], s[0:96, 2:3], tb[0:96, c + 1:c + 2], ALU.mult),
                  [tbt, st], [st])
                T(self.pe, lambda c=c: nc.tensor.matmul(tb[:, c + 2:c + 3], self.ones32[0:96, :], s[0:96, 3:4],
                                                        start=True, stop=True, skip_group_check=True),
                  [st, self.const_t], [tbt])
                f0 = 192 * h
                if h % 2 == 0:
                    cfull, chalf, phalf, fs0, hs0 = f0 // 128, f0 // 128 + 1, 0, 0, 128
                else:
                    chalf, cfull, phalf, hs0, fs0 = f0 // 128, f0 // 128 + 1, 64, 0, 64
                pr = slice(phalf, phalf + 64)
                self.t0_cproj(hv, ht, wname, j, 768 + f0 + fs0, 128, c + 3)
                self.t0_cproj(hv, ht, wname, j, 768 + f0 + hs0, 64, c + 4, pbase=phalf)
                T(self.dve, lambda: nc.vector.memset(s[:, 4:6], 0.0), [st], [st])
                T(self.act, lambda c=c: nc.scalar.copy(s[:, 4:5], tb[:, c + 3:c + 4]), [tbt], [st])
                T(self.act, lambda c=c, pr=pr: nc.scalar.copy(s[pr, 5:6], tb[pr, c + 4:c + 5]), [tbt], [st])
                T(self.dve, lambda: nc.vector.tensor_tensor(s[:, 16:18], s[:, 4:6], s[:, 4:6], ALU.mult), [st], [st])
                T(self.dve, lambda: nc.vector.tensor_reduce(out=s[:, 6:7], in_=s[:, 16:18], axis=X, op=ALU.add), [st], [st])
                T(self.pe, lambda c=c: nc.tensor.matmul(tb[:, c + 5:c + 6], self.ones32[:], s[:, 6:7],
                                                        start=True, stop=True, skip_group_check=True),
                  [st, self.const_t], [tbt])
                T(self.act, lambda c=c: nc.scalar.activation(out=s[:, 7:8], in_=tb[:, c + 2:c + 3], func=AF.Copy,
                                                             scale=96.0 ** -0.5), [tbt], [st])
                T(self.dve, lambda: nc.vector.tensor_tensor(s[:, 18:19], s[:, 7:8], s[:, 7:8], ALU.mult), [st], [st])
                T(self.dve, lambda c=c: nc.vector.tensor_tensor(s[:, 18:19], s[:, 18:19], tb[:, c + 5:c + 6], ALU.mult),
                  [st, tbt], [st])
                T(self.act, lambda: nc.scalar.activation(out=s[:, 18:19], in_=s[:, 18:19], func=AF.Ln,
                                                         scale=1.0 / 192, bias=EPS), [st], [st])
                T(self.act, lambda: nc.scalar.activation(out=s[:, 18:19], in_=s[:, 18:19], func=AF.Exp, scale=-0.5),
                  [st], [st])
                T(self.dve, lambda: nc.vector.tensor_tensor(s[:, 19:20], s[:, 7:8], s[:, 18:19], ALU.mult), [st], [st])
                T(self.dve, lambda cfull=cfull: nc.vector.scalar_tensor_tensor(
                    self.m0T[:, cfull:cfull + 1], s[:, 4:5], s[:, 19:20], self.gnw[:, 6 * j + cfull:6 * j + cfull + 1],
                    ALU.mult, ALU.mult), [st, self.const_t], [self.m0_t])
                T(self.dve, lambda chalf=chalf, pr=pr: nc.vector.scalar_tensor_tensor(
                    self.m0T[pr, chalf:chalf + 1], s[pr, 5:6], s[pr, 19:20], self.gnw[pr, 6 * j + chalf:6 * j + chalf + 1],
                    ALU.mult, ALU.mult), [st, self.const_t], [self.m0_t])
        else:
            wname, qm0, g0 = "winb", 6912, 7168
            for h in range(6):
                for g in range(3):
                    base = g * 2304
                    c = self.tcol(4)
                    self.t0_cproj(hv, ht, wname, j, base + 128 * h, 128, c)
                    self.t0_cproj(hv, ht, wname, j, base + 768 + 128 * h, 128, c + 1)
                    self.t0_cproj(hv, ht, wname, j, base + 1536 + 128 * h, 128, c + 2)
                    T(self.act, lambda c=c: nc.scalar.copy(s[:, 2:3], tb[:, c:c + 1]), [tbt], [st])
                    T(self.dve, lambda c=c: nc.vector.tensor_tensor(s[:, 3:4], s[:, 2:3], tb[:, c + 1:c + 2], ALU.mult),
                      [tbt, st], [st])
                    T(self.pe, lambda c=c: nc.tensor.matmul(tb[:, c + 3:c + 4], self.ones32[:], s[:, 3:4],
                                                            start=True, stop=True, skip_group_check=True),
                      [st, self.const_t], [tbt])
                    T(self.act, lambda c=c, g=g: nc.scalar.activation(out=s[:, 8 + g:9 + g], in_=tb[:, c + 3:c + 4],
                                                                      func=AF.Exp, scale=128.0 ** -0.5), [tbt], [st])
                    T(self.act, lambda c=c, g=g: nc.scalar.copy(s[:, 12 + g:13 + g], tb[:, c + 2:c + 3]), [tbt], [st])
                T(self.dve, lambda: nc.vector.tensor_reduce(out=s[:, 16:17], in_=s[:, 8:11], axis=X, op=ALU.add), [st], [st])
                T(self.dve, lambda: nc.vector.reciprocal(s[:, 16:17], s[:, 16:17]), [st], [st])
                T(self.dve, lambda: nc.vector.tensor_tensor(s[:, 20:23], s[:, 8:11], s[:, 12:15], ALU.mult), [st], [st])
                T(self.dve, lambda: nc.vector.tensor_reduce(out=s[:, 17:18], in_=s[:, 20:23], axis=X, op=ALU.add), [st], [st])
                T(self.dve, lambda h=h: nc.vector.tensor_tensor(self.m0T[:, h:h + 1], s[:, 17:18], s[:, 16:17], ALU.mult),
                  [st], [self.m0_t])
        if T0S < 4 or mode == "heads":
            return
        c = self.tcol(2)
        for c2 in range(2):
            self.t0_cproj(hv, ht, wname, j, qm0 + 128 * c2, 128, c + c2)
        T(self.act, lambda c=c: nc.scalar.copy(s[:, 24:26], tb[:, c:c + 2]), [tbt], [st])
        c = self.tcol(8)
        for hm in range(4):
            c2, half = hm // 2, hm % 2
            pr = slice(64 * half, 64 * half + 64)
            for kt in range(2):
                T(self.pe, lambda c=c, hm=hm, kt=kt, c2=c2, pr=pr: nc.tensor.matmul(
                    tb[:, c + 2 * hm + kt:c + 2 * hm + kt + 1], self.kmT32[pr, c2, kt * 128:(kt + 1) * 128],
                    s[pr, 24 + c2:25 + c2], start=True, stop=True, skip_group_check=True),
                  [self.km32_t, st], [tbt], inc=(hm == 3 and kt == 1), acc=not (hm == 0 and kt == 0))
        T(self.act, lambda c=c: nc.scalar.activation(out=self.e8[:], in_=tb[:, c:c + 8], func=AF.Exp, scale=0.125),
          [tbt], [st])
        c = self.tcol(4)
        first = True
        for hm in range(4):
            c2, half = hm // 2, hm % 2
            pr = slice(64 * half, 64 * half + 64)
            for kt in range(2):
                T(self.pe, lambda c=c, hm=hm, kt=kt, c2=c2, pr=pr: nc.tensor.matmul(
                    tb[pr, c + c2:c + c2 + 1], self.vm32[:, kt, 64 * hm:64 * hm + 64], self.e8[:, 2 * hm + kt:2 * hm + kt + 1],
                    start=(kt == 0), stop=(kt == 1), skip_group_check=True),
                  [self.vm32_t, st], [tbt], inc=False, acc=not first)
                first = False
            for kt in range(2):
                T(self.pe, lambda c=c, hm=hm, kt=kt, c2=c2, pr=pr: nc.tensor.matmul(
                    tb[pr, c + 2 + c2:c + 3 + c2], self.ones32[:, 0:64], self.e8[:, 2 * hm + kt:2 * hm + kt + 1],
                    start=(kt == 0), stop=(kt == 1), skip_group_check=True),
                  [self.const_t, st], [tbt], inc=(hm == 3 and kt == 1), acc=True)
        T(self.dve, lambda c=c: nc.vector.reciprocal(s[:, 26:28], tb[:, c + 2:c + 4]), [tbt], [st])
        T(self.dve, lambda c=c: nc.vector.tensor_tensor(self.m0T[:, 6:8], tb[:, c:c + 2], s[:, 26:28], ALU.mult),
          [tbt, st], [self.m0_t])
        if T0S < 5:
            return
        c = self.tcol(8)
        for cc in range(8):
            self.t0_cproj(hv, ht, wname, j, g0 + 128 * cc, 128, c + cc)
        T(self.act, lambda c=c: nc.scalar.activation(out=self.sg0[:], in_=tb[:, c:c + 8], func=AF.Silu), [tbt], [st])
        T(self.dve, lambda: nc.vector.tensor_tensor(self.m0T[:], self.m0T[:], self.sg0[:], ALU.mult),
          [self.m0_t, st], [self.m0_t])
        c = self.tcol(8)
        for cc in range(8):
            self.t0_cproj(self.m0T, self.m0_t, "wout", li, 128 * cc, 128, c + cc)
        T(self.dve, lambda c=c: nc.vector.tensor_tensor(self.x0T[:], self.x0T[:], tb[:, c:c + 8], ALU.add),
          [tbt, self.x0_t], [self.x0_t])

    def mixer_a(self, j):
        nc = self.nc
        win = self.wina_d[j]
        with ExitStack() as es:
            sbt = lambda name, shape, dtype: es.enter_context(nc.sbuf_tensor(self.un(name), shape, dtype))
            wgl = sbt("wgl", [128, 8, 16], BF16)
            wgl_t = Trk()
            glT = sbt("glT", [16, 512], F32)
            glT_t = Trk()
            sp = sbt("sp", [96, 512], F32)
            cc_ = sbt("cc", [96, 512], F32)
            eb = sbt("eb", [96, 512], F32)
            enb = sbt("enb", [96, 512], F32)
            g_t = Trk()
            qinT = sbt("qinT", [96, 512], BF16)
            kinT = sbt("kinT", [96, 512], BF16)
            qk_t = Trk()
            kin_tok = sbt("kin_tok", [128, 4, 96], BF16)
            kin_tok_t = Trk()
            V = sbt("Vh", [128, 4, 192], BF16)
            V_t = Trk()
            R = [[sbt(f"R{h}_{i}", [96, 192], F32) for i in range(2)] for h in range(4)]
            R_t = [[Trk() for i in range(2)] for h in range(4)]
            Sbf = [sbt(f"Sbf{i}", [96, 192], BF16) for i in range(4)]
            Sbf_t = [Trk() for _ in range(4)]
            ATm = [sbt(f"ATm{i}", [128, 128], BF16) for i in range(2)]
            ATm_t = [Trk() for _ in range(2)]
            ssh = sbt("ssh", [128, 4], F32)
            ssh_t = [Trk() for _ in range(4)]
            mixt = [sbt(f"mixt{i}", [128, 192], BF16) for i in range(2)]
            mixt_t = [Trk() for _ in range(2)]
            dec = sbt("dec", [96, 4], F32)
            dec_t = [Trk() for _ in range(4)]
            self.junk = sbt("junkA", [128, 192], BF16)
            self.junk_t = Trk()
            wgu = sbt("wgu", [16, 384], F32)
            rmask = sbt("rmask", [128, 512], BF16)
            cl_t = Trk()
            self.dma(self.qsp, wgu[:], self.wgu_d[j], writes=[cl_t])
            self.dma(self.qpl, rmask[:], self.c_rmask_d, writes=[cl_t])
            self.dma(self.qpl, wgl[:], win[:, 1536:1552].rearrange("(kc p) c -> p kc c", p=128), writes=[wgl_t])
            sk = 0
            ak = 0
            mk = 0
            for T in range(4):
                cols = slice(T * 512, (T + 1) * 512)
                bank = self.ps_alloc()
                for kc in range(8):
                    self.mm(self.ps[bank][0:16, :], wgl[:, kc, :], self.hT[:, kc, cols], start=(kc == 0),
                            stop=(kc == 7), reads=[wgl_t, self.hT_t[T]], writes=[self.ps_t[bank]], inc=(kc == 7))
                self.op(self.act, lambda bank=bank: nc.scalar.copy(glT[:], self.ps[bank][0:16, :]),
                        reads=[self.ps_t[bank]], writes=[glT_t])
                self.ps_release(bank)
                for h in range(4):
                    s, st = self.wnext([("wina", j, 96 * h, 96, 0), ("wina", j, 384 + 96 * h, 96, 96),
                                        ("wina", j, 768 + 192 * h, 192, 192)])
                    bank = self.ps_alloc()
                    self.mm(self.ps[bank][0:96, :], wgu[:, 96 * h:96 * h + 96], glT[:], start=True, stop=True,
                            reads=[cl_t, glT_t], writes=[self.ps_t[bank]], inc=True)
                    self.op(self.act, lambda bank=bank, h=h: nc.scalar.activation(
                        out=sp[:], in_=self.ps[bank][0:96, :], func=AF.Exp, scale=-1.0,
                        bias=self.nbg[:, 4 * j + h:4 * j + h + 1]),
                        reads=[self.ps_t[bank], self.const_t], writes=[g_t])
                    self.ps_release(bank)
                    self.op(self.act, lambda: nc.scalar.activation(out=sp[:], in_=sp[:], func=AF.Ln, bias=1.0),
                            reads=[g_t], writes=[g_t])
                    self.op(self.dve, lambda: nc.vector.tensor_tensor_scan(
                        cc_[:], rmask[0:96, :], sp[:], 0.0, ALU.mult, ALU.add),
                        reads=[g_t, cl_t], writes=[g_t])
                    self.op(self.act, lambda: nc.scalar.activation(out=eb[:], in_=cc_[:], func=AF.Exp, scale=-1.0 / 16),
                            reads=[g_t], writes=[g_t])
                    self.op(self.act, lambda: nc.scalar.activation(out=enb[:], in_=cc_[:], func=AF.Exp, scale=1.0 / 16),
                            reads=[g_t], writes=[g_t])
                    bank = self.ps_alloc()
                    self.proj_fm(bank, 96, s, st, 0, T)
                    self.op(self.dve, lambda bank=bank: nc.vector.scalar_tensor_tensor(
                        qinT[:], self.ps[bank][0:96, :], 96.0 ** -0.5, eb[:], ALU.mult, ALU.mult),
                        reads=[self.ps_t[bank], g_t], writes=[qk_t])
                    self.ps_release(bank)
                    bank = self.ps_alloc()
                    self.proj_fm(bank, 96, s, st, 96, T)
                    self.op(self.dve, lambda bank=bank: nc.vector.tensor_tensor(
                        kinT[:], self.ps[bank][0:96, :], enb[:], ALU.mult),
                        reads=[self.ps_t[bank], g_t], writes=[qk_t])
                    self.ps_release(bank)
                    for sub in range(4):
                        tt = 4 * T + sub
                        bank = self.ps_alloc()
                        out = self.ps[bank][:, 0:192]
                        for kc in range(8):
                            self.mm(out, self.hT[:, kc, tt * 128:(tt + 1) * 128], self.wslot[s][:, kc, 192:384],
                                    start=(kc == 0), stop=(kc == 7), reads=[st, self.hT_t[T]],
                                    writes=[self.ps_t[bank]], inc=(kc == 7))
                        self.op(self.act, lambda sub=sub, out=out: nc.scalar.copy(V[:, sub, :], out),
                                reads=[self.ps_t[bank]], writes=[V_t])
                        self.ps_release(bank)
                    bank = self.ps_alloc()
                    pb = self.ps[bank][:].bitcast(BF16)
                    for sub in range(4):
                        self.op(self.pe, lambda sub=sub, pb=pb: nc.tensor.transpose(
                            pb[:, sub * 96:(sub + 1) * 96], kinT[:, sub * 128:(sub + 1) * 128], self.ident[0:96, 0:96]),
                            reads=[qk_t, self.const_t], writes=[self.ps_t[bank]], inc=(sub == 3), acc=(sub > 0))
                    self.op(self.act, lambda pb=pb: nc.scalar.copy(
                        kin_tok[:].rearrange("p a b -> p (a b)"), pb[:, 0:384]),
                        reads=[self.ps_t[bank]], writes=[kin_tok_t])
                    self.ps_release(bank)
                    for sub in range(4):
                        tt = 4 * T + sub
                        tc_ = slice(sub * 128, (sub + 1) * 128)
                        bA = self.ps_alloc()
                        self.mm(self.ps[bA][:, 0:128], kinT[:, tc_], qinT[:, tc_], start=True, stop=True,
                                reads=[qk_t], writes=[self.ps_t[bA]], inc=True)
                        a = ak % 2
                        ak += 1
                        self.op(self.dve, lambda a=a, bA=bA: nc.vector.tensor_tensor(
                            ATm[a][:], self.ps[bA][:, 0:128], self.gmask[:], ALU.mult),
                            reads=[self.ps_t[bA], self.const_t], writes=[ATm_t[a]])
                        self.ps_release(bA)
                        bO = self.ps_alloc()
                        self.mm(self.ps[bO][:, 0:192], ATm[a][:], V[:, sub, :], start=True, stop=False,
                                reads=[ATm_t[a], V_t], writes=[self.ps_t[bO]], inc=False)
                        for ch in range(2):
                            c = 2 * tt + ch
                            pr = slice(64 * ch, 64 * ch + 64)
                            lc = 64 * (2 * sub + ch)
                            if c > 0:
                                dk = eb[:, lc - 1:lc] if lc > 0 else dec[:, h:h + 1]
                                dk_t = g_t if lc > 0 else dec_t[h]
                                sb_ = sk % 4
                                sk += 1
                                rp = (c - 1) % 2
                                self.op(self.act, lambda sb_=sb_, h=h, dk=dk, rp=rp: nc.scalar.activation(
                                    out=Sbf[sb_][:], in_=R[h][rp][:], func=AF.Copy, scale=dk),
                                    reads=[R_t[h][rp], dk_t], writes=[Sbf_t[sb_]])
                                self.mm(self.ps[bO][pr, 0:192], qinT[:, lc:lc + 64], Sbf[sb_][:], start=False,
                                        stop=(ch == 1), reads=[qk_t, Sbf_t[sb_]], writes=[self.ps_t[bO]],
                                        inc=(ch == 1))
                            bU = self.ps_alloc()
                            self.mm(self.ps[bU][0:96, 0:192], kin_tok[pr, sub, :], V[pr, sub, :], start=True, stop=True,
                                    reads=[kin_tok_t, V_t], writes=[self.ps_t[bU]], inc=True)
                            rn = c % 2
                            if c == 0:
                                self.op(self.dve, lambda bU=bU, h=h, rn=rn: nc.vector.tensor_copy(
                                    R[h][rn][:], self.ps[bU][0:96, 0:192]),
                                    reads=[self.ps_t[bU]], writes=[R_t[h][rn]])
                            else:
                                self.op(self.dve, lambda bU=bU, h=h, dk=dk, rn=rn: nc.vector.scalar_tensor_tensor(
                                    R[h][rn][:], R[h][1 - rn][:], dk, self.ps[bU][0:96, 0:192], ALU.mult, ALU.add),
                                    reads=[self.ps_t[bU], R_t[h][1 - rn], dk_t], writes=[R_t[h][rn]])
                            self.ps_release(bU)
                        m = mk % 2
                        mk += 1
                        self.op(self.act, lambda bO=bO, h=h: nc.scalar.activation(
                            out=self.junk[:, 0:192], in_=self.ps[bO][:, 0:192], func=AF.Square,
                            accum_out=ssh[:, h:h + 1]),
                            reads=[self.ps_t[bO]], writes=[ssh_t[h], self.junk_t])
                        self.op(self.act, lambda h=h: nc.scalar.activation(
                            out=ssh[:, h:h + 1], in_=ssh[:, h:h + 1], func=AF.Ln, scale=1.0 / 192, bias=EPS),
                            reads=[ssh_t[h]], writes=[ssh_t[h]])
                        self.op(self.act, lambda h=h: nc.scalar.activation(
                            out=ssh[:, h:h + 1], in_=ssh[:, h:h + 1], func=AF.Exp, scale=-0.5),
                            reads=[ssh_t[h]], writes=[ssh_t[h]])
                        self.op(self.act, lambda bO=bO, h=h, m=m: nc.scalar.activation(
                            out=mixt[m][:], in_=self.ps[bO][:, 0:192], func=AF.Copy, scale=ssh[:, h:h + 1]),
                            reads=[self.ps_t[bO], ssh_t[h]], writes=[mixt_t[m]])
                        self.ps_release(bO)
                        f0 = 192 * h
                        if h % 2 == 0:
                            cfull, chalf, phalf = f0 // 128, f0 // 128 + 1, 0
                            full_src, half_src = slice(0, 128), slice(128, 192)
                        else:
                            chalf, cfull, phalf = f0 // 128, f0 // 128 + 1, 64
                            half_src, full_src = slice(0, 64), slice(64, 192)
                        bT = self.ps_alloc()
                        pb = self.ps[bT][:].bitcast(BF16)
                        self.op(self.pe, lambda m=m, pb=pb, full_src=full_src: nc.tensor.transpose(
                            pb[:, 0:128], mixt[m][:, full_src], self.ident[:]),
                            reads=[mixt_t[m], self.const_t], writes=[self.ps_t[bT]], inc=False)
                        self.op(self.pe, lambda m=m, pb=pb, half_src=half_src, phalf=phalf: nc.tensor.transpose(
                            pb[phalf:phalf + 64, 128:256], mixt[m][:, half_src], self.ident[:]),
                            reads=[mixt_t[m], self.const_t], writes=[self.ps_t[bT]], inc=True, acc=True)
                        tcol = slice(tt * 128, (tt + 1) * 128)
                        self.op(self.dve, lambda pb=pb, cfull=cfull, tcol=tcol: nc.vector.tensor_scalar(
                            self.preT[:, cfull, tcol], pb[:, 0:128], self.gnw[:, 6 * j + cfull:6 * j + cfull + 1],
                            None, ALU.mult),
                            reads=[self.ps_t[bT], self.const_t], writes=[self.preT_t[cfull][T]])
                        self.op(self.dve, lambda pb=pb, chalf=chalf, phalf=phalf, tcol=tcol: nc.vector.tensor_scalar(
                            self.preT[phalf:phalf + 64, chalf, tcol], pb[phalf:phalf + 64, 128:256],
                            self.gnw[phalf:phalf + 64, 6 * j + chalf:6 * j + chalf + 1], None, ALU.mult),
                            reads=[self.ps_t[bT], self.const_t], writes=[self.preT_t[chalf][T]])
                        self.ps_release(bT)
                    self.op(self.dve, lambda h=h: nc.vector.tensor_copy(dec[:, h:h + 1], eb[:, 511:512]),
                            reads=[g_t], writes=[dec_t[h]])
            self.barrier()

    def mixer_b(self, j):
        nc = self.nc
        win = self.winb_d[j]
        with ExitStack() as es:
            sbt = lambda name, shape, dtype: es.enter_context(nc.sbuf_tensor(self.un(name), shape, dtype))
            qkT = sbt("qkT", [128, 2, SEQ], BF16)
            qkT_t = [Trk() for _ in range(4)]
            Vb = sbt("Vb", [128, 16, 128], BF16)
            Vb_t = [Trk() for _ in range(4)]
            qkt = [sbt(f"qkt{i}", [128, 2, 128], BF16) for i in range(3)]
            qkt_t = [Trk() for _ in range(3)]
            ta = sbt("ropeA", [128, 2, 2, 16], F32)
            tb = sbt("ropeB", [128, 2, 2, 16], F32)
            rope_t = Trk()
            pT = [sbt(f"pT{i}", [128, 256], BF16) for i in range(6)]
            pT_t = [Trk() for _ in range(6)]
            accN = sbt("accN", [128, SEQ], F32)
            accD = sbt("accD", [128, SEQ], F32)
            acc_t = [Trk() for _ in range(4)]
            pk = 0
            qk_i = 0
            carry = []
            for h in range(6):
                for g in range(3):
                    r = DIL[g]
                    nbp = 16 // r
                    base = g * 2304
                    s, st = self.wnext([("winb", j, base + 128 * h, 128, 0), ("winb", j, base + 768 + 128 * h, 128, 128),
                                        ("winb", j, base + 1536 + 128 * h, 128, 256)])
                    pend_tr = []
                    for b in range(16):
                        phase, jb = b // nbp, b % nbp
                        t0 = r * 128 * jb + phase
                        tsl = slice(t0, t0 + 127 * r + 1, r) if r > 1 else slice(t0, t0 + 128)
                        if r == 1:
                            hts = [self.hT_t[jb // 4]]
                        elif r == 4:
                            hts = [self.hT_t[jb]]
                        else:
                            hts = self.hT_t
                        bank = self.ps_alloc()
                        out = self.ps[bank][:, 0:384]
                        for kc in range(8):
                            self.mm(out, self.hT[:, kc, tsl], self.wslot[s][:, kc, :], start=(kc == 0), stop=(kc == 7),
                                    reads=[st] + list(hts), writes=[self.ps_t[bank]], inc=(kc == 7))
                        ps3 = out.rearrange("p (a d) -> p a d", a=3)
                        qi = qk_i % 3
                        qk_i += 1
                        rt = [self.ps_t[bank], self.const_t]
                        X = ps3[:, 0:2, 0:32].rearrange("p a (u d) -> p a u d", u=2)
                        Cb = self.cos[:, g, b, :].unsqueeze(1).unsqueeze(1).to_broadcast([128, 2, 2, 16])
                        Sb = self.sin[:, g, b, :].unsqueeze(1).unsqueeze(1).to_broadcast([128, 2, 2, 16])
                        self.op(self.dve, lambda X=X, Cb=Cb: nc.vector.tensor_tensor(ta[:], X, Cb, ALU.mult),
                                reads=rt, writes=[rope_t])
                        self.op(self.dve, lambda X=X, Sb=Sb: nc.vector.tensor_tensor(tb[:], X, Sb, ALU.mult),
                                reads=rt, writes=[rope_t])
                        self.op(self.dve, lambda qi=qi: nc.vector.tensor_tensor(
                            qkt[qi][:, :, 0:16], ta[:, :, 0, :], tb[:, :, 1, :], ALU.subtract),
                            reads=[rope_t], writes=[qkt_t[qi]])
                        self.op(self.dve, lambda qi=qi: nc.vector.tensor_tensor(
                            qkt[qi][:, :, 16:32], ta[:, :, 1, :], tb[:, :, 0, :], ALU.add),
                            reads=[rope_t], writes=[qkt_t[qi]])
                        self.op(self.act, lambda b=b, ps3=ps3: nc.scalar.copy(Vb[:, b, :], ps3[:, 2, :]),
                                reads=[self.ps_t[bank]], writes=[Vb_t[b // 4]])
                        self.op(self.act, lambda qi=qi, ps3=ps3: nc.scalar.copy(
                            qkt[qi][:, :, 32:128], ps3[:, 0:2, 32:128]),
                            reads=[self.ps_t[bank]], writes=[qkt_t[qi]])
                        self.ps_release(bank)
                        def emit_tr(b=b, qi=qi):
                            if b % 4 == 0:
                                self.bankT = self.ps_alloc()
                            bankT = self.bankT
                            pb = self.ps[bankT][:].bitcast(BF16).rearrange("p (a t) -> p a t", a=2)
                            for a in range(2):
                                self.op(self.pe, lambda a=a, pb=pb, qi=qi, b=b: nc.tensor.transpose(
                                    pb[:, a, (b % 4) * 128:(b % 4 + 1) * 128], qkt[qi][:, a, :], self.ident[:]),
                                    reads=[qkt_t[qi], self.const_t], writes=[self.ps_t[bankT]],
                                    inc=(a == 1), acc=not (b % 4 == 0 and a == 0))
                            if b % 4 == 3:
                                b0 = b - 3
                                self.op(self.act, lambda pb=pb, b0=b0: nc.scalar.copy(
                                    qkT[:, :, b0 * 128:(b0 + 4) * 128], pb),
                                    reads=[self.ps_t[bankT]], writes=[qkT_t[b // 4]])
                                self.ps_release(bankT)
                        pend_tr.append(emit_tr)
                        if len(pend_tr) > 2:
                            pend_tr.pop(0)()
                        if carry and b % 3 == 2:
                            carry.pop(0)()
                    while pend_tr:
                        pend_tr.pop(0)()
                    while carry:
                        carry.pop(0)()
                    if KSTAGE <= 2:
                        continue
                    bn = {}
                    bd = {}
                    started = set()
                    pend_pv = []
                    for kb in range(16):
                        phase, jb = kb // nbp, kb % nbp
                        has_next = jb < nbp - 1
                        N = 256 if has_next else 128
                        m = kb // 4
                        if m not in bn:
                            bn[m] = self.ps_alloc()
                            bd[m] = self.ps_alloc()
                        if has_next and (kb + 1) // 4 not in bn:
                            bn[m + 1] = self.ps_alloc()
                            bd[m + 1] = self.ps_alloc()
                        bs = self.ps_alloc()
                        qts = [qkT_t[kb // 4]] + ([qkT_t[(kb + 1) // 4]] if has_next else [])
                        self.mm(self.ps[bs][:, 0:N], qkT[:, 1, kb * 128:(kb + 1) * 128], qkT[:, 0, kb * 128:kb * 128 + N],
                                start=True, stop=True, reads=qts, writes=[self.ps_t[bs]], inc=True)
                        p = pk % 6
                        pk += 1
                        self.op(self.act, lambda p=p, bs=bs, N=N: nc.scalar.activation(
                            out=pT[p][:, 0:N], in_=self.ps[bs][:, 0:N], func=AF.Exp, scale=128.0 ** -0.5),
                            reads=[self.ps_t[bs]], writes=[pT_t[p]])
                        self.ps_release(bs)
                        self.op(self.dve, lambda p=p, N=N: nc.vector.tensor_tensor(
                            pT[p][:, 0:N], pT[p][:, 0:N], self.mask2[:, 0:N], ALU.mult),
                            reads=[pT_t[p], self.const_t], writes=[pT_t[p]])
                        def emit_pv(kb=kb, has_next=has_next, m=m, p=p):
                            if has_next and (kb + 1) // 4 == m:
                                segs = [(m, (kb % 4) * 128, 0, 256)]
                            elif has_next:
                                segs = [(m, (kb % 4) * 128, 0, 128), (m + 1, 0, 128, 128)]
                            else:
                                segs = [(m, (kb % 4) * 128, 0, 128)]
                            for si_, (mm_, oc, pc, n) in enumerate(segs):
                                first = mm_ not in started
                                started.add(mm_)
                                last = (si_ == len(segs) - 1)
                                self.mm(self.ps[bn[mm_]][:, oc:oc + n], Vb[:, kb, :], pT[p][:, pc:pc + n], start=first, stop=False,
                                        reads=[Vb_t[kb // 4], pT_t[p]], writes=[self.ps_t[bn[mm_]]], inc=False)
                                self.mm(self.ps[bd[mm_]][:, oc:oc + n], self.ones[:], pT[p][:, pc:pc + n], start=first, stop=False,
                                        reads=[pT_t[p]], writes=[self.ps_t[bd[mm_]]], inc=last)
                            if kb % 4 == 3 and KSTAGE <= 3:
                                self.ps_release(bn[m])
                                self.ps_release(bd[m])
                            elif kb % 4 == 3:
                                if r == 1:
                                    dN, dD = accN[:, m * 512:(m + 1) * 512], accD[:, m * 512:(m + 1) * 512]
                                    sN, sD = self.ps[bn[m]][:, :], self.ps[bd[m]][:, :]
                                    trks = [acc_t[m]]
                                elif r == 4:
                                    dN = accN[:, m:SEQ:4]
                                    dD = accD[:, m:SEQ:4]
                                    sN, sD = self.ps[bn[m]][:, :], self.ps[bd[m]][:, :]
                                    trks = acc_t
                                else:
                                    dN = accN[:].rearrange("d (s ph) -> d ph s", ph=16)[:, 4 * m:4 * m + 4, :]
                                    dD = accD[:].rearrange("d (s ph) -> d ph s", ph=16)[:, 4 * m:4 * m + 4, :]
                                    sN = self.ps[bn[m]][:, :].rearrange("d (ph s) -> d ph s", ph=4)
                                    sD = self.ps[bd[m]][:, :].rearrange("d (ph s) -> d ph s", ph=4)
                                    trks = acc_t
                                def do_acc(dN=dN, dD=dD, sN=sN, sD=sD, trks=trks, bnm=bn[m], bdm=bd[m], g=g):
                                    if g == 0:
                                        self.op(self.act, lambda: nc.scalar.copy(dN, sN),
                                                reads=[self.ps_t[bnm]], writes=trks)
                                        self.op(self.act, lambda: nc.scalar.copy(dD, sD),
                                                reads=[self.ps_t[bdm]], writes=trks)
                                    else:
                                        self.op(self.dve, lambda: nc.vector.tensor_tensor(dN, sN, dN, ALU.add),
                                                reads=[self.ps_t[bnm]] + trks, writes=trks)
                                        self.op(self.dve, lambda: nc.vector.tensor_tensor(dD, sD, dD, ALU.add),
                                                reads=[self.ps_t[bdm]] + trks, writes=trks)
                                    self.ps_release(bnm)
                                    self.ps_release(bdm)
                                if m == 3:
                                    carry.append(do_acc)
                                else:
                                    do_acc()
                        pend_pv.append(emit_pv)
                        if len(pend_pv) > 4:
                            pend_pv.pop(0)()
                    while pend_pv:
                        pend_pv.pop(0)()
                for T in range(4 if KSTAGE > 4 else 0):
                    def do_fin(T=T, h=h):
                        cols = slice(T * 512, (T + 1) * 512)
                        self.op(self.dve, lambda: nc.vector.reciprocal(accD[:, cols], accD[:, cols]),
                                reads=[acc_t[T]], writes=[acc_t[T]])
                        self.op(self.dve, lambda: nc.vector.tensor_tensor(
                            self.preT[:, h, cols], accN[:, cols], accD[:, cols], ALU.mult),
                            reads=[acc_t[T]], writes=[self.preT_t[h][T]])
                    carry.append(do_fin)
            while carry:
                carry.pop(0)()
            self.barrier()


def _consts():
    p = np.arange(128)
    ident = np.eye(128, dtype=np.float32)
    mask2 = np.zeros((128, 256), np.float32)
    mask2[:, 0:128] = (p[:, None] <= p[None, :])
    mask2[:, 128:256] = (p[:, None] >= p[None, :])
    gmask = ((p[:, None] // 64 == p[None, :] // 64) & (p[:, None] <= p[None, :])).astype(np.float32)
    rmask = np.ones((128, 512), np.float32)
    rmask[:, 0::64] = 0.0
    half = 16
    inv = (np.float32(500000.0) ** (-(np.arange(half, dtype=np.float32) / np.float32(half)))).astype(np.float32)
    cos = np.zeros((128, 3, 16, 16), np.float32)
    sin = np.zeros((128, 3, 16, 16), np.float32)
    for g, r in enumerate(DIL):
        nbp = 16 // r
        for b in range(16):
            phase, jb = b // nbp, b % nbp
            t = (r * (128 * jb + p) + phase).astype(np.float32)
            ang = (t[:, None] * inv[None, :]).astype(np.float32)
            cos[:, g, b, :] = np.cos(ang)
            sin[:, g, b, :] = np.sin(ang)
    return dict(c_ident=ident, c_ones=np.ones((128, 128), np.float32), c_mask2=mask2, c_gmask=gmask, c_rmask=rmask,
                c_cos=cos.reshape(128, -1), c_sin=sin.reshape(128, -1))


_NC_CACHE = {}


def _get_nc(nseq, nlayers, dbg=None):
    key = (nseq, nlayers, None if dbg is None else tuple(sorted(dbg)))
    if key not in _NC_CACHE:
        k = Kern(nseq, nlayers, dbg)
        _NC_CACHE[key] = k.build()
    return _NC_CACHE[key]


def _layout_params(mem_norm_w, norm_w, w_memkv, w_out, w_in_a, w_gate_up, b_gate, gla_norm_w, w_in_b, final_norm_w):
    f = lambda a: np.ascontiguousarray(np.asarray(a, dtype=np.float32))
    fm8 = lambda v: f(np.asarray(v).reshape(8, 128).T)
    d = {}
    d["memnw_fm"] = fm8(mem_norm_w)
    d["normw_fm"] = f(np.concatenate([fm8(norm_w[i]) for i in range(4)], axis=1))
    d["w_memkv"] = f(w_memkv)
    d["w_out"] = f(w_out)
    d["w_in_a"] = f(w_in_a)
    d["w_gate_up"] = f(w_gate_up)
    bg = np.asarray(b_gate)
    d["bg_fm"] = f(np.concatenate([bg[j].reshape(4, 96).T for j in range(2)], axis=1))
    gn = np.asarray(gla_norm_w)
    idx = (np.arange(768) % 192).reshape(6, 128).T
    d["gnw_fm"] = f(np.concatenate([gn[j][idx] for j in range(2)], axis=1))
    d["w_in_b"] = f(w_in_b)
    d["fnw_bc"] = f(np.broadcast_to(np.asarray(final_norm_w)[None, :], (128, D)))
    d.update(_consts())
    return d


def kernel(x, mem, mem_norm_w, norm_w, w_memkv, w_out, w_in_a, w_gate_up, b_gate, gla_norm_w, w_in_b,
           final_norm_w, _nlayers=4, _nseq_launch=4):
    x = np.asarray(x, dtype=np.float32)
    mem = np.asarray(mem, dtype=np.float32)
    B = x.shape[0]
    per_core = B // NCORES
    params = _layout_params(mem_norm_w, norm_w, w_memkv, w_out, w_in_a, w_gate_up, b_gate, gla_norm_w,
                            w_in_b, final_norm_w)
    out = np.empty_like(x)
    nseq = _nseq_launch
    nc = _get_nc(nseq, _nlayers)
    for s0 in range(0, per_core, nseq):
        in_maps = []
        for c in range(NCORES):
            b0 = c * per_core + s0
            m = dict(params)
            m["x"] = np.ascontiguousarray(x[b0:b0 + nseq])
            m["mem"] = np.ascontiguousarray(mem[b0:b0 + nseq])
            in_maps.append(m)
        res = run_bass_kernel_spmd(nc, in_maps, core_ids=list(range(NCORES)))
        for c in range(NCORES):
            b0 = c * per_core + s0
            out[b0:b0 + nseq] = np.asarray(res.results[c]["out"]).reshape(nseq, SEQ, D)
    return out
```

```python
import os
import numpy as np
from contextlib import ExitStack
import concourse.bass as bass
import concourse.mybir as mybir
from concourse.bass_utils import run_bass_kernel_spmd

F32 = mybir.dt.float32
BF16 = mybir.dt.bfloat16
AF = mybir.ActivationFunctionType
ALU = mybir.AluOpType

NCORES = 8
SEQ = 2048
D = 1024
NMEM = 256
DIL = [1, 4, 16]
EPOCH = 12000
NSLOT = 3
SLOTW = 384
EPS = 1e-6
KSTAGE = int(os.environ.get('KSTAGE', '9'))
KSUB = int(os.environ.get('KSUB', '9'))
T0S = int(os.environ.get('T0S', '9'))


class Trk:
    __slots__ = ("w", "r", "rd", "excl")

    def __init__(self, excl=False):
        self.excl = excl
        self.w = None
        self.r = {}
        self.rd = []


class Eng:
    def __init__(self, name, h):
        self.name = name
        self.h = h
        self.count = 0
        self.pending = False
        self.sems = []
        self.waited = {}


class Queue:
    def __init__(self, name, h, nslots):
        self.name = name
        self.h = h
        self.sems = []
        self.uses = [0] * nslots
        self.k = 0
        self.waited = {}


class Kern:
    def __init__(self, nseq, nlayers, dbg=None):
        self.nseq = nseq
        self.nlayers = nlayers
        self.dry = True
        self.jobs = []
        self.dbg = dbg

    def setup_engines(self, es):
        nc = self.nc
        self.pe = Eng("pe", nc.tensor)
        self.act = Eng("act", nc.scalar)
        self.dve = Eng("dve", nc.vector)
        self.engs = {"pe": self.pe, "act": self.act, "dve": self.dve}
        self.qsp = Queue("sp", nc.sync, 8)
        self.qpl = Queue("pool", nc.gpsimd, 8)
        if not self.dry:
            for e in self.engs.values():
                nep = self.est_counts[e.name] // EPOCH + 1
                for i in range(nep):
                    e.sems.append(es.enter_context(nc.semaphore(f"s_{e.name}{i}")))
            for q in (self.qsp, self.qpl):
                for i in range(len(q.uses)):
                    q.sems.append(es.enter_context(nc.semaphore(f"q_{q.name}{i}")))

    def _tok_sem(self, tok):
        if tok[0] == "d":
            return tok[1], tok[2]
        e = self.engs[tok[0]]
        c = tok[1]
        ep = (c - 1) // EPOCH
        return e.sems[ep], c - ep * EPOCH

    def _need(self, waiter, tok, out):
        if tok is None:
            return
        if tok[0] == "d":
            key = ("d", tok[3])
            if waiter.waited.get(key, 0) >= tok[2]:
                return
            if out.get(key, (None, 0))[1] < tok[2]:
                out[key] = (tok, tok[2])
        else:
            key = tok[0]
            if waiter.waited.get(key, 0) >= tok[1]:
                return
            if out.get(key, (None, 0))[1] < tok[1]:
                out[key] = (tok, tok[1])

    def _collect(self, waiter, reads, writes, acc, own):
        out = {}
        for t in reads:
            if t.w is not None and not (t.w[0] == own and own == "pe"):
                self._need(waiter, t.w, out)
            if t.excl:
                for en, c in t.r.items():
                    if en != own:
                        self._need(waiter, (en, c), out)
        if not acc:
            for t in writes:
                if t.w is not None and not (t.w[0] == own and own == "pe"):
                    self._need(waiter, t.w, out)
                for en, c in t.r.items():
                    if not (en == own and own == "pe"):
                        self._need(waiter, (en, c), out)
                for dt_ in t.rd:
                    self._need(waiter, dt_, out)
        return out

    def _emit_waits(self, waiter, h, need):
        for key, (tok, v) in need.items():
            waiter.waited[key] = v
            if not self.dry:
                sem, val = self._tok_sem(tok)
                h.wait_ge(sem, val)

    def op(self, eng, fn, reads=(), writes=(), inc=True, acc=False):
        need = self._collect(eng, reads, writes, acc, eng.name)
        self._emit_waits(eng, eng.h, need)
        if inc:
            eng.count += 1
            eng.pending = False
            c = eng.count
        else:
            eng.pending = True
            c = eng.count + 1
        if not self.dry:
            ins = fn()
            if inc:
                ep = (c - 1) // EPOCH
                ins.then_inc(eng.sems[ep], 1)
        tok = (eng.name, c)
        for t in reads:
            if t.r.get(eng.name, 0) < c:
                t.r[eng.name] = c
        for t in writes:
            if acc:
                t.w = tok
            else:
                t.w = tok
                t.r = {}
                t.rd = []
        return tok

    def dma(self, q, out_ap, in_ap=None, reads=(), writes=()):
        pairs = out_ap if in_ap is None else [(out_ap, in_ap)]
        need = self._collect(q, reads, writes, False, q.name)
        self._emit_waits(q, q.h, need)
        slot = q.k % len(q.uses)
        q.k += 1
        prev = q.uses[slot] * 16
        q.uses[slot] += len(pairs)
        val = q.uses[slot] * 16
        if not self.dry:
            sem = q.sems[slot]
            if prev > 0:
                q.h.wait_ge(sem, prev)
            for (o, i) in pairs:
                q.h.dma_start(out=o, in_=i).then_inc(sem, 16)
            tok = ("d", sem, val, (q.name, slot))
        else:
            tok = ("d", None, val, (q.name, slot))
        for t in reads:
            t.rd.append(tok)
        for t in writes:
            t.w = tok
            t.r = {}
            t.rd = []
        return tok

    def barrier(self):
        for e in self.engs.values():
            assert not e.pending, e.name
        waiters = list(self.engs.values()) + [self.qsp, self.qpl]
        for w in waiters:
            need = {}
            for o in self.engs.values():
                if o is w or o.count == 0:
                    continue
                self._need(w, (o.name, o.count), need)
            self._emit_waits(w, w.h, need)

    def un(self, name):
        self.uid += 1
        return f"{name}_{self.uid}"

    def ps_alloc(self):
        return self.ps_free.pop(0)

    def ps_release(self, i):
        self.ps_free.append(i)

    def wnext(self, pieces):
        if self.dry:
            self.jobs.append(pieces)
            i = len(self.jobs) - 1
            return i % NSLOT, self.wtrk[i % NSLOT]
        i = self.wjob
        self.wjob += 1
        while self.wissued < min(i + NSLOT, len(self.jobs)):
            j = self.wissued
            s = j % NSLOT
            pairs = []
            for (name, idx, sc0, n, dc0) in self.jobs[j]:
                src = getattr(self, name + "_d")[idx, :, sc0:sc0 + n]
                pairs.append((self.wslot[s][:, :, dc0:dc0 + n], src.rearrange("(kc p) c -> p kc c", p=128)))
            self.dma(self.qpl, pairs, writes=[self.wtrk[s]])
            self.wissued += 1
        return i % NSLOT, self.wtrk[i % NSLOT]

    def drip(self, n=1):
        while n > 0 and self.t0q:
            self.t0q.pop(0)()
            n -= 1

    def flush_t0(self):
        while self.t0q:
            self.t0q.pop(0)()

    def mm(self, out, lhsT, rhs, start, stop, reads, writes, inc):
        nc = self.nc
        self.mmc += 1
        if self.mmc % 3 == 0:
            self.drip()
        return self.op(self.pe,
                       lambda: nc.tensor.matmul(out, lhsT, rhs, start=start, stop=stop,
                                                skip_group_check=True),
                       reads=reads, writes=writes, inc=inc, acc=not start)

    def proj_fm(self, bank, M, slot, strk, c0, T, ncols=512, pbase=0):
        out = self.ps[bank][pbase:pbase + M, 0:ncols]
        for kc in range(8):
            self.mm(out, self.wslot[slot][:, kc, c0:c0 + M],
                    self.hT[:, kc, T * 512:T * 512 + ncols],
                    start=(kc == 0), stop=(kc == 7),
                    reads=[strk, self.hT_t[T]], writes=[self.ps_t[bank]], inc=(kc == 7))

    def build(self):
        self.dry = True
        self._build_once()
        self.est_counts = {n: e.count for n, e in self.engs.items()}
        jobs = self.jobs
        self.dry = False
        self.jobs = jobs
        self._build_once()
        return self.nc

    def _build_once(self):
        nseq = self.nseq
        nc = bass.Bass("TRN2", target_bir_lowering=False)
        self.nc = nc
        self.wjob = 0
        self.wissued = 0
        dt = lambda name, shape, kind="ExternalInput", dtype=F32: nc.dram_tensor(name, shape, dtype, kind=kind).ap()
        self.x_d = dt("x", [nseq, SEQ, D])
        self.mem_d = dt("mem", [nseq, NMEM, D])
        self.memnw_d = dt("memnw_fm", [128, 8])
        self.normw_d = dt("normw_fm", [128, 4 * 8])
        self.wmemkv_d = dt("w_memkv", [4, D, 512])
        self.wout_d = dt("w_out", [4, D, D])
        self.wina_d = dt("w_in_a", [2, D, 2832])
        self.wgu_d = dt("w_gate_up", [2, 16, 384])
        self.bg_d = dt("bg_fm", [96, 2 * 4])
        self.gnw_d = dt("gnw_fm", [128, 2 * 6])
        self.winb_d = dt("w_in_b", [2, D, 8192])
        self.fnw_d = dt("fnw_bc", [128, D])
        self.c_ident_d = dt("c_ident", [128, 128])
        self.c_ones_d = dt("c_ones", [128, 128])
        self.c_mask2_d = dt("c_mask2", [128, 256])
        self.c_gmask_d = dt("c_gmask", [128, 128])
        self.c_rmask_d = dt("c_rmask", [128, 512])
        self.c_cos_d = dt("c_cos", [128, 3 * 16 * 16])
        self.c_sin_d = dt("c_sin", [128, 3 * 16 * 16])
        self.out_d = dt("out", [nseq, SEQ, D], kind="ExternalOutput")
        if self.dbg:
            self.dbg_d = {k: dt("dbg_" + k, shp, kind="ExternalOutput") for k, shp in self.dbg.items()}

        with ExitStack() as es:
            self.setup_engines(es)
            sb = lambda name, shape, dtype: es.enter_context(nc.sbuf_tensor(name, shape, dtype))
            self.x = sb("x_sb", [128, 16, D], F32)
            self.x_t = [Trk() for _ in range(16)]
            self.hT = sb("hT", [128, 8, SEQ], BF16)
            self.hT_t = [Trk() for _ in range(4)]
            self.preT = sb("preT", [128, 8, SEQ], BF16)
            self.preT_t = [[Trk() for _ in range(4)] for _ in range(8)]
            self.memT = sb("memT", [128, 8, NMEM], BF16)
            self.memT_t = Trk()
            self.kmT = sb("kmT", [128, 2, NMEM], BF16)
            self.kmT_t = Trk()
            self.vm = sb("vm", [128, 2, 256], BF16)
            self.vm_t = Trk()
            self.wslot = [sb(f"wslot{i}", [128, 8, SLOTW], BF16) for i in range(NSLOT)]
            self.wtrk = [Trk() for _ in range(NSLOT)]
            self.ps = [es.enter_context(nc.psum_tensor(f"ps{i}", [128, 512], F32)) for i in range(8)]
            self.ps_t = [Trk(excl=True) for _ in range(8)]
            self.ps_free = list(range(7))
            self.ident = sb("ident", [128, 128], BF16)
            self.mask2 = sb("mask2", [128, 256], BF16)
            self.gmask = sb("gmask", [128, 128], BF16)
            self.ones = sb("ones", [128, 128], BF16)
            self.cos = sb("cos", [128, 3, 16, 16], F32)
            self.sin = sb("sin", [128, 3, 16, 16], F32)
            self.memnw = sb("memnw", [128, 8], F32)
            self.normw = sb("normw", [128, 4, 8], F32)
            self.bg = sb("bg", [96, 8], F32)
            self.nbg = sb("nbg", [96, 8], F32)
            self.gnw = sb("gnw", [128, 12], F32)
            self.const_t = Trk()
            self.ident32 = sb("ident32", [128, 128], F32)
            self.ones32 = sb("ones32", [128, 128], F32)
            self.w32 = [sb(f"w32_{i}", [128, 8, 128], F32) for i in range(2)]
            self.w32_t = [Trk() for _ in range(2)]
            self.w32_n = 0
            self.t0_prev_pos = None
            self.kmT32 = sb("kmT32", [128, 2, NMEM], F32)
            self.vm32 = sb("vm32", [128, 2, 256], F32)
            self.km32_t = Trk()
            self.vm32_t = Trk()
            self.x0T = sb("x0T", [128, 8], F32)
            self.h0T = sb("h0T", [128, 8], F32)
            self.m0T = sb("m0T", [128, 8], F32)
            self.sg0 = sb("sg0", [128, 8], F32)
            self.t0s = sb("t0s", [128, 32], F32)
            self.e8 = sb("e8", [128, 8], F32)
            self.x0_t = Trk()
            self.h0_t = Trk()
            self.m0_t = Trk()
            self.t0s_t = Trk()
            self.tb = 7
            self.tbc = 0
            self.t0q = []
            self.mmc = 0
            self.junk_t = Trk()
            self.ss16 = sb("ss16", [128, 16], F32)
            self.rstd16 = sb("rstd16", [128, 16], F32)
            self.st_t = Trk()
            self.ss_t = [Trk() for _ in range(16)]
            self.uid = 0

            self.load_consts()
            for si in range(nseq):
                self.do_seq(si)
            if not self.dry:
                for tok in self.out_toks:
                    key = ("d", tok[3])
                    if self.qsp.waited.get(key, 0) < tok[2]:
                        self.qsp.waited[key] = tok[2]
                        nc.sync.wait_ge(tok[1], tok[2])

    def load_consts(self):
        nc = self.nc
        toks = []
        D_ = lambda q, o, i: toks.append(self.dma(q, o, i))
        D_(self.qpl, self.ident[:], self.c_ident_d)
        D_(self.qsp, self.ident32[:], self.c_ident_d)
        D_(self.qsp, self.ones32[:], self.c_ones_d)
        D_(self.qpl, self.mask2[:], self.c_mask2_d)
        D_(self.qpl, self.gmask[:], self.c_gmask_d)
        D_(self.qsp, self.cos[:].rearrange("p a b c -> p (a b c)"), self.c_cos_d)
        D_(self.qsp, self.sin[:].rearrange("p a b c -> p (a b c)"), self.c_sin_d)
        D_(self.qsp, self.memnw[:], self.memnw_d)
        D_(self.qsp, self.normw[:].rearrange("p a b -> p (a b)"), self.normw_d)
        D_(self.qsp, self.bg[:], self.bg_d)
        D_(self.qsp, self.gnw[:], self.gnw_d)
        for e in self.engs.values():
            need = {}
            for tok in toks:
                self._need(e, tok, need)
            self._emit_waits(e, e.h, need)
        self.op(self.dve, lambda: nc.vector.memset(self.ones[:], 1.0), writes=[self.const_t])
        self.op(self.dve, lambda: nc.vector.tensor_scalar(self.nbg[:], self.bg[:], -1.0, None, ALU.mult),
                writes=[self.const_t])
        self.out_toks = []

    def norm_to_fm(self, src_aps, src_trks, w_ap, dst, dst_trk_of, ntiles):
        nc = self.nc
        with ExitStack() as es:
            self.xs = [es.enter_context(nc.sbuf_tensor(self.un("xs"), [128, D], BF16)) for _ in range(2)]
            self.xs_t = [Trk(), Trk()]
            self.junk = es.enter_context(nc.sbuf_tensor(self.un("junk"), [128, D], BF16))
            self.junk_t = Trk()
            self._norm_to_fm(src_aps, src_trks, w_ap, dst, dst_trk_of, ntiles)
            self.barrier()

    def _norm_to_fm(self, src_aps, src_trks, w_ap, dst, dst_trk_of, ntiles):
        nc = self.nc
        for i in range(ntiles):
            self.op(self.act, lambda i=i: nc.scalar.activation(
                out=self.junk[:], in_=src_aps[i], func=AF.Square, accum_out=self.ss16[:, i:i + 1]),
                reads=[src_trks[i]], writes=[self.ss_t[i], self.junk_t])
        self.op(self.act, lambda: nc.scalar.activation(
            out=self.rstd16[:, 0:ntiles], in_=self.ss16[:, 0:ntiles], func=AF.Ln, scale=1.0 / D, bias=EPS),
            reads=self.ss_t[0:ntiles], writes=[self.st_t])
        self.op(self.act, lambda: nc.scalar.activation(
            out=self.rstd16[:, 0:ntiles], in_=self.rstd16[:, 0:ntiles], func=AF.Exp, scale=-0.5),
            reads=[self.st_t], writes=[self.st_t])
        for i in range(ntiles):
            b = i % 2
            self.op(self.act, lambda i=i, b=b: nc.scalar.activation(
                out=self.xs[b][:], in_=src_aps[i], func=AF.Copy, scale=self.rstd16[:, i:i + 1]),
                reads=[src_trks[i], self.st_t], writes=[self.xs_t[b]])
            bank = self.ps_alloc()
            pb = self.ps[bank][:].bitcast(BF16)
            for kc in range(8):
                self.op(self.pe, lambda kc=kc, b=b, pb=pb: nc.tensor.transpose(
                    pb[:, kc * 128:(kc + 1) * 128], self.xs[b][:, kc * 128:(kc + 1) * 128], self.ident[:]),
                    reads=[self.xs_t[b], self.const_t], writes=[self.ps_t[bank]], inc=(kc == 7), acc=(kc > 0))
            dtrk = dst_trk_of(i)
            self.op(self.dve, lambda i=i, pb=pb: nc.vector.tensor_tensor(
                dst[:, :, i * 128:(i + 1) * 128],
                pb.rearrange("p (k t) -> p k t", k=8),
                w_ap.unsqueeze(2).to_broadcast([128, 8, 128]), ALU.mult),
                reads=[self.ps_t[bank], self.const_t], writes=[dtrk])
            self.ps_release(bank)

    def do_seq(self, si):
        nc = self.nc
        for tt in range(16):
            self.dma(self.qsp, self.x[:, tt, :], self.x_d[si, tt * 128:(tt + 1) * 128, :], writes=[self.x_t[tt]])
        with ExitStack() as es:
            memraw = es.enter_context(nc.sbuf_tensor(self.un("memraw"), [128, 2, D], F32))
            mr_t = [Trk(), Trk()]
            for i in range(2):
                self.dma(self.qsp, memraw[:, i, :], self.mem_d[si, i * 128:(i + 1) * 128, :], writes=[mr_t[i]])
            self.norm_to_fm([memraw[:, i, :] for i in range(2)], mr_t, self.memnw[:], self.memT,
                            lambda i: self.memT_t, 2)
            self.barrier()
        self.t0_init()
        for li in range(self.nlayers):
            self.do_layer(si, li)
        self.final_norm(si)

    def do_layer(self, si, li):
        nc = self.nc
        j = li // 2
        self.norm_to_fm([self.x[:, tt, :] for tt in range(16)], self.x_t, self.normw[:, li, :], self.hT,
                        lambda i: self.hT_t[i // 4], 16)
        self.mem_kv(li)
        self.t0_layer(li)
        mode = self.t0_mode(li)
        if li % 2 == 0:
            self.mixer_a(j)
            wname, qm0, g0 = "wina", 1552, 1808
        else:
            self.mixer_b(j)
            wname, qm0, g0 = "winb", 6912, 7168
        if mode == "heads" and T0S >= 3:
            self.flush_t0()
            self.op(self.dve, lambda: nc.vector.tensor_copy(self.preT[:, 0:6, 0:1], self.m0T[:, 0:6].unsqueeze(2)),
                    reads=[self.m0_t], writes=[self.preT_t[c][0] for c in range(6)])
        self.tail(li, j, wname, qm0, g0)
        if mode == "full":
            self.flush_t0()
            self.t0_inject()

    def mem_kv(self, li):
        nc = self.nc
        s, st = self.wnext([("wmemkv", li, 0, 256, 0)])
        for c2 in range(2):
            bank = self.ps_alloc()
            out = self.ps[bank][:, 0:256]
            for kc in range(8):
                self.mm(out, self.wslot[s][:, kc, c2 * 128:(c2 + 1) * 128], self.memT[:, kc, :],
                        start=(kc == 0), stop=(kc == 7), reads=[st, self.memT_t], writes=[self.ps_t[bank]],
                        inc=(kc == 7))
            self.op(self.act, lambda c2=c2, out=out: nc.scalar.copy(self.kmT[:, c2, :], out),
                    reads=[self.ps_t[bank]], writes=[self.kmT_t])
            self.op(self.act, lambda c2=c2, out=out: nc.scalar.copy(self.kmT32[:, c2, :], out),
                    reads=[self.ps_t[bank]], writes=[self.km32_t])
            self.ps_release(bank)
        s, st = self.wnext([("wmemkv", li, 256, 256, 0)])
        for kt in range(2):
            bank = self.ps_alloc()
            out = self.ps[bank][:, 0:256]
            for kc in range(8):
                self.mm(out, self.memT[:, kc, kt * 128:(kt + 1) * 128], self.wslot[s][:, kc, 0:256],
                        start=(kc == 0), stop=(kc == 7), reads=[st, self.memT_t], writes=[self.ps_t[bank]],
                        inc=(kc == 7))
            self.op(self.act, lambda kt=kt, out=out: nc.scalar.copy(self.vm[:, kt, :], out),
                    reads=[self.ps_t[bank]], writes=[self.vm_t])
            self.op(self.act, lambda kt=kt, out=out: nc.scalar.copy(self.vm32[:, kt, :], out),
                    reads=[self.ps_t[bank]], writes=[self.vm32_t])
            self.ps_release(bank)

    def tail(self, li, j, wname, qm0, g0):
        nc = self.nc
        with ExitStack() as es:
            sbt = lambda name, shape, dtype: es.enter_context(nc.sbuf_tensor(self.un(name), shape, dtype))
            qm = sbt("qm", [128, 2, 512], BF16)
            qm_t = Trk()
            pT = [sbt(f"pTm{i}", [128, 512], BF16) for i in range(3)]
            pT_t = [Trk() for _ in range(3)]
            rden = sbt("rdenm", [128, 512], F32)
            rden_t = Trk()
            sg = [sbt(f"sg{i}", [128, 512], BF16) for i in range(2)]
            sg_t = [Trk() for _ in range(2)]
            pk = 0
            for T in range(4):
                cols = slice(T * 512, (T + 1) * 512)
                s, st = self.wnext([(wname, j, qm0, 256, 0)])
                for c2 in range(2):
                    bank = self.ps_alloc()
                    self.proj_fm(bank, 128, s, st, c2 * 128, T)
                    self.op(self.act, lambda c2=c2, bank=bank: nc.scalar.copy(qm[:, c2, :], self.ps[bank][:, :]),
                            reads=[self.ps_t[bank]], writes=[qm_t])
                    self.ps_release(bank)
                gate_jobs = [(0, 384), (384, 384), (768, 256)]
                gstate = {"ji": 0, "cc": 0, "s": None, "st": None}

                def emit_gate_chunk():
                    if gstate["ji"] >= len(gate_jobs):
                        return False
                    c0, n = gate_jobs[gstate["ji"]]
                    if gstate["cc"] == 0:
                        gstate["s"], gstate["st"] = self.wnext([(wname, j, g0 + c0, n, 0)])
                    s_, st_ = gstate["s"], gstate["st"]
                    cc = gstate["cc"]
                    c = c0 // 128 + cc
                    bank = self.ps_alloc()
                    self.proj_fm(bank, 128, s_, st_, cc * 128, T)
                    b_ = c % 2
                    self.op(self.act, lambda b_=b_, bank=bank: nc.scalar.activation(
                        out=sg[b_][:], in_=self.ps[bank][:, :], func=AF.Silu),
                        reads=[self.ps_t[bank]], writes=[sg_t[b_]])
                    self.ps_release(bank)
                    self.gate_pending.append((c, b_))
                    gstate["cc"] += 1
                    if gstate["cc"] >= n // 128:
                        gstate["cc"] = 0
                        gstate["ji"] += 1
                    return True

                def emit_gate_mults(upto=None):
                    keep = []
                    for (c, b_) in self.gate_pending:
                        if c < 6 or self.mem_done:
                            self.op(self.dve, lambda b_=b_, c=c: nc.vector.tensor_tensor(
                                self.preT[:, c, cols], self.preT[:, c, cols], sg[b_][:], ALU.mult),
                                reads=[sg_t[b_], self.preT_t[c][T]], writes=[self.preT_t[c][T]])
                        else:
                            keep.append((c, b_))
                    self.gate_pending = keep

                self.gate_pending = []
                self.mem_done = False
                pend = []
                banks = {}
                step = 0
                for c2 in range(2):
                    for half in range(2):
                        hm = 2 * c2 + half
                        pr = slice(64 * half, 64 * half + 64)
                        for kt in range(2):
                            if c2 not in banks:
                                banks[c2] = (self.ps_alloc(), self.ps_alloc())
                            bs = self.ps_alloc()
                            self.mm(self.ps[bs][:, :], self.kmT[pr, c2, kt * 128:(kt + 1) * 128], qm[pr, c2, :],
                                    start=True, stop=True, reads=[self.kmT_t, qm_t], writes=[self.ps_t[bs]], inc=True)
                            p = pk % 3
                            pk += 1
                            self.op(self.act, lambda p=p, bs=bs: nc.scalar.activation(
                                out=pT[p][:], in_=self.ps[bs][:, :], func=AF.Exp, scale=0.125),
                                reads=[self.ps_t[bs]], writes=[pT_t[p]])
                            self.ps_release(bs)

                            def emit_pv(c2=c2, half=half, hm=hm, pr=pr, kt=kt, p=p):
                                bn, bd = banks[c2]
                                self.mm(self.ps[bn][pr, :], self.vm[:, kt, hm * 64:(hm + 1) * 64], pT[p][:],
                                        start=(kt == 0), stop=(kt == 1), reads=[self.vm_t, pT_t[p]],
                                        writes=[self.ps_t[bn]], inc=False)
                                self.mm(self.ps[bd][pr, :], self.ones[:, 0:64], pT[p][:],
                                        start=(kt == 0), stop=(kt == 1), reads=[pT_t[p]],
                                        writes=[self.ps_t[bd]], inc=True)
                                if half == 1 and kt == 1:
                                    self.op(self.dve, lambda bd=bd: nc.vector.reciprocal(rden[:], self.ps[bd][:, :]),
                                            reads=[self.ps_t[bd]], writes=[rden_t])
                                    self.op(self.dve, lambda bn=bn, c2=c2: nc.vector.tensor_tensor(
                                        self.preT[:, 6 + c2, cols], self.ps[bn][:, :], rden[:], ALU.mult),
                                        reads=[self.ps_t[bn], rden_t], writes=[self.preT_t[6 + c2][T]])
                                    self.ps_release(bn)
                                    self.ps_release(bd)
                            pend.append(emit_pv)
                            if len(pend) > 1:
                                pend.pop(0)()
                            if step < 6:
                                emit_gate_chunk()
                                emit_gate_mults()
                            step += 1
                while pend:
                    pend.pop(0)()
                self.mem_done = True
                while emit_gate_chunk():
                    emit_gate_mults()
                emit_gate_mults()
                for (c0, n) in [(0, 384), (384, 384), (768, 256)]:
                    s, st = self.wnext([("wout", li, c0, n, 0)])
                    for sub in range(4):
                        tt = 4 * T + sub
                        bank = self.ps_alloc()
                        out = self.ps[bank][:, 0:n]
                        for kc in range(8):
                            self.mm(out, self.preT[:, kc, tt * 128:(tt + 1) * 128], self.wslot[s][:, kc, 0:n],
                                    start=(kc == 0), stop=(kc == 7), reads=[st, self.preT_t[kc][T]],
                                    writes=[self.ps_t[bank]], inc=(kc == 7))
                        self.op(self.dve, lambda tt=tt, out=out, c0=c0, n=n: nc.vector.tensor_tensor(
                            self.x[:, tt, c0:c0 + n], out, self.x[:, tt, c0:c0 + n], ALU.add),
                            reads=[self.ps_t[bank], self.x_t[tt]], writes=[self.x_t[tt]])
                        self.ps_release(bank)
            self.barrier()

    def final_norm(self, si):
        nc = self.nc
        with ExitStack() as es:
            fnw = es.enter_context(nc.sbuf_tensor(self.un("fnw"), [128, D], F32))
            self.junk = es.enter_context(nc.sbuf_tensor(self.un("junk"), [128, D], BF16))
            self.junk_t = Trk()
            fnw_t = Trk()
            self.dma(self.qsp, fnw[:], self.fnw_d, writes=[fnw_t])
            self._final_norm(si, fnw, fnw_t)
            self.barrier()

    def _final_norm(self, si, fnw, fnw_t):
        nc = self.nc
        for i in range(16):
            self.op(self.act, lambda i=i: nc.scalar.activation(
                out=self.junk[:], in_=self.x[:, i, :], func=AF.Square, accum_out=self.ss16[:, i:i + 1]),
                reads=[self.x_t[i]], writes=[self.ss_t[i], self.junk_t])
        self.op(self.act, lambda: nc.scalar.activation(
            out=self.rstd16[:], in_=self.ss16[:], func=AF.Ln, scale=1.0 / D, bias=EPS),
            reads=self.ss_t, writes=[self.st_t])
        self.op(self.act, lambda: nc.scalar.activation(
            out=self.rstd16[:], in_=self.rstd16[:], func=AF.Exp, scale=-0.5),
            reads=[self.st_t], writes=[self.st_t])
        for i in range(16):
            self.op(self.dve, lambda i=i: nc.vector.scalar_tensor_tensor(
                self.x[:, i, :], self.x[:, i, :], self.rstd16[:, i:i + 1], fnw[:], ALU.mult, ALU.mult),
                reads=[self.x_t[i], self.st_t, fnw_t], writes=[self.x_t[i]])
            tok = self.dma(self.qsp, self.out_d[si, i * 128:(i + 1) * 128, :], self.x[:, i, :], reads=[self.x_t[i]])
            self.out_toks.append(tok)


    def tcol(self, n=1):
        if self.tbc + n > 512:
            self.tbc = 0
        c = self.tbc
        self.tbc += n
        return c

    def T(self, eng, fn, reads, writes, **kw):
        self.t0q.append(lambda: self.op(eng, fn, reads=reads, writes=writes, **kw))

    def t0_init(self):
        if T0S < 1:
            return
        nc = self.nc
        tb = self.ps[self.tb]
        tbt = self.ps_t[self.tb]
        c = self.tcol(8)
        for kc in range(8):
            self.op(self.pe, lambda kc=kc: nc.tensor.matmul(
                tb[:, c + kc:c + kc + 1], self.x[0:1, 0, kc * 128:(kc + 1) * 128], self.ones32[0:1, 0:1],
                start=True, stop=True, skip_group_check=True),
                reads=[self.x_t[0], self.const_t], writes=[tbt], inc=(kc == 7), acc=(kc > 0))
        self.op(self.dve, lambda: nc.vector.tensor_copy(self.x0T[:], tb[:, c:c + 8]),
                reads=[tbt], writes=[self.x0_t])

    def t0_inject(self):
        if T0S < 1:
            return
        nc = self.nc
        tb = self.ps[self.tb]
        tbt = self.ps_t[self.tb]
        for hf in range(2):
            for k4 in range(4):
                kc = 4 * hf + k4
                self.op(self.pe, lambda kc=kc, k4=k4: nc.tensor.matmul(
                    tb[0:1, k4 * 128:(k4 + 1) * 128], self.x0T[:, kc:kc + 1], self.ident32[:],
                    start=True, stop=True, skip_group_check=True),
                    reads=[self.x0_t, self.const_t], writes=[tbt], inc=(k4 == 3), acc=(k4 > 0))
            self.op(self.dve, lambda hf=hf: nc.vector.tensor_copy(self.x[0:1, 0, hf * 512:(hf + 1) * 512], tb[0:1, :]),
                    reads=[tbt], writes=[self.x_t[0]])
        self.tbc = 0

    def t0_cproj(self, vec, vec_t, wname, idx, c0, M, col, pbase=0):
        nc = self.nc
        tb = self.ps[self.tb]
        tbt = self.ps_t[self.tb]
        k = self.w32_n % 2
        self.w32_n += 1
        w32, w32_t = self.w32[k], self.w32_t[k]

        def load():
            srcap = getattr(self, wname + "_d")[idx, :, c0:c0 + M].rearrange("(kc p) c -> p kc c", p=128)
            self.dma(self.qsp, w32[:, :, 0:M], srcap, writes=[w32_t])
        if self.t0_prev_pos is not None:
            self.t0q.insert(self.t0_prev_pos, load)
        else:
            self.t0q.append(load)
        self.t0_prev_pos = len(self.t0q)
        for kc in range(8):
            self.T(self.pe, lambda kc=kc: nc.tensor.matmul(
                tb[pbase:pbase + M, col:col + 1], w32[:, kc, 0:M], vec[:, kc:kc + 1],
                start=(kc == 0), stop=(kc == 7), skip_group_check=True),
                [w32_t, vec_t], [tbt], inc=(kc == 7), acc=(kc > 0))

    def t0_mode(self, li):
        last_gla = max([l for l in range(self.nlayers) if l % 2 == 0], default=-1)
        if li < last_gla:
            return "full"
        if li == last_gla:
            return "heads"
        return "none"

    def t0_layer(self, li):
        nc = self.nc
        j = li // 2
        self.t0_prev_pos = None
        assert not self.t0q
        mode = self.t0_mode(li)
        if mode == "none":
            return
        tb = self.ps[self.tb]
        tbt = self.ps_t[self.tb]
        s = self.t0s
        st = self.t0s_t
        T = self.T
        X = mybir.AxisListType.X
        if T0S < 2:
            return
        c = self.tcol()
        T(self.dve, lambda: nc.vector.tensor_tensor(s[:, 8:16], self.x0T[:], self.x0T[:], ALU.mult), [self.x0_t], [st])
        T(self.dve, lambda: nc.vector.tensor_reduce(out=s[:, 0:1], in_=s[:, 8:16], axis=X, op=ALU.add), [st], [st])
        T(self.pe, lambda: nc.tensor.matmul(tb[:, c:c + 1], self.ones32[:], s[:, 0:1], start=True, stop=True,
                                            skip_group_check=True), [st, self.const_t], [tbt])
        T(self.act, lambda: nc.scalar.activation(out=s[:, 1:2], in_=tb[:, c:c + 1], func=AF.Ln, scale=1.0 / D, bias=EPS),
          [tbt], [st])
        T(self.act, lambda: nc.scalar.activation(out=s[:, 1:2], in_=s[:, 1:2], func=AF.Exp, scale=-0.5), [st], [st])
        T(self.dve, lambda: nc.vector.scalar_tensor_tensor(
            self.h0T[:], self.x0T[:], s[:, 1:2], self.normw[:, li, :], ALU.mult, ALU.mult),
          [self.x0_t, st, self.const_t], [self.h0_t])
        hv, ht = self.h0T, self.h0_t
        if T0S < 3:
            return
        if li % 2 == 0:
            wname, qm0, g0 = "wina", 1552, 1808
            for h in range(4):
                c = self.tcol(8)
                self.t0_cproj(hv, ht, wname, j, 96 * h, 96, c)
                self.t0_cproj(hv, ht, wname, j, 384 + 96 * h, 96, c + 1)
                T(self.act, lambda c=c: nc.scalar.copy(s[0:96, 2:3], tb[0:96, c:c + 1]), [tbt], [st])
                T(self.dve, lambda c=c: nc.vector.tensor_tensor(s[0:96, 3:4], s[0:96, 2:3], tb[0:96, c + 1:c + 2], ALU.mult),
                  [tbt, st], [st])
                T(self.pe, lambda c=c: nc.tensor.matmul(tb[:, c + 2:c + 3], self.ones32[0:96, :], s[0:96, 3:4],
                                                        start=True, stop=True, skip_group_check=True),
                  [st, self.const_t], [tbt])
                f0 = 192 * h
                if h % 2 == 0:
                    cfull, chalf, phalf, fs0, hs0 = f0 // 128, f0 // 128 + 1, 0, 0, 128
                else:
                    chalf, cfull, phalf, hs0, fs0 = f0 // 128, f0 // 128 + 1, 64, 0, 64
                pr = slice(phalf, phalf + 64)
                self.t0_cproj(hv, ht, wname, j, 768 + f0 + fs0, 128, c + 3)
                self.t0_cproj(hv, ht, wname, j, 768 + f0 + hs0, 64, c + 4, pbase=phalf)
                T(self.dve, lambda: nc.vector.memset(s[:, 4:6], 0.0), [st], [st])
                T(self.act, lambda c=c: nc.scalar.copy(s[:, 4:5], tb[:, c + 3:c + 4]), [tbt], [st])
                T(self.act, lambda c=c, pr=pr: nc.scalar.copy(s[pr, 5:6], tb[pr, c + 4:c + 5]), [tbt], [st])
                T(self.dve, lambda: nc.vector.tensor_tensor(s[:, 16:18], s[:, 4:6], s[:, 4:6], ALU.mult), [st], [st])
                T(self.dve, lambda: nc.vector.tensor_reduce(out=s[:, 6:7], in_=s[:, 16:18], axis=X, op=ALU.add), [st], [st])
                T(self.pe, lambda c=c: nc.tensor.matmul(tb[:, c + 5:c + 6], self.ones32[:], s[:, 6:7],
                                                        start=True, stop=True, skip_group_check=True),
                  [st, self.const_t], [tbt])
                T(self.act, lambda c=c: nc.scalar.activation(out=s[:, 7:8], in_=tb[:, c + 2:c + 3], func=AF.Copy,
                                                             scale=96.0 ** -0.5), [tbt], [st])
                T(self.dve, lambda: nc.vector.tensor_tensor(s[:, 18:19], s[:, 7:8], s[:, 7:8], ALU.mult), [st], [st])
                T(self.dve, lambda c=c: nc.vector.tensor_tensor(s[:, 18:19], s[:, 18:19], tb[:, c + 5:c + 6], ALU.mult),
                  [st, tbt], [st])
                T(self.act, lambda: nc.scalar.activation(out=s[:, 18:19], in_=s[:, 18:19], func=AF.Ln,
                                                         scale=1.0 / 192, bias=EPS), [st], [st])
                T(self.act, lambda: nc.scalar.activation(out=s[:, 18:19], in_=s[:, 18:19], func=AF.Exp, scale=-0.5),
                  [st], [st])
                T(self.dve, lambda: nc.vector.tensor_tensor(s[:, 19:20], s[:, 7:8], s[:, 18:19], ALU.mult), [st], [st])
                T(self.dve, lambda cfull=cfull: nc.vector.scalar_tensor_tensor(
                    self.m0T[:, cfull:cfull + 1], s[:, 4:5], s[:, 19:20], self.gnw[:, 6 * j + cfull:6 * j + cfull + 1],
                    ALU.mult, ALU.mult), [st, self.const_t], [self.m0_t])
                T(self.dve, lambda chalf=chalf, pr=pr: nc.vector.scalar_tensor_tensor(
                    self.m0T[pr, chalf:chalf + 1], s[pr, 5:6], s[pr, 19:20], self.gnw[pr, 6 * j + chalf:6 * j + chalf + 1],
                    ALU.mult, ALU.mult), [st, self.const_t], [self.m0_t])
        else:
            wname, qm0, g0 = "winb", 6912, 7168
            for h in range(6):
                for g in range(3):
                    base = g * 2304
                    c = self.tcol(4)
                    self.t0_cproj(hv, ht, wname, j, base + 128 * h, 128, c)
                    self.t0_cproj(hv, ht, wname, j, base + 768 + 128 * h, 128, c + 1)
                    self.t0_cproj(hv, ht, wname, j, base + 1536 + 128 * h, 128, c + 2)
                    T(self.act, lambda c=c: nc.scalar.copy(s[:, 2:3], tb[:, c:c + 1]), [tbt], [st])
                    T(self.dve, lambda c=c: nc.vector.tensor_tensor(s[:, 3:4], s[:, 2:3], tb[:, c + 1:c + 2], ALU.mult),
                      [tbt, st], [st])
                    T(self.pe, lambda c=c: nc.tensor.matmul(tb[:, c + 3:c + 4], self.ones32[:], s[:, 3:4],
                                                            start=True, stop=True, skip_group_check=True),
                      [st, self.const_t], [tbt])
                    T(self.act, lambda c=c, g=g: nc.scalar.activation(out=s[:, 8 + g:9 + g], in_=tb[:, c + 3:c + 4],
                                                                      func=AF.Exp, scale=128.0 ** -0.5), [tbt], [st])
                    T(self.act, lambda c=c, g=g: nc.scalar.copy(s[:, 12 + g:13 + g], tb[:, c + 2:c + 3]), [tbt], [st])
                T(self.dve, lambda: nc.vector.tensor_reduce(out=s[:, 16:17], in_=s[:, 8:11], axis=X, op=ALU.add), [st], [st])
                T(self.dve, lambda: nc.vector.reciprocal(s[:, 16:17], s[:, 16:17]), [st], [st])
                T(self.dve, lambda: nc.vector.tensor_tensor(s[:, 20:23], s[:, 8:11], s[:, 12:15], ALU.mult), [st], [st])
                T(self.dve, lambda: nc.vector.tensor_reduce(out=s[:, 17:18], in_=s[:, 20:23], axis=X, op=ALU.add), [st], [st])
                T(self.dve, lambda h=h: nc.vector.tensor_tensor(self.m0T[:, h:h + 1], s[:, 17:18], s[:, 16:17], ALU.mult),
                  [st], [self.m0_t])
        if T0S < 4 or mode == "heads":
            return
        c = self.tcol(2)
        for c2 in range(2):
            self.t0_cproj(hv, ht, wname, j, qm0 + 128 * c2, 128, c + c2)
        T(self.act, lambda c=c: nc.scalar.copy(s[:, 24:26], tb[:, c:c + 2]), [tbt], [st])
        c = self.tcol(8)
        for hm in range(4):
            c2, half = hm // 2, hm % 2
            pr = slice(64 * half, 64 * half + 64)
            for kt in range(2):
                T(self.pe, lambda c=c, hm=hm, kt=kt, c2=c2, pr=pr: nc.tensor.matmul(
                    tb[:, c + 2 * hm + kt:c + 2 * hm + kt + 1], self.kmT32[pr, c2, kt * 128:(kt + 1) * 128],
                    s[pr, 24 + c2:25 + c2], start=True, stop=True, skip_group_check=True),
                  [self.km32_t, st], [tbt], inc=(hm == 3 and kt == 1), acc=not (hm == 0 and kt == 0))
        T(self.act, lambda c=c: nc.scalar.activation(out=self.e8[:], in_=tb[:, c:c + 8], func=AF.Exp, scale=0.125),
          [tbt], [st])
        c = self.tcol(4)
        first = True
        for hm in range(4):
            c2, half = hm // 2, hm % 2
            pr = slice(64 * half, 64 * half + 64)
            for kt in range(2):
                T(self.pe, lambda c=c, hm=hm, kt=kt, c2=c2, pr=pr: nc.tensor.matmul(
                    tb[pr, c + c2:c + c2 + 1], self.vm32[:, kt, 64 * hm:64 * hm + 64], self.e8[:, 2 * hm + kt:2 * hm + kt + 1],
                    start=(kt == 0), stop=(kt == 1), skip_group_check=True),
                  [self.vm32_t, st], [tbt], inc=False, acc=not first)
                first = False
            for kt in range(2):
                T(self.pe, lambda c=c, hm=hm, kt=kt, c2=c2, pr=pr: nc.tensor.matmul(
                    tb[pr, c + 2 + c2:c + 3 + c2], self.ones32[:, 0:64], self.e8[:, 2 * hm + kt:2 * hm + kt + 1],
                    start=(kt == 0), stop=(kt == 1), skip_group_check=True),
                  [self.const_t, st], [tbt], inc=(hm == 3 and kt == 1), acc=True)
        T(self.dve, lambda c=c: nc.vector.reciprocal(s[:, 26:28], tb[:, c + 2:c + 4]), [tbt], [st])
        T(self.dve, lambda c=c: nc.vector.tensor_tensor(self.m0T[:, 6:8], tb[:, c:c + 2], s[:, 26:28], ALU.mult),
          [tbt, st], [self.m0_t])
        if T0S < 5:
            return
        c = self.tcol(8)
        for cc in range(8):
            self.t0_cproj(hv, ht, wname, j, g0 + 128 * cc, 128, c + cc)
        T(self.act, lambda c=c: nc.scalar.activation(out=self.sg0[:], in_=tb[:, c:c + 8], func=AF.Silu), [tbt], [st])
        T(self.dve, lambda: nc.vector.tensor_tensor(self.m0T[:], self.m0T[:], self.sg0[:], ALU.mult),
          [self.m0_t, st], [self.m0_t])
        c = self.tcol(8)
        for cc in range(8):
            self.t0_cproj(self.m0T, self.m0_t, "wout", li, 128 * cc, 128, c + cc)
        T(self.dve, lambda c=c: nc.vector.tensor_tensor(self.x0T[:], self.x0T[:], tb[:, c:c + 8], ALU.add),
          [tbt, self.x0_t], [self.x0_t])

    def mixer_a(self, j):
        nc = self.nc
        win = self.wina_d[j]
        with ExitStack() as es:
            sbt = lambda name, shape, dtype: es.enter_context(nc.sbuf_tensor(self.un(name), shape, dtype))
            wgl = sbt("wgl", [128, 8, 16], BF16)
            wgl_t = Trk()
            glT = sbt("glT", [16, 512], F32)
            glT_t = Trk()
            sp = sbt("sp", [96, 512], F32)
            cc_ = sbt("cc", [96, 512], F32)
            eb = sbt("eb", [96, 512], F32)
            enb = sbt("enb", [96, 512], F32)
            g_t = Trk()
            qinT = sbt("qinT", [96, 512], BF16)
            kinT = sbt("kinT", [96, 512], BF16)
            qk_t = Trk()
            kin_tok = sbt("kin_tok", [128, 4, 96], BF16)
            kin_tok_t = Trk()
            V = sbt("Vh", [128, 4, 192], BF16)
            V_t = Trk()
            R = [[sbt(f"R{h}_{i}", [96, 192], F32) for i in range(2)] for h in range(4)]
            R_t = [[Trk() for i in range(2)] for h in range(4)]
            Sbf = [sbt(f"Sbf{i}", [96, 192], BF16) for i in range(4)]
            Sbf_t = [Trk() for _ in range(4)]
            ATm = [sbt(f"ATm{i}", [128, 128], BF16) for i in range(2)]
            ATm_t = [Trk() for _ in range(2)]
            ssh = sbt("ssh", [128, 4], F32)
            ssh_t = [Trk() for _ in range(4)]
            mixt = [sbt(f"mixt{i}", [128, 192], BF16) for i in range(2)]
            mixt_t = [Trk() for _ in range(2)]
            dec = sbt("dec", [96, 4], F32)
            dec_t = [Trk() for _ in range(4)]
            self.junk = sbt("junkA", [128, 192], BF16)
            self.junk_t = Trk()
            wgu = sbt("wgu", [16, 384], F32)
            rmask = sbt("rmask", [128, 512], BF16)
            cl_t = Trk()
            self.dma(self.qsp, wgu[:], self.wgu_d[j], writes=[cl_t])
            self.dma(self.qpl, rmask[:], self.c_rmask_d, writes=[cl_t])
            self.dma(self.qpl, wgl[:], win[:, 1536:1552].rearrange("(kc p) c -> p kc c", p=128), writes=[wgl_t])
            sk = 0
            ak = 0
            mk = 0
            for T in range(4):
                cols = slice(T * 512, (T + 1) * 512)
                bank = self.ps_alloc()
                for kc in range(8):
                    self.mm(self.ps[bank][0:16, :], wgl[:, kc, :], self.hT[:, kc, cols], start=(kc == 0),
                            stop=(kc == 7), reads=[wgl_t, self.hT_t[T]], writes=[self.ps_t[bank]], inc=(kc == 7))
                self.op(self.act, lambda bank=bank: nc.scalar.copy(glT[:], self.ps[bank][0:16, :]),
                        reads=[self.ps_t[bank]], writes=[glT_t])
                self.ps_release(bank)
                for h in range(4):
                    s, st = self.wnext([("wina", j, 96 * h, 96, 0), ("wina", j, 384 + 96 * h, 96, 96),
                                        ("wina", j, 768 + 192 * h, 192, 192)])
                    bank = self.ps_alloc()
                    self.mm(self.ps[bank][0:96, :], wgu[:, 96 * h:96 * h + 96], glT[:], start=True, stop=True,
                            reads=[cl_t, glT_t], writes=[self.ps_t[bank]], inc=True)
                    self.op(self.act, lambda bank=bank, h=h: nc.scalar.activation(
                        out=sp[:], in_=self.ps[bank][0:96, :], func=AF.Exp, scale=-1.0,
                        bias=self.nbg[:, 4 * j + h:4 * j + h + 1]),
                        reads=[self.ps_t[bank], self.const_t], writes=[g_t])
                    self.ps_release(bank)
                    self.op(self.act, lambda: nc.scalar.activation(out=sp[:], in_=sp[:], func=AF.Ln, bias=1.0),
                            reads=[g_t], writes=[g_t])
                    self.op(self.dve, lambda: nc.vector.tensor_tensor_scan(
                        cc_[:], rmask[0:96, :], sp[:], 0.0, ALU.mult, ALU.add),
                        reads=[g_t, cl_t], writes=[g_t])
                    self.op(self.act, lambda: nc.scalar.activation(out=eb[:], in_=cc_[:], func=AF.Exp, scale=-1.0 / 16),
                            reads=[g_t], writes=[g_t])
                    self.op(self.act, lambda: nc.scalar.activation(out=enb[:], in_=cc_[:], func=AF.Exp, scale=1.0 / 16),
                            reads=[g_t], writes=[g_t])
                    bank = self.ps_alloc()
                    self.proj_fm(bank, 96, s, st, 0, T)
                    self.op(self.dve, lambda bank=bank: nc.vector.scalar_tensor_tensor(
                        qinT[:], self.ps[bank][0:96, :], 96.0 ** -0.5, eb[:], ALU.mult, ALU.mult),
                        reads=[self.ps_t[bank], g_t], writes=[qk_t])
                    self.ps_release(bank)
                    bank = self.ps_alloc()
                    self.proj_fm(bank, 96, s, st, 96, T)
                    self.op(self.dve, lambda bank=bank: nc.vector.tensor_tensor(
                        kinT[:], self.ps[bank][0:96, :], enb[:], ALU.mult),
                        reads=[self.ps_t[bank], g_t], writes=[qk_t])
                    self.ps_release(bank)
                    for sub in range(4):
                        tt = 4 * T + sub
                        bank = self.ps_alloc()
                        out = self.ps[bank][:, 0:192]
                        for kc in range(8):
                            self.mm(out, self.hT[:, kc, tt * 128:(tt + 1) * 128], self.wslot[s][:, kc, 192:384],
                                    start=(kc == 0), stop=(kc == 7), reads=[st, self.hT_t[T]],
                                    writes=[self.ps_t[bank]], inc=(kc == 7))
                        self.op(self.act, lambda sub=sub, out=out: nc.scalar.copy(V[:, sub, :], out),
                                reads=[self.ps_t[bank]], writes=[V_t])
                        self.ps_release(bank)
                    bank = self.ps_alloc()
                    pb = self.ps[bank][:].bitcast(BF16)
                    for sub in range(4):
                        self.op(self.pe, lambda sub=sub, pb=pb: nc.tensor.transpose(
                            pb[:, sub * 96:(sub + 1) * 96], kinT[:, sub * 128:(sub + 1) * 128], self.ident[0:96, 0:96]),
                            reads=[qk_t, self.const_t], writes=[self.ps_t[bank]], inc=(sub == 3), acc=(sub > 0))
                    self.op(self.act, lambda pb=pb: nc.scalar.copy(
                        kin_tok[:].rearrange("p a b -> p (a b)"), pb[:, 0:384]),
                        reads=[self.ps_t[bank]], writes=[kin_tok_t])
                    self.ps_release(bank)
                    def front(sub):
                        nonlocal ak, sk
                        tt = 4 * T + sub
                        tc_ = slice(sub * 128, (sub + 1) * 128)
                        bA = self.ps_alloc()
                        self.mm(self.ps[bA][:, 0:128], kinT[:, tc_], qinT[:, tc_], start=True, stop=True,
                                reads=[qk_t], writes=[self.ps_t[bA]], inc=True)
                        a = ak % 2
                        ak += 1
                        self.op(self.dve, lambda a=a, bA=bA: nc.vector.tensor_tensor(
                            ATm[a][:], self.ps[bA][:, 0:128], self.gmask[:], ALU.mult),
                            reads=[self.ps_t[bA], self.const_t], writes=[ATm_t[a]])
                        self.ps_release(bA)
                        sbs = [None, None]
                        for ch in range(2):
                            c = 2 * tt + ch
                            pr = slice(64 * ch, 64 * ch + 64)
                            lc = 64 * (2 * sub + ch)
                            dk = None
                            if c > 0:
                                dk = eb[:, lc - 1:lc] if lc > 0 else dec[:, h:h + 1]
                                dk_t = g_t if lc > 0 else dec_t[h]
                                sb_ = sk % 4
                                sk += 1
                                sbs[ch] = sb_
                                rp = (c - 1) % 2
                                self.op(self.act, lambda sb_=sb_, dk=dk, rp=rp: nc.scalar.activation(
                                    out=Sbf[sb_][:], in_=R[h][rp][:], func=AF.Copy, scale=dk),
                                    reads=[R_t[h][rp], dk_t], writes=[Sbf_t[sb_]])
                            bU = self.ps_alloc()
                            self.mm(self.ps[bU][0:96, 0:192], kin_tok[pr, sub, :], V[pr, sub, :], start=True, stop=True,
                                    reads=[kin_tok_t, V_t], writes=[self.ps_t[bU]], inc=True)
                            rn = c % 2
                            if c == 0:
                                self.op(self.dve, lambda bU=bU, rn=rn: nc.vector.tensor_copy(
                                    R[h][rn][:], self.ps[bU][0:96, 0:192]),
                                    reads=[self.ps_t[bU]], writes=[R_t[h][rn]])
                            else:
                                self.op(self.dve, lambda bU=bU, dk=dk, rn=rn: nc.vector.scalar_tensor_tensor(
                                    R[h][rn][:], R[h][1 - rn][:], dk, self.ps[bU][0:96, 0:192], ALU.mult, ALU.add),
                                    reads=[self.ps_t[bU], R_t[h][1 - rn], dk_t], writes=[R_t[h][rn]])
                            self.ps_release(bU)
                        return a, sbs

                    def back(sub, a, sbs):
                        nonlocal mk
                        tt = 4 * T + sub
                        bO = self.ps_alloc()
                        self.mm(self.ps[bO][:, 0:192], ATm[a][:], V[:, sub, :], start=True, stop=False,
                                reads=[ATm_t[a], V_t], writes=[self.ps_t[bO]], inc=False)
                        for ch in range(2):
                            pr = slice(64 * ch, 64 * ch + 64)
                            lc = 64 * (2 * sub + ch)
                            sb_ = sbs[ch]
                            if sb_ is not None:
                                self.mm(self.ps[bO][pr, 0:192], qinT[:, lc:lc + 64], Sbf[sb_][:], start=False,
                                        stop=(ch == 1), reads=[qk_t, Sbf_t[sb_]], writes=[self.ps_t[bO]],
                                        inc=(ch == 1))
                        m = mk % 2
                        mk += 1
                        self.op(self.act, lambda bO=bO: nc.scalar.activation(
                            out=self.junk[:, 0:192], in_=self.ps[bO][:, 0:192], func=AF.Square,
                            accum_out=ssh[:, h:h + 1]),
                            reads=[self.ps_t[bO]], writes=[ssh_t[h], self.junk_t])
                        self.op(self.act, lambda: nc.scalar.activation(
                            out=ssh[:, h:h + 1], in_=ssh[:, h:h + 1], func=AF.Ln, scale=1.0 / 192, bias=EPS),
                            reads=[ssh_t[h]], writes=[ssh_t[h]])
                        self.op(self.act, lambda: nc.scalar.activation(
                            out=ssh[:, h:h + 1], in_=ssh[:, h:h + 1], func=AF.Exp, scale=-0.5),
                            reads=[ssh_t[h]], writes=[ssh_t[h]])
                        self.op(self.act, lambda bO=bO, m=m: nc.scalar.activation(
                            out=mixt[m][:], in_=self.ps[bO][:, 0:192], func=AF.Copy, scale=ssh[:, h:h + 1]),
                            reads=[self.ps_t[bO], ssh_t[h]], writes=[mixt_t[m]])
                        self.ps_release(bO)
                        f0 = 192 * h
                        if h % 2 == 0:
                            cfull, chalf, phalf = f0 // 128, f0 // 128 + 1, 0
                            full_src, half_src = slice(0, 128), slice(128, 192)
                        else:
                            chalf, cfull, phalf = f0 // 128, f0 // 128 + 1, 64
                            half_src, full_src = slice(0, 64), slice(64, 192)
                        bT = self.ps_alloc()
                        pb = self.ps[bT][:].bitcast(BF16)
                        self.op(self.pe, lambda m=m, pb=pb, full_src=full_src: nc.tensor.transpose(
                            pb[:, 0:128], mixt[m][:, full_src], self.ident[:]),
                            reads=[mixt_t[m], self.const_t], writes=[self.ps_t[bT]], inc=False)
                        self.op(self.pe, lambda m=m, pb=pb, half_src=half_src, phalf=phalf: nc.tensor.transpose(
                            pb[phalf:phalf + 64, 128:256], mixt[m][:, half_src], self.ident[:]),
                            reads=[mixt_t[m], self.const_t], writes=[self.ps_t[bT]], inc=True, acc=True)
                        tcol = slice(tt * 128, (tt + 1) * 128)
                        self.op(self.dve, lambda pb=pb, cfull=cfull, tcol=tcol: nc.vector.tensor_scalar(
                            self.preT[:, cfull, tcol], pb[:, 0:128], self.gnw[:, 6 * j + cfull:6 * j + cfull + 1],
                            None, ALU.mult),
                            reads=[self.ps_t[bT], self.const_t], writes=[self.preT_t[cfull][T]])
                        self.op(self.dve, lambda pb=pb, chalf=chalf, phalf=phalf, tcol=tcol: nc.vector.tensor_scalar(
                            self.preT[phalf:phalf + 64, chalf, tcol], pb[phalf:phalf + 64, 128:256],
                            self.gnw[phalf:phalf + 64, 6 * j + chalf:6 * j + chalf + 1], None, ALU.mult),
                            reads=[self.ps_t[bT], self.const_t], writes=[self.preT_t[chalf][T]])
                        self.ps_release(bT)

                    nxt = front(0)
                    for sub in range(4):
                        cur = nxt
                        if sub < 3:
                            nxt = front(sub + 1)
                        back(sub, *cur)
                    self.op(self.dve, lambda h=h: nc.vector.tensor_copy(dec[:, h:h + 1], eb[:, 511:512]),
                            reads=[g_t], writes=[dec_t[h]])
            self.barrier()

    def mixer_b(self, j):
        nc = self.nc
        win = self.winb_d[j]
        with ExitStack() as es:
            sbt = lambda name, shape, dtype: es.enter_context(nc.sbuf_tensor(self.un(name), shape, dtype))
            qkT = sbt("qkT", [128, 2, SEQ], BF16)
            qkT_t = [Trk() for _ in range(4)]
            Vb = sbt("Vb", [128, 16, 128], BF16)
            Vb_t = [Trk() for _ in range(4)]
            qkt = [sbt(f"qkt{i}", [128, 2, 128], BF16) for i in range(3)]
            qkt_t = [Trk() for _ in range(3)]
            ta = sbt("ropeA", [128, 2, 2, 16], F32)
            tb = sbt("ropeB", [128, 2, 2, 16], F32)
            rope_t = Trk()
            pT = [sbt(f"pT{i}", [128, 256], BF16) for i in range(6)]
            pT_t = [Trk() for _ in range(6)]
            accN = sbt("accN", [128, SEQ], F32)
            accD = sbt("accD", [128, SEQ], F32)
            acc_t = [Trk() for _ in range(4)]
            pk = 0
            qk_i = 0
            carry = []
            for h in range(6):
                for g in range(3):
                    r = DIL[g]
                    nbp = 16 // r
                    base = g * 2304
                    s, st = self.wnext([("winb", j, base + 128 * h, 128, 0), ("winb", j, base + 768 + 128 * h, 128, 128),
                                        ("winb", j, base + 1536 + 128 * h, 128, 256)])
                    pend_tr = []
                    for b in range(16):
                        phase, jb = b // nbp, b % nbp
                        t0 = r * 128 * jb + phase
                        tsl = slice(t0, t0 + 127 * r + 1, r) if r > 1 else slice(t0, t0 + 128)
                        if r == 1:
                            hts = [self.hT_t[jb // 4]]
                        elif r == 4:
                            hts = [self.hT_t[jb]]
                        else:
                            hts = self.hT_t
                        bank = self.ps_alloc()
                        out = self.ps[bank][:, 0:384]
                        for kc in range(8):
                            self.mm(out, self.hT[:, kc, tsl], self.wslot[s][:, kc, :], start=(kc == 0), stop=(kc == 7),
                                    reads=[st] + list(hts), writes=[self.ps_t[bank]], inc=(kc == 7))
                        ps3 = out.rearrange("p (a d) -> p a d", a=3)
                        qi = qk_i % 3
                        qk_i += 1
                        rt = [self.ps_t[bank], self.const_t]
                        X = ps3[:, 0:2, 0:32].rearrange("p a (u d) -> p a u d", u=2)
                        Cb = self.cos[:, g, b, :].unsqueeze(1).unsqueeze(1).to_broadcast([128, 2, 2, 16])
                        Sb = self.sin[:, g, b, :].unsqueeze(1).unsqueeze(1).to_broadcast([128, 2, 2, 16])
                        self.op(self.dve, lambda X=X, Cb=Cb: nc.vector.tensor_tensor(ta[:], X, Cb, ALU.mult),
                                reads=rt, writes=[rope_t])
                        self.op(self.dve, lambda X=X, Sb=Sb: nc.vector.tensor_tensor(tb[:], X, Sb, ALU.mult),
                                reads=rt, writes=[rope_t])
                        self.op(self.dve, lambda qi=qi: nc.vector.tensor_tensor(
                            qkt[qi][:, :, 0:16], ta[:, :, 0, :], tb[:, :, 1, :], ALU.subtract),
                            reads=[rope_t], writes=[qkt_t[qi]])
                        self.op(self.dve, lambda qi=qi: nc.vector.tensor_tensor(
                            qkt[qi][:, :, 16:32], ta[:, :, 1, :], tb[:, :, 0, :], ALU.add),
                            reads=[rope_t], writes=[qkt_t[qi]])
                        self.op(self.act, lambda b=b, ps3=ps3: nc.scalar.copy(Vb[:, b, :], ps3[:, 2, :]),
                                reads=[self.ps_t[bank]], writes=[Vb_t[b // 4]])
                        self.op(self.act, lambda qi=qi, ps3=ps3: nc.scalar.copy(
                            qkt[qi][:, :, 32:128], ps3[:, 0:2, 32:128]),
                            reads=[self.ps_t[bank]], writes=[qkt_t[qi]])
                        self.ps_release(bank)
                        def emit_tr(b=b, qi=qi):
                            if b % 4 == 0:
                                self.bankT = self.ps_alloc()
                            bankT = self.bankT
                            pb = self.ps[bankT][:].bitcast(BF16).rearrange("p (a t) -> p a t", a=2)
                            for a in range(2):
                                self.op(self.pe, lambda a=a, pb=pb, qi=qi, b=b: nc.tensor.transpose(
                                    pb[:, a, (b % 4) * 128:(b % 4 + 1) * 128], qkt[qi][:, a, :], self.ident[:]),
                                    reads=[qkt_t[qi], self.const_t], writes=[self.ps_t[bankT]],
                                    inc=(a == 1), acc=not (b % 4 == 0 and a == 0))
                            if b % 4 == 3:
                                b0 = b - 3
                                self.op(self.act, lambda pb=pb, b0=b0: nc.scalar.copy(
                                    qkT[:, :, b0 * 128:(b0 + 4) * 128], pb),
                                    reads=[self.ps_t[bankT]], writes=[qkT_t[b // 4]])
                                self.ps_release(bankT)
                        pend_tr.append(emit_tr)
                        if len(pend_tr) > 2:
                            pend_tr.pop(0)()
                        if carry and b % 3 == 2:
                            carry.pop(0)()
                    while pend_tr:
                        pend_tr.pop(0)()
                    while carry:
                        carry.pop(0)()
                    if KSTAGE <= 2:
                        continue
                    bn = {}
                    bd = {}
                    started = set()
                    pend_pv = []
                    for kb in range(16):
                        phase, jb = kb // nbp, kb % nbp
                        has_next = jb < nbp - 1
                        N = 256 if has_next else 128
                        m = kb // 4
                        if m not in bn:
                            bn[m] = self.ps_alloc()
                            bd[m] = self.ps_alloc()
                        if has_next and (kb + 1) // 4 not in bn:
                            bn[m + 1] = self.ps_alloc()
                            bd[m + 1] = self.ps_alloc()
                        bs = self.ps_alloc()
                        qts = [qkT_t[kb // 4]] + ([qkT_t[(kb + 1) // 4]] if has_next else [])
                        self.mm(self.ps[bs][:, 0:N], qkT[:, 1, kb * 128:(kb + 1) * 128], qkT[:, 0, kb * 128:kb * 128 + N],
                                start=True, stop=True, reads=qts, writes=[self.ps_t[bs]], inc=True)
                        p = pk % 6
                        pk += 1
                        self.op(self.act, lambda p=p, bs=bs, N=N: nc.scalar.activation(
                            out=pT[p][:, 0:N], in_=self.ps[bs][:, 0:N], func=AF.Exp, scale=128.0 ** -0.5),
                            reads=[self.ps_t[bs]], writes=[pT_t[p]])
                        self.ps_release(bs)
                        self.op(self.dve, lambda p=p, N=N: nc.vector.tensor_tensor(
                            pT[p][:, 0:N], pT[p][:, 0:N], self.mask2[:, 0:N], ALU.mult),
                            reads=[pT_t[p], self.const_t], writes=[pT_t[p]])
                        def emit_pv(kb=kb, has_next=has_next, m=m, p=p):
                            if has_next and (kb + 1) // 4 == m:
                                segs = [(m, (kb % 4) * 128, 0, 256)]
                            elif has_next:
                                segs = [(m, (kb % 4) * 128, 0, 128), (m + 1, 0, 128, 128)]
                            else:
                                segs = [(m, (kb % 4) * 128, 0, 128)]
                            for si_, (mm_, oc, pc, n) in enumerate(segs):
                                first = mm_ not in started
                                started.add(mm_)
                                last = (si_ == len(segs) - 1)
                                self.mm(self.ps[bn[mm_]][:, oc:oc + n], Vb[:, kb, :], pT[p][:, pc:pc + n], start=first, stop=False,
                                        reads=[Vb_t[kb // 4], pT_t[p]], writes=[self.ps_t[bn[mm_]]], inc=False)
                                self.mm(self.ps[bd[mm_]][:, oc:oc + n], self.ones[:], pT[p][:, pc:pc + n], start=first, stop=False,
                                        reads=[pT_t[p]], writes=[self.ps_t[bd[mm_]]], inc=last)
                            if kb % 4 == 3 and KSTAGE <= 3:
                                self.ps_release(bn[m])
                                self.ps_release(bd[m])
                            elif kb % 4 == 3:
                                if r == 1:
                                    dN, dD = accN[:, m * 512:(m + 1) * 512], accD[:, m * 512:(m + 1) * 512]
                                    sN, sD = self.ps[bn[m]][:, :], self.ps[bd[m]][:, :]
                                    trks = [acc_t[m]]
                                elif r == 4:
                                    dN = accN[:, m:SEQ:4]
                                    dD = accD[:, m:SEQ:4]
                                    sN, sD = self.ps[bn[m]][:, :], self.ps[bd[m]][:, :]
                                    trks = acc_t
                                else:
                                    dN = accN[:].rearrange("d (s ph) -> d ph s", ph=16)[:, 4 * m:4 * m + 4, :]
                                    dD = accD[:].rearrange("d (s ph) -> d ph s", ph=16)[:, 4 * m:4 * m + 4, :]
                                    sN = self.ps[bn[m]][:, :].rearrange("d (ph s) -> d ph s", ph=4)
                                    sD = self.ps[bd[m]][:, :].rearrange("d (ph s) -> d ph s", ph=4)
                                    trks = acc_t
                                def do_acc(dN=dN, dD=dD, sN=sN, sD=sD, trks=trks, bnm=bn[m], bdm=bd[m], g=g):
                                    if g == 0:
                                        self.op(self.act, lambda: nc.scalar.copy(dN, sN),
                                                reads=[self.ps_t[bnm]], writes=trks)
                                        self.op(self.act, lambda: nc.scalar.copy(dD, sD),
                                                reads=[self.ps_t[bdm]], writes=trks)
                                    else:
                                        self.op(self.dve, lambda: nc.vector.tensor_tensor(dN, sN, dN, ALU.add),
                                                reads=[self.ps_t[bnm]] + trks, writes=trks)
                                        self.op(self.dve, lambda: nc.vector.tensor_tensor(dD, sD, dD, ALU.add),
                                                reads=[self.ps_t[bdm]] + trks, writes=trks)
                                    self.ps_release(bnm)
                                    self.ps_release(bdm)
                                if m == 3:
                                    carry.append(do_acc)
                                else:
                                    do_acc()
                        pend_pv.append(emit_pv)
                        if len(pend_pv) > 4:
                            pend_pv.pop(0)()
                    while pend_pv:
                        pend_pv.pop(0)()
                for T in range(4 if KSTAGE > 4 else 0):
                    def do_fin(T=T, h=h):
                        cols = slice(T * 512, (T + 1) * 512)
                        self.op(self.dve, lambda: nc.vector.reciprocal(accD[:, cols], accD[:, cols]),
                                reads=[acc_t[T]], writes=[acc_t[T]])
                        self.op(self.dve, lambda: nc.vector.tensor_tensor(
                            self.preT[:, h, cols], accN[:, cols], accD[:, cols], ALU.mult),
                            reads=[acc_t[T]], writes=[self.preT_t[h][T]])
                    carry.append(do_fin)
            while carry:
                carry.pop(0)()
            self.barrier()


def _consts():
    p = np.arange(128)
    ident = np.eye(128, dtype=np.float32)
    mask2 = np.zeros((128, 256), np.float32)
    mask2[:, 0:128] = (p[:, None] <= p[None, :])
    mask2[:, 128:256] = (p[:, None] >= p[None, :])
    gmask = ((p[:, None] // 64 == p[None, :] // 64) & (p[:, None] <= p[None, :])).astype(np.float32)
    rmask = np.ones((128, 512), np.float32)
    rmask[:, 0::64] = 0.0
    half = 16
    inv = (np.float32(500000.0) ** (-(np.arange(half, dtype=np.float32) / np.float32(half)))).astype(np.float32)
    cos = np.zeros((128, 3, 16, 16), np.float32)
    sin = np.zeros((128, 3, 16, 16), np.float32)
    for g, r in enumerate(DIL):
        nbp = 16 // r
        for b in range(16):
            phase, jb = b // nbp, b % nbp
            t = (r * (128 * jb + p) + phase).astype(np.float32)
            ang = (t[:, None] * inv[None, :]).astype(np.float32)
            cos[:, g, b, :] = np.cos(ang)
            sin[:, g, b, :] = np.sin(ang)
    return dict(c_ident=ident, c_ones=np.ones((128, 128), np.float32), c_mask2=mask2, c_gmask=gmask, c_rmask=rmask,
                c_cos=cos.reshape(128, -1), c_sin=sin.reshape(128, -1))


_NC_CACHE = {}


def _get_nc(nseq, nlayers, dbg=None):
    key = (nseq, nlayers, None if dbg is None else tuple(sorted(dbg)))
    if key not in _NC_CACHE:
        k = Kern(nseq, nlayers, dbg)
        _NC_CACHE[key] = k.build()
    return _NC_CACHE[key]


def _layout_params(mem_norm_w, norm_w, w_memkv, w_out, w_in_a, w_gate_up, b_gate, gla_norm_w, w_in_b, final_norm_w):
    f = lambda a: np.ascontiguousarray(np.asarray(a, dtype=np.float32))
    fm8 = lambda v: f(np.asarray(v).reshape(8, 128).T)
    d = {}
    d["memnw_fm"] = fm8(mem_norm_w)
    d["normw_fm"] = f(np.concatenate([fm8(norm_w[i]) for i in range(4)], axis=1))
    d["w_memkv"] = f(w_memkv)
    d["w_out"] = f(w_out)
    d["w_in_a"] = f(w_in_a)
    d["w_gate_up"] = f(w_gate_up)
    bg = np.asarray(b_gate)
    d["bg_fm"] = f(np.concatenate([bg[j].reshape(4, 96).T for j in range(2)], axis=1))
    gn = np.asarray(gla_norm_w)
    idx = (np.arange(768) % 192).reshape(6, 128).T
    d["gnw_fm"] = f(np.concatenate([gn[j][idx] for j in range(2)], axis=1))
    d["w_in_b"] = f(w_in_b)
    d["fnw_bc"] = f(np.broadcast_to(np.asarray(final_norm_w)[None, :], (128, D)))
    d.update(_consts())
    return d


def kernel(x, mem, mem_norm_w, norm_w, w_memkv, w_out, w_in_a, w_gate_up, b_gate, gla_norm_w, w_in_b,
           final_norm_w, _nlayers=4, _nseq_launch=4):
    x = np.asarray(x, dtype=np.float32)
    mem = np.asarray(mem, dtype=np.float32)
    B = x.shape[0]
    per_core = B // NCORES
    params = _layout_params(mem_norm_w, norm_w, w_memkv, w_out, w_in_a, w_gate_up, b_gate, gla_norm_w,
                            w_in_b, final_norm_w)
    out = np.empty_like(x)
    nseq = _nseq_launch
    nc = _get_nc(nseq, _nlayers)
    for s0 in range(0, per_core, nseq):
        in_maps = []
        for c in range(NCORES):
            b0 = c * per_core + s0
            m = dict(params)
            m["x"] = np.ascontiguousarray(x[b0:b0 + nseq])
            m["mem"] = np.ascontiguousarray(mem[b0:b0 + nseq])
            in_maps.append(m)
        res = run_bass_kernel_spmd(nc, in_maps, core_ids=list(range(NCORES)))
        for c in range(NCORES):
            b0 = c * per_core + s0
            out[b0:b0 + nseq] = np.asarray(res.results[c]["out"]).reshape(nseq, SEQ, D)
    return out
```

```python
import os
import numpy as np
from contextlib import ExitStack
import concourse.bass as bass
import concourse.mybir as mybir
from concourse.bass_utils import run_bass_kernel_spmd

F32 = mybir.dt.float32
BF16 = mybir.dt.bfloat16
AF = mybir.ActivationFunctionType
ALU = mybir.AluOpType

NCORES = 8
SEQ = 2048
D = 1024
NMEM = 256
DIL = [1, 4, 16]
EPOCH = 12000
NSLOT = 3
SLOTW = 384
EPS = 1e-6
KSTAGE = int(os.environ.get('KSTAGE', '9'))
KSUB = int(os.environ.get('KSUB', '9'))
T0S = int(os.environ.get('T0S', '9'))


class Trk:
    __slots__ = ("w", "r", "rd", "excl")

    def __init__(self, excl=False):
        self.excl = excl
        self.w = None
        self.r = {}
        self.rd = []


class Eng:
    def __init__(self, name, h):
        self.name = name
        self.h = h
        self.count = 0
        self.pending = False
        self.sems = []
        self.waited = {}


class Queue:
    def __init__(self, name, h, nslots):
        self.name = name
        self.h = h
        self.sems = []
        self.uses = [0] * nslots
        self.k = 0
        self.waited = {}


class Kern:
    def __init__(self, nseq, nlayers, dbg=None):
        self.nseq = nseq
        self.nlayers = nlayers
        self.dry = True
        self.jobs = []
        self.dbg = dbg

    def setup_engines(self, es):
        nc = self.nc
        self.pe = Eng("pe", nc.tensor)
        self.act = Eng("act", nc.scalar)
        self.dve = Eng("dve", nc.vector)
        self.engs = {"pe": self.pe, "act": self.act, "dve": self.dve}
        self.qsp = Queue("sp", nc.sync, 8)
        self.qpl = Queue("pool", nc.gpsimd, 8)
        if not self.dry:
            for e in self.engs.values():
                nep = self.est_counts[e.name] // EPOCH + 1
                for i in range(nep):
                    e.sems.append(es.enter_context(nc.semaphore(f"s_{e.name}{i}")))
            for q in (self.qsp, self.qpl):
                for i in range(len(q.uses)):
                    q.sems.append(es.enter_context(nc.semaphore(f"q_{q.name}{i}")))

    def _tok_sem(self, tok):
        if tok[0] == "d":
            return tok[1], tok[2]
        e = self.engs[tok[0]]
        c = tok[1]
        ep = (c - 1) // EPOCH
        return e.sems[ep], c - ep * EPOCH

    def _need(self, waiter, tok, out):
        if tok is None:
            return
        if tok[0] == "d":
            key = ("d", tok[3])
            if waiter.waited.get(key, 0) >= tok[2]:
                return
            if out.get(key, (None, 0))[1] < tok[2]:
                out[key] = (tok, tok[2])
        else:
            key = tok[0]
            if waiter.waited.get(key, 0) >= tok[1]:
                return
            if out.get(key, (None, 0))[1] < tok[1]:
                out[key] = (tok, tok[1])

    def _collect(self, waiter, reads, writes, acc, own):
        out = {}
        for t in reads:
            if t.w is not None and not (t.w[0] == own and own == "pe"):
                self._need(waiter, t.w, out)
            if t.excl:
                for en, c in t.r.items():
                    if en != own:
                        self._need(waiter, (en, c), out)
        if not acc:
            for t in writes:
                if t.w is not None and not (t.w[0] == own and own == "pe"):
                    self._need(waiter, t.w, out)
                for en, c in t.r.items():
                    if not (en == own and own == "pe"):
                        self._need(waiter, (en, c), out)
                for dt_ in t.rd:
                    self._need(waiter, dt_, out)
        return out

    def _emit_waits(self, waiter, h, need):
        for key, (tok, v) in need.items():
            waiter.waited[key] = v
            if not self.dry:
                sem, val = self._tok_sem(tok)
                h.wait_ge(sem, val)

    def op(self, eng, fn, reads=(), writes=(), inc=True, acc=False):
        need = self._collect(eng, reads, writes, acc, eng.name)
        self._emit_waits(eng, eng.h, need)
        if inc:
            eng.count += 1
            eng.pending = False
            c = eng.count
        else:
            eng.pending = True
            c = eng.count + 1
        if not self.dry:
            ins = fn()
            if inc:
                ep = (c - 1) // EPOCH
                ins.then_inc(eng.sems[ep], 1)
        tok = (eng.name, c)
        for t in reads:
            if t.r.get(eng.name, 0) < c:
                t.r[eng.name] = c
        for t in writes:
            if acc:
                t.w = tok
            else:
                t.w = tok
                t.r = {}
                t.rd = []
        return tok

    def dma(self, q, out_ap, in_ap=None, reads=(), writes=()):
        pairs = out_ap if in_ap is None else [(out_ap, in_ap)]
        need = self._collect(q, reads, writes, False, q.name)
        self._emit_waits(q, q.h, need)
        slot = q.k % len(q.uses)
        q.k += 1
        prev = q.uses[slot] * 16
        q.uses[slot] += len(pairs)
        val = q.uses[slot] * 16
        if not self.dry:
            sem = q.sems[slot]
            if prev > 0:
                q.h.wait_ge(sem, prev)
            for (o, i) in pairs:
                q.h.dma_start(out=o, in_=i).then_inc(sem, 16)
            tok = ("d", sem, val, (q.name, slot))
        else:
            tok = ("d", None, val, (q.name, slot))
        for t in reads:
            t.rd.append(tok)
        for t in writes:
            t.w = tok
            t.r = {}
            t.rd = []
        return tok

    def barrier(self):
        for e in self.engs.values():
            assert not e.pending, e.name
        waiters = list(self.engs.values()) + [self.qsp, self.qpl]
        for w in waiters:
            need = {}
            for o in self.engs.values():
                if o is w or o.count == 0:
                    continue
                self._need(w, (o.name, o.count), need)
            self._emit_waits(w, w.h, need)

    def un(self, name):
        self.uid += 1
        return f"{name}_{self.uid}"

    def ps_alloc(self):
        return self.ps_free.pop(0)

    def ps_release(self, i):
        self.ps_free.append(i)

    def wnext(self, pieces):
        if self.dry:
            self.jobs.append(pieces)
            i = len(self.jobs) - 1
            return i % NSLOT, self.wtrk[i % NSLOT]
        i = self.wjob
        self.wjob += 1
        while self.wissued < min(i + NSLOT, len(self.jobs)):
            j = self.wissued
            s = j % NSLOT
            pairs = []
            for (name, idx, sc0, n, dc0) in self.jobs[j]:
                src = getattr(self, name + "_d")[idx, :, sc0:sc0 + n]
                pairs.append((self.wslot[s][:, :, dc0:dc0 + n], src.rearrange("(kc p) c -> p kc c", p=128)))
            self.dma(self.qpl, pairs, writes=[self.wtrk[s]])
            self.wissued += 1
        return i % NSLOT, self.wtrk[i % NSLOT]

    def drip(self, n=1):
        while n > 0 and self.t0q:
            self.t0q.pop(0)()
            n -= 1

    def flush_t0(self):
        while self.t0q:
            self.t0q.pop(0)()

    def mm(self, out, lhsT, rhs, start, stop, reads, writes, inc):
        nc = self.nc
        self.mmc += 1
        if self.mmc % 3 == 0:
            self.drip()
        return self.op(self.pe,
                       lambda: nc.tensor.matmul(out, lhsT, rhs, start=start, stop=stop,
                                                skip_group_check=True),
                       reads=reads, writes=writes, inc=inc, acc=not start)

    def proj_fm(self, bank, M, slot, strk, c0, T, ncols=512, pbase=0):
        out = self.ps[bank][pbase:pbase + M, 0:ncols]
        for kc in range(8):
            self.mm(out, self.wslot[slot][:, kc, c0:c0 + M],
                    self.hT[:, kc, T * 512:T * 512 + ncols],
                    start=(kc == 0), stop=(kc == 7),
                    reads=[strk, self.hT_t[T]], writes=[self.ps_t[bank]], inc=(kc == 7))

    def build(self):
        self.dry = True
        self._build_once()
        self.est_counts = {n: e.count for n, e in self.engs.items()}
        jobs = self.jobs
        self.dry = False
        self.jobs = jobs
        self._build_once()
        return self.nc

    def _build_once(self):
        nseq = self.nseq
        nc = bass.Bass("TRN2", target_bir_lowering=False)
        self.nc = nc
        self.wjob = 0
        self.wissued = 0
        dt = lambda name, shape, kind="ExternalInput", dtype=F32: nc.dram_tensor(name, shape, dtype, kind=kind).ap()
        self.x_d = dt("x", [nseq, SEQ, D])
        self.mem_d = dt("mem", [nseq, NMEM, D])
        self.memnw_d = dt("memnw_fm", [128, 8])
        self.normw_d = dt("normw_fm", [128, 4 * 8])
        self.wmemkv_d = dt("w_memkv", [4, D, 512])
        self.wout_d = dt("w_out", [4, D, D])
        self.wina_d = dt("w_in_a", [2, D, 2832])
        self.wgu_d = dt("w_gate_up", [2, 16, 384])
        self.bg_d = dt("bg_fm", [96, 2 * 4])
        self.gnw_d = dt("gnw_fm", [128, 2 * 6])
        self.winb_d = dt("w_in_b", [2, D, 8192])
        self.fnw_d = dt("fnw_bc", [128, D])
        self.c_ident_d = dt("c_ident", [128, 128])
        self.c_ones_d = dt("c_ones", [128, 128])
        self.c_mask2_d = dt("c_mask2", [128, 256])
        self.c_maskb_d = dt("c_maskb", [128, 256])
        self.c_gmask_d = dt("c_gmask", [128, 128])
        self.c_rmask_d = dt("c_rmask", [128, 512])
        self.c_cos_d = dt("c_cos", [128, 3 * 16 * 16])
        self.c_sin_d = dt("c_sin", [128, 3 * 16 * 16])
        self.out_d = dt("out", [nseq, SEQ, D], kind="ExternalOutput")
        if self.dbg:
            self.dbg_d = {k: dt("dbg_" + k, shp, kind="ExternalOutput") for k, shp in self.dbg.items()}

        with ExitStack() as es:
            self.setup_engines(es)
            sb = lambda name, shape, dtype: es.enter_context(nc.sbuf_tensor(name, shape, dtype))
            self.x = sb("x_sb", [128, 16, D], F32)
            self.x_t = [Trk() for _ in range(16)]
            self.hT = sb("hT", [128, 8, SEQ], BF16)
            self.hT_t = [Trk() for _ in range(4)]
            self.preT = sb("preT", [128, 8, SEQ], BF16)
            self.preT_t = [[Trk() for _ in range(4)] for _ in range(8)]
            self.memT = sb("memT", [128, 8, NMEM], BF16)
            self.memT_t = Trk()
            self.kmT = sb("kmT", [128, 2, NMEM], BF16)
            self.kmT_t = Trk()
            self.vm = sb("vm", [128, 2, 256], BF16)
            self.vm_t = Trk()
            self.wslot = [sb(f"wslot{i}", [128, 8, SLOTW], BF16) for i in range(NSLOT)]
            self.wtrk = [Trk() for _ in range(NSLOT)]
            self.ps = [es.enter_context(nc.psum_tensor(f"ps{i}", [128, 512], F32)) for i in range(8)]
            self.ps_t = [Trk(excl=True) for _ in range(8)]
            self.ps_free = list(range(7))
            self.ident = sb("ident", [128, 128], BF16)
            self.mask2 = sb("mask2", [128, 256], BF16)
            self.maskb = sb("maskb", [128, 256], BF16)
            self.gmask = sb("gmask", [128, 128], BF16)
            self.ones = sb("ones", [128, 128], BF16)
            self.cos = sb("cos", [128, 3, 16, 16], F32)
            self.sin = sb("sin", [128, 3, 16, 16], F32)
            self.memnw = sb("memnw", [128, 8], F32)
            self.normw = sb("normw", [128, 4, 8], F32)
            self.bg = sb("bg", [96, 8], F32)
            self.nbg = sb("nbg", [96, 8], F32)
            self.gnw = sb("gnw", [128, 12], F32)
            self.const_t = Trk()
            self.ident32 = sb("ident32", [128, 128], F32)
            self.ones32 = sb("ones32", [128, 128], F32)
            self.w32 = [sb(f"w32_{i}", [128, 8, 128], F32) for i in range(2)]
            self.w32_t = [Trk() for _ in range(2)]
            self.w32_n = 0
            self.t0_prev_pos = None
            self.kmT32 = sb("kmT32", [128, 2, NMEM], F32)
            self.vm32 = sb("vm32", [128, 2, 256], F32)
            self.km32_t = Trk()
            self.vm32_t = Trk()
            self.x0T = sb("x0T", [128, 8], F32)
            self.h0T = sb("h0T", [128, 8], F32)
            self.m0T = sb("m0T", [128, 8], F32)
            self.sg0 = sb("sg0", [128, 8], F32)
            self.t0s = sb("t0s", [128, 32], F32)
            self.e8 = sb("e8", [128, 8], F32)
            self.x0_t = Trk()
            self.h0_t = Trk()
            self.m0_t = Trk()
            self.t0s_t = Trk()
            self.tb = 7
            self.tbc = 0
            self.t0q = []
            self.mmc = 0
            self.junk_t = Trk()
            self.ss16 = sb("ss16", [128, 16], F32)
            self.rstd16 = sb("rstd16", [128, 16], F32)
            self.st_t = Trk()
            self.ss_t = [Trk() for _ in range(16)]
            self.uid = 0

            self.load_consts()
            for si in range(nseq):
                self.do_seq(si)
            if not self.dry:
                for tok in self.out_toks:
                    key = ("d", tok[3])
                    if self.qsp.waited.get(key, 0) < tok[2]:
                        self.qsp.waited[key] = tok[2]
                        nc.sync.wait_ge(tok[1], tok[2])

    def load_consts(self):
        nc = self.nc
        toks = []
        D_ = lambda q, o, i: toks.append(self.dma(q, o, i))
        D_(self.qpl, self.ident[:], self.c_ident_d)
        D_(self.qsp, self.ident32[:], self.c_ident_d)
        D_(self.qsp, self.ones32[:], self.c_ones_d)
        D_(self.qpl, self.mask2[:], self.c_mask2_d)
        D_(self.qpl, self.maskb[:], self.c_maskb_d)
        D_(self.qpl, self.gmask[:], self.c_gmask_d)
        D_(self.qsp, self.cos[:].rearrange("p a b c -> p (a b c)"), self.c_cos_d)
        D_(self.qsp, self.sin[:].rearrange("p a b c -> p (a b c)"), self.c_sin_d)
        D_(self.qsp, self.memnw[:], self.memnw_d)
        D_(self.qsp, self.normw[:].rearrange("p a b -> p (a b)"), self.normw_d)
        D_(self.qsp, self.bg[:], self.bg_d)
        D_(self.qsp, self.gnw[:], self.gnw_d)
        for e in self.engs.values():
            need = {}
            for tok in toks:
                self._need(e, tok, need)
            self._emit_waits(e, e.h, need)
        self.op(self.dve, lambda: nc.vector.memset(self.ones[:], 1.0), writes=[self.const_t])
        self.op(self.dve, lambda: nc.vector.tensor_scalar(self.nbg[:], self.bg[:], -1.0, None, ALU.mult),
                writes=[self.const_t])
        self.out_toks = []

    def norm_to_fm(self, src_aps, src_trks, w_ap, dst, dst_trk_of, ntiles):
        nc = self.nc
        with ExitStack() as es:
            self.xs = [es.enter_context(nc.sbuf_tensor(self.un("xs"), [128, D], BF16)) for _ in range(2)]
            self.xs_t = [Trk(), Trk()]
            self.junk = es.enter_context(nc.sbuf_tensor(self.un("junk"), [128, D], BF16))
            self.junk_t = Trk()
            self._norm_to_fm(src_aps, src_trks, w_ap, dst, dst_trk_of, ntiles)
            self.barrier()

    def _norm_to_fm(self, src_aps, src_trks, w_ap, dst, dst_trk_of, ntiles):
        nc = self.nc
        for i in range(ntiles):
            self.op(self.act, lambda i=i: nc.scalar.activation(
                out=self.junk[:], in_=src_aps[i], func=AF.Square, accum_out=self.ss16[:, i:i + 1]),
                reads=[src_trks[i]], writes=[self.ss_t[i], self.junk_t])
        self.op(self.act, lambda: nc.scalar.activation(
            out=self.rstd16[:, 0:ntiles], in_=self.ss16[:, 0:ntiles], func=AF.Ln, scale=1.0 / D, bias=EPS),
            reads=self.ss_t[0:ntiles], writes=[self.st_t])
        self.op(self.act, lambda: nc.scalar.activation(
            out=self.rstd16[:, 0:ntiles], in_=self.rstd16[:, 0:ntiles], func=AF.Exp, scale=-0.5),
            reads=[self.st_t], writes=[self.st_t])
        for i in range(ntiles):
            b = i % 2
            self.op(self.act, lambda i=i, b=b: nc.scalar.activation(
                out=self.xs[b][:], in_=src_aps[i], func=AF.Copy, scale=self.rstd16[:, i:i + 1]),
                reads=[src_trks[i], self.st_t], writes=[self.xs_t[b]])
            bank = self.ps_alloc()
            pb = self.ps[bank][:].bitcast(BF16)
            for kc in range(8):
                self.op(self.pe, lambda kc=kc, b=b, pb=pb: nc.tensor.transpose(
                    pb[:, kc * 128:(kc + 1) * 128], self.xs[b][:, kc * 128:(kc + 1) * 128], self.ident[:]),
                    reads=[self.xs_t[b], self.const_t], writes=[self.ps_t[bank]], inc=(kc == 7), acc=(kc > 0))
            dtrk = dst_trk_of(i)
            self.op(self.dve, lambda i=i, pb=pb: nc.vector.tensor_tensor(
                dst[:, :, i * 128:(i + 1) * 128],
                pb.rearrange("p (k t) -> p k t", k=8),
                w_ap.unsqueeze(2).to_broadcast([128, 8, 128]), ALU.mult),
                reads=[self.ps_t[bank], self.const_t], writes=[dtrk])
            self.ps_release(bank)

    def do_seq(self, si):
        nc = self.nc
        for tt in range(16):
            self.dma(self.qsp, self.x[:, tt, :], self.x_d[si, tt * 128:(tt + 1) * 128, :], writes=[self.x_t[tt]])
        with ExitStack() as es:
            memraw = es.enter_context(nc.sbuf_tensor(self.un("memraw"), [128, 2, D], F32))
            mr_t = [Trk(), Trk()]
            for i in range(2):
                self.dma(self.qsp, memraw[:, i, :], self.mem_d[si, i * 128:(i + 1) * 128, :], writes=[mr_t[i]])
            self.norm_to_fm([memraw[:, i, :] for i in range(2)], mr_t, self.memnw[:], self.memT,
                            lambda i: self.memT_t, 2)
            self.barrier()
        self.t0_init()
        for li in range(self.nlayers):
            self.do_layer(si, li)
        self.final_norm(si)

    def do_layer(self, si, li):
        nc = self.nc
        j = li // 2
        self.norm_to_fm([self.x[:, tt, :] for tt in range(16)], self.x_t, self.normw[:, li, :], self.hT,
                        lambda i: self.hT_t[i // 4], 16)
        self.mem_kv(li)
        self.t0_layer(li)
        mode = self.t0_mode(li)
        if li % 2 == 0:
            self.mixer_a(j)
            wname, qm0, g0 = "wina", 1552, 1808
        else:
            self.mixer_b(j)
            wname, qm0, g0 = "winb", 6912, 7168
        if mode == "heads" and T0S >= 3:
            self.flush_t0()
            self.op(self.dve, lambda: nc.vector.tensor_copy(self.preT[:, 0:6, 0:1], self.m0T[:, 0:6].unsqueeze(2)),
                    reads=[self.m0_t], writes=[self.preT_t[c][0] for c in range(6)])
        self.tail(li, j, wname, qm0, g0)
        if mode == "full":
            self.flush_t0()
            self.t0_inject()

    def mem_kv(self, li):
        nc = self.nc
        s, st = self.wnext([("wmemkv", li, 0, 256, 0)])
        for c2 in range(2):
            bank = self.ps_alloc()
            out = self.ps[bank][:, 0:256]
            for kc in range(8):
                self.mm(out, self.wslot[s][:, kc, c2 * 128:(c2 + 1) * 128], self.memT[:, kc, :],
                        start=(kc == 0), stop=(kc == 7), reads=[st, self.memT_t], writes=[self.ps_t[bank]],
                        inc=(kc == 7))
            self.op(self.act, lambda c2=c2, out=out: nc.scalar.copy(self.kmT[:, c2, :], out),
                    reads=[self.ps_t[bank]], writes=[self.kmT_t])
            self.op(self.act, lambda c2=c2, out=out: nc.scalar.copy(self.kmT32[:, c2, :], out),
                    reads=[self.ps_t[bank]], writes=[self.km32_t])
            self.ps_release(bank)
        s, st = self.wnext([("wmemkv", li, 256, 256, 0)])
        for kt in range(2):
            bank = self.ps_alloc()
            out = self.ps[bank][:, 0:256]
            for kc in range(8):
                self.mm(out, self.memT[:, kc, kt * 128:(kt + 1) * 128], self.wslot[s][:, kc, 0:256],
                        start=(kc == 0), stop=(kc == 7), reads=[st, self.memT_t], writes=[self.ps_t[bank]],
                        inc=(kc == 7))
            self.op(self.act, lambda kt=kt, out=out: nc.scalar.copy(self.vm[:, kt, :], out),
                    reads=[self.ps_t[bank]], writes=[self.vm_t])
            self.op(self.act, lambda kt=kt, out=out: nc.scalar.copy(self.vm32[:, kt, :], out),
                    reads=[self.ps_t[bank]], writes=[self.vm32_t])
            self.ps_release(bank)

    def tail(self, li, j, wname, qm0, g0):
        nc = self.nc
        with ExitStack() as es:
            sbt = lambda name, shape, dtype: es.enter_context(nc.sbuf_tensor(self.un(name), shape, dtype))
            qm = sbt("qm", [128, 2, 512], BF16)
            qm_t = Trk()
            pT = [sbt(f"pTm{i}", [128, 512], BF16) for i in range(3)]
            pT_t = [Trk() for _ in range(3)]
            rden = sbt("rdenm", [128, 512], F32)
            rden_t = Trk()
            sg = [sbt(f"sg{i}", [128, 512], BF16) for i in range(2)]
            sg_t = [Trk() for _ in range(2)]
            pk = 0
            for T in range(4):
                cols = slice(T * 512, (T + 1) * 512)
                s, st = self.wnext([(wname, j, qm0, 256, 0)])
                for c2 in range(2):
                    bank = self.ps_alloc()
                    self.proj_fm(bank, 128, s, st, c2 * 128, T)
                    self.op(self.act, lambda c2=c2, bank=bank: nc.scalar.copy(qm[:, c2, :], self.ps[bank][:, :]),
                            reads=[self.ps_t[bank]], writes=[qm_t])
                    self.ps_release(bank)
                gate_jobs = [(0, 384), (384, 384), (768, 256)]
                gstate = {"ji": 0, "cc": 0, "s": None, "st": None}

                def emit_gate_chunk():
                    if gstate["ji"] >= len(gate_jobs):
                        return False
                    c0, n = gate_jobs[gstate["ji"]]
                    if gstate["cc"] == 0:
                        gstate["s"], gstate["st"] = self.wnext([(wname, j, g0 + c0, n, 0)])
                    s_, st_ = gstate["s"], gstate["st"]
                    cc = gstate["cc"]
                    c = c0 // 128 + cc
                    bank = self.ps_alloc()
                    self.proj_fm(bank, 128, s_, st_, cc * 128, T)
                    b_ = c % 2
                    self.op(self.act, lambda b_=b_, bank=bank: nc.scalar.activation(
                        out=sg[b_][:], in_=self.ps[bank][:, :], func=AF.Silu),
                        reads=[self.ps_t[bank]], writes=[sg_t[b_]])
                    self.ps_release(bank)
                    self.gate_pending.append((c, b_))
                    gstate["cc"] += 1
                    if gstate["cc"] >= n // 128:
                        gstate["cc"] = 0
                        gstate["ji"] += 1
                    return True

                def emit_gate_mults(upto=None):
                    keep = []
                    for (c, b_) in self.gate_pending:
                        if c < 6 or self.mem_done:
                            self.op(self.dve, lambda b_=b_, c=c: nc.vector.tensor_tensor(
                                self.preT[:, c, cols], self.preT[:, c, cols], sg[b_][:], ALU.mult),
                                reads=[sg_t[b_], self.preT_t[c][T]], writes=[self.preT_t[c][T]])
                        else:
                            keep.append((c, b_))
                    self.gate_pending = keep

                self.gate_pending = []
                self.mem_done = False
                pend = []
                banks = {}
                step = 0
                for c2 in range(2):
                    for half in range(2):
                        hm = 2 * c2 + half
                        pr = slice(64 * half, 64 * half + 64)
                        for kt in range(2):
                            if c2 not in banks:
                                banks[c2] = (self.ps_alloc(), self.ps_alloc())
                            bs = self.ps_alloc()
                            self.mm(self.ps[bs][:, :], self.kmT[pr, c2, kt * 128:(kt + 1) * 128], qm[pr, c2, :],
                                    start=True, stop=True, reads=[self.kmT_t, qm_t], writes=[self.ps_t[bs]], inc=True)
                            p = pk % 3
                            pk += 1
                            self.op(self.act, lambda p=p, bs=bs: nc.scalar.activation(
                                out=pT[p][:], in_=self.ps[bs][:, :], func=AF.Exp, scale=0.125),
                                reads=[self.ps_t[bs]], writes=[pT_t[p]])
                            self.ps_release(bs)

                            def emit_pv(c2=c2, half=half, hm=hm, pr=pr, kt=kt, p=p):
                                bn, bd = banks[c2]
                                self.mm(self.ps[bn][pr, :], self.vm[:, kt, hm * 64:(hm + 1) * 64], pT[p][:],
                                        start=(kt == 0), stop=(kt == 1), reads=[self.vm_t, pT_t[p]],
                                        writes=[self.ps_t[bn]], inc=False)
                                self.mm(self.ps[bd][pr, :], self.ones[:, 0:64], pT[p][:],
                                        start=(kt == 0), stop=(kt == 1), reads=[pT_t[p]],
                                        writes=[self.ps_t[bd]], inc=True)
                                if half == 1 and kt == 1:
                                    self.op(self.dve, lambda bd=bd: nc.vector.reciprocal(rden[:], self.ps[bd][:, :]),
                                            reads=[self.ps_t[bd]], writes=[rden_t])
                                    self.op(self.dve, lambda bn=bn, c2=c2: nc.vector.tensor_tensor(
                                        self.preT[:, 6 + c2, cols], self.ps[bn][:, :], rden[:], ALU.mult),
                                        reads=[self.ps_t[bn], rden_t], writes=[self.preT_t[6 + c2][T]])
                                    self.ps_release(bn)
                                    self.ps_release(bd)
                            pend.append(emit_pv)
                            if len(pend) > 1:
                                pend.pop(0)()
                            if step < 6:
                                emit_gate_chunk()
                                emit_gate_mults()
                            step += 1
                while pend:
                    pend.pop(0)()
                self.mem_done = True
                while emit_gate_chunk():
                    emit_gate_mults()
                emit_gate_mults()
                for (c0, n) in [(0, 384), (384, 384), (768, 256)]:
                    s, st = self.wnext([("wout", li, c0, n, 0)])
                    for sub in range(4):
                        tt = 4 * T + sub
                        bank = self.ps_alloc()
                        out = self.ps[bank][:, 0:n]
                        for kc in range(8):
                            self.mm(out, self.preT[:, kc, tt * 128:(tt + 1) * 128], self.wslot[s][:, kc, 0:n],
                                    start=(kc == 0), stop=(kc == 7), reads=[st, self.preT_t[kc][T]],
                                    writes=[self.ps_t[bank]], inc=(kc == 7))
                        self.op(self.dve, lambda tt=tt, out=out, c0=c0, n=n: nc.vector.tensor_tensor(
                            self.x[:, tt, c0:c0 + n], out, self.x[:, tt, c0:c0 + n], ALU.add),
                            reads=[self.ps_t[bank], self.x_t[tt]], writes=[self.x_t[tt]])
                        self.ps_release(bank)
            self.barrier()

    def final_norm(self, si):
        nc = self.nc
        with ExitStack() as es:
            fnw = es.enter_context(nc.sbuf_tensor(self.un("fnw"), [128, D], F32))
            self.junk = es.enter_context(nc.sbuf_tensor(self.un("junk"), [128, D], BF16))
            self.junk_t = Trk()
            fnw_t = Trk()
            self.dma(self.qsp, fnw[:], self.fnw_d, writes=[fnw_t])
            self._final_norm(si, fnw, fnw_t)
            self.barrier()

    def _final_norm(self, si, fnw, fnw_t):
        nc = self.nc
        for i in range(16):
            self.op(self.act, lambda i=i: nc.scalar.activation(
                out=self.junk[:], in_=self.x[:, i, :], func=AF.Square, accum_out=self.ss16[:, i:i + 1]),
                reads=[self.x_t[i]], writes=[self.ss_t[i], self.junk_t])
        self.op(self.act, lambda: nc.scalar.activation(
            out=self.rstd16[:], in_=self.ss16[:], func=AF.Ln, scale=1.0 / D, bias=EPS),
            reads=self.ss_t, writes=[self.st_t])
        self.op(self.act, lambda: nc.scalar.activation(
            out=self.rstd16[:], in_=self.rstd16[:], func=AF.Exp, scale=-0.5),
            reads=[self.st_t], writes=[self.st_t])
        for i in range(16):
            self.op(self.dve, lambda i=i: nc.vector.scalar_tensor_tensor(
                self.x[:, i, :], self.x[:, i, :], self.rstd16[:, i:i + 1], fnw[:], ALU.mult, ALU.mult),
                reads=[self.x_t[i], self.st_t, fnw_t], writes=[self.x_t[i]])
            tok = self.dma(self.qsp, self.out_d[si, i * 128:(i + 1) * 128, :], self.x[:, i, :], reads=[self.x_t[i]])
            self.out_toks.append(tok)


    def tcol(self, n=1):
        if self.tbc + n > 512:
            self.tbc = 0
        c = self.tbc
        self.tbc += n
        return c

    def T(self, eng, fn, reads, writes, **kw):
        self.t0q.append(lambda: self.op(eng, fn, reads=reads, writes=writes, **kw))

    def t0_init(self):
        if T0S < 1:
            return
        nc = self.nc
        tb = self.ps[self.tb]
        tbt = self.ps_t[self.tb]
        c = self.tcol(8)
        for kc in range(8):
            self.op(self.pe, lambda kc=kc: nc.tensor.matmul(
                tb[:, c + kc:c + kc + 1], self.x[0:1, 0, kc * 128:(kc + 1) * 128], self.ones32[0:1, 0:1],
                start=True, stop=True, skip_group_check=True),
                reads=[self.x_t[0], self.const_t], writes=[tbt], inc=(kc == 7), acc=(kc > 0))
        self.op(self.dve, lambda: nc.vector.tensor_copy(self.x0T[:], tb[:, c:c + 8]),
                reads=[tbt], writes=[self.x0_t])

    def t0_inject(self):
        if T0S < 1:
            return
        nc = self.nc
        tb = self.ps[self.tb]
        tbt = self.ps_t[self.tb]
        for hf in range(2):
            for k4 in range(4):
                kc = 4 * hf + k4
                self.op(self.pe, lambda kc=kc, k4=k4: nc.tensor.matmul(
                    tb[0:1, k4 * 128:(k4 + 1) * 128], self.x0T[:, kc:kc + 1], self.ident32[:],
                    start=True, stop=True, skip_group_check=True),
                    reads=[self.x0_t, self.const_t], writes=[tbt], inc=(k4 == 3), acc=(k4 > 0))
            self.op(self.dve, lambda hf=hf: nc.vector.tensor_copy(self.x[0:1, 0, hf * 512:(hf + 1) * 512], tb[0:1, :]),
                    reads=[tbt], writes=[self.x_t[0]])
        self.tbc = 0

    def t0_cproj(self, vec, vec_t, wname, idx, c0, M, col, pbase=0):
        nc = self.nc
        tb = self.ps[self.tb]
        tbt = self.ps_t[self.tb]
        k = self.w32_n % 2
        self.w32_n += 1
        w32, w32_t = self.w32[k], self.w32_t[k]

        def load():
            srcap = getattr(self, wname + "_d")[idx, :, c0:c0 + M].rearrange("(kc p) c -> p kc c", p=128)
            self.dma(self.qsp, w32[:, :, 0:M], srcap, writes=[w32_t])
        if self.t0_prev_pos is not None:
            self.t0q.insert(self.t0_prev_pos, load)
        else:
            self.t0q.append(load)
        self.t0_prev_pos = len(self.t0q)
        for kc in range(8):
            self.T(self.pe, lambda kc=kc: nc.tensor.matmul(
                tb[pbase:pbase + M, col:col + 1], w32[:, kc, 0:M], vec[:, kc:kc + 1],
                start=(kc == 0), stop=(kc == 7), skip_group_check=True),
                [w32_t, vec_t], [tbt], inc=(kc == 7), acc=(kc > 0))

    def t0_mode(self, li):
        last_gla = max([l for l in range(self.nlayers) if l % 2 == 0], default=-1)
        if li < last_gla:
            return "full"
        if li == last_gla:
            return "heads"
        return "none"

    def t0_layer(self, li):
        nc = self.nc
        j = li // 2
        self.t0_prev_pos = None
        assert not self.t0q
        mode = self.t0_mode(li)
        if mode == "none":
            return
        tb = self.ps[self.tb]
        tbt = self.ps_t[self.tb]
        s = self.t0s
        st = self.t0s_t
        T = self.T
        X = mybir.AxisListType.X
        if T0S < 2:
            return
        c = self.tcol()
        T(self.dve, lambda: nc.vector.tensor_tensor(s[:, 8:16], self.x0T[:], self.x0T[:], ALU.mult), [self.x0_t], [st])
        T(self.dve, lambda: nc.vector.tensor_reduce(out=s[:, 0:1], in_=s[:, 8:16], axis=X, op=ALU.add), [st], [st])
        T(self.pe, lambda: nc.tensor.matmul(tb[:, c:c + 1], self.ones32[:], s[:, 0:1], start=True, stop=True,
                                            skip_group_check=True), [st, self.const_t], [tbt])
        T(self.act, lambda: nc.scalar.activation(out=s[:, 1:2], in_=tb[:, c:c + 1], func=AF.Ln, scale=1.0 / D, bias=EPS),
          [tbt], [st])
        T(self.act, lambda: nc.scalar.activation(out=s[:, 1:2], in_=s[:, 1:2], func=AF.Exp, scale=-0.5), [st], [st])
        T(self.dve, lambda: nc.vector.scalar_tensor_tensor(
            self.h0T[:], self.x0T[:], s[:, 1:2], self.normw[:, li, :], ALU.mult, ALU.mult),
          [self.x0_t, st, self.const_t], [self.h0_t])
        hv, ht = self.h0T, self.h0_t
        if T0S < 3:
            return
        if li % 2 == 0:
            wname, qm0, g0 = "wina", 1552, 1808
            for h in range(4):
                c = self.tcol(8)
                self.t0_cproj(hv, ht, wname, j, 96 * h, 96, c)
                self.t0_cproj(hv, ht, wname, j, 384 + 96 * h, 96, c + 1)
                T(self.act, lambda c=c: nc.scalar.copy(s[0:96, 2:3], tb[0:96, c:c + 1]), [tbt], [st])
                T(self.dve, lambda c=c: nc.vector.tensor_tensor(s[0:96, 3:4], s[0:96, 2:3], tb[0:96, c + 1:c + 2], ALU.mult),
                  [tbt, st], [st])
                T(self.pe, lambda c=c: nc.tensor.matmul(tb[:, c + 2:c + 3], self.ones32[0:96, :], s[0:96, 3:4],
                                                        start=True, stop=True, skip_group_check=True),
                  [st, self.const_t], [tbt])
                f0 = 192 * h
                if h % 2 == 0:
                    cfull, chalf, phalf, fs0, hs0 = f0 // 128, f0 // 128 + 1, 0, 0, 128
                else:
                    chalf, cfull, phalf, hs0, fs0 = f0 // 128, f0 // 128 + 1, 64, 0, 64
                pr = slice(phalf, phalf + 64)
                self.t0_cproj(hv, ht, wname, j, 768 + f0 + fs0, 128, c + 3)
                self.t0_cproj(hv, ht, wname, j, 768 + f0 + hs0, 64, c + 4, pbase=phalf)
                T(self.dve, lambda: nc.vector.memset(s[:, 4:6], 0.0), [st], [st])
                T(self.act, lambda c=c: nc.scalar.copy(s[:, 4:5], tb[:, c + 3:c + 4]), [tbt], [st])
                T(self.act, lambda c=c, pr=pr: nc.scalar.copy(s[pr, 5:6], tb[pr, c + 4:c + 5]), [tbt], [st])
                T(self.dve, lambda: nc.vector.tensor_tensor(s[:, 16:18], s[:, 4:6], s[:, 4:6], ALU.mult), [st], [st])
                T(self.dve, lambda: nc.vector.tensor_reduce(out=s[:, 6:7], in_=s[:, 16:18], axis=X, op=ALU.add), [st], [st])
                T(self.pe, lambda c=c: nc.tensor.matmul(tb[:, c + 5:c + 6], self.ones32[:], s[:, 6:7],
                                                        start=True, stop=True, skip_group_check=True),
                  [st, self.const_t], [tbt])
                T(self.act, lambda c=c: nc.scalar.activation(out=s[:, 7:8], in_=tb[:, c + 2:c + 3], func=AF.Copy,
                                                             scale=96.0 ** -0.5), [tbt], [st])
                T(self.dve, lambda: nc.vector.tensor_tensor(s[:, 18:19], s[:, 7:8], s[:, 7:8], ALU.mult), [st], [st])
                T(self.dve, lambda c=c: nc.vector.tensor_tensor(s[:, 18:19], s[:, 18:19], tb[:, c + 5:c + 6], ALU.mult),
                  [st, tbt], [st])
                T(self.act, lambda: nc.scalar.activation(out=s[:, 18:19], in_=s[:, 18:19], func=AF.Ln,
                                                         scale=1.0 / 192, bias=EPS), [st], [st])
                T(self.act, lambda: nc.scalar.activation(out=s[:, 18:19], in_=s[:, 18:19], func=AF.Exp, scale=-0.5),
                  [st], [st])
                T(self.dve, lambda: nc.vector.tensor_tensor(s[:, 19:20], s[:, 7:8], s[:, 18:19], ALU.mult), [st], [st])
                T(self.dve, lambda cfull=cfull: nc.vector.scalar_tensor_tensor(
                    self.m0T[:, cfull:cfull + 1], s[:, 4:5], s[:, 19:20], self.gnw[:, 6 * j + cfull:6 * j + cfull + 1],
                    ALU.mult, ALU.mult), [st, self.const_t], [self.m0_t])
                T(self.dve, lambda chalf=chalf, pr=pr: nc.vector.scalar_tensor_tensor(
                    self.m0T[pr, chalf:chalf + 1], s[pr, 5:6], s[pr, 19:20], self.gnw[pr, 6 * j + chalf:6 * j + chalf + 1],
                    ALU.mult, ALU.mult), [st, self.const_t], [self.m0_t])
        else:
            wname, qm0, g0 = "winb", 6912, 7168
            for h in range(6):
                for g in range(3):
                    base = g * 2304
                    c = self.tcol(4)
                    self.t0_cproj(hv, ht, wname, j, base + 128 * h, 128, c)
                    self.t0_cproj(hv, ht, wname, j, base + 768 + 128 * h, 128, c + 1)
                    self.t0_cproj(hv, ht, wname, j, base + 1536 + 128 * h, 128, c + 2)
                    T(self.act, lambda c=c: nc.scalar.copy(s[:, 2:3], tb[:, c:c + 1]), [tbt], [st])
                    T(self.dve, lambda c=c: nc.vector.tensor_tensor(s[:, 3:4], s[:, 2:3], tb[:, c + 1:c + 2], ALU.mult),
                      [tbt, st], [st])
                    T(self.pe, lambda c=c: nc.tensor.matmul(tb[:, c + 3:c + 4], self.ones32[:], s[:, 3:4],
                                                            start=True, stop=True, skip_group_check=True),
                      [st, self.const_t], [tbt])
                    T(self.act, lambda c=c, g=g: nc.scalar.activation(out=s[:, 8 + g:9 + g], in_=tb[:, c + 3:c + 4],
                                                                      func=AF.Exp, scale=128.0 ** -0.5), [tbt], [st])
                    T(self.act, lambda c=c, g=g: nc.scalar.copy(s[:, 12 + g:13 + g], tb[:, c + 2:c + 3]), [tbt], [st])
                T(self.dve, lambda: nc.vector.tensor_reduce(out=s[:, 16:17], in_=s[:, 8:11], axis=X, op=ALU.add), [st], [st])
                T(self.dve, lambda: nc.vector.reciprocal(s[:, 16:17], s[:, 16:17]), [st], [st])
                T(self.dve, lambda: nc.vector.tensor_tensor(s[:, 20:23], s[:, 8:11], s[:, 12:15], ALU.mult), [st], [st])
                T(self.dve, lambda: nc.vector.tensor_reduce(out=s[:, 17:18], in_=s[:, 20:23], axis=X, op=ALU.add), [st], [st])
                T(self.dve, lambda h=h: nc.vector.tensor_tensor(self.m0T[:, h:h + 1], s[:, 17:18], s[:, 16:17], ALU.mult),
                  [st], [self.m0_t])
        if T0S < 4 or mode == "heads":
            return
        c = self.tcol(2)
        for c2 in range(2):
            self.t0_cproj(hv, ht, wname, j, qm0 + 128 * c2, 128, c + c2)
        T(self.act, lambda c=c: nc.scalar.copy(s[:, 24:26], tb[:, c:c + 2]), [tbt], [st])
        c = self.tcol(8)
        for hm in range(4):
            c2, half = hm // 2, hm % 2
            pr = slice(64 * half, 64 * half + 64)
            for kt in range(2):
                T(self.pe, lambda c=c, hm=hm, kt=kt, c2=c2, pr=pr: nc.tensor.matmul(
                    tb[:, c + 2 * hm + kt:c + 2 * hm + kt + 1], self.kmT32[pr, c2, kt * 128:(kt + 1) * 128],
                    s[pr, 24 + c2:25 + c2], start=True, stop=True, skip_group_check=True),
                  [self.km32_t, st], [tbt], inc=(hm == 3 and kt == 1), acc=not (hm == 0 and kt == 0))
        T(self.act, lambda c=c: nc.scalar.activation(out=self.e8[:], in_=tb[:, c:c + 8], func=AF.Exp, scale=0.125),
          [tbt], [st])
        c = self.tcol(4)
        first = True
        for hm in range(4):
            c2, half = hm // 2, hm % 2
            pr = slice(64 * half, 64 * half + 64)
            for kt in range(2):
                T(self.pe, lambda c=c, hm=hm, kt=kt, c2=c2, pr=pr: nc.tensor.matmul(
                    tb[pr, c + c2:c + c2 + 1], self.vm32[:, kt, 64 * hm:64 * hm + 64], self.e8[:, 2 * hm + kt:2 * hm + kt + 1],
                    start=(kt == 0), stop=(kt == 1), skip_group_check=True),
                  [self.vm32_t, st], [tbt], inc=False, acc=not first)
                first = False
            for kt in range(2):
                T(self.pe, lambda c=c, hm=hm, kt=kt, c2=c2, pr=pr: nc.tensor.matmul(
                    tb[pr, c + 2 + c2:c + 3 + c2], self.ones32[:, 0:64], self.e8[:, 2 * hm + kt:2 * hm + kt + 1],
                    start=(kt == 0), stop=(kt == 1), skip_group_check=True),
                  [self.const_t, st], [tbt], inc=(hm == 3 and kt == 1), acc=True)
        T(self.dve, lambda c=c: nc.vector.reciprocal(s[:, 26:28], tb[:, c + 2:c + 4]), [tbt], [st])
        T(self.dve, lambda c=c: nc.vector.tensor_tensor(self.m0T[:, 6:8], tb[:, c:c + 2], s[:, 26:28], ALU.mult),
          [tbt, st], [self.m0_t])
        if T0S < 5:
            return
        c = self.tcol(8)
        for cc in range(8):
            self.t0_cproj(hv, ht, wname, j, g0 + 128 * cc, 128, c + cc)
        T(self.act, lambda c=c: nc.scalar.activation(out=self.sg0[:], in_=tb[:, c:c + 8], func=AF.Silu), [tbt], [st])
        T(self.dve, lambda: nc.vector.tensor_tensor(self.m0T[:], self.m0T[:], self.sg0[:], ALU.mult),
          [self.m0_t, st], [self.m0_t])
        c = self.tcol(8)
        for cc in range(8):
            self.t0_cproj(self.m0T, self.m0_t, "wout", li, 128 * cc, 128, c + cc)
        T(self.dve, lambda c=c: nc.vector.tensor_tensor(self.x0T[:], self.x0T[:], tb[:, c:c + 8], ALU.add),
          [tbt, self.x0_t], [self.x0_t])

    def mixer_a(self, j):
        nc = self.nc
        win = self.wina_d[j]
        with ExitStack() as es:
            sbt = lambda name, shape, dtype: es.enter_context(nc.sbuf_tensor(self.un(name), shape, dtype))
            wgl = sbt("wgl", [128, 8, 16], BF16)
            wgl_t = Trk()
            glT = sbt("glT", [16, 512], F32)
            glT_t = Trk()
            sp = sbt("sp", [96, 512], F32)
            cc_ = sbt("cc", [96, 512], F32)
            eb = sbt("eb", [96, 512], F32)
            enb = sbt("enb", [96, 512], F32)
            g_t = Trk()
            qinT = sbt("qinT", [96, 512], BF16)
            kinT = sbt("kinT", [96, 512], BF16)
            qk_t = Trk()
            kin_tok = sbt("kin_tok", [128, 4, 96], BF16)
            kin_tok_t = Trk()
            V = sbt("Vh", [128, 4, 192], BF16)
            V_t = Trk()
            R = [[sbt(f"R{h}_{i}", [96, 192], F32) for i in range(2)] for h in range(4)]
            R_t = [[Trk() for i in range(2)] for h in range(4)]
            Sbf = [sbt(f"Sbf{i}", [96, 192], BF16) for i in range(4)]
            Sbf_t = [Trk() for _ in range(4)]
            ATm = [sbt(f"ATm{i}", [128, 128], BF16) for i in range(2)]
            ATm_t = [Trk() for _ in range(2)]
            ssh = sbt("ssh", [128, 4], F32)
            ssh_t = [Trk() for _ in range(4)]
            mixt = [sbt(f"mixt{i}", [128, 192], BF16) for i in range(2)]
            mixt_t = [Trk() for _ in range(2)]
            dec = sbt("dec", [96, 4], F32)
            dec_t = [Trk() for _ in range(4)]
            self.junk = sbt("junkA", [128, 192], BF16)
            self.junk_t = Trk()
            wgu = sbt("wgu", [16, 384], F32)
            rmask = sbt("rmask", [128, 512], BF16)
            cl_t = Trk()
            self.dma(self.qsp, wgu[:], self.wgu_d[j], writes=[cl_t])
            self.dma(self.qpl, rmask[:], self.c_rmask_d, writes=[cl_t])
            self.dma(self.qpl, wgl[:], win[:, 1536:1552].rearrange("(kc p) c -> p kc c", p=128), writes=[wgl_t])
            sk = 0
            ak = 0
            mk = 0
            for T in range(4):
                cols = slice(T * 512, (T + 1) * 512)
                bank = self.ps_alloc()
                for kc in range(8):
                    self.mm(self.ps[bank][0:16, :], wgl[:, kc, :], self.hT[:, kc, cols], start=(kc == 0),
                            stop=(kc == 7), reads=[wgl_t, self.hT_t[T]], writes=[self.ps_t[bank]], inc=(kc == 7))
                self.op(self.act, lambda bank=bank: nc.scalar.copy(glT[:], self.ps[bank][0:16, :]),
                        reads=[self.ps_t[bank]], writes=[glT_t])
                self.ps_release(bank)
                for h in range(4):
                    s, st = self.wnext([("wina", j, 96 * h, 96, 0), ("wina", j, 384 + 96 * h, 96, 96),
                                        ("wina", j, 768 + 192 * h, 192, 192)])
                    bank = self.ps_alloc()
                    self.mm(self.ps[bank][0:96, :], wgu[:, 96 * h:96 * h + 96], glT[:], start=True, stop=True,
                            reads=[cl_t, glT_t], writes=[self.ps_t[bank]], inc=True)
                    self.op(self.act, lambda bank=bank, h=h: nc.scalar.activation(
                        out=sp[:], in_=self.ps[bank][0:96, :], func=AF.Exp, scale=-1.0,
                        bias=self.nbg[:, 4 * j + h:4 * j + h + 1]),
                        reads=[self.ps_t[bank], self.const_t], writes=[g_t])
                    self.ps_release(bank)
                    self.op(self.act, lambda: nc.scalar.activation(out=sp[:], in_=sp[:], func=AF.Ln, bias=1.0),
                            reads=[g_t], writes=[g_t])
                    self.op(self.dve, lambda: nc.vector.tensor_tensor_scan(
                        cc_[:], rmask[0:96, :], sp[:], 0.0, ALU.mult, ALU.add),
                        reads=[g_t, cl_t], writes=[g_t])
                    self.op(self.act, lambda: nc.scalar.activation(out=eb[:], in_=cc_[:], func=AF.Exp, scale=-1.0 / 16),
                            reads=[g_t], writes=[g_t])
                    self.op(self.act, lambda: nc.scalar.activation(out=enb[:], in_=cc_[:], func=AF.Exp, scale=1.0 / 16),
                            reads=[g_t], writes=[g_t])
                    bank = self.ps_alloc()
                    self.proj_fm(bank, 96, s, st, 0, T)
                    self.op(self.dve, lambda bank=bank: nc.vector.scalar_tensor_tensor(
                        qinT[:], self.ps[bank][0:96, :], 96.0 ** -0.5, eb[:], ALU.mult, ALU.mult),
                        reads=[self.ps_t[bank], g_t], writes=[qk_t])
                    self.ps_release(bank)
                    bank = self.ps_alloc()
                    self.proj_fm(bank, 96, s, st, 96, T)
                    self.op(self.dve, lambda bank=bank: nc.vector.tensor_tensor(
                        kinT[:], self.ps[bank][0:96, :], enb[:], ALU.mult),
                        reads=[self.ps_t[bank], g_t], writes=[qk_t])
                    self.ps_release(bank)
                    for sub in range(4):
                        tt = 4 * T + sub
                        bank = self.ps_alloc()
                        out = self.ps[bank][:, 0:192]
                        for kc in range(8):
                            self.mm(out, self.hT[:, kc, tt * 128:(tt + 1) * 128], self.wslot[s][:, kc, 192:384],
                                    start=(kc == 0), stop=(kc == 7), reads=[st, self.hT_t[T]],
                                    writes=[self.ps_t[bank]], inc=(kc == 7))
                        self.op(self.act, lambda sub=sub, out=out: nc.scalar.copy(V[:, sub, :], out),
                                reads=[self.ps_t[bank]], writes=[V_t])
                        self.ps_release(bank)
                    bank = self.ps_alloc()
                    pb = self.ps[bank][:].bitcast(BF16)
                    for sub in range(4):
                        self.op(self.pe, lambda sub=sub, pb=pb: nc.tensor.transpose(
                            pb[:, sub * 96:(sub + 1) * 96], kinT[:, sub * 128:(sub + 1) * 128], self.ident[0:96, 0:96]),
                            reads=[qk_t, self.const_t], writes=[self.ps_t[bank]], inc=(sub == 3), acc=(sub > 0))
                    self.op(self.act, lambda pb=pb: nc.scalar.copy(
                        kin_tok[:].rearrange("p a b -> p (a b)"), pb[:, 0:384]),
                        reads=[self.ps_t[bank]], writes=[kin_tok_t])
                    self.ps_release(bank)
                    def front(sub):
                        nonlocal ak, sk
                        tt = 4 * T + sub
                        tc_ = slice(sub * 128, (sub + 1) * 128)
                        bA = self.ps_alloc()
                        self.mm(self.ps[bA][:, 0:128], kinT[:, tc_], qinT[:, tc_], start=True, stop=True,
                                reads=[qk_t], writes=[self.ps_t[bA]], inc=True)
                        a = ak % 2
                        ak += 1
                        self.op(self.dve, lambda a=a, bA=bA: nc.vector.tensor_tensor(
                            ATm[a][:], self.ps[bA][:, 0:128], self.gmask[:], ALU.mult),
                            reads=[self.ps_t[bA], self.const_t], writes=[ATm_t[a]])
                        self.ps_release(bA)
                        sbs = [None, None]
                        for ch in range(2):
                            c = 2 * tt + ch
                            pr = slice(64 * ch, 64 * ch + 64)
                            lc = 64 * (2 * sub + ch)
                            dk = None
                            if c > 0:
                                dk = eb[:, lc - 1:lc] if lc > 0 else dec[:, h:h + 1]
                                dk_t = g_t if lc > 0 else dec_t[h]
                                sb_ = sk % 4
                                sk += 1
                                sbs[ch] = sb_
                                rp = (c - 1) % 2
                                self.op(self.act, lambda sb_=sb_, dk=dk, rp=rp: nc.scalar.activation(
                                    out=Sbf[sb_][:], in_=R[h][rp][:], func=AF.Copy, scale=dk),
                                    reads=[R_t[h][rp], dk_t], writes=[Sbf_t[sb_]])
                            bU = self.ps_alloc()
                            self.mm(self.ps[bU][0:96, 0:192], kin_tok[pr, sub, :], V[pr, sub, :], start=True, stop=True,
                                    reads=[kin_tok_t, V_t], writes=[self.ps_t[bU]], inc=True)
                            rn = c % 2
                            if c == 0:
                                self.op(self.dve, lambda bU=bU, rn=rn: nc.vector.tensor_copy(
                                    R[h][rn][:], self.ps[bU][0:96, 0:192]),
                                    reads=[self.ps_t[bU]], writes=[R_t[h][rn]])
                            else:
                                self.op(self.dve, lambda bU=bU, dk=dk, rn=rn: nc.vector.scalar_tensor_tensor(
                                    R[h][rn][:], R[h][1 - rn][:], dk, self.ps[bU][0:96, 0:192], ALU.mult, ALU.add),
                                    reads=[self.ps_t[bU], R_t[h][1 - rn], dk_t], writes=[R_t[h][rn]])
                            self.ps_release(bU)
                        return a, sbs

                    def back(sub, a, sbs):
                        nonlocal mk
                        tt = 4 * T + sub
                        bO = self.ps_alloc()
                        self.mm(self.ps[bO][:, 0:192], ATm[a][:], V[:, sub, :], start=True, stop=False,
                                reads=[ATm_t[a], V_t], writes=[self.ps_t[bO]], inc=False)
                        for ch in range(2):
                            pr = slice(64 * ch, 64 * ch + 64)
                            lc = 64 * (2 * sub + ch)
                            sb_ = sbs[ch]
                            if sb_ is not None:
                                self.mm(self.ps[bO][pr, 0:192], qinT[:, lc:lc + 64], Sbf[sb_][:], start=False,
                                        stop=(ch == 1), reads=[qk_t, Sbf_t[sb_]], writes=[self.ps_t[bO]],
                                        inc=(ch == 1))
                        m = mk % 2
                        mk += 1
                        self.op(self.act, lambda bO=bO: nc.scalar.activation(
                            out=self.junk[:, 0:192], in_=self.ps[bO][:, 0:192], func=AF.Square,
                            accum_out=ssh[:, h:h + 1]),
                            reads=[self.ps_t[bO]], writes=[ssh_t[h], self.junk_t])
                        self.op(self.act, lambda: nc.scalar.activation(
                            out=ssh[:, h:h + 1], in_=ssh[:, h:h + 1], func=AF.Ln, scale=1.0 / 192, bias=EPS),
                            reads=[ssh_t[h]], writes=[ssh_t[h]])
                        self.op(self.act, lambda: nc.scalar.activation(
                            out=ssh[:, h:h + 1], in_=ssh[:, h:h + 1], func=AF.Exp, scale=-0.5),
                            reads=[ssh_t[h]], writes=[ssh_t[h]])
                        self.op(self.act, lambda bO=bO, m=m: nc.scalar.activation(
                            out=mixt[m][:], in_=self.ps[bO][:, 0:192], func=AF.Copy, scale=ssh[:, h:h + 1]),
                            reads=[self.ps_t[bO], ssh_t[h]], writes=[mixt_t[m]])
                        self.ps_release(bO)
                        f0 = 192 * h
                        if h % 2 == 0:
                            cfull, chalf, phalf = f0 // 128, f0 // 128 + 1, 0
                            full_src, half_src = slice(0, 128), slice(128, 192)
                        else:
                            chalf, cfull, phalf = f0 // 128, f0 // 128 + 1, 64
                            half_src, full_src = slice(0, 64), slice(64, 192)
                        bT = self.ps_alloc()
                        pb = self.ps[bT][:].bitcast(BF16)
                        self.op(self.pe, lambda m=m, pb=pb, full_src=full_src: nc.tensor.transpose(
                            pb[:, 0:128], mixt[m][:, full_src], self.ident[:]),
                            reads=[mixt_t[m], self.const_t], writes=[self.ps_t[bT]], inc=False)
                        self.op(self.pe, lambda m=m, pb=pb, half_src=half_src, phalf=phalf: nc.tensor.transpose(
                            pb[phalf:phalf + 64, 128:256], mixt[m][:, half_src], self.ident[:]),
                            reads=[mixt_t[m], self.const_t], writes=[self.ps_t[bT]], inc=True, acc=True)
                        tcol = slice(tt * 128, (tt + 1) * 128)
                        self.op(self.dve, lambda pb=pb, cfull=cfull, tcol=tcol: nc.vector.tensor_scalar(
                            self.preT[:, cfull, tcol], pb[:, 0:128], self.gnw[:, 6 * j + cfull:6 * j + cfull + 1],
                            None, ALU.mult),
                            reads=[self.ps_t[bT], self.const_t], writes=[self.preT_t[cfull][T]])
                        self.op(self.dve, lambda pb=pb, chalf=chalf, phalf=phalf, tcol=tcol: nc.vector.tensor_scalar(
                            self.preT[phalf:phalf + 64, chalf, tcol], pb[phalf:phalf + 64, 128:256],
                            self.gnw[phalf:phalf + 64, 6 * j + chalf:6 * j + chalf + 1], None, ALU.mult),
                            reads=[self.ps_t[bT], self.const_t], writes=[self.preT_t[chalf][T]])
                        self.ps_release(bT)

                    nxt = front(0)
                    for sub in range(4):
                        cur = nxt
                        if sub < 3:
                            nxt = front(sub + 1)
                        back(sub, *cur)
                    self.op(self.dve, lambda h=h: nc.vector.tensor_copy(dec[:, h:h + 1], eb[:, 511:512]),
                            reads=[g_t], writes=[dec_t[h]])
            self.barrier()

    def mixer_b(self, j):
        nc = self.nc
        win = self.winb_d[j]
        with ExitStack() as es:
            sbt = lambda name, shape, dtype: es.enter_context(nc.sbuf_tensor(self.un(name), shape, dtype))
            qkT = sbt("qkT", [128, 2, SEQ], BF16)
            qkT_t = [Trk() for _ in range(4)]
            Vb = sbt("Vb", [128, 16, 128], BF16)
            Vb_t = [Trk() for _ in range(4)]
            qkt = [sbt(f"qkt{i}", [128, 2, 128], BF16) for i in range(3)]
            qkt_t = [Trk() for _ in range(3)]
            ta = sbt("ropeA", [128, 2, 2, 16], F32)
            tb = sbt("ropeB", [128, 2, 2, 16], F32)
            rope_t = Trk()
            pT = [sbt(f"pT{i}", [128, 256], BF16) for i in range(6)]
            pT_t = [Trk() for _ in range(6)]
            accN = sbt("accN", [128, SEQ], F32)
            accD = sbt("accD", [128, SEQ], F32)
            acc_t = [Trk() for _ in range(4)]
            pk = 0
            qk_i = 0
            carry = []
            for h in range(6):
                for g in range(3):
                    r = DIL[g]
                    nbp = 16 // r
                    base = g * 2304
                    s, st = self.wnext([("winb", j, base + 128 * h, 128, 0), ("winb", j, base + 768 + 128 * h, 128, 128),
                                        ("winb", j, base + 1536 + 128 * h, 128, 256)])
                    pend_tr = []
                    for b in range(16):
                        phase, jb = b // nbp, b % nbp
                        t0 = r * 128 * jb + phase
                        tsl = slice(t0, t0 + 127 * r + 1, r) if r > 1 else slice(t0, t0 + 128)
                        if r == 1:
                            hts = [self.hT_t[jb // 4]]
                        elif r == 4:
                            hts = [self.hT_t[jb]]
                        else:
                            hts = self.hT_t
                        bank = self.ps_alloc()
                        out = self.ps[bank][:, 0:384]
                        for kc in range(8):
                            self.mm(out, self.hT[:, kc, tsl], self.wslot[s][:, kc, :], start=(kc == 0), stop=(kc == 7),
                                    reads=[st] + list(hts), writes=[self.ps_t[bank]], inc=(kc == 7))
                        ps3 = out.rearrange("p (a d) -> p a d", a=3)
                        qi = qk_i % 3
                        qk_i += 1
                        rt = [self.ps_t[bank], self.const_t]
                        X = ps3[:, 0:2, 0:32].rearrange("p a (u d) -> p a u d", u=2)
                        Cb = self.cos[:, g, b, :].unsqueeze(1).unsqueeze(1).to_broadcast([128, 2, 2, 16])
                        Sb = self.sin[:, g, b, :].unsqueeze(1).unsqueeze(1).to_broadcast([128, 2, 2, 16])
                        self.op(self.dve, lambda X=X, Cb=Cb: nc.vector.tensor_tensor(ta[:], X, Cb, ALU.mult),
                                reads=rt, writes=[rope_t])
                        self.op(self.dve, lambda X=X, Sb=Sb: nc.vector.tensor_tensor(tb[:], X, Sb, ALU.mult),
                                reads=rt, writes=[rope_t])
                        self.op(self.dve, lambda qi=qi: nc.vector.tensor_tensor(
                            qkt[qi][:, :, 0:16], ta[:, :, 0, :], tb[:, :, 1, :], ALU.subtract),
                            reads=[rope_t], writes=[qkt_t[qi]])
                        self.op(self.dve, lambda qi=qi: nc.vector.tensor_tensor(
                            qkt[qi][:, :, 16:32], ta[:, :, 1, :], tb[:, :, 0, :], ALU.add),
                            reads=[rope_t], writes=[qkt_t[qi]])
                        self.op(self.act, lambda b=b, ps3=ps3: nc.scalar.copy(Vb[:, b, :], ps3[:, 2, :]),
                                reads=[self.ps_t[bank]], writes=[Vb_t[b // 4]])
                        self.op(self.act, lambda qi=qi, ps3=ps3: nc.scalar.copy(
                            qkt[qi][:, :, 32:128], ps3[:, 0:2, 32:128]),
                            reads=[self.ps_t[bank]], writes=[qkt_t[qi]])
                        self.ps_release(bank)
                        def emit_tr(b=b, qi=qi):
                            if b % 4 == 0:
                                self.bankT = self.ps_alloc()
                            bankT = self.bankT
                            pb = self.ps[bankT][:].bitcast(BF16).rearrange("p (a t) -> p a t", a=2)
                            for a in range(2):
                                self.op(self.pe, lambda a=a, pb=pb, qi=qi, b=b: nc.tensor.transpose(
                                    pb[:, a, (b % 4) * 128:(b % 4 + 1) * 128], qkt[qi][:, a, :], self.ident[:]),
                                    reads=[qkt_t[qi], self.const_t], writes=[self.ps_t[bankT]],
                                    inc=(a == 1), acc=not (b % 4 == 0 and a == 0))
                            if b % 4 == 3:
                                b0 = b - 3
                                self.op(self.act, lambda pb=pb, b0=b0: nc.scalar.copy(
                                    qkT[:, :, b0 * 128:(b0 + 4) * 128], pb),
                                    reads=[self.ps_t[bankT]], writes=[qkT_t[b // 4]])
                                self.ps_release(bankT)
                        pend_tr.append(emit_tr)
                        if len(pend_tr) > 2:
                            pend_tr.pop(0)()
                        if carry and b % 3 == 2:
                            carry.pop(0)()
                    while pend_tr:
                        pend_tr.pop(0)()
                    while carry:
                        carry.pop(0)()
                    if KSTAGE <= 2:
                        continue
                    bn = {}
                    bd = {}
                    started = set()
                    pend_pv = []
                    for kb in range(16):
                        phase, jb = kb // nbp, kb % nbp
                        has_next = jb < nbp - 1
                        N = 256 if has_next else 128
                        m = kb // 4
                        if m not in bn:
                            bn[m] = self.ps_alloc()
                            bd[m] = self.ps_alloc()
                        if has_next and (kb + 1) // 4 not in bn:
                            bn[m + 1] = self.ps_alloc()
                            bd[m + 1] = self.ps_alloc()
                        bs = self.ps_alloc()
                        qts = [qkT_t[kb // 4]] + ([qkT_t[(kb + 1) // 4]] if has_next else [])
                        self.mm(self.ps[bs][:, 0:N], qkT[:, 1, kb * 128:(kb + 1) * 128], qkT[:, 0, kb * 128:kb * 128 + N],
                                start=True, stop=False, reads=qts, writes=[self.ps_t[bs]], inc=False)
                        self.mm(self.ps[bs][:, 0:N], self.ident[:], self.maskb[:, 0:N],
                                start=False, stop=True, reads=[self.const_t], writes=[self.ps_t[bs]], inc=True)
                        p = pk % 6
                        pk += 1
                        self.op(self.act, lambda p=p, bs=bs, N=N: nc.scalar.activation(
                            out=pT[p][:, 0:N], in_=self.ps[bs][:, 0:N], func=AF.Exp, scale=128.0 ** -0.5),
                            reads=[self.ps_t[bs]], writes=[pT_t[p]])
                        self.ps_release(bs)
                        def emit_pv(kb=kb, has_next=has_next, m=m, p=p):
                            if has_next and (kb + 1) // 4 == m:
                                segs = [(m, (kb % 4) * 128, 0, 256)]
                            elif has_next:
                                segs = [(m, (kb % 4) * 128, 0, 128), (m + 1, 0, 128, 128)]
                            else:
                                segs = [(m, (kb % 4) * 128, 0, 128)]
                            for si_, (mm_, oc, pc, n) in enumerate(segs):
                                first = mm_ not in started
                                started.add(mm_)
                                last = (si_ == len(segs) - 1)
                                self.mm(self.ps[bn[mm_]][:, oc:oc + n], Vb[:, kb, :], pT[p][:, pc:pc + n], start=first, stop=False,
                                        reads=[Vb_t[kb // 4], pT_t[p]], writes=[self.ps_t[bn[mm_]]], inc=False)
                                self.mm(self.ps[bd[mm_]][:, oc:oc + n], self.ones[:], pT[p][:, pc:pc + n], start=first, stop=False,
                                        reads=[pT_t[p]], writes=[self.ps_t[bd[mm_]]], inc=last)
                            if kb % 4 == 3 and KSTAGE <= 3:
                                self.ps_release(bn[m])
                                self.ps_release(bd[m])
                            elif kb % 4 == 3:
                                if r == 1:
                                    dN, dD = accN[:, m * 512:(m + 1) * 512], accD[:, m * 512:(m + 1) * 512]
                                    sN, sD = self.ps[bn[m]][:, :], self.ps[bd[m]][:, :]
                                    trks = [acc_t[m]]
                                elif r == 4:
                                    dN = accN[:, m:SEQ:4]
                                    dD = accD[:, m:SEQ:4]
                                    sN, sD = self.ps[bn[m]][:, :], self.ps[bd[m]][:, :]
                                    trks = acc_t
                                else:
                                    dN = accN[:].rearrange("d (s ph) -> d ph s", ph=16)[:, 4 * m:4 * m + 4, :]
                                    dD = accD[:].rearrange("d (s ph) -> d ph s", ph=16)[:, 4 * m:4 * m + 4, :]
                                    sN = self.ps[bn[m]][:, :].rearrange("d (ph s) -> d ph s", ph=4)
                                    sD = self.ps[bd[m]][:, :].rearrange("d (ph s) -> d ph s", ph=4)
                                    trks = acc_t
                                def do_acc(dN=dN, dD=dD, sN=sN, sD=sD, trks=trks, bnm=bn[m], bdm=bd[m], g=g):
                                    if g == 0:
                                        self.op(self.act, lambda: nc.scalar.copy(dN, sN),
                                                reads=[self.ps_t[bnm]], writes=trks)
                                        self.op(self.act, lambda: nc.scalar.copy(dD, sD),
                                                reads=[self.ps_t[bdm]], writes=trks)
                                    else:
                                        self.op(self.dve, lambda: nc.vector.tensor_tensor(dN, sN, dN, ALU.add),
                                                reads=[self.ps_t[bnm]] + trks, writes=trks)
                                        self.op(self.dve, lambda: nc.vector.tensor_tensor(dD, sD, dD, ALU.add),
                                                reads=[self.ps_t[bdm]] + trks, writes=trks)
                                    self.ps_release(bnm)
                                    self.ps_release(bdm)
                                if m == 3:
                                    carry.append(do_acc)
                                else:
                                    do_acc()
                        pend_pv.append(emit_pv)
                        if len(pend_pv) > 4:
                            pend_pv.pop(0)()
                    while pend_pv:
                        pend_pv.pop(0)()
                for T in range(4 if KSTAGE > 4 else 0):
                    def do_fin(T=T, h=h):
                        cols = slice(T * 512, (T + 1) * 512)
                        self.op(self.dve, lambda: nc.vector.reciprocal(accD[:, cols], accD[:, cols]),
                                reads=[acc_t[T]], writes=[acc_t[T]])
                        self.op(self.dve, lambda: nc.vector.tensor_tensor(
                            self.preT[:, h, cols], accN[:, cols], accD[:, cols], ALU.mult),
                            reads=[acc_t[T]], writes=[self.preT_t[h][T]])
                    carry.append(do_fin)
            while carry:
                carry.pop(0)()
            self.barrier()


def _consts():
    p = np.arange(128)
    ident = np.eye(128, dtype=np.float32)
    mask2 = np.zeros((128, 256), np.float32)
    mask2[:, 0:128] = (p[:, None] <= p[None, :])
    mask2[:, 128:256] = (p[:, None] >= p[None, :])
    gmask = ((p[:, None] // 64 == p[None, :] // 64) & (p[:, None] <= p[None, :])).astype(np.float32)
    rmask = np.ones((128, 512), np.float32)
    rmask[:, 0::64] = 0.0
    half = 16
    inv = (np.float32(500000.0) ** (-(np.arange(half, dtype=np.float32) / np.float32(half)))).astype(np.float32)
    cos = np.zeros((128, 3, 16, 16), np.float32)
    sin = np.zeros((128, 3, 16, 16), np.float32)
    for g, r in enumerate(DIL):
        nbp = 16 // r
        for b in range(16):
            phase, jb = b // nbp, b % nbp
            t = (r * (128 * jb + p) + phase).astype(np.float32)
            ang = (t[:, None] * inv[None, :]).astype(np.float32)
            cos[:, g, b, :] = np.cos(ang)
            sin[:, g, b, :] = np.sin(ang)
    return dict(c_ident=ident, c_ones=np.ones((128, 128), np.float32), c_mask2=mask2, c_maskb=((mask2 - 1.0) * 30000.0).astype(np.float32), c_gmask=gmask, c_rmask=rmask,
                c_cos=cos.reshape(128, -1), c_sin=sin.reshape(128, -1))


_NC_CACHE = {}


def _get_nc(nseq, nlayers, dbg=None):
    key = (nseq, nlayers, None if dbg is None else tuple(sorted(dbg)))
    if key not in _NC_CACHE:
        k = Kern(nseq, nlayers, dbg)
        _NC_CACHE[key] = k.build()
    return _NC_CACHE[key]


def _layout_params(mem_norm_w, norm_w, w_memkv, w_out, w_in_a, w_gate_up, b_gate, gla_norm_w, w_in_b, final_norm_w):
    f = lambda a: np.ascontiguousarray(np.asarray(a, dtype=np.float32))
    fm8 = lambda v: f(np.asarray(v).reshape(8, 128).T)
    d = {}
    d["memnw_fm"] = fm8(mem_norm_w)
    d["normw_fm"] = f(np.concatenate([fm8(norm_w[i]) for i in range(4)], axis=1))
    d["w_memkv"] = f(w_memkv)
    d["w_out"] = f(w_out)
    d["w_in_a"] = f(w_in_a)
    d["w_gate_up"] = f(w_gate_up)
    bg = np.asarray(b_gate)
    d["bg_fm"] = f(np.concatenate([bg[j].reshape(4, 96).T for j in range(2)], axis=1))
    gn = np.asarray(gla_norm_w)
    idx = (np.arange(768) % 192).reshape(6, 128).T
    d["gnw_fm"] = f(np.concatenate([gn[j][idx] for j in range(2)], axis=1))
    d["w_in_b"] = f(w_in_b)
    d["fnw_bc"] = f(np.broadcast_to(np.asarray(final_norm_w)[None, :], (128, D)))
    d.update(_consts())
    return d


def kernel(x, mem, mem_norm_w, norm_w, w_memkv, w_out, w_in_a, w_gate_up, b_gate, gla_norm_w, w_in_b,
           final_norm_w, _nlayers=4, _nseq_launch=4):
    x = np.asarray(x, dtype=np.float32)
    mem = np.asarray(mem, dtype=np.float32)
    B = x.shape[0]
    per_core = B // NCORES
    params = _layout_params(mem_norm_w, norm_w, w_memkv, w_out, w_in_a, w_gate_up, b_gate, gla_norm_w,
                            w_in_b, final_norm_w)
    out = np.empty_like(x)
    nseq = _nseq_launch
    nc = _get_nc(nseq, _nlayers)
    for s0 in range(0, per_core, nseq):
        in_maps = []
        for c in range(NCORES):
            b0 = c * per_core + s0
            m = dict(params)
            m["x"] = np.ascontiguousarray(x[b0:b0 + nseq])
            m["mem"] = np.ascontiguousarray(mem[b0:b0 + nseq])
            in_maps.append(m)
        res = run_bass_kernel_spmd(nc, in_maps, core_ids=list(range(NCORES)))
        for c in range(NCORES):
            b0 = c * per_core + s0
            out[b0:b0 + nseq] = np.asarray(res.results[c]["out"]).reshape(nseq, SEQ, D)
    return out
```

```python
import os
import numpy as np
from contextlib import ExitStack
import concourse.bass as bass
import concourse.mybir as mybir
from concourse.bass_utils import run_bass_kernel_spmd

F32 = mybir.dt.float32
BF16 = mybir.dt.bfloat16
AF = mybir.ActivationFunctionType
ALU = mybir.AluOpType

NCORES = 8
SEQ = 2048
D = 1024
NMEM = 256
DIL = [1, 4, 16]
EPOCH = 12000
NSLOT = 3
SLOTW = 384
EPS = 1e-6
KSTAGE = int(os.environ.get('KSTAGE', '9'))
KSUB = int(os.environ.get('KSUB', '9'))
T0S = int(os.environ.get('T0S', '9'))


class Trk:
    __slots__ = ("w", "r", "rd", "excl")

    def __init__(self, excl=False):
        self.excl = excl
        self.w = None
        self.r = {}
        self.rd = []


class Eng:
    def __init__(self, name, h):
        self.name = name
        self.h = h
        self.count = 0
        self.pending = False
        self.sems = []
        self.waited = {}


class Queue:
    def __init__(self, name, h, nslots):
        self.name = name
        self.h = h
        self.sems = []
        self.uses = [0] * nslots
        self.k = 0
        self.waited = {}


class Kern:
    def __init__(self, nseq, nlayers, dbg=None):
        self.nseq = nseq
        self.nlayers = nlayers
        self.dry = True
        self.jobs = []
        self.dbg = dbg

    def setup_engines(self, es):
        nc = self.nc
        self.pe = Eng("pe", nc.tensor)
        self.act = Eng("act", nc.scalar)
        self.dve = Eng("dve", nc.vector)
        self.engs = {"pe": self.pe, "act": self.act, "dve": self.dve}
        self.qsp = Queue("sp", nc.sync, 8)
        self.qpl = Queue("pool", nc.gpsimd, 8)
        if not self.dry:
            for e in self.engs.values():
                nep = self.est_counts[e.name] // EPOCH + 1
                for i in range(nep):
                    e.sems.append(es.enter_context(nc.semaphore(f"s_{e.name}{i}")))
            for q in (self.qsp, self.qpl):
                for i in range(len(q.uses)):
                    q.sems.append(es.enter_context(nc.semaphore(f"q_{q.name}{i}")))

    def _tok_sem(self, tok):
        if tok[0] == "d":
            return tok[1], tok[2]
        e = self.engs[tok[0]]
        c = tok[1]
        ep = (c - 1) // EPOCH
        return e.sems[ep], c - ep * EPOCH

    def _need(self, waiter, tok, out):
        if tok is None:
            return
        if tok[0] == "d":
            key = ("d", tok[3])
            if waiter.waited.get(key, 0) >= tok[2]:
                return
            if out.get(key, (None, 0))[1] < tok[2]:
                out[key] = (tok, tok[2])
        else:
            key = tok[0]
            if waiter.waited.get(key, 0) >= tok[1]:
                return
            if out.get(key, (None, 0))[1] < tok[1]:
                out[key] = (tok, tok[1])

    def _collect(self, waiter, reads, writes, acc, own):
        out = {}
        for t in reads:
            if t.w is not None and not (t.w[0] == own and own == "pe"):
                self._need(waiter, t.w, out)
            if t.excl:
                for en, c in t.r.items():
                    if en != own:
                        self._need(waiter, (en, c), out)
        if not acc:
            for t in writes:
                if t.w is not None and not (t.w[0] == own and own == "pe"):
                    self._need(waiter, t.w, out)
                for en, c in t.r.items():
                    if not (en == own and own == "pe"):
                        self._need(waiter, (en, c), out)
                for dt_ in t.rd:
                    self._need(waiter, dt_, out)
        return out

    def _emit_waits(self, waiter, h, need):
        for key, (tok, v) in need.items():
            waiter.waited[key] = v
            if not self.dry:
                sem, val = self._tok_sem(tok)
                h.wait_ge(sem, val)

    def op(self, eng, fn, reads=(), writes=(), inc=True, acc=False):
        need = self._collect(eng, reads, writes, acc, eng.name)
        self._emit_waits(eng, eng.h, need)
        if inc:
            eng.count += 1
            eng.pending = False
            c = eng.count
        else:
            eng.pending = True
            c = eng.count + 1
        if not self.dry:
            ins = fn()
            if inc:
                ep = (c - 1) // EPOCH
                ins.then_inc(eng.sems[ep], 1)
        tok = (eng.name, c)
        for t in reads:
            if t.r.get(eng.name, 0) < c:
                t.r[eng.name] = c
        for t in writes:
            if acc:
                t.w = tok
            else:
                t.w = tok
                t.r = {}
                t.rd = []
        return tok

    def dma(self, q, out_ap, in_ap=None, reads=(), writes=()):
        pairs = out_ap if in_ap is None else [(out_ap, in_ap)]
        need = self._collect(q, reads, writes, False, q.name)
        self._emit_waits(q, q.h, need)
        slot = q.k % len(q.uses)
        q.k += 1
        prev = q.uses[slot] * 16
        q.uses[slot] += len(pairs)
        val = q.uses[slot] * 16
        if not self.dry:
            sem = q.sems[slot]
            if prev > 0:
                q.h.wait_ge(sem, prev)
            for (o, i) in pairs:
                q.h.dma_start(out=o, in_=i).then_inc(sem, 16)
            tok = ("d", sem, val, (q.name, slot))
        else:
            tok = ("d", None, val, (q.name, slot))
        for t in reads:
            t.rd.append(tok)
        for t in writes:
            t.w = tok
            t.r = {}
            t.rd = []
        return tok

    def barrier(self):
        for e in self.engs.values():
            assert not e.pending, e.name
        waiters = [e for e in self.engs.values() if e.name != "pe"] + [self.qsp, self.qpl]
        for w in waiters:
            need = {}
            for o in self.engs.values():
                if o is w or o.count == 0:
                    continue
                self._need(w, (o.name, o.count), need)
            self._emit_waits(w, w.h, need)

    def un(self, name):
        self.uid += 1
        return f"{name}_{self.uid}"

    def ps_alloc(self):
        return self.ps_free.pop(0)

    def ps_release(self, i):
        self.ps_free.append(i)

    def wnext(self, pieces):
        if self.dry:
            self.jobs.append(pieces)
            i = len(self.jobs) - 1
            return i % NSLOT, self.wtrk[i % NSLOT]
        i = self.wjob
        self.wjob += 1
        while self.wissued < min(i + NSLOT, len(self.jobs)):
            j = self.wissued
            s = j % NSLOT
            pairs = []
            for (name, idx, sc0, n, dc0) in self.jobs[j]:
                src = getattr(self, name + "_d")[idx, :, sc0:sc0 + n]
                pairs.append((self.wslot[s][:, :, dc0:dc0 + n], src.rearrange("(kc p) c -> p kc c", p=128)))
            self.dma(self.qpl, pairs, writes=[self.wtrk[s]])
            self.wissued += 1
        return i % NSLOT, self.wtrk[i % NSLOT]

    def drip(self, n=1):
        while n > 0 and self.t0q:
            self.t0q.pop(0)()
            n -= 1

    def flush_t0(self):
        while self.t0q:
            self.t0q.pop(0)()

    def mm(self, out, lhsT, rhs, start, stop, reads, writes, inc):
        nc = self.nc
        self.mmc += 1
        if self.mmc % 3 == 0:
            self.drip()
        return self.op(self.pe,
                       lambda: nc.tensor.matmul(out, lhsT, rhs, start=start, stop=stop,
                                                skip_group_check=True),
                       reads=reads, writes=writes, inc=inc, acc=not start)

    def proj_fm(self, bank, M, slot, strk, c0, T, ncols=512, pbase=0):
        out = self.ps[bank][pbase:pbase + M, 0:ncols]
        for kc in range(8):
            self.mm(out, self.wslot[slot][:, kc, c0:c0 + M],
                    self.hT[:, kc, T * 512:T * 512 + ncols],
                    start=(kc == 0), stop=(kc == 7),
                    reads=[strk, self.hT_t[T]], writes=[self.ps_t[bank]], inc=(kc == 7))

    def build(self):
        self.dry = True
        self._build_once()
        self.est_counts = {n: e.count for n, e in self.engs.items()}
        jobs = self.jobs
        self.dry = False
        self.jobs = jobs
        self._build_once()
        return self.nc

    def _build_once(self):
        nseq = self.nseq
        nc = bass.Bass("TRN2", target_bir_lowering=False)
        self.nc = nc
        self.wjob = 0
        self.wissued = 0
        dt = lambda name, shape, kind="ExternalInput", dtype=F32: nc.dram_tensor(name, shape, dtype, kind=kind).ap()
        self.x_d = dt("x", [nseq, SEQ, D])
        self.mem_d = dt("mem", [nseq, NMEM, D])
        self.memnw_d = dt("memnw_fm", [128, 8])
        self.normw_d = dt("normw_fm", [128, 4 * 8])
        self.wmemkv_d = dt("w_memkv", [4, D, 512])
        self.wout_d = dt("w_out", [4, D, D])
        self.wina_d = dt("w_in_a", [2, D, 2832])
        self.wgu_d = dt("w_gate_up", [2, 16, 384])
        self.bg_d = dt("bg_fm", [96, 2 * 4])
        self.gnw_d = dt("gnw_fm", [128, 2 * 6])
        self.winb_d = dt("w_in_b", [2, D, 8192])
        self.fnw_d = dt("fnw_bc", [128, D])
        self.c_ident_d = dt("c_ident", [128, 128])
        self.c_ones_d = dt("c_ones", [128, 128])
        self.c_mask2_d = dt("c_mask2", [128, 256])
        self.c_maskb_d = dt("c_maskb", [128, 256])
        self.c_gmask_d = dt("c_gmask", [128, 128])
        self.c_rmask_d = dt("c_rmask", [128, 512])
        self.c_cos_d = dt("c_cos", [128, 3 * 16 * 16])
        self.c_sin_d = dt("c_sin", [128, 3 * 16 * 16])
        self.out_d = dt("out", [nseq, SEQ, D], kind="ExternalOutput")
        if self.dbg:
            self.dbg_d = {k: dt("dbg_" + k, shp, kind="ExternalOutput") for k, shp in self.dbg.items()}

        with ExitStack() as es:
            self.setup_engines(es)
            sb = lambda name, shape, dtype: es.enter_context(nc.sbuf_tensor(name, shape, dtype))
            self.x = sb("x_sb", [128, 16, D], F32)
            self.x_t = [Trk() for _ in range(16)]
            self.hT = sb("hT", [128, 8, SEQ], BF16)
            self.hT_t = [Trk() for _ in range(4)]
            self.preT = sb("preT", [128, 8, SEQ], BF16)
            self.preT_t = [[Trk() for _ in range(4)] for _ in range(8)]
            self.memT = sb("memT", [128, 8, NMEM], BF16)
            self.memT_t = Trk()
            self.kmT = sb("kmT", [128, 2, NMEM], BF16)
            self.kmT_t = Trk()
            self.vm = sb("vm", [128, 2, 256], BF16)
            self.vm_t = Trk()
            self.wslot = [sb(f"wslot{i}", [128, 8, SLOTW], BF16) for i in range(NSLOT)]
            self.wtrk = [Trk() for _ in range(NSLOT)]
            self.ps = [es.enter_context(nc.psum_tensor(f"ps{i}", [128, 512], F32)) for i in range(8)]
            self.ps_t = [Trk(excl=True) for _ in range(8)]
            self.ps_free = list(range(7))
            self.ident = sb("ident", [128, 128], BF16)
            self.mask2 = sb("mask2", [128, 256], BF16)
            self.maskb = sb("maskb", [128, 256], BF16)
            self.gmask = sb("gmask", [128, 128], BF16)
            self.ones = sb("ones", [128, 128], BF16)
            self.cos = sb("cos", [128, 3, 16, 16], F32)
            self.sin = sb("sin", [128, 3, 16, 16], F32)
            self.memnw = sb("memnw", [128, 8], F32)
            self.normw = sb("normw", [128, 4, 8], F32)
            self.bg = sb("bg", [96, 8], F32)
            self.nbg = sb("nbg", [96, 8], F32)
            self.gnw = sb("gnw", [128, 12], F32)
            self.const_t = Trk()
            self.ident32 = sb("ident32", [128, 128], F32)
            self.ones32 = sb("ones32", [128, 128], F32)
            self.w32 = [sb(f"w32_{i}", [128, 8, 128], F32) for i in range(2)]
            self.w32_t = [Trk() for _ in range(2)]
            self.w32_n = 0
            self.t0_prev_pos = None
            self.kmT32 = sb("kmT32", [128, 2, NMEM], F32)
            self.vm32 = sb("vm32", [128, 2, 256], F32)
            self.km32_t = Trk()
            self.vm32_t = Trk()
            self.x0T = sb("x0T", [128, 8], F32)
            self.h0T = sb("h0T", [128, 8], F32)
            self.m0T = sb("m0T", [128, 8], F32)
            self.sg0 = sb("sg0", [128, 8], F32)
            self.t0s = sb("t0s", [128, 32], F32)
            self.e8 = sb("e8", [128, 8], F32)
            self.x0_t = Trk()
            self.h0_t = Trk()
            self.m0_t = Trk()
            self.t0s_t = Trk()
            self.tb = 7
            self.tbc = 0
            self.t0q = []
            self.mmc = 0
            self.junk_t = Trk()
            self.ss16 = sb("ss16", [128, 16], F32)
            self.rstd16 = sb("rstd16", [128, 16], F32)
            self.st_t = Trk()
            self.ss_t = [Trk() for _ in range(16)]
            self.uid = 0

            self.load_consts()
            for si in range(nseq):
                self.do_seq(si)
            if not self.dry:
                for tok in self.out_toks:
                    key = ("d", tok[3])
                    if self.qsp.waited.get(key, 0) < tok[2]:
                        self.qsp.waited[key] = tok[2]
                        nc.sync.wait_ge(tok[1], tok[2])

    def load_consts(self):
        nc = self.nc
        toks = []
        D_ = lambda q, o, i: toks.append(self.dma(q, o, i))
        D_(self.qpl, self.ident[:], self.c_ident_d)
        D_(self.qsp, self.ident32[:], self.c_ident_d)
        D_(self.qsp, self.ones32[:], self.c_ones_d)
        D_(self.qpl, self.mask2[:], self.c_mask2_d)
        D_(self.qpl, self.maskb[:], self.c_maskb_d)
        D_(self.qpl, self.gmask[:], self.c_gmask_d)
        D_(self.qsp, self.cos[:].rearrange("p a b c -> p (a b c)"), self.c_cos_d)
        D_(self.qsp, self.sin[:].rearrange("p a b c -> p (a b c)"), self.c_sin_d)
        D_(self.qsp, self.memnw[:], self.memnw_d)
        D_(self.qsp, self.normw[:].rearrange("p a b -> p (a b)"), self.normw_d)
        D_(self.qsp, self.bg[:], self.bg_d)
        D_(self.qsp, self.gnw[:], self.gnw_d)
        for e in self.engs.values():
            need = {}
            for tok in toks:
                self._need(e, tok, need)
            self._emit_waits(e, e.h, need)
        self.op(self.dve, lambda: nc.vector.memset(self.ones[:], 1.0), writes=[self.const_t])
        self.op(self.dve, lambda: nc.vector.tensor_scalar(self.nbg[:], self.bg[:], -1.0, None, ALU.mult),
                writes=[self.const_t])
        self.out_toks = []

    def norm_to_fm(self, src_aps, src_trks, w_ap, dst, dst_trk_of, ntiles):
        nc = self.nc
        with ExitStack() as es:
            self.xs = [es.enter_context(nc.sbuf_tensor(self.un("xs"), [128, D], BF16)) for _ in range(2)]
            self.xs_t = [Trk(), Trk()]
            self.junk = es.enter_context(nc.sbuf_tensor(self.un("junk"), [128, D], BF16))
            self.junk_t = Trk()
            self._norm_to_fm(src_aps, src_trks, w_ap, dst, dst_trk_of, ntiles)
            self.barrier()

    def _norm_to_fm(self, src_aps, src_trks, w_ap, dst, dst_trk_of, ntiles):
        nc = self.nc
        for i in range(ntiles):
            self.op(self.act, lambda i=i: nc.scalar.activation(
                out=self.junk[:], in_=src_aps[i], func=AF.Square, accum_out=self.ss16[:, i:i + 1]),
                reads=[src_trks[i]], writes=[self.ss_t[i], self.junk_t])
        self.op(self.act, lambda: nc.scalar.activation(
            out=self.rstd16[:, 0:ntiles], in_=self.ss16[:, 0:ntiles], func=AF.Ln, scale=1.0 / D, bias=EPS),
            reads=self.ss_t[0:ntiles], writes=[self.st_t])
        self.op(self.act, lambda: nc.scalar.activation(
            out=self.rstd16[:, 0:ntiles], in_=self.rstd16[:, 0:ntiles], func=AF.Exp, scale=-0.5),
            reads=[self.st_t], writes=[self.st_t])
        for i in range(ntiles):
            b = i % 2
            self.op(self.act, lambda i=i, b=b: nc.scalar.activation(
                out=self.xs[b][:], in_=src_aps[i], func=AF.Copy, scale=self.rstd16[:, i:i + 1]),
                reads=[src_trks[i], self.st_t], writes=[self.xs_t[b]])
            bank = self.ps_alloc()
            pb = self.ps[bank][:].bitcast(BF16)
            for kc in range(8):
                self.op(self.pe, lambda kc=kc, b=b, pb=pb: nc.tensor.transpose(
                    pb[:, kc * 128:(kc + 1) * 128], self.xs[b][:, kc * 128:(kc + 1) * 128], self.ident[:]),
                    reads=[self.xs_t[b], self.const_t], writes=[self.ps_t[bank]], inc=(kc == 7), acc=(kc > 0))
            dtrk = dst_trk_of(i)
            self.op(self.dve, lambda i=i, pb=pb: nc.vector.tensor_tensor(
                dst[:, :, i * 128:(i + 1) * 128],
                pb.rearrange("p (k t) -> p k t", k=8),
                w_ap.unsqueeze(2).to_broadcast([128, 8, 128]), ALU.mult),
                reads=[self.ps_t[bank], self.const_t], writes=[dtrk])
            self.ps_release(bank)

    def do_seq(self, si):
        nc = self.nc
        for tt in range(16):
            self.dma(self.qsp, self.x[:, tt, :], self.x_d[si, tt * 128:(tt + 1) * 128, :], writes=[self.x_t[tt]])
        with ExitStack() as es:
            memraw = es.enter_context(nc.sbuf_tensor(self.un("memraw"), [128, 2, D], F32))
            mr_t = [Trk(), Trk()]
            for i in range(2):
                self.dma(self.qsp, memraw[:, i, :], self.mem_d[si, i * 128:(i + 1) * 128, :], writes=[mr_t[i]])
            self.norm_to_fm([memraw[:, i, :] for i in range(2)], mr_t, self.memnw[:], self.memT,
                            lambda i: self.memT_t, 2)
            self.barrier()
        self.t0_init()
        for li in range(self.nlayers):
            self.do_layer(si, li)
        self.final_norm(si)

    def do_layer(self, si, li):
        nc = self.nc
        j = li // 2
        self.norm_to_fm([self.x[:, tt, :] for tt in range(16)], self.x_t, self.normw[:, li, :], self.hT,
                        lambda i: self.hT_t[i // 4], 16)
        self.mem_kv(li)
        self.t0_layer(li)
        mode = self.t0_mode(li)
        if li % 2 == 0:
            self.mixer_a(j)
            wname, qm0, g0 = "wina", 1552, 1808
        else:
            self.mixer_b(j)
            wname, qm0, g0 = "winb", 6912, 7168
        if mode == "heads" and T0S >= 3:
            self.flush_t0()
            self.op(self.dve, lambda: nc.vector.tensor_copy(self.preT[:, 0:6, 0:1], self.m0T[:, 0:6].unsqueeze(2)),
                    reads=[self.m0_t], writes=[self.preT_t[c][0] for c in range(6)])
        self.tail(li, j, wname, qm0, g0)
        if mode == "full":
            self.flush_t0()
            self.t0_inject()

    def mem_kv(self, li):
        nc = self.nc
        s, st = self.wnext([("wmemkv", li, 0, 256, 0)])
        for c2 in range(2):
            bank = self.ps_alloc()
            out = self.ps[bank][:, 0:256]
            for kc in range(8):
                self.mm(out, self.wslot[s][:, kc, c2 * 128:(c2 + 1) * 128], self.memT[:, kc, :],
                        start=(kc == 0), stop=(kc == 7), reads=[st, self.memT_t], writes=[self.ps_t[bank]],
                        inc=(kc == 7))
            self.op(self.act, lambda c2=c2, out=out: nc.scalar.copy(self.kmT[:, c2, :], out),
                    reads=[self.ps_t[bank]], writes=[self.kmT_t])
            self.op(self.act, lambda c2=c2, out=out: nc.scalar.copy(self.kmT32[:, c2, :], out),
                    reads=[self.ps_t[bank]], writes=[self.km32_t])
            self.ps_release(bank)
        s, st = self.wnext([("wmemkv", li, 256, 256, 0)])
        for kt in range(2):
            bank = self.ps_alloc()
            out = self.ps[bank][:, 0:256]
            for kc in range(8):
                self.mm(out, self.memT[:, kc, kt * 128:(kt + 1) * 128], self.wslot[s][:, kc, 0:256],
                        start=(kc == 0), stop=(kc == 7), reads=[st, self.memT_t], writes=[self.ps_t[bank]],
                        inc=(kc == 7))
            self.op(self.act, lambda kt=kt, out=out: nc.scalar.copy(self.vm[:, kt, :], out),
                    reads=[self.ps_t[bank]], writes=[self.vm_t])
            self.op(self.act, lambda kt=kt, out=out: nc.scalar.copy(self.vm32[:, kt, :], out),
                    reads=[self.ps_t[bank]], writes=[self.vm32_t])
            self.ps_release(bank)

    def tail(self, li, j, wname, qm0, g0):
        nc = self.nc
        with ExitStack() as es:
            sbt = lambda name, shape, dtype: es.enter_context(nc.sbuf_tensor(self.un(name), shape, dtype))
            qm = sbt("qm", [128, 2, 512], BF16)
            qm_t = Trk()
            pT = [sbt(f"pTm{i}", [128, 512], BF16) for i in range(3)]
            pT_t = [Trk() for _ in range(3)]
            rden = sbt("rdenm", [128, 512], F32)
            rden_t = Trk()
            sg = [sbt(f"sg{i}", [128, 512], BF16) for i in range(2)]
            sg_t = [Trk() for _ in range(2)]
            pk = 0
            for T in range(4):
                cols = slice(T * 512, (T + 1) * 512)
                s, st = self.wnext([(wname, j, qm0, 256, 0)])
                for c2 in range(2):
                    bank = self.ps_alloc()
                    self.proj_fm(bank, 128, s, st, c2 * 128, T)
                    self.op(self.act, lambda c2=c2, bank=bank: nc.scalar.copy(qm[:, c2, :], self.ps[bank][:, :]),
                            reads=[self.ps_t[bank]], writes=[qm_t])
                    self.ps_release(bank)
                gate_jobs = [(0, 384), (384, 384), (768, 256)]
                gstate = {"ji": 0, "cc": 0, "s": None, "st": None}

                def emit_gate_chunk():
                    if gstate["ji"] >= len(gate_jobs):
                        return False
                    c0, n = gate_jobs[gstate["ji"]]
                    if gstate["cc"] == 0:
                        gstate["s"], gstate["st"] = self.wnext([(wname, j, g0 + c0, n, 0)])
                    s_, st_ = gstate["s"], gstate["st"]
                    cc = gstate["cc"]
                    c = c0 // 128 + cc
                    bank = self.ps_alloc()
                    self.proj_fm(bank, 128, s_, st_, cc * 128, T)
                    b_ = c % 2
                    self.op(self.act, lambda b_=b_, bank=bank: nc.scalar.activation(
                        out=sg[b_][:], in_=self.ps[bank][:, :], func=AF.Silu),
                        reads=[self.ps_t[bank]], writes=[sg_t[b_]])
                    self.ps_release(bank)
                    self.gate_pending.append((c, b_))
                    gstate["cc"] += 1
                    if gstate["cc"] >= n // 128:
                        gstate["cc"] = 0
                        gstate["ji"] += 1
                    return True

                def emit_gate_mults(upto=None):
                    keep = []
                    for (c, b_) in self.gate_pending:
                        if c < 6 or self.mem_done:
                            self.op(self.dve, lambda b_=b_, c=c: nc.vector.tensor_tensor(
                                self.preT[:, c, cols], self.preT[:, c, cols], sg[b_][:], ALU.mult),
                                reads=[sg_t[b_], self.preT_t[c][T]], writes=[self.preT_t[c][T]])
                        else:
                            keep.append((c, b_))
                    self.gate_pending = keep

                self.gate_pending = []
                self.mem_done = False
                pend = []
                banks = {}
                step = 0
                for c2 in range(2):
                    for half in range(2):
                        hm = 2 * c2 + half
                        pr = slice(64 * half, 64 * half + 64)
                        for kt in range(2):
                            if c2 not in banks:
                                banks[c2] = (self.ps_alloc(), self.ps_alloc())
                            bs = self.ps_alloc()
                            self.mm(self.ps[bs][:, :], self.kmT[pr, c2, kt * 128:(kt + 1) * 128], qm[pr, c2, :],
                                    start=True, stop=True, reads=[self.kmT_t, qm_t], writes=[self.ps_t[bs]], inc=True)
                            p = pk % 3
                            pk += 1
                            self.op(self.act, lambda p=p, bs=bs: nc.scalar.activation(
                                out=pT[p][:], in_=self.ps[bs][:, :], func=AF.Exp, scale=0.125),
                                reads=[self.ps_t[bs]], writes=[pT_t[p]])
                            self.ps_release(bs)

                            def emit_pv(c2=c2, half=half, hm=hm, pr=pr, kt=kt, p=p):
                                bn, bd = banks[c2]
                                self.mm(self.ps[bn][pr, :], self.vm[:, kt, hm * 64:(hm + 1) * 64], pT[p][:],
                                        start=(kt == 0), stop=(kt == 1), reads=[self.vm_t, pT_t[p]],
                                        writes=[self.ps_t[bn]], inc=False)
                                self.mm(self.ps[bd][pr, :], self.ones[:, 0:64], pT[p][:],
                                        start=(kt == 0), stop=(kt == 1), reads=[pT_t[p]],
                                        writes=[self.ps_t[bd]], inc=True)
                                if half == 1 and kt == 1:
                                    self.op(self.dve, lambda bd=bd: nc.vector.reciprocal(rden[:], self.ps[bd][:, :]),
                                            reads=[self.ps_t[bd]], writes=[rden_t])
                                    self.op(self.dve, lambda bn=bn, c2=c2: nc.vector.tensor_tensor(
                                        self.preT[:, 6 + c2, cols], self.ps[bn][:, :], rden[:], ALU.mult),
                                        reads=[self.ps_t[bn], rden_t], writes=[self.preT_t[6 + c2][T]])
                                    self.ps_release(bn)
                                    self.ps_release(bd)
                            pend.append(emit_pv)
                            if len(pend) > 1:
                                pend.pop(0)()
                            if step < 6:
                                emit_gate_chunk()
                                emit_gate_mults()
                            step += 1
                while pend:
                    pend.pop(0)()
                self.mem_done = True
                while emit_gate_chunk():
                    emit_gate_mults()
                emit_gate_mults()
                for (c0, n) in [(0, 384), (384, 384), (768, 256)]:
                    s, st = self.wnext([("wout", li, c0, n, 0)])
                    for sub in range(4):
                        tt = 4 * T + sub
                        bank = self.ps_alloc()
                        out = self.ps[bank][:, 0:n]
                        for kc in range(8):
                            self.mm(out, self.preT[:, kc, tt * 128:(tt + 1) * 128], self.wslot[s][:, kc, 0:n],
                                    start=(kc == 0), stop=(kc == 7), reads=[st, self.preT_t[kc][T]],
                                    writes=[self.ps_t[bank]], inc=(kc == 7))
                        self.op(self.dve, lambda tt=tt, out=out, c0=c0, n=n: nc.vector.tensor_tensor(
                            self.x[:, tt, c0:c0 + n], out, self.x[:, tt, c0:c0 + n], ALU.add),
                            reads=[self.ps_t[bank], self.x_t[tt]], writes=[self.x_t[tt]])
                        self.ps_release(bank)
            self.barrier()

    def final_norm(self, si):
        nc = self.nc
        with ExitStack() as es:
            fnw = es.enter_context(nc.sbuf_tensor(self.un("fnw"), [128, D], F32))
            self.junk = es.enter_context(nc.sbuf_tensor(self.un("junk"), [128, D], BF16))
            self.junk_t = Trk()
            fnw_t = Trk()
            self.dma(self.qsp, fnw[:], self.fnw_d, writes=[fnw_t])
            self._final_norm(si, fnw, fnw_t)
            self.barrier()

    def _final_norm(self, si, fnw, fnw_t):
        nc = self.nc
        for i in range(16):
            self.op(self.act, lambda i=i: nc.scalar.activation(
                out=self.junk[:], in_=self.x[:, i, :], func=AF.Square, accum_out=self.ss16[:, i:i + 1]),
                reads=[self.x_t[i]], writes=[self.ss_t[i], self.junk_t])
        self.op(self.act, lambda: nc.scalar.activation(
            out=self.rstd16[:], in_=self.ss16[:], func=AF.Ln, scale=1.0 / D, bias=EPS),
            reads=self.ss_t, writes=[self.st_t])
        self.op(self.act, lambda: nc.scalar.activation(
            out=self.rstd16[:], in_=self.rstd16[:], func=AF.Exp, scale=-0.5),
            reads=[self.st_t], writes=[self.st_t])
        for i in range(16):
            self.op(self.dve, lambda i=i: nc.vector.scalar_tensor_tensor(
                self.x[:, i, :], self.x[:, i, :], self.rstd16[:, i:i + 1], fnw[:], ALU.mult, ALU.mult),
                reads=[self.x_t[i], self.st_t, fnw_t], writes=[self.x_t[i]])
            tok = self.dma(self.qsp, self.out_d[si, i * 128:(i + 1) * 128, :], self.x[:, i, :], reads=[self.x_t[i]])
            self.out_toks.append(tok)


    def tcol(self, n=1):
        if self.tbc + n > 512:
            self.tbc = 0
        c = self.tbc
        self.tbc += n
        return c

    def T(self, eng, fn, reads, writes, **kw):
        self.t0q.append(lambda: self.op(eng, fn, reads=reads, writes=writes, **kw))

    def t0_init(self):
        if T0S < 1:
            return
        nc = self.nc
        tb = self.ps[self.tb]
        tbt = self.ps_t[self.tb]
        c = self.tcol(8)
        for kc in range(8):
            self.op(self.pe, lambda kc=kc: nc.tensor.matmul(
                tb[:, c + kc:c + kc + 1], self.x[0:1, 0, kc * 128:(kc + 1) * 128], self.ones32[0:1, 0:1],
                start=True, stop=True, skip_group_check=True),
                reads=[self.x_t[0], self.const_t], writes=[tbt], inc=(kc == 7), acc=(kc > 0))
        self.op(self.dve, lambda: nc.vector.tensor_copy(self.x0T[:], tb[:, c:c + 8]),
                reads=[tbt], writes=[self.x0_t])

    def t0_inject(self):
        if T0S < 1:
            return
        nc = self.nc
        tb = self.ps[self.tb]
        tbt = self.ps_t[self.tb]
        for hf in range(2):
            for k4 in range(4):
                kc = 4 * hf + k4
                self.op(self.pe, lambda kc=kc, k4=k4: nc.tensor.matmul(
                    tb[0:1, k4 * 128:(k4 + 1) * 128], self.x0T[:, kc:kc + 1], self.ident32[:],
                    start=True, stop=True, skip_group_check=True),
                    reads=[self.x0_t, self.const_t], writes=[tbt], inc=(k4 == 3), acc=(k4 > 0))
            self.op(self.dve, lambda hf=hf: nc.vector.tensor_copy(self.x[0:1, 0, hf * 512:(hf + 1) * 512], tb[0:1, :]),
                    reads=[tbt], writes=[self.x_t[0]])
        self.tbc = 0

    def t0_cproj(self, vec, vec_t, wname, idx, c0, M, col, pbase=0):
        nc = self.nc
        tb = self.ps[self.tb]
        tbt = self.ps_t[self.tb]
        k = self.w32_n % 2
        self.w32_n += 1
        w32, w32_t = self.w32[k], self.w32_t[k]

        def load():
            srcap = getattr(self, wname + "_d")[idx, :, c0:c0 + M].rearrange("(kc p) c -> p kc c", p=128)
            self.dma(self.qsp, w32[:, :, 0:M], srcap, writes=[w32_t])
        if self.t0_prev_pos is not None:
            self.t0q.insert(self.t0_prev_pos, load)
        else:
            self.t0q.append(load)
        self.t0_prev_pos = len(self.t0q)
        for kc in range(8):
            self.T(self.pe, lambda kc=kc: nc.tensor.matmul(
                tb[pbase:pbase + M, col:col + 1], w32[:, kc, 0:M], vec[:, kc:kc + 1],
                start=(kc == 0), stop=(kc == 7), skip_group_check=True),
                [w32_t, vec_t], [tbt], inc=(kc == 7), acc=(kc > 0))

    def t0_mode(self, li):
        last_gla = max([l for l in range(self.nlayers) if l % 2 == 0], default=-1)
        if li < last_gla:
            return "full"
        if li == last_gla:
            return "heads"
        return "none"

    def t0_layer(self, li):
        nc = self.nc
        j = li // 2
        self.t0_prev_pos = None
        assert not self.t0q
        mode = self.t0_mode(li)
        if mode == "none":
            return
        tb = self.ps[self.tb]
        tbt = self.ps_t[self.tb]
        s = self.t0s
        st = self.t0s_t
        T = self.T
        X = mybir.AxisListType.X
        if T0S < 2:
            return
        c = self.tcol()
        T(self.dve, lambda: nc.vector.tensor_tensor(s[:, 8:16], self.x0T[:], self.x0T[:], ALU.mult), [self.x0_t], [st])
        T(self.dve, lambda: nc.vector.tensor_reduce(out=s[:, 0:1], in_=s[:, 8:16], axis=X, op=ALU.add), [st], [st])
        T(self.pe, lambda: nc.tensor.matmul(tb[:, c:c + 1], self.ones32[:], s[:, 0:1], start=True, stop=True,
                                            skip_group_check=True), [st, self.const_t], [tbt])
        T(self.act, lambda: nc.scalar.activation(out=s[:, 1:2], in_=tb[:, c:c + 1], func=AF.Ln, scale=1.0 / D, bias=EPS),
          [tbt], [st])
        T(self.act, lambda: nc.scalar.activation(out=s[:, 1:2], in_=s[:, 1:2], func=AF.Exp, scale=-0.5), [st], [st])
        T(self.dve, lambda: nc.vector.scalar_tensor_tensor(
            self.h0T[:], self.x0T[:], s[:, 1:2], self.normw[:, li, :], ALU.mult, ALU.mult),
          [self.x0_t, st, self.const_t], [self.h0_t])
        hv, ht = self.h0T, self.h0_t
        if T0S < 3:
            return
        if li % 2 == 0:
            wname, qm0, g0 = "wina", 1552, 1808
            for h in range(4):
                c = self.tcol(8)
                self.t0_cproj(hv, ht, wname, j, 96 * h, 96, c)
                self.t0_cproj(hv, ht, wname, j, 384 + 96 * h, 96, c + 1)
                T(self.act, lambda c=c: nc.scalar.copy(s[0:96, 2:3], tb[0:96, c:c + 1]), [tbt], [st])
                T(self.dve, lambda c=c: nc.vector.tensor_tensor(s[0:96, 3:4], s[0:96, 2:3], tb[0:96, c + 1:c + 2], ALU.mult),
                  [tbt, st], [st])
                T(self.pe, lambda c=c: nc.tensor.matmul(tb[:, c + 2:c + 3], self.ones32[0:96, :], s[0:96, 3:4],
                                                        start=True, stop=True, skip_group_check=True),
                  [st, self.const_t], [tbt])
                f0 = 192 * h
                if h % 2 == 0:
                    cfull, chalf, phalf, fs0, hs0 = f0 // 128, f0 // 128 + 1, 0, 0, 128
                else:
                    chalf, cfull, phalf, hs0, fs0 = f0 // 128, f0 // 128 + 1, 64, 0, 64
                pr = slice(phalf, phalf + 64)
                self.t0_cproj(hv, ht, wname, j, 768 + f0 + fs0, 128, c + 3)
                self.t0_cproj(hv, ht, wname, j, 768 + f0 + hs0, 64, c + 4, pbase=phalf)
                T(self.dve, lambda: nc.vector.memset(s[:, 4:6], 0.0), [st], [st])
                T(self.act, lambda c=c: nc.scalar.copy(s[:, 4:5], tb[:, c + 3:c + 4]), [tbt], [st])
                T(self.act, lambda c=c, pr=pr: nc.scalar.copy(s[pr, 5:6], tb[pr, c + 4:c + 5]), [tbt], [st])
                T(self.dve, lambda: nc.vector.tensor_tensor(s[:, 16:18], s[:, 4:6], s[:, 4:6], ALU.mult), [st], [st])
                T(self.dve, lambda: nc.vector.tensor_reduce(out=s[:, 6:7], in_=s[:, 16:18], axis=X, op=ALU.add), [st], [st])
                T(self.pe, lambda c=c: nc.tensor.matmul(tb[:, c + 5:c + 6], self.ones32[:], s[:, 6:7],
                                                        start=True, stop=True, skip_group_check=True),
                  [st, self.const_t], [tbt])
                T(self.act, lambda c=c: nc.scalar.activation(out=s[:, 7:8], in_=tb[:, c + 2:c + 3], func=AF.Copy,
                                                             scale=96.0 ** -0.5), [tbt], [st])
                T(self.dve, lambda: nc.vector.tensor_tensor(s[:, 18:19], s[:, 7:8], s[:, 7:8], ALU.mult), [st], [st])
                T(self.dve, lambda c=c: nc.vector.tensor_tensor(s[:, 18:19], s[:, 18:19], tb[:, c + 5:c + 6], ALU.mult),
                  [st, tbt], [st])
                T(self.act, lambda: nc.scalar.activation(out=s[:, 18:19], in_=s[:, 18:19], func=AF.Ln,
                                                         scale=1.0 / 192, bias=EPS), [st], [st])
                T(self.act, lambda: nc.scalar.activation(out=s[:, 18:19], in_=s[:, 18:19], func=AF.Exp, scale=-0.5),
                  [st], [st])
                T(self.dve, lambda: nc.vector.tensor_tensor(s[:, 19:20], s[:, 7:8], s[:, 18:19], ALU.mult), [st], [st])
                T(self.dve, lambda cfull=cfull: nc.vector.scalar_tensor_tensor(
                    self.m0T[:, cfull:cfull + 1], s[:, 4:5], s[:, 19:20], self.gnw[:, 6 * j + cfull:6 * j + cfull + 1],
                    ALU.mult, ALU.mult), [st, self.const_t], [self.m0_t])
                T(self.dve, lambda chalf=chalf, pr=pr: nc.vector.scalar_tensor_tensor(
                    self.m0T[pr, chalf:chalf + 1], s[pr, 5:6], s[pr, 19:20], self.gnw[pr, 6 * j + chalf:6 * j + chalf + 1],
                    ALU.mult, ALU.mult), [st, self.const_t], [self.m0_t])
        else:
            wname, qm0, g0 = "winb", 6912, 7168
            for h in range(6):
                for g in range(3):
                    base = g * 2304
                    c = self.tcol(4)
                    self.t0_cproj(hv, ht, wname, j, base + 128 * h, 128, c)
                    self.t0_cproj(hv, ht, wname, j, base + 768 + 128 * h, 128, c + 1)
                    self.t0_cproj(hv, ht, wname, j, base + 1536 + 128 * h, 128, c + 2)
                    T(self.act, lambda c=c: nc.scalar.copy(s[:, 2:3], tb[:, c:c + 1]), [tbt], [st])
                    T(self.dve, lambda c=c: nc.vector.tensor_tensor(s[:, 3:4], s[:, 2:3], tb[:, c + 1:c + 2], ALU.mult),
                      [tbt, st], [st])
                    T(self.pe, lambda c=c: nc.tensor.matmul(tb[:, c + 3:c + 4], self.ones32[:], s[:, 3:4],
                                                            start=True, stop=True, skip_group_check=True),
                      [st, self.const_t], [tbt])
                    T(self.act, lambda c=c, g=g: nc.scalar.activation(out=s[:, 8 + g:9 + g], in_=tb[:, c + 3:c + 4],
                                                                      func=AF.Exp, scale=128.0 ** -0.5), [tbt], [st])
                    T(self.act, lambda c=c, g=g: nc.scalar.copy(s[:, 12 + g:13 + g], tb[:, c + 2:c + 3]), [tbt], [st])
                T(self.dve, lambda: nc.vector.tensor_reduce(out=s[:, 16:17], in_=s[:, 8:11], axis=X, op=ALU.add), [st], [st])
                T(self.dve, lambda: nc.vector.reciprocal(s[:, 16:17], s[:, 16:17]), [st], [st])
                T(self.dve, lambda: nc.vector.tensor_tensor(s[:, 20:23], s[:, 8:11], s[:, 12:15], ALU.mult), [st], [st])
                T(self.dve, lambda: nc.vector.tensor_reduce(out=s[:, 17:18], in_=s[:, 20:23], axis=X, op=ALU.add), [st], [st])
                T(self.dve, lambda h=h: nc.vector.tensor_tensor(self.m0T[:, h:h + 1], s[:, 17:18], s[:, 16:17], ALU.mult),
                  [st], [self.m0_t])
        if T0S < 4 or mode == "heads":
            return
        c = self.tcol(2)
        for c2 in range(2):
            self.t0_cproj(hv, ht, wname, j, qm0 + 128 * c2, 128, c + c2)
        T(self.act, lambda c=c: nc.scalar.copy(s[:, 24:26], tb[:, c:c + 2]), [tbt], [st])
        c = self.tcol(8)
        for hm in range(4):
            c2, half = hm // 2, hm % 2
            pr = slice(64 * half, 64 * half + 64)
            for kt in range(2):
                T(self.pe, lambda c=c, hm=hm, kt=kt, c2=c2, pr=pr: nc.tensor.matmul(
                    tb[:, c + 2 * hm + kt:c + 2 * hm + kt + 1], self.kmT32[pr, c2, kt * 128:(kt + 1) * 128],
                    s[pr, 24 + c2:25 + c2], start=True, stop=True, skip_group_check=True),
                  [self.km32_t, st], [tbt], inc=(hm == 3 and kt == 1), acc=not (hm == 0 and kt == 0))
        T(self.act, lambda c=c: nc.scalar.activation(out=self.e8[:], in_=tb[:, c:c + 8], func=AF.Exp, scale=0.125),
          [tbt], [st])
        c = self.tcol(4)
        first = True
        for hm in range(4):
            c2, half = hm // 2, hm % 2
            pr = slice(64 * half, 64 * half + 64)
            for kt in range(2):
                T(self.pe, lambda c=c, hm=hm, kt=kt, c2=c2, pr=pr: nc.tensor.matmul(
                    tb[pr, c + c2:c + c2 + 1], self.vm32[:, kt, 64 * hm:64 * hm + 64], self.e8[:, 2 * hm + kt:2 * hm + kt + 1],
                    start=(kt == 0), stop=(kt == 1), skip_group_check=True),
                  [self.vm32_t, st], [tbt], inc=False, acc=not first)
                first = False
            for kt in range(2):
                T(self.pe, lambda c=c, hm=hm, kt=kt, c2=c2, pr=pr: nc.tensor.matmul(
                    tb[pr, c + 2 + c2:c + 3 + c2], self.ones32[:, 0:64], self.e8[:, 2 * hm + kt:2 * hm + kt + 1],
                    start=(kt == 0), stop=(kt == 1), skip_group_check=True),
                  [self.const_t, st], [tbt], inc=(hm == 3 and kt == 1), acc=True)
        T(self.dve, lambda c=c: nc.vector.reciprocal(s[:, 26:28], tb[:, c + 2:c + 4]), [tbt], [st])
        T(self.dve, lambda c=c: nc.vector.tensor_tensor(self.m0T[:, 6:8], tb[:, c:c + 2], s[:, 26:28], ALU.mult),
          [tbt, st], [self.m0_t])
        if T0S < 5:
            return
        c = self.tcol(8)
        for cc in range(8):
            self.t0_cproj(hv, ht, wname, j, g0 + 128 * cc, 128, c + cc)
        T(self.act, lambda c=c: nc.scalar.activation(out=self.sg0[:], in_=tb[:, c:c + 8], func=AF.Silu), [tbt], [st])
        T(self.dve, lambda: nc.vector.tensor_tensor(self.m0T[:], self.m0T[:], self.sg0[:], ALU.mult),
          [self.m0_t, st], [self.m0_t])
        c = self.tcol(8)
        for cc in range(8):
            self.t0_cproj(self.m0T, self.m0_t, "wout", li, 128 * cc, 128, c + cc)
        T(self.dve, lambda c=c: nc.vector.tensor_tensor(self.x0T[:], self.x0T[:], tb[:, c:c + 8], ALU.add),
          [tbt, self.x0_t], [self.x0_t])

    def mixer_a(self, j):
        nc = self.nc
        win = self.wina_d[j]
        with ExitStack() as es:
            sbt = lambda name, shape, dtype: es.enter_context(nc.sbuf_tensor(self.un(name), shape, dtype))
            wgl = sbt("wgl", [128, 8, 16], BF16)
            wgl_t = Trk()
            glT = sbt("glT", [16, 512], F32)
            glT_t = Trk()
            sp = sbt("sp", [96, 512], F32)
            cc_ = sbt("cc", [96, 512], F32)
            eb = sbt("eb", [96, 512], F32)
            enb = sbt("enb", [96, 512], F32)
            g_t = Trk()
            qinT = sbt("qinT", [96, 512], BF16)
            kinT = sbt("kinT", [96, 512], BF16)
            qk_t = Trk()
            kin_tok = sbt("kin_tok", [128, 4, 96], BF16)
            kin_tok_t = Trk()
            V = sbt("Vh", [128, 4, 192], BF16)
            V_t = Trk()
            R = [[sbt(f"R{h}_{i}", [96, 192], F32) for i in range(2)] for h in range(4)]
            R_t = [[Trk() for i in range(2)] for h in range(4)]
            Sbf = [sbt(f"Sbf{i}", [96, 192], BF16) for i in range(4)]
            Sbf_t = [Trk() for _ in range(4)]
            ATm = [sbt(f"ATm{i}", [128, 128], BF16) for i in range(2)]
            ATm_t = [Trk() for _ in range(2)]
            ssh = sbt("ssh", [128, 4], F32)
            ssh_t = [Trk() for _ in range(4)]
            mixt = [sbt(f"mixt{i}", [128, 192], BF16) for i in range(2)]
            mixt_t = [Trk() for _ in range(2)]
            dec = sbt("dec", [96, 4], F32)
            dec_t = [Trk() for _ in range(4)]
            self.junk = sbt("junkA", [128, 192], BF16)
            self.junk_t = Trk()
            wgu = sbt("wgu", [16, 384], F32)
            rmask = sbt("rmask", [128, 512], BF16)
            cl_t = Trk()
            self.dma(self.qsp, wgu[:], self.wgu_d[j], writes=[cl_t])
            self.dma(self.qpl, rmask[:], self.c_rmask_d, writes=[cl_t])
            self.dma(self.qpl, wgl[:], win[:, 1536:1552].rearrange("(kc p) c -> p kc c", p=128), writes=[wgl_t])
            sk = 0
            ak = 0
            mk = 0
            for T in range(4):
                cols = slice(T * 512, (T + 1) * 512)
                bank = self.ps_alloc()
                for kc in range(8):
                    self.mm(self.ps[bank][0:16, :], wgl[:, kc, :], self.hT[:, kc, cols], start=(kc == 0),
                            stop=(kc == 7), reads=[wgl_t, self.hT_t[T]], writes=[self.ps_t[bank]], inc=(kc == 7))
                self.op(self.act, lambda bank=bank: nc.scalar.copy(glT[:], self.ps[bank][0:16, :]),
                        reads=[self.ps_t[bank]], writes=[glT_t])
                self.ps_release(bank)
                for h in range(4):
                    s, st = self.wnext([("wina", j, 96 * h, 96, 0), ("wina", j, 384 + 96 * h, 96, 96),
                                        ("wina", j, 768 + 192 * h, 192, 192)])
                    bank = self.ps_alloc()
                    self.mm(self.ps[bank][0:96, :], wgu[:, 96 * h:96 * h + 96], glT[:], start=True, stop=True,
                            reads=[cl_t, glT_t], writes=[self.ps_t[bank]], inc=True)
                    self.op(self.act, lambda bank=bank, h=h: nc.scalar.activation(
                        out=sp[:], in_=self.ps[bank][0:96, :], func=AF.Exp, scale=-1.0,
                        bias=self.nbg[:, 4 * j + h:4 * j + h + 1]),
                        reads=[self.ps_t[bank], self.const_t], writes=[g_t])
                    self.ps_release(bank)
                    self.op(self.act, lambda: nc.scalar.activation(out=sp[:], in_=sp[:], func=AF.Ln, bias=1.0),
                            reads=[g_t], writes=[g_t])
                    self.op(self.dve, lambda: nc.vector.tensor_tensor_scan(
                        cc_[:], rmask[0:96, :], sp[:], 0.0, ALU.mult, ALU.add),
                        reads=[g_t, cl_t], writes=[g_t])
                    self.op(self.act, lambda: nc.scalar.activation(out=eb[:], in_=cc_[:], func=AF.Exp, scale=-1.0 / 16),
                            reads=[g_t], writes=[g_t])
                    self.op(self.act, lambda: nc.scalar.activation(out=enb[:], in_=cc_[:], func=AF.Exp, scale=1.0 / 16),
                            reads=[g_t], writes=[g_t])
                    bank = self.ps_alloc()
                    self.proj_fm(bank, 96, s, st, 0, T)
                    self.op(self.dve, lambda bank=bank: nc.vector.scalar_tensor_tensor(
                        qinT[:], self.ps[bank][0:96, :], 96.0 ** -0.5, eb[:], ALU.mult, ALU.mult),
                        reads=[self.ps_t[bank], g_t], writes=[qk_t])
                    self.ps_release(bank)
                    bank = self.ps_alloc()
                    self.proj_fm(bank, 96, s, st, 96, T)
                    self.op(self.dve, lambda bank=bank: nc.vector.tensor_tensor(
                        kinT[:], self.ps[bank][0:96, :], enb[:], ALU.mult),
                        reads=[self.ps_t[bank], g_t], writes=[qk_t])
                    self.ps_release(bank)
                    for sub in range(4):
                        tt = 4 * T + sub
                        bank = self.ps_alloc()
                        out = self.ps[bank][:, 0:192]
                        for kc in range(8):
                            self.mm(out, self.hT[:, kc, tt * 128:(tt + 1) * 128], self.wslot[s][:, kc, 192:384],
                                    start=(kc == 0), stop=(kc == 7), reads=[st, self.hT_t[T]],
                                    writes=[self.ps_t[bank]], inc=(kc == 7))
                        self.op(self.act, lambda sub=sub, out=out: nc.scalar.copy(V[:, sub, :], out),
                                reads=[self.ps_t[bank]], writes=[V_t])
                        self.ps_release(bank)
                    bank = self.ps_alloc()
                    pb = self.ps[bank][:].bitcast(BF16)
                    for sub in range(4):
                        self.op(self.pe, lambda sub=sub, pb=pb: nc.tensor.transpose(
                            pb[:, sub * 96:(sub + 1) * 96], kinT[:, sub * 128:(sub + 1) * 128], self.ident[0:96, 0:96]),
                            reads=[qk_t, self.const_t], writes=[self.ps_t[bank]], inc=(sub == 3), acc=(sub > 0))
                    self.op(self.act, lambda pb=pb: nc.scalar.copy(
                        kin_tok[:].rearrange("p a b -> p (a b)"), pb[:, 0:384]),
                        reads=[self.ps_t[bank]], writes=[kin_tok_t])
                    self.ps_release(bank)
                    def front(sub):
                        nonlocal ak, sk
                        tt = 4 * T + sub
                        tc_ = slice(sub * 128, (sub + 1) * 128)
                        bA = self.ps_alloc()
                        self.mm(self.ps[bA][:, 0:128], kinT[:, tc_], qinT[:, tc_], start=True, stop=True,
                                reads=[qk_t], writes=[self.ps_t[bA]], inc=True)
                        a = ak % 2
                        ak += 1
                        self.op(self.dve, lambda a=a, bA=bA: nc.vector.tensor_tensor(
                            ATm[a][:], self.ps[bA][:, 0:128], self.gmask[:], ALU.mult),
                            reads=[self.ps_t[bA], self.const_t], writes=[ATm_t[a]])
                        self.ps_release(bA)
                        sbs = [None, None]
                        for ch in range(2):
                            c = 2 * tt + ch
                            pr = slice(64 * ch, 64 * ch + 64)
                            lc = 64 * (2 * sub + ch)
                            dk = None
                            if c > 0:
                                dk = eb[:, lc - 1:lc] if lc > 0 else dec[:, h:h + 1]
                                dk_t = g_t if lc > 0 else dec_t[h]
                                sb_ = sk % 4
                                sk += 1
                                sbs[ch] = sb_
                                rp = (c - 1) % 2
                                self.op(self.act, lambda sb_=sb_, dk=dk, rp=rp: nc.scalar.activation(
                                    out=Sbf[sb_][:], in_=R[h][rp][:], func=AF.Copy, scale=dk),
                                    reads=[R_t[h][rp], dk_t], writes=[Sbf_t[sb_]])
                            bU = self.ps_alloc()
                            self.mm(self.ps[bU][0:96, 0:192], kin_tok[pr, sub, :], V[pr, sub, :], start=True, stop=True,
                                    reads=[kin_tok_t, V_t], writes=[self.ps_t[bU]], inc=True)
                            rn = c % 2
                            if c == 0:
                                self.op(self.dve, lambda bU=bU, rn=rn: nc.vector.tensor_copy(
                                    R[h][rn][:], self.ps[bU][0:96, 0:192]),
                                    reads=[self.ps_t[bU]], writes=[R_t[h][rn]])
                            else:
                                self.op(self.dve, lambda bU=bU, dk=dk, rn=rn: nc.vector.scalar_tensor_tensor(
                                    R[h][rn][:], R[h][1 - rn][:], dk, self.ps[bU][0:96, 0:192], ALU.mult, ALU.add),
                                    reads=[self.ps_t[bU], R_t[h][1 - rn], dk_t], writes=[R_t[h][rn]])
                            self.ps_release(bU)
                        return a, sbs

                    def back(sub, a, sbs):
                        nonlocal mk
                        tt = 4 * T + sub
                        bO = self.ps_alloc()
                        self.mm(self.ps[bO][:, 0:192], ATm[a][:], V[:, sub, :], start=True, stop=False,
                                reads=[ATm_t[a], V_t], writes=[self.ps_t[bO]], inc=False)
                        for ch in range(2):
                            pr = slice(64 * ch, 64 * ch + 64)
                            lc = 64 * (2 * sub + ch)
                            sb_ = sbs[ch]
                            if sb_ is not None:
                                self.mm(self.ps[bO][pr, 0:192], qinT[:, lc:lc + 64], Sbf[sb_][:], start=False,
                                        stop=(ch == 1), reads=[qk_t, Sbf_t[sb_]], writes=[self.ps_t[bO]],
                                        inc=(ch == 1))
                        m = mk % 2
                        mk += 1
                        self.op(self.act, lambda bO=bO: nc.scalar.activation(
                            out=self.junk[:, 0:192], in_=self.ps[bO][:, 0:192], func=AF.Square,
                            accum_out=ssh[:, h:h + 1]),
                            reads=[self.ps_t[bO]], writes=[ssh_t[h], self.junk_t])
                        self.op(self.act, lambda: nc.scalar.activation(
                            out=ssh[:, h:h + 1], in_=ssh[:, h:h + 1], func=AF.Ln, scale=1.0 / 192, bias=EPS),
                            reads=[ssh_t[h]], writes=[ssh_t[h]])
                        self.op(self.act, lambda: nc.scalar.activation(
                            out=ssh[:, h:h + 1], in_=ssh[:, h:h + 1], func=AF.Exp, scale=-0.5),
                            reads=[ssh_t[h]], writes=[ssh_t[h]])
                        self.op(self.act, lambda bO=bO, m=m: nc.scalar.activation(
                            out=mixt[m][:], in_=self.ps[bO][:, 0:192], func=AF.Copy, scale=ssh[:, h:h + 1]),
                            reads=[self.ps_t[bO], ssh_t[h]], writes=[mixt_t[m]])
                        self.ps_release(bO)
                        f0 = 192 * h
                        if h % 2 == 0:
                            cfull, chalf, phalf = f0 // 128, f0 // 128 + 1, 0
                            full_src, half_src = slice(0, 128), slice(128, 192)
                        else:
                            chalf, cfull, phalf = f0 // 128, f0 // 128 + 1, 64
                            half_src, full_src = slice(0, 64), slice(64, 192)
                        bT = self.ps_alloc()
                        pb = self.ps[bT][:].bitcast(BF16)
                        self.op(self.pe, lambda m=m, pb=pb, full_src=full_src: nc.tensor.transpose(
                            pb[:, 0:128], mixt[m][:, full_src], self.ident[:]),
                            reads=[mixt_t[m], self.const_t], writes=[self.ps_t[bT]], inc=False)
                        self.op(self.pe, lambda m=m, pb=pb, half_src=half_src, phalf=phalf: nc.tensor.transpose(
                            pb[phalf:phalf + 64, 128:256], mixt[m][:, half_src], self.ident[:]),
                            reads=[mixt_t[m], self.const_t], writes=[self.ps_t[bT]], inc=True, acc=True)
                        tcol = slice(tt * 128, (tt + 1) * 128)
                        self.op(self.dve, lambda pb=pb, cfull=cfull, tcol=tcol: nc.vector.tensor_scalar(
                            self.preT[:, cfull, tcol], pb[:, 0:128], self.gnw[:, 6 * j + cfull:6 * j + cfull + 1],
                            None, ALU.mult),
                            reads=[self.ps_t[bT], self.const_t], writes=[self.preT_t[cfull][T]])
                        self.op(self.dve, lambda pb=pb, chalf=chalf, phalf=phalf, tcol=tcol: nc.vector.tensor_scalar(
                            self.preT[phalf:phalf + 64, chalf, tcol], pb[phalf:phalf + 64, 128:256],
                            self.gnw[phalf:phalf + 64, 6 * j + chalf:6 * j + chalf + 1], None, ALU.mult),
                            reads=[self.ps_t[bT], self.const_t], writes=[self.preT_t[chalf][T]])
                        self.ps_release(bT)

                    nxt = front(0)
                    for sub in range(4):
                        cur = nxt
                        if sub < 3:
                            nxt = front(sub + 1)
                        back(sub, *cur)
                    self.op(self.dve, lambda h=h: nc.vector.tensor_copy(dec[:, h:h + 1], eb[:, 511:512]),
                            reads=[g_t], writes=[dec_t[h]])
            self.barrier()

    def mixer_b(self, j):
        nc = self.nc
        win = self.winb_d[j]
        with ExitStack() as es:
            sbt = lambda name, shape, dtype: es.enter_context(nc.sbuf_tensor(self.un(name), shape, dtype))
            qkT = sbt("qkT", [128, 2, SEQ], BF16)
            qkT_t = [Trk() for _ in range(4)]
            Vb = sbt("Vb", [128, 16, 128], BF16)
            Vb_t = [Trk() for _ in range(4)]
            qkt = [sbt(f"qkt{i}", [128, 2, 128], BF16) for i in range(3)]
            qkt_t = [Trk() for _ in range(3)]
            ta = sbt("ropeA", [128, 2, 2, 16], F32)
            tb = sbt("ropeB", [128, 2, 2, 16], F32)
            rope_t = Trk()
            pT = [sbt(f"pT{i}", [128, 256], BF16) for i in range(6)]
            pT_t = [Trk() for _ in range(6)]
            accN = sbt("accN", [128, SEQ], F32)
            accD = sbt("accD", [128, SEQ], F32)
            acc_t = [Trk() for _ in range(4)]
            pk = 0
            qk_i = 0
            carry = []
            for h in range(6):
                for g in range(3):
                    r = DIL[g]
                    nbp = 16 // r
                    base = g * 2304
                    s, st = self.wnext([("winb", j, base + 128 * h, 128, 0), ("winb", j, base + 768 + 128 * h, 128, 128),
                                        ("winb", j, base + 1536 + 128 * h, 128, 256)])
                    pend_tr = []
                    for b in range(16):
                        phase, jb = b // nbp, b % nbp
                        t0 = r * 128 * jb + phase
                        tsl = slice(t0, t0 + 127 * r + 1, r) if r > 1 else slice(t0, t0 + 128)
                        if r == 1:
                            hts = [self.hT_t[jb // 4]]
                        elif r == 4:
                            hts = [self.hT_t[jb]]
                        else:
                            hts = self.hT_t
                        bank = self.ps_alloc()
                        out = self.ps[bank][:, 0:384]
                        for kc in range(8):
                            self.mm(out, self.hT[:, kc, tsl], self.wslot[s][:, kc, :], start=(kc == 0), stop=(kc == 7),
                                    reads=[st] + list(hts), writes=[self.ps_t[bank]], inc=(kc == 7))
                        ps3 = out.rearrange("p (a d) -> p a d", a=3)
                        qi = qk_i % 3
                        qk_i += 1
                        rt = [self.ps_t[bank], self.const_t]
                        X = ps3[:, 0:2, 0:32].rearrange("p a (u d) -> p a u d", u=2)
                        Cb = self.cos[:, g, b, :].unsqueeze(1).unsqueeze(1).to_broadcast([128, 2, 2, 16])
                        Sb = self.sin[:, g, b, :].unsqueeze(1).unsqueeze(1).to_broadcast([128, 2, 2, 16])
                        self.op(self.dve, lambda X=X, Cb=Cb: nc.vector.tensor_tensor(ta[:], X, Cb, ALU.mult),
                                reads=rt, writes=[rope_t])
                        self.op(self.dve, lambda X=X, Sb=Sb: nc.vector.tensor_tensor(tb[:], X, Sb, ALU.mult),
                                reads=rt, writes=[rope_t])
                        self.op(self.dve, lambda qi=qi: nc.vector.tensor_tensor(
                            qkt[qi][:, :, 0:16], ta[:, :, 0, :], tb[:, :, 1, :], ALU.subtract),
                            reads=[rope_t], writes=[qkt_t[qi]])
                        self.op(self.dve, lambda qi=qi: nc.vector.tensor_tensor(
                            qkt[qi][:, :, 16:32], ta[:, :, 1, :], tb[:, :, 0, :], ALU.add),
                            reads=[rope_t], writes=[qkt_t[qi]])
                        self.op(self.act, lambda b=b, ps3=ps3: nc.scalar.copy(Vb[:, b, :], ps3[:, 2, :]),
                                reads=[self.ps_t[bank]], writes=[Vb_t[b // 4]])
                        self.op(self.act, lambda qi=qi, ps3=ps3: nc.scalar.copy(
                            qkt[qi][:, :, 32:128], ps3[:, 0:2, 32:128]),
                            reads=[self.ps_t[bank]], writes=[qkt_t[qi]])
                        self.ps_release(bank)
                        def emit_tr(b=b, qi=qi):
                            if b % 4 == 0:
                                self.bankT = self.ps_alloc()
                            bankT = self.bankT
                            pb = self.ps[bankT][:].bitcast(BF16).rearrange("p (a t) -> p a t", a=2)
                            for a in range(2):
                                self.op(self.pe, lambda a=a, pb=pb, qi=qi, b=b: nc.tensor.transpose(
                                    pb[:, a, (b % 4) * 128:(b % 4 + 1) * 128], qkt[qi][:, a, :], self.ident[:]),
                                    reads=[qkt_t[qi], self.const_t], writes=[self.ps_t[bankT]],
                                    inc=(a == 1), acc=not (b % 4 == 0 and a == 0))
                            if b % 4 == 3:
                                b0 = b - 3
                                self.op(self.act, lambda pb=pb, b0=b0: nc.scalar.copy(
                                    qkT[:, :, b0 * 128:(b0 + 4) * 128], pb),
                                    reads=[self.ps_t[bankT]], writes=[qkT_t[b // 4]])
                                self.ps_release(bankT)
                        pend_tr.append(emit_tr)
                        if len(pend_tr) > 2:
                            pend_tr.pop(0)()
                        if carry and b % 3 == 2:
                            carry.pop(0)()
                    while pend_tr:
                        pend_tr.pop(0)()
                    while carry:
                        carry.pop(0)()
                    if KSTAGE <= 2:
                        continue
                    bn = {}
                    bd = {}
                    started = set()
                    pend_pv = []
                    for kb in range(16):
                        phase, jb = kb // nbp, kb % nbp
                        has_next = jb < nbp - 1
                        N = 256 if has_next else 128
                        m = kb // 4
                        if m not in bn:
                            bn[m] = self.ps_alloc()
                            bd[m] = self.ps_alloc()
                        if has_next and (kb + 1) // 4 not in bn:
                            bn[m + 1] = self.ps_alloc()
                            bd[m + 1] = self.ps_alloc()
                        bs = self.ps_alloc()
                        qts = [qkT_t[kb // 4]] + ([qkT_t[(kb + 1) // 4]] if has_next else [])
                        self.mm(self.ps[bs][:, 0:N], qkT[:, 1, kb * 128:(kb + 1) * 128], qkT[:, 0, kb * 128:kb * 128 + N],
                                start=True, stop=False, reads=qts, writes=[self.ps_t[bs]], inc=False)
                        self.mm(self.ps[bs][:, 0:N], self.ident[:], self.maskb[:, 0:N],
                                start=False, stop=True, reads=[self.const_t], writes=[self.ps_t[bs]], inc=True)
                        p = pk % 6
                        pk += 1
                        self.op(self.act, lambda p=p, bs=bs, N=N: nc.scalar.activation(
                            out=pT[p][:, 0:N], in_=self.ps[bs][:, 0:N], func=AF.Exp, scale=128.0 ** -0.5),
                            reads=[self.ps_t[bs]], writes=[pT_t[p]])
                        self.ps_release(bs)
                        def emit_pv(kb=kb, has_next=has_next, m=m, p=p):
                            if has_next and (kb + 1) // 4 == m:
                                segs = [(m, (kb % 4) * 128, 0, 256)]
                            elif has_next:
                                segs = [(m, (kb % 4) * 128, 0, 128), (m + 1, 0, 128, 128)]
                            else:
                                segs = [(m, (kb % 4) * 128, 0, 128)]
                            for si_, (mm_, oc, pc, n) in enumerate(segs):
                                first = mm_ not in started
                                started.add(mm_)
                                last = (si_ == len(segs) - 1)
                                self.mm(self.ps[bn[mm_]][:, oc:oc + n], Vb[:, kb, :], pT[p][:, pc:pc + n], start=first, stop=False,
                                        reads=[Vb_t[kb // 4], pT_t[p]], writes=[self.ps_t[bn[mm_]]], inc=False)
                                self.mm(self.ps[bd[mm_]][:, oc:oc + n], self.ones[:], pT[p][:, pc:pc + n], start=first, stop=False,
                                        reads=[pT_t[p]], writes=[self.ps_t[bd[mm_]]], inc=last)
                            if kb % 4 == 3 and KSTAGE <= 3:
                                self.ps_release(bn[m])
                                self.ps_release(bd[m])
                            elif kb % 4 == 3:
                                if r == 1:
                                    dN, dD = accN[:, m * 512:(m + 1) * 512], accD[:, m * 512:(m + 1) * 512]
                                    sN, sD = self.ps[bn[m]][:, :], self.ps[bd[m]][:, :]
                                    trks = [acc_t[m]]
                                elif r == 4:
                                    dN = accN[:, m:SEQ:4]
                                    dD = accD[:, m:SEQ:4]
                                    sN, sD = self.ps[bn[m]][:, :], self.ps[bd[m]][:, :]
                                    trks = acc_t
                                else:
                                    dN = accN[:].rearrange("d (s ph) -> d ph s", ph=16)[:, 4 * m:4 * m + 4, :]
                                    dD = accD[:].rearrange("d (s ph) -> d ph s", ph=16)[:, 4 * m:4 * m + 4, :]
                                    sN = self.ps[bn[m]][:, :].rearrange("d (ph s) -> d ph s", ph=4)
                                    sD = self.ps[bd[m]][:, :].rearrange("d (ph s) -> d ph s", ph=4)
                                    trks = acc_t
                                def do_acc(dN=dN, dD=dD, sN=sN, sD=sD, trks=trks, bnm=bn[m], bdm=bd[m], g=g):
                                    if g == 0:
                                        self.op(self.act, lambda: nc.scalar.copy(dN, sN),
                                                reads=[self.ps_t[bnm]], writes=trks)
                                        self.op(self.act, lambda: nc.scalar.copy(dD, sD),
                                                reads=[self.ps_t[bdm]], writes=trks)
                                    else:
                                        self.op(self.dve, lambda: nc.vector.tensor_tensor(dN, sN, dN, ALU.add),
                                                reads=[self.ps_t[bnm]] + trks, writes=trks)
                                        self.op(self.dve, lambda: nc.vector.tensor_tensor(dD, sD, dD, ALU.add),
                                                reads=[self.ps_t[bdm]] + trks, writes=trks)
                                    self.ps_release(bnm)
                                    self.ps_release(bdm)
                                if m == 3:
                                    carry.append(do_acc)
                                else:
                                    do_acc()
                        pend_pv.append(emit_pv)
                        if len(pend_pv) > 4:
                            pend_pv.pop(0)()
                    while pend_pv:
                        pend_pv.pop(0)()
                for T in range(4 if KSTAGE > 4 else 0):
                    def do_fin(T=T, h=h):
                        cols = slice(T * 512, (T + 1) * 512)
                        self.op(self.dve, lambda: nc.vector.reciprocal(accD[:, cols], accD[:, cols]),
                                reads=[acc_t[T]], writes=[acc_t[T]])
                        self.op(self.dve, lambda: nc.vector.tensor_tensor(
                            self.preT[:, h, cols], accN[:, cols], accD[:, cols], ALU.mult),
                            reads=[acc_t[T]], writes=[self.preT_t[h][T]])
                    carry.append(do_fin)
            while carry:
                carry.pop(0)()
            self.barrier()


def _consts():
    p = np.arange(128)
    ident = np.eye(128, dtype=np.float32)
    mask2 = np.zeros((128, 256), np.float32)
    mask2[:, 0:128] = (p[:, None] <= p[None, :])
    mask2[:, 128:256] = (p[:, None] >= p[None, :])
    gmask = ((p[:, None] // 64 == p[None, :] // 64) & (p[:, None] <= p[None, :])).astype(np.float32)
    rmask = np.ones((128, 512), np.float32)
    rmask[:, 0::64] = 0.0
    half = 16
    inv = (np.float32(500000.0) ** (-(np.arange(half, dtype=np.float32) / np.float32(half)))).astype(np.float32)
    cos = np.zeros((128, 3, 16, 16), np.float32)
    sin = np.zeros((128, 3, 16, 16), np.float32)
    for g, r in enumerate(DIL):
        nbp = 16 // r
        for b in range(16):
            phase, jb = b // nbp, b % nbp
            t = (r * (128 * jb + p) + phase).astype(np.float32)
            ang = (t[:, None] * inv[None, :]).astype(np.float32)
            cos[:, g, b, :] = np.cos(ang)
            sin[:, g, b, :] = np.sin(ang)
    return dict(c_ident=ident, c_ones=np.ones((128, 128), np.float32), c_mask2=mask2, c_maskb=((mask2 - 1.0) * 30000.0).astype(np.float32), c_gmask=gmask, c_rmask=rmask,
                c_cos=cos.reshape(128, -1), c_sin=sin.reshape(128, -1))


_NC_CACHE = {}


def _get_nc(nseq, nlayers, dbg=None):
    key = (nseq, nlayers, None if dbg is None else tuple(sorted(dbg)))
    if key not in _NC_CACHE:
        k = Kern(nseq, nlayers, dbg)
        _NC_CACHE[key] = k.build()
    return _NC_CACHE[key]


def _layout_params(mem_norm_w, norm_w, w_memkv, w_out, w_in_a, w_gate_up, b_gate, gla_norm_w, w_in_b, final_norm_w):
    f = lambda a: np.ascontiguousarray(np.asarray(a, dtype=np.float32))
    fm8 = lambda v: f(np.asarray(v).reshape(8, 128).T)
    d = {}
    d["memnw_fm"] = fm8(mem_norm_w)
    d["normw_fm"] = f(np.concatenate([fm8(norm_w[i]) for i in range(4)], axis=1))
    d["w_memkv"] = f(w_memkv)
    d["w_out"] = f(w_out)
    d["w_in_a"] = f(w_in_a)
    d["w_gate_up"] = f(w_gate_up)
    bg = np.asarray(b_gate)
    d["bg_fm"] = f(np.concatenate([bg[j].reshape(4, 96).T for j in range(2)], axis=1))
    gn = np.asarray(gla_norm_w)
    idx = (np.arange(768) % 192).reshape(6, 128).T
    d["gnw_fm"] = f(np.concatenate([gn[j][idx] for j in range(2)], axis=1))
    d["w_in_b"] = f(w_in_b)
    d["fnw_bc"] = f(np.broadcast_to(np.asarray(final_norm_w)[None, :], (128, D)))
    d.update(_consts())
    return d


def kernel(x, mem, mem_norm_w, norm_w, w_memkv, w_out, w_in_a, w_gate_up, b_gate, gla_norm_w, w_in_b,
           final_norm_w, _nlayers=4, _nseq_launch=4):
    x = np.asarray(x, dtype=np.float32)
    mem = np.asarray(mem, dtype=np.float32)
    B = x.shape[0]
    per_core = B // NCORES
    params = _layout_params(mem_norm_w, norm_w, w_memkv, w_out, w_in_a, w_gate_up, b_gate, gla_norm_w,
                            w_in_b, final_norm_w)
    out = np.empty_like(x)
    nseq = _nseq_launch
    nc = _get_nc(nseq, _nlayers)
    for s0 in range(0, per_core, nseq):
        in_maps = []
        for c in range(NCORES):
            b0 = c * per_core + s0
            m = dict(params)
            m["x"] = np.ascontiguousarray(x[b0:b0 + nseq])
            m["mem"] = np.ascontiguousarray(mem[b0:b0 + nseq])
            in_maps.append(m)
        res = run_bass_kernel_spmd(nc, in_maps, core_ids=list(range(NCORES)))
        for c in range(NCORES):
            b0 = c * per_core + s0
            out[b0:b0 + nseq] = np.asarray(res.results[c]["out"]).reshape(nseq, SEQ, D)
    return out
```
